# Optimizing a Trainium2 kernel written in Bass

```python
import math
import jax, jax.numpy as jnp
from jax import lax
import numpy as np

D_MODEL = 2048
BATCH = 1
SEQ = 8192
DEPTH = 4

GRID_W = 64
CTX_LEN = 256
D_MIX = D_MODEL
GROUP_W = D_MIX // 4
S5_GROUP_CH = 16
S5_GROUPS = GROUP_W // S5_GROUP_CH
S5_STATE = 64
S5_DT_MIN = 0.001
S5_DT_MAX = 0.1
NA_HEAD_DIM = 64
NA_HEADS = GROUP_W // NA_HEAD_DIM
NA_WIN_H = 8
NA_WIN_W = 16
NA_QBLOCK = 128
SSD_HEAD_DIM = 64
SSD_HEADS = GROUP_W // SSD_HEAD_DIM
SSD_GROUPS = 2
SSD_STATE = 128
SSD_CONV = 3
SSD_CHUNK = 128
SSD_XBC = GROUP_W + 2 * SSD_GROUPS * SSD_STATE
GLA_HEADS = 4
GLA_DV = GROUP_W // GLA_HEADS
GLA_DK = GLA_DV // 2
GLA_RANK = 16
GLA_TAU = 16.0
GLA_CHUNK = 64
D_FF = ((8 * D_MODEL // 3 + 127) // 128) * 128
FFN_CONV = 3
ROPE_BASE = 10000.0
DEEPNORM_ALPHA = (2 * DEPTH) ** 0.25
DEEPNORM_BETA = (8 * DEPTH) ** -0.25
PROJ_SPLITS = (GROUP_W, GROUP_W, GROUP_W, GROUP_W, GROUP_W, SSD_XBC, 2 * SSD_HEADS,
               GLA_HEADS * GLA_DK, GLA_HEADS * GLA_DK, GROUP_W, GROUP_W, 2 * GLA_RANK)
D_PROJ = sum(PROJ_SPLITS)

kernel_name = "hybrid_s5_natten_ssd_gla_dit_block"


def layer_norm(x, gain=None, bias=None, eps=1e-6):
    xf = x.astype(jnp.float32)
    xc = xf - jnp.mean(xf, -1, keepdims=True)
    y = xc * lax.rsqrt(jnp.mean(xc * xc, -1, keepdims=True) + eps)
    if gain is not None:
        y = y * gain.astype(jnp.float32) + bias.astype(jnp.float32)
    return y.astype(x.dtype)


def rms_norm(x, gain, eps=1e-6):
    xf = x.astype(jnp.float32)
    y = xf * lax.rsqrt(jnp.mean(xf * xf, -1, keepdims=True) + eps) * gain.astype(jnp.float32)
    return y.astype(x.dtype)


def modulate(h, shift, scale):
    return h * (1.0 + scale) + shift


def dwconv_centred(x, w, b):
    k, ch = w.shape
    left = (k - 1) // 2
    y = lax.conv_general_dilated(x, w[:, None, :].astype(x.dtype), (1,), [(left, k - 1 - left)],
                                 dimension_numbers=('NWC', 'WIO', 'NWC'), feature_group_count=ch)
    return y + b


def axial_rope(x, rows, cols):
    n = x.shape[-1] // 4
    inv = ROPE_BASE ** (-jnp.arange(n, dtype=jnp.float32) / n)
    ang = jnp.concatenate([rows[:, None] * inv, cols[:, None] * inv], -1)
    cos = jnp.cos(ang)[None, :, None, :]
    sin = jnp.sin(ang)[None, :, None, :]
    x1, x2 = jnp.split(x.astype(jnp.float32), 2, -1)
    return jnp.concatenate([x1 * cos - x2 * sin, x1 * sin + x2 * cos], -1).astype(x.dtype)


def s5_discretise(lam_re, lam_im, log_step, b_re, b_im):
    step = jnp.exp(log_step.astype(jnp.float32))[:, None]
    lr, li = lam_re.astype(jnp.float32), lam_im.astype(jnp.float32)
    mag = jnp.exp(lr * step)
    ab_re, ab_im = mag * jnp.cos(li * step), mag * jnp.sin(li * step)
    den = lr * lr + li * li
    f_re = ((ab_re - 1.0) * lr + ab_im * li) / den
    f_im = (ab_im * lr - (ab_re - 1.0) * li) / den
    br, bi = b_re.astype(jnp.float32), b_im.astype(jnp.float32)
    bb_re = f_re[..., None] * br - f_im[..., None] * bi
    bb_im = f_re[..., None] * bi + f_im[..., None] * br
    return ab_re, ab_im, bb_re, bb_im


def _complex_affine_combine(e1, e2):
    a1r, a1i, b1r, b1i = e1
    a2r, a2i, b2r, b2i = e2
    return (a1r * a2r - a1i * a2i, a1r * a2i + a1i * a2r,
            a2r * b1r - a2i * b1i + b2r, a2r * b1i + a2i * b1r + b2i)


def s5_scan(u, ab_re, ab_im, bb_re, bb_im, h0_re, h0_im, reverse):
    uf = u.astype(jnp.float32)
    if reverse:
        uf = jnp.flip(uf, 1)
    bu_re = jnp.einsum('gpc,blgc->blgp', bb_re, uf)
    bu_im = jnp.einsum('gpc,blgc->blgp', bb_im, uf)
    bu_re = bu_re.at[:, 0].add(ab_re * h0_re - ab_im * h0_im)
    bu_im = bu_im.at[:, 0].add(ab_re * h0_im + ab_im * h0_re)
    a_re = jnp.broadcast_to(ab_re, bu_re.shape)
    a_im = jnp.broadcast_to(ab_im, bu_im.shape)
    _, _, s_re, s_im = lax.associative_scan(_complex_affine_combine, (a_re, a_im, bu_re, bu_im), axis=1)
    if reverse:
        s_re, s_im = jnp.flip(s_re, 1), jnp.flip(s_im, 1)
    return s_re, s_im


def s5_readout(s_re, s_im, c_re, c_im):
    y = (jnp.einsum('gcp,blgp->blgc', c_re.astype(jnp.float32), s_re)
         - jnp.einsum('gcp,blgp->blgc', c_im.astype(jnp.float32), s_im))
    return y.reshape(y.shape[0], y.shape[1], -1)


def s5_branch(u_c, u_x, lam_re, lam_im, log_step, b_re, b_im, c_re, c_im, d_skip, glu_w, glu_b, need_ctx):
    grp = lambda u: u.reshape(u.shape[0], u.shape[1], S5_GROUPS, S5_GROUP_CH)
    uc, ux = grp(u_c), grp(u_x)
    zero = jnp.zeros((u_c.shape[0], S5_GROUPS, S5_STATE), jnp.float32)
    y_x = d_skip * u_x.astype(jnp.float32)
    y_c = d_skip * u_c.astype(jnp.float32) if need_ctx else None
    for direction, reverse in ((0, False), (1, True)):
        disc = s5_discretise(lam_re[direction], lam_im[direction], log_step[direction],
                             b_re[direction], b_im[direction])
        sc_re, sc_im = s5_scan(uc, *disc, zero, zero, reverse)
        last = 0 if reverse else -1
        sx_re, sx_im = s5_scan(ux, *disc, sc_re[:, last], sc_im[:, last], reverse)
        y_x = y_x + s5_readout(sx_re, sx_im, c_re[direction], c_im[direction])
        if need_ctx:
            y_c = y_c + s5_readout(sc_re, sc_im, c_re[direction], c_im[direction])

    def glu(y):
        y = jax.nn.gelu(y)
        return y * jax.nn.sigmoid(y @ glu_w + glu_b)

    out_c = glu(y_c).astype(u_c.dtype) if need_ctx else None
    return out_c, glu(y_x).astype(u_x.dtype)


def neighbourhood_index(n_tokens):
    rows = n_tokens // GRID_W
    kh = min(NA_WIN_H, rows)
    r = np.arange(rows)
    cidx = np.arange(GRID_W)
    rs = np.clip(r - kh // 2, 0, rows - kh)
    cs = np.clip(cidx - NA_WIN_W // 2, 0, GRID_W - NA_WIN_W)
    key_r = rs[:, None] + np.arange(kh)
    key_c = cs[:, None] + np.arange(NA_WIN_W)
    key_idx = key_r[:, None, :, None] * GRID_W + key_c[None, :, None, :]
    dr = key_r - r[:, None]
    dc = key_c - cidx[:, None]
    bias_idx = ((dr[:, None, :, None] + NA_WIN_H - 1) * (2 * NA_WIN_W - 1)
                + (dc[None, :, None, :] + NA_WIN_W - 1))
    n_keys = kh * NA_WIN_W
    return key_idx.reshape(n_tokens, n_keys), bias_idx.reshape(n_tokens, n_keys)


def na_branch(q_c, k_c, v_c, q_x, k_x, v_x, rpb, need_ctx):
    bsz, n_tok, _ = q_x.shape
    scale = NA_HEAD_DIM ** -0.5
    heads = lambda t: t.reshape(t.shape[0], t.shape[1], NA_HEADS, NA_HEAD_DIM)
    qc, kc, vc = heads(q_c) * scale, heads(k_c), heads(v_c)
    qx, kx, vx = heads(q_x) * scale, heads(k_x), heads(v_x)
    out_c = None
    if need_ctx:
        p = jax.nn.softmax(jnp.einsum('bqhd,bkhd->bhqk', qc, kc).astype(jnp.float32), -1).astype(vc.dtype)
        out_c = jnp.einsum('bhqk,bkhd->bqhd', p, vc).reshape(bsz, -1, GROUP_W)
    key_idx, bias_idx = neighbourhood_index(n_tok)
    n_keys = key_idx.shape[1]
    nb = n_tok // NA_QBLOCK
    q_blocks = jnp.moveaxis(qx.reshape(bsz, nb, NA_QBLOCK, NA_HEADS, NA_HEAD_DIM), 1, 0)
    kidx = jnp.asarray(key_idx.reshape(nb, NA_QBLOCK, n_keys), jnp.int32)
    bidx = jnp.asarray(bias_idx.reshape(nb, NA_QBLOCK, n_keys), jnp.int32)
    rpb_flat = rpb.reshape(NA_HEADS, -1)

    def attend_block(args):
        qb, ki, bi = args
        kg = jnp.take(kx, ki, axis=1)
        vg = jnp.take(vx, ki, axis=1)
        s_loc = jnp.einsum('bqhd,bqkhd->bhqk', qb, kg) + rpb_flat[:, bi][None]
        s_ctx = jnp.einsum('bqhd,bkhd->bhqk', qb, kc)
        p = jax.nn.softmax(jnp.concatenate([s_loc, s_ctx], -1).astype(jnp.float32), -1).astype(vx.dtype)
        return (jnp.einsum('bhqk,bqkhd->bqhd', p[..., :n_keys], vg)
                + jnp.einsum('bhqk,bkhd->bqhd', p[..., n_keys:], vc))

    o = lax.map(attend_block, (q_blocks, kidx, bidx))
    out_x = jnp.moveaxis(o, 0, 1).reshape(bsz, n_tok, GROUP_W)
    return out_c, out_x


def ssd_scan(x, dt, a, bm, cm, h0, with_output):
    bsz, n_tok, nh, hp = x.shape
    ns = bm.shape[-1]
    q = SSD_CHUNK
    nc = n_tok // q
    xr = x.astype(jnp.float32).reshape(bsz, nc, q, nh, hp)
    dtr = dt.astype(jnp.float32).reshape(bsz, nc, q, nh)
    br = bm.astype(jnp.float32).reshape(bsz, nc, q, nh, ns)
    cr = cm.astype(jnp.float32).reshape(bsz, nc, q, nh, ns)
    acs = jnp.cumsum(dtr * a, axis=2)
    a_last = acs[:, :, -1]
    w_state = jnp.exp(a_last[:, :, None] - acs) * dtr
    chunk_states = jnp.einsum('bcqhn,bcqhp->bchpn', br, xr * w_state[..., None])

    def carry_chunk(h, inp):
        dec, st = inp
        return jnp.exp(dec)[:, :, None, None] * h + st, (h if with_output else None)

    h_final, h_prev = lax.scan(carry_chunk, h0.astype(jnp.float32),
                               (jnp.moveaxis(a_last, 1, 0), jnp.moveaxis(chunk_states, 1, 0)))
    if not with_output:
        return None, h_final
    h_prev = jnp.moveaxis(h_prev, 0, 1)
    causal = jnp.tril(jnp.ones((q, q), bool))
    seg = jnp.moveaxis(acs, 3, 2)
    decay = jnp.exp(jnp.where(causal, seg[..., :, None] - seg[..., None, :], -jnp.inf))
    scores = (jnp.einsum('bcihn,bcjhn->bchij', cr, br) * decay
              * jnp.moveaxis(dtr, 3, 2)[:, :, :, None, :])
    y = jnp.einsum('bchij,bcjhp->bcihp', scores, xr)
    y = y + jnp.einsum('bcihn,bchpn->bcihp', cr, h_prev) * jnp.exp(acs)[..., None]
    return y.reshape(bsz, n_tok, nh, hp), h_final


def ssd_direction(x, dt, a, bm, cm, h0, reverse, with_output):
    if reverse:
        x, dt, bm, cm = jnp.flip(x, 1), jnp.flip(dt, 1), jnp.flip(bm, 1), jnp.flip(cm, 1)
    y, h = ssd_scan(x, dt, a, bm, cm, h0, with_output)
    if reverse and y is not None:
        y = jnp.flip(y, 1)
    return y, h


def ssd_branch(z_c, xbc_c, dt_c, z_x, xbc_x, dt_x, conv_w, conv_b, dt_bias, a_log, d_skip, norm_w, need_ctx):
    rep = SSD_HEADS // SSD_GROUPS

    def prep(xbc):
        bsz, n_tok = xbc.shape[:2]
        xbc = jax.nn.silu(dwconv_centred(xbc, conv_w, conv_b))
        xs, bm, cm = jnp.split(xbc, [GROUP_W, GROUP_W + SSD_GROUPS * SSD_STATE], axis=-1)
        xs = xs.reshape(bsz, n_tok, SSD_HEADS, SSD_HEAD_DIM)
        bm = jnp.repeat(bm.reshape(bsz, n_tok, SSD_GROUPS, SSD_STATE), rep, axis=2)
        cm = jnp.repeat(cm.reshape(bsz, n_tok, SSD_GROUPS, SSD_STATE), rep, axis=2)
        return xs, bm, cm

    xs_c, bm_c, cm_c = prep(xbc_c)
    xs_x, bm_x, cm_x = prep(xbc_x)
    y_x = xs_x.astype(jnp.float32) * d_skip[:, None]
    y_c = xs_c.astype(jnp.float32) * d_skip[:, None] if need_ctx else None
    h0 = jnp.zeros((xs_c.shape[0], SSD_HEADS, SSD_HEAD_DIM, SSD_STATE), jnp.float32)
    for direction, reverse in ((0, False), (1, True)):
        a = -jnp.exp(a_log[direction].astype(jnp.float32))
        sl = slice(direction * SSD_HEADS, (direction + 1) * SSD_HEADS)
        dtc = jax.nn.softplus(dt_c[..., sl].astype(jnp.float32) + dt_bias[direction])
        dtx = jax.nn.softplus(dt_x[..., sl].astype(jnp.float32) + dt_bias[direction])
        yc_d, h_ctx = ssd_direction(xs_c, dtc, a, bm_c, cm_c, h0, reverse, need_ctx)
        yx_d, _ = ssd_direction(xs_x, dtx, a, bm_x, cm_x, h_ctx, reverse, True)
        y_x = y_x + yx_d
        if need_ctx:
            y_c = y_c + yc_d

    def gated_norm(y, z):
        y = y.reshape(z.shape)
        return rms_norm(y * jax.nn.silu(z.astype(jnp.float32)), norm_w).astype(z.dtype)

    out_c = gated_norm(y_c, z_c) if need_ctx else None
    return out_c, gated_norm(y_x, z_x)


def gla_scan(q, k, v, log_a, s0, with_output):
    bsz, n_tok, nh, dk = q.shape
    dv = v.shape[-1]
    qc = GLA_CHUNK
    nc = n_tok // qc
    qr = q.astype(jnp.float32).reshape(bsz, nc, qc, nh, dk)
    kr = k.astype(jnp.float32).reshape(bsz, nc, qc, nh, dk)
    vr = v.astype(jnp.float32).reshape(bsz, nc, qc, nh, dv)
    b = jnp.cumsum(log_a.astype(jnp.float32).reshape(bsz, nc, qc, nh, dk), axis=2)
    b_last = b[:, :, -1]
    chunk_kv = jnp.einsum('bcqhk,bcqhv->bchkv', kr * jnp.exp(b_last[:, :, None] - b), vr)

    def carry_chunk(s, inp):
        dec, kv = inp
        return jnp.exp(dec)[..., None] * s + kv, (s if with_output else None)

    s_final, s_prev = lax.scan(carry_chunk, s0.astype(jnp.float32),
                               (jnp.moveaxis(b_last, 1, 0), jnp.moveaxis(chunk_kv, 1, 0)))
    if not with_output:
        return None, s_final
    s_prev = jnp.moveaxis(s_prev, 0, 1)
    q_t = qr * jnp.exp(b)
    k_t = kr * jnp.exp(-b)
    causal = jnp.tril(jnp.ones((qc, qc), bool))
    att = jnp.where(causal, jnp.einsum('bcihk,bcjhk->bchij', q_t, k_t), 0.0)
    o = (jnp.einsum('bchij,bcjhv->bcihv', att, vr)
         + jnp.einsum('bcihk,bchkv->bcihv', q_t, s_prev))
    return o.reshape(bsz, n_tok, nh, dv), s_final


def gla_direction(q, k, v, log_a, s0, reverse, with_output):
    if reverse:
        q, k, v, log_a = jnp.flip(q, 1), jnp.flip(k, 1), jnp.flip(v, 1), jnp.flip(log_a, 1)
    o, s = gla_scan(q, k, v, log_a, s0, with_output)
    if reverse and o is not None:
        o = jnp.flip(o, 1)
    return o, s


def gla_branch(q_c, k_c, v_c, r_c, g_c, q_x, k_x, v_x, r_x, g_x, gate_w, gate_b, norm_w, need_ctx):
    heads = lambda t, d: t.reshape(t.shape[0], t.shape[1], GLA_HEADS, d)
    n_tok = q_x.shape[1]
    pos = jnp.arange(n_tok)
    rows = (pos // GRID_W).astype(jnp.float32)
    cols = (pos % GRID_W).astype(jnp.float32)
    kscale = GLA_DK ** -0.5
    qc, kc, vc = heads(q_c, GLA_DK), heads(k_c, GLA_DK) * kscale, heads(v_c, GLA_DV)
    qx = axial_rope(heads(q_x, GLA_DK), rows, cols)
    kx = axial_rope(heads(k_x, GLA_DK), rows, cols) * kscale
    vx = heads(v_x, GLA_DV)
    s0 = jnp.zeros((q_c.shape[0], GLA_HEADS, GLA_DK, GLA_DV), jnp.float32)
    o_x = None
    o_c = None
    for direction, reverse in ((0, False), (1, True)):
        lr = slice(direction * GLA_RANK, (direction + 1) * GLA_RANK)

        def log_gate(g):
            logits = (g[..., lr] @ gate_w[direction] + gate_b[direction]).astype(jnp.float32)
            return heads(jax.nn.log_sigmoid(logits) / GLA_TAU, GLA_DK)

        oc_d, s_ctx = gla_direction(qc, kc, vc, log_gate(g_c), s0, reverse, need_ctx)
        ox_d, _ = gla_direction(qx, kx, vx, log_gate(g_x), s_ctx, reverse, True)
        o_x = ox_d if o_x is None else o_x + ox_d
        if need_ctx:
            o_c = oc_d if o_c is None else o_c + oc_d

    def out(o, r):
        o = rms_norm(o, norm_w).reshape(r.shape)
        return (o * jax.nn.silu(r.astype(jnp.float32))).astype(r.dtype)

    out_c = out(o_c, r_c) if need_ctx else None
    return out_c, out(o_x, r_x)


def conv_ffn(h, w_up, conv_w, conv_b, w_down):
    u = dwconv_centred(h @ w_up, conv_w, conv_b)
    a, g = jnp.split(u, 2, axis=-1)
    return (a * jax.nn.silu(g)) @ w_down


def setup_inputs(seed: int = 0) -> dict:
    key = jax.random.key(seed)
    ks = iter(jax.random.split(key, 48))
    nrm = lambda shape, scale: scale * jax.random.normal(next(ks), shape, jnp.float32)
    beta = DEEPNORM_BETA
    n_idx = jnp.arange(S5_STATE, dtype=jnp.float32)
    s5_log_step = jax.random.uniform(next(ks), (DEPTH, 2, S5_GROUPS), jnp.float32,
                                     math.log(S5_DT_MIN), math.log(S5_DT_MAX))
    ssd_dt = jnp.exp(jax.random.uniform(next(ks), (DEPTH, 2, SSD_HEADS), jnp.float32,
                                        math.log(0.001), math.log(0.1)))
    ssd_dt_bias = ssd_dt + jnp.log(-jnp.expm1(-ssd_dt))
    ssd_a_log = jnp.log(jax.random.uniform(next(ks), (DEPTH, 2, SSD_HEADS), jnp.float32, 1.0, 16.0))
    return {
        "x": nrm((BATCH, SEQ, D_MODEL), 1.0),
        "c": nrm((BATCH, D_MODEL), 1.0),
        "ctx": nrm((BATCH, CTX_LEN, D_MODEL), 1.0),
        "c_ctx": nrm((D_MODEL,), 1.0),
        "w_mod": nrm((DEPTH, D_MODEL, 6 * D_MODEL), D_MODEL ** -0.5),
        "b_mod": nrm((DEPTH, 6 * D_MODEL), 0.01),
        "w_in": nrm((DEPTH, D_MODEL, D_PROJ), D_MODEL ** -0.5),
        "w_out": nrm((DEPTH, D_MIX, D_MODEL), beta * D_MIX ** -0.5),
        "s5_lam_re": -0.5 + nrm((DEPTH, 2, S5_GROUPS, S5_STATE), 0.01),
        "s5_lam_im": math.pi * n_idx + nrm((DEPTH, 2, S5_GROUPS, S5_STATE), 0.01),
        "s5_log_step": s5_log_step,
        "s5_b_re": nrm((DEPTH, 2, S5_GROUPS, S5_STATE, S5_GROUP_CH), (2 * S5_GROUP_CH) ** -0.5),
        "s5_b_im": nrm((DEPTH, 2, S5_GROUPS, S5_STATE, S5_GROUP_CH), (2 * S5_GROUP_CH) ** -0.5),
        "s5_c_re": nrm((DEPTH, 2, S5_GROUPS, S5_GROUP_CH, S5_STATE), 0.5),
        "s5_c_im": nrm((DEPTH, 2, S5_GROUPS, S5_GROUP_CH, S5_STATE), 0.5),
        "s5_d": nrm((DEPTH, GROUP_W), 1.0),
        "s5_glu_w": nrm((DEPTH, GROUP_W, GROUP_W), GROUP_W ** -0.5),
        "s5_glu_b": nrm((DEPTH, GROUP_W), 0.01),
        "na_rpb": nrm((DEPTH, NA_HEADS, 2 * NA_WIN_H - 1, 2 * NA_WIN_W - 1), 0.02),
        "ssd_conv_w": nrm((DEPTH, SSD_CONV, SSD_XBC), SSD_CONV ** -0.5),
        "ssd_conv_b": nrm((DEPTH, SSD_XBC), 0.01),
        "ssd_dt_bias": ssd_dt_bias,
        "ssd_a_log": ssd_a_log,
        "ssd_d": 1.0 + nrm((DEPTH, SSD_HEADS), 0.01),
        "ssd_norm_w": 1.0 + nrm((DEPTH, GROUP_W), 0.01),
        "gla_gate_w": nrm((DEPTH, 2, GLA_RANK, GLA_HEADS * GLA_DK), GLA_RANK ** -0.5),
        "gla_gate_b": nrm((DEPTH, 2, GLA_HEADS * GLA_DK), 0.1),
        "gla_norm_w": 1.0 + nrm((DEPTH, GLA_DV), 0.01),
        "ffn_up": nrm((DEPTH, D_MODEL, 2 * D_FF), D_MODEL ** -0.5),
        "ffn_conv_w": nrm((DEPTH, FFN_CONV, 2 * D_FF), FFN_CONV ** -0.5),
        "ffn_conv_b": nrm((DEPTH, 2 * D_FF), 0.01),
        "ffn_down": nrm((DEPTH, D_FF, D_MODEL), beta * D_FF ** -0.5),
        "ln_g": 1.0 + nrm((DEPTH, 2, D_MODEL), 0.01),
        "ln_b": nrm((DEPTH, 2, D_MODEL), 0.01),
    }


def reference(x, c, ctx, c_ctx, w_mod, b_mod, w_in, w_out, s5_lam_re, s5_lam_im, s5_log_step,
              s5_b_re, s5_b_im, s5_c_re, s5_c_im, s5_d, s5_glu_w, s5_glu_b, na_rpb, ssd_conv_w,
              ssd_conv_b, ssd_dt_bias, ssd_a_log, ssd_d, ssd_norm_w, gla_gate_w, gla_gate_b,
              gla_norm_w, ffn_up, ffn_conv_w, ffn_conv_b, ffn_down, ln_g, ln_b):
    alpha = DEEPNORM_ALPHA
    cuts = np.cumsum(PROJ_SPLITS)[:-1].tolist()
    h_ctx = ctx
    for l in range(DEPTH):
        need_ctx = l < DEPTH - 1
        mod_x = jax.nn.silu(c) @ w_mod[l] + b_mod[l]
        mod_c = (jax.nn.silu(c_ctx) @ w_mod[l] + b_mod[l])[None]
        sh1_x, sc1_x, g1_x, sh2_x, sc2_x, g2_x = jnp.split(mod_x[:, None], 6, -1)
        sh1_c, sc1_c, g1_c, sh2_c, sc2_c, g2_c = jnp.split(mod_c[:, None], 6, -1)

        px = modulate(layer_norm(x), sh1_x, sc1_x) @ w_in[l]
        pc = modulate(layer_norm(h_ctx), sh1_c, sc1_c) @ w_in[l]
        (s5u_x, naq_x, nak_x, nav_x, z_x, xbc_x, dt_x, gq_x, gk_x, gv_x, gr_x, gg_x) = jnp.split(px, cuts, -1)
        (s5u_c, naq_c, nak_c, nav_c, z_c, xbc_c, dt_c, gq_c, gk_c, gv_c, gr_c, gg_c) = jnp.split(pc, cuts, -1)

        s5_oc, s5_ox = s5_branch(s5u_c, s5u_x, s5_lam_re[l], s5_lam_im[l], s5_log_step[l], s5_b_re[l],
                                 s5_b_im[l], s5_c_re[l], s5_c_im[l], s5_d[l], s5_glu_w[l], s5_glu_b[l], need_ctx)
        na_oc, na_ox = na_branch(naq_c, nak_c, nav_c, naq_x, nak_x, nav_x, na_rpb[l], need_ctx)
        ssd_oc, ssd_ox = ssd_branch(z_c, xbc_c, dt_c, z_x, xbc_x, dt_x, ssd_conv_w[l], ssd_conv_b[l],
                                    ssd_dt_bias[l], ssd_a_log[l], ssd_d[l], ssd_norm_w[l], need_ctx)
        gla_oc, gla_ox = gla_branch(gq_c, gk_c, gv_c, gr_c, gg_c, gq_x, gk_x, gv_x, gr_x, gg_x,
                                    gla_gate_w[l], gla_gate_b[l], gla_norm_w[l], need_ctx)

        mix_x = jnp.concatenate([s5_ox, na_ox, ssd_ox, gla_ox], -1) @ w_out[l]
        x = layer_norm(alpha * x + g1_x * mix_x, ln_g[l, 0], ln_b[l, 0])
        ffn_x = conv_ffn(modulate(layer_norm(x), sh2_x, sc2_x), ffn_up[l], ffn_conv_w[l], ffn_conv_b[l], ffn_down[l])
        x = layer_norm(alpha * x + g2_x * ffn_x, ln_g[l, 1], ln_b[l, 1])

        if need_ctx:
            mix_c = jnp.concatenate([s5_oc, na_oc, ssd_oc, gla_oc], -1) @ w_out[l]
            h_ctx = layer_norm(alpha * h_ctx + g1_c * mix_c, ln_g[l, 0], ln_b[l, 0])
            ffn_c = conv_ffn(modulate(layer_norm(h_ctx), sh2_c, sc2_c), ffn_up[l], ffn_conv_w[l], ffn_conv_b[l], ffn_down[l])
            h_ctx = layer_norm(alpha * h_ctx + g2_c * ffn_c, ln_g[l, 1], ln_b[l, 1])
    return x
```

```python
import numpy as np
from contextlib import ExitStack
import concourse.bass as bass
import concourse.mybir as mybir
from concourse.bass_utils import run_bass_kernel_spmd

F32 = mybir.dt.float32
F32R = mybir.dt.float32r
BF16 = mybir.dt.bfloat16
AF = mybir.ActivationFunctionType
ALU = mybir.AluOpType
AX = mybir.AxisListType

NCORES = 8
D = 2048
KC = D // 128
DEPTH = 4
SEQ = 8192
CTX = 256
DPROJ = 5168
DFF = 5504


class Prog:
    ENG = ("pe", "act", "dve", "pool", "sp")

    def __init__(self, nc, ndma=24):
        self.nc = nc
        import os as _os
        self.maxops = int(_os.environ.get("PROG_MAXOPS", str(1 << 60)))
        nc.dge_precook = False
        self.stream = {e: [] for e in self.ENG}
        self.cnt = {e: 0 for e in self.ENG}
        self.clock = {e: {} for e in self.ENG}
        self.lastw = {}
        self.readers = {}
        self.ndma = ndma
        self.dma_uses = [0] * ndma
        self.dma_ev = [None] * ndma
        self.dma_rr = 0
        self.es = ExitStack()
        self.nalloc = 0

    def sb(self, shape, dtype=F32, name=None):
        self.nalloc += 1
        name = name or f"sb{self.nalloc}"
        return self.es.enter_context(self.nc.sbuf_tensor(name, list(shape), dtype))

    def ps(self, shape, dtype=F32, name=None):
        self.nalloc += 1
        name = name or f"ps{self.nalloc}"
        return self.es.enter_context(self.nc.psum_tensor(name, list(shape), dtype))

    def _deps(self, e, reads, writes):
        evs = []
        for k in reads:
            ev = self.lastw.get(k)
            if ev is not None:
                evs.append(ev)
        for k in writes:
            ev = self.lastw.get(k)
            if ev is not None:
                evs.append(ev)
            evs.extend(self.readers.get(k, ()))
        clk = self.clock[e]
        waits = {}
        for (sk, v, snap) in evs:
            if e == "pe" and sk == "pe":
                continue
            if clk.get(sk, 0) >= v:
                continue
            waits[sk] = max(waits.get(sk, 0), v)
            for s2, v2 in snap.items():
                if clk.get(s2, 0) < v2:
                    clk[s2] = v2
            clk[sk] = v
        return list(waits.items())

    def _commit(self, ev, reads, writes):
        for k in writes:
            self.lastw[k] = ev
            self.readers[k] = []
        for k in reads:
            if k in writes:
                continue
            self.readers.setdefault(k, []).append(ev)

    def op(self, e, fn, r=(), w=()):
        self.nops = getattr(self, "nops", 0) + 1
        if self.nops > getattr(self, "maxops", 1 << 60):
            return
        waits = self._deps(e, r, w)
        self.cnt[e] += 1
        idx = self.cnt[e]
        ev = (e, idx, dict(self.clock[e]))
        self.stream[e].append((waits, fn, (e, 1)))
        self._commit(ev, r, w)

    def dma(self, fn, r=(), w=(), q="sp"):
        self.nops = getattr(self, "nops", 0) + 1
        if self.nops > getattr(self, "maxops", 1 << 60):
            return
        k = self.dma_rr
        self.dma_rr = (self.dma_rr + 1) % self.ndma
        waits = self._deps(q, r, w)
        prev = self.dma_ev[k]
        if prev is not None:
            sk, v, snap = prev
            if self.clock[q].get(sk, 0) < v:
                waits.append((sk, v))
                self.clock[q][sk] = v
        self.dma_uses[k] += 1
        sk = ("d", k)
        ev = (sk, 16 * self.dma_uses[k], dict(self.clock[q]))
        self.dma_ev[k] = ev
        self.stream[q].append((waits, fn, (sk, 16)))
        self._commit(ev, r, w)

    def finish(self):
        waits = []
        for ev in self.dma_ev:
            if ev is not None and self.clock["sp"].get(ev[0], 0) < ev[1]:
                waits.append((ev[0], ev[1]))
        self.stream["sp"].append((waits, None, None))

    def emit(self):
        nc = self.nc
        sems = {}
        for e in self.ENG:
            sems[e] = self.es.enter_context(nc.semaphore(f"s_{e}"))
        for k in range(self.ndma):
            sems[("d", k)] = self.es.enter_context(nc.semaphore(f"s_d{k}"))
        streams = self.stream

        def replay(e, eng):
            for waits, fn, inc in streams[e]:
                for sk, v in waits:
                    eng.wait_ge(sems[sk], v)
                if fn is None:
                    continue
                ins = fn(eng)
                ins.then_inc(sems[inc[0]], inc[1])

        with nc.Block() as block:
            @block.tensor
            def _(eng):
                replay("pe", eng)

            @block.scalar
            def _(eng):
                replay("act", eng)

            @block.vector
            def _(eng):
                replay("dve", eng)

            @block.gpsimd
            def _(eng):
                replay("pool", eng)

            @block.sync
            def _(eng):
                replay("sp", eng)
        self.es.close()

    def mm(self, out, lhsT, rhs, start, stop, r, w):
        self.op("pe", lambda eng: eng.matmul(out, lhsT, rhs, start=start, stop=stop), r, w)

    def act(self, out, in_, func, r, w, bias=None, scale=None, accum_out=None):
        kw = {}
        if bias is not None:
            kw["bias"] = bias
        if scale is not None:
            kw["scale"] = scale
        if accum_out is not None:
            kw["accum_out"] = accum_out
        self.op("act", lambda eng: eng.activation(out, in_, func, **kw), r, w)

    def tt(self, out, a, b, op, r, w, e="dve"):
        self.op(e, lambda eng: eng.tensor_tensor(out, a, b, op), r, w)

    def ts(self, out, a, s1, s2, op0, op1, r, w, e="dve"):
        if s2 is None:
            self.op(e, lambda eng: eng.tensor_scalar(out, a, s1, None, op0), r, w)
        else:
            self.op(e, lambda eng: eng.tensor_scalar(out, a, s1, s2, op0, op1), r, w)

    def stt(self, out, a, s, b, op0, op1, r, w, e="dve"):
        self.op(e, lambda eng: eng.scalar_tensor_tensor(out, a, s, b, op0, op1), r, w)

    def copy(self, out, in_, r, w, e="dve"):
        if e == "act":
            self.op(e, lambda eng: eng.copy(out, in_), r, w)
        else:
            self.op(e, lambda eng: eng.tensor_copy(out, in_), r, w)

    def memset(self, ap, val, w, e="dve"):
        self.op(e, lambda eng: eng.memset(ap, val), (), w)

    def load(self, out, in_, w, r=(), q="sp"):
        self.dma(lambda eng: eng.dma_start(out=out, in_=in_), r, w, q)

    def store(self, out, in_, r, w=(), q="sp"):
        self.dma(lambda eng: eng.dma_start(out=out, in_=in_), r, w, q)


def R(ap):
    return ap.bitcast(F32R)


def run_spmd(nc, in_maps):
    res = run_bass_kernel_spmd(nc, in_maps, core_ids=list(range(NCORES)))
    return res.results


def ln_stats(p, xT, ncols, tiles, onesN, sq, ps_m, ps_q, mean, rstd, tag):
    for (c0, cn) in tiles:
        for k in range(KC):
            p.act(sq[:, 0:cn], xT[:, k, c0:c0 + cn], AF.Square, r=[(tag, "x", k)], w=["sq"])
            p.mm(ps_m[:, 0:cn], onesN[:], xT[:, k, c0:c0 + cn], k == 0, k == KC - 1,
                 r=[(tag, "x", k), "ones"], w=["ps_m"])
            p.mm(ps_q[:, 0:cn], onesN[:], sq[:, 0:cn], k == 0, k == KC - 1,
                 r=["sq", "ones"], w=["ps_q"])
        p.copy(mean[:, c0:c0 + cn], ps_m[:, 0:cn], r=["ps_m"], w=[(tag, "mean")], e="act")
        p.tt(rstd[:, c0:c0 + cn], mean[:, c0:c0 + cn], mean[:, c0:c0 + cn], ALU.mult,
             r=[(tag, "mean")], w=[(tag, "rstd")])
        p.tt(rstd[:, c0:c0 + cn], ps_q[:, 0:cn], rstd[:, c0:c0 + cn], ALU.subtract,
             r=["ps_q", (tag, "rstd")], w=[(tag, "rstd")])
        p.ts(rstd[:, c0:c0 + cn], rstd[:, c0:c0 + cn], 1e-6, None, ALU.add, None,
             r=[(tag, "rstd")], w=[(tag, "rstd")])
        p.act(rstd[:, c0:c0 + cn], rstd[:, c0:c0 + cn], AF.Sqrt, r=[(tag, "rstd")], w=[(tag, "rstd")])
        p.op("dve", lambda eng, a=rstd[:, c0:c0 + cn]: eng.reciprocal(a, a),
             r=[(tag, "rstd")], w=[(tag, "rstd")])


def build_phase_a(NT, NCX):
    nc = bass.Bass("TRN2", target_bir_lowering=False)
    xT_d = nc.dram_tensor("xT", [D, NT], F32, kind="ExternalInput").ap()
    mod_d = nc.dram_tensor("modA", [128, 4 * KC], F32, kind="ExternalInput").ap()
    w_d = nc.dram_tensor("w_in", [D, DPROJ], F32, kind="ExternalInput").ap()
    out_d = nc.dram_tensor("pxT", [DPROJ, NT], F32, kind="ExternalOutput").ap()
    p = Prog(nc)
    xT = p.sb([128, KC, NT])
    hT = xT
    mod = p.sb([128, 4 * KC])
    onesN = p.sb([128, 128])
    sq = p.sb([128, 512])
    mean = p.sb([128, NT])
    rstd = p.sb([128, NT])
    GW = 512
    wbuf = [p.sb([128, KC, GW]) for _ in range(2)]
    obuf = [p.sb([128, NT]) for _ in range(2)]
    ps_m = p.ps([128, 512])
    ps_q = p.ps([128, 512])
    ps_o = [p.ps([128, 512]) for _ in range(4)]

    tiles = []
    c = 0
    while c < NT:
        cn = min(352, NT - c)
        tiles.append((c, cn))
        c += cn

    p.memset(onesN[:], 1.0 / D, w=["ones"])
    modr = p.sb([128, 4 * KC])
    p.load(modr[:], mod_d, w=["modr"])
    p.copy(mod[:], modr[:], r=["modr"], w=["mod"])
    p.ts(mod[:, 0:KC], modr[:, 0:KC], 1.0, None, ALU.add, None, r=["modr", "mod"], w=["mod"])
    p.ts(mod[:, 2 * KC:3 * KC], modr[:, 2 * KC:3 * KC], 1.0, None, ALU.add, None, r=["modr", "mod"], w=["mod"])
    xv = xT_d.rearrange("(k p) n -> p k n", p=128)
    for k in range(KC):
        p.load(R(xT[:, k, :]), R(xv[:, k, :]), w=[("A", "x", k)])
    ln_stats(p, xT, NT, tiles, onesN, sq, ps_m, ps_q, mean, rstd, "A")
    for k in range(KC):
        p.tt(R(hT[:, k, :]), xT[:, k, :], mean[:], ALU.subtract, r=[("A", "x", k), ("A", "mean")],
             w=[("h", k), ("A", "x", k)], e="pool")
        p.tt(R(hT[:, k, :]), hT[:, k, :], rstd[:], ALU.mult, r=[("h", k), ("A", "rstd")], w=[("h", k)])
        if NCX > 0:
            p.act(R(hT[:, k, 0:NCX]), hT[:, k, 0:NCX], AF.Identity, r=[("h", k), "mod"], w=[("h", k)],
                  scale=mod[:, 2 * KC + k:2 * KC + k + 1], bias=mod[:, 3 * KC + k:3 * KC + k + 1])
        p.act(R(hT[:, k, NCX:NT]), hT[:, k, NCX:NT], AF.Identity, r=[("h", k), "mod"], w=[("h", k)],
              scale=mod[:, k:k + 1], bias=mod[:, KC + k:KC + k + 1])
    wv = w_d.rearrange("(k p) m -> p k m", p=128)
    ngrp = (DPROJ + GW - 1) // GW
    oi = 0
    for g in range(ngrp):
        g0 = g * GW
        gw = min(GW, DPROJ - g0)
        wb = wbuf[g % 2]
        for k in range(KC):
            p.load(R(wb[:, k, 0:gw]), R(wv[:, k, g0:g0 + gw]), w=[("w", g % 2, k)])
        for m0 in range(0, gw, 128):
            mw = min(128, gw - m0)
            ob = obuf[oi % 2]
            for ti, (c0, cn) in enumerate(tiles):
                ps = ps_o[ti % 4]
                for k in range(KC):
                    p.mm(ps[0:mw, 0:cn], R(wb[:, k, m0:m0 + mw]), R(hT[:, k, c0:c0 + cn]), k == 0, k == KC - 1,
                         r=[("w", g % 2, k), ("h", k)], w=[("pso", ti % 4)])
                p.copy(ob[0:mw, c0:c0 + cn], ps[0:mw, 0:cn], r=[("pso", ti % 4)], w=[("ob", oi % 2)],
                       e="act" if ti % 2 == 0 else "dve")
            p.store(out_d[g0 + m0:g0 + m0 + mw, :], ob[0:mw, :], r=[("ob", oi % 2)])
            oi += 1
    p.finish()
    p.emit()
    return nc


NPASS = 4
PC = 268
NCXC = 10
ALPHA = float((2 * DEPTH) ** 0.25)
NV_C = 4 + 4 + 4 + 1 + 64 + 86 * 4


def build_phase_c(inject=False):
    nc = bass.Bass("TRN2", target_bir_lowering=False)
    P_ = PC
    xT_d = nc.dram_tensor("xT", [NPASS, D, P_], F32, kind="ExternalInput").ap()
    mi_d = nc.dram_tensor("mixin", [NPASS, 40 * 128, P_], F32, kind="ExternalInput").ap()
    mod_d = nc.dram_tensor("modC", [128, 8 * KC], F32, kind="ExternalInput").ap()
    vec_d = nc.dram_tensor("vecs", [128, NV_C], F32, kind="ExternalInput").ap()
    hm_d = nc.dram_tensor("hmask", [NPASS, 128, 4], F32, kind="ExternalInput").ap()
    glu_d = nc.dram_tensor("glu_w", [512, 512], F32, kind="ExternalInput").ap()
    wo_d = nc.dram_tensor("w_out", [D, D], F32, kind="ExternalInput").ap()
    up_d = nc.dram_tensor("ffn_up", [D, 2 * DFF], F32, kind="ExternalInput").ap()
    dn_d = nc.dram_tensor("ffn_down", [DFF, D], F32, kind="ExternalInput").ap()
    out_d = nc.dram_tensor("x2T", [NPASS, D, P_], F32, kind="ExternalOutput").ap()
    dbg_d = nc.dram_tensor("mixT", [NPASS, D, P_], F32, kind="ExternalOutput").ap()
    p = Prog(nc)
    x = p.sb([128, KC, P_])
    hff = p.sb([128, 43, P_])
    mi = hff
    mix = p.sb([128, KC, P_])
    wb = [p.sb([128, 8192]) for _ in range(2)]
    mod = p.sb([128, 8 * KC])
    sc1 = p.sb([128, 2 * KC])
    vec = p.sb([128, NV_C])
    hm = p.sb([128, 4])
    glu = p.sb([128, 4, 512])
    onesN = p.sb([128, 128])
    sq = p.sb([128, 512])
    mean = p.sb([128, P_])
    rstd = p.sb([128, P_])
    t1 = p.sb([128, 4, P_])
    t2 = p.sb([128, 4, P_])
    gR = p.sb([128, 4, P_])
    ca = p.sb([128, P_])
    cg = p.sb([128, P_])
    ps_m = p.ps([128, 512])
    ps_q = p.ps([128, 512])
    psr = [p.ps([128, 512]) for _ in range(6)]
    pi = [0]

    def nps():
        pi[0] = (pi[0] + 1) % 6
        return psr[pi[0]], ("psr", pi[0])

    V_S5D, V_GLUB, V_SSDW, V_GLAW, V_LN, V_CW = 0, 4, 8, 12, 13, 77
    p.memset(onesN[:], 1.0, w=["ones"])
    zt = p.sb([128, 43, 1])
    p.memset(zt[:], 0.0, w=["zt"])
    p.load(mod[:], mod_d, w=["mod"])
    p.load(vec[:], vec_d, w=["vec"])
    p.load(R(glu[:]), R(glu_d.rearrange("(k p) m -> p k m", p=128)), w=["glu"])
    p.ts(sc1[:, 0:KC], mod[:, 3 * KC:4 * KC], 1.0, None, ALU.add, None, r=["mod"], w=["sc1"])
    p.ts(sc1[:, KC:2 * KC], mod[:, 5 * KC:6 * KC], 1.0, None, ALU.add, None, r=["mod"], w=["sc1"])
    tiles = [(0, P_)]
    wi = [0]

    def nwb():
        wi[0] = (wi[0] + 1) % 2
        return wb[wi[0]], ("wb", wi[0])

    def stats(tag):
        for k in range(KC):
            p.act(sq[:, 0:P_], x[:, k, :], AF.Square, r=[("x", k)], w=["sq"])
            p.mm(ps_m[:, 0:P_], onesN[:], x[:, k, :], k == 0, k == KC - 1, r=[("x", k), "ones"], w=["ps_m"])
            p.mm(ps_q[:, 0:P_], onesN[:], sq[:, 0:P_], k == 0, k == KC - 1, r=["sq", "ones"], w=["ps_q"])
        p.ts(mean[:], ps_m[:, 0:P_], 1.0 / D, None, ALU.mult, None, r=["ps_m"], w=["mean"])
        p.tt(rstd[:], mean[:], mean[:], ALU.mult, r=["mean"], w=["rstd"])
        p.stt(rstd[:], ps_q[:, 0:P_], 1.0 / D, rstd[:], ALU.mult, ALU.subtract, r=["ps_q", "rstd"], w=["rstd"])
        p.ts(rstd[:], rstd[:], 1e-6, None, ALU.add, None, r=["rstd"], w=["rstd"])
        p.act(rstd[:], rstd[:], AF.Sqrt, r=["rstd"], w=["rstd"])
        p.op("dve", lambda eng: eng.reciprocal(rstd[:], rstd[:]), r=["rstd"], w=["rstd"])

    def ln_affine(gcol, bcol):
        stats("x")
        for k in range(KC):
            p.tt(x[:, k, :], x[:, k, :], mean[:], ALU.subtract, r=[("x", k), "mean"], w=[("x", k)], e="pool")
            p.tt(x[:, k, :], x[:, k, :], rstd[:], ALU.mult, r=[("x", k), "rstd"], w=[("x", k)])
            p.act(x[:, k, :], x[:, k, :], AF.Identity, r=[("x", k), "vec"], w=[("x", k)],
                  scale=vec[:, gcol + k:gcol + k + 1], bias=vec[:, bcol + k:bcol + k + 1])

    def residual(ps, m, gx, gc):
        p.ts(x[:, m, :], x[:, m, :], ALPHA, None, ALU.mult, None, r=[("x", m)], w=[("x", m)], e="pool")
        p.stt(x[:, m, 0:NCXC], ps[:, 0:NCXC], mod[:, gc + m:gc + m + 1], x[:, m, 0:NCXC], ALU.mult, ALU.add,
              r=[("x", m), "mod", pk], w=[("x", m)])
        p.stt(x[:, m, NCXC:P_], ps[:, NCXC:P_], mod[:, gx + m:gx + m + 1], x[:, m, NCXC:P_], ALU.mult, ALU.add,
              r=[("x", m), "mod", pk], w=[("x", m)])

    for s in range(NPASS):
        xv = xT_d[s].rearrange("(k p) n -> p k n", p=128)
        miv = mi_d[s].rearrange("(k p) n -> p k n", p=128)
        for k in range(KC):
            p.load(x[:, k, :], xv[:, k, :], w=[("x", k)])
        for k in range(40):
            p.load(R(mi[:, k, :]), R(miv[:, k, :]), w=[("mi", k // 4), "hff"])
        p.load(hm[:], hm_d[s], w=["hm"])
        if inject:
            for k in range(KC):
                p.copy(R(mix[:, k, :]), mi[:, k, :], r=[("mi", k // 4)], w=[("mix", k // 4)], e="pool")
        else:
            p.tt(t1[:], mi[:, 0:4, :], mi[:, 4:8, :], ALU.add, r=[("mi", 0), ("mi", 1)], w=["t1"])
            for k in range(4):
                p.stt(t1[:, k, :], mi[:, 8 + k, :], vec[:, V_S5D + k:V_S5D + k + 1], t1[:, k, :], ALU.mult, ALU.add,
                      r=[("mi", 2), "vec", "t1"], w=["t1"])
            p.tt(t2[:], t1[:], t1[:], ALU.mult, r=["t1"], w=["t2"])
            p.ts(t2[:], t2[:], 0.044715, 1.0, ALU.mult, ALU.add, r=["t2"], w=["t2"])
            p.tt(t2[:], t2[:], t1[:], ALU.mult, r=["t1", "t2"], w=["t2"])
            p.act(t2[:], t2[:], AF.Sigmoid, r=["t2"], w=["t2"], scale=1.5957691216057308)
            p.tt(R(gR[:]), t1[:], t2[:], ALU.mult, r=["t1", "t2"], w=["gR"])
            for m in range(4):
                ps, pk = nps()
                for k in range(4):
                    p.mm(ps[:, 0:P_], R(glu[:, k, m * 128:(m + 1) * 128]), R(gR[:, k, :]), k == 0, k == 3,
                         r=["glu", "gR"], w=[pk])
                p.act(t2[:, m, :], ps[:, 0:P_], AF.Sigmoid, r=[pk, "vec"], w=["t2"],
                      bias=vec[:, V_GLUB + m:V_GLUB + m + 1])
            p.tt(R(mix[:, 0:4, :]), gR[:], t2[:], ALU.mult, r=["gR", "t2"], w=[("mix", 0)])
            p.copy(R(mix[:, 4:8, :]), mi[:, 12:16, :], r=[("mi", 3)], w=[("mix", 1)], e="pool")
            p.tt(t1[:], mi[:, 16:20, :], mi[:, 20:24, :], ALU.add, r=[("mi", 4), ("mi", 5)], w=["t1"])
            p.act(t2[:], mi[:, 24:28, :], AF.Silu, r=[("mi", 6)], w=["t2"])
            p.tt(t1[:], t1[:], t2[:], ALU.mult, r=["t1", "t2"], w=["t1"])
            p.tt(t2[:], t1[:], t1[:], ALU.mult, r=["t1"], w=["t2"])
            ps, pk = nps()
            for k in range(4):
                p.mm(ps[:, 0:P_], onesN[:], t2[:, k, :], k == 0, k == 3, r=["ones", "t2"], w=[pk])
            p.ts(ca[:], ps[:, 0:P_], 1.0 / 512, 1e-6, ALU.mult, ALU.add, r=[pk], w=["ca"])
            p.act(ca[:], ca[:], AF.Sqrt, r=["ca"], w=["ca"])
            p.op("dve", lambda eng: eng.reciprocal(ca[:], ca[:]), r=["ca"], w=["ca"])
            for k in range(4):
                p.stt(R(mix[:, 8 + k, :]), t1[:, k, :], vec[:, V_SSDW + k:V_SSDW + k + 1], ca[:], ALU.mult, ALU.mult,
                      r=["t1", "vec", "ca"], w=[("mix", 2)])
            p.tt(t1[:], mi[:, 28:32, :], mi[:, 32:36, :], ALU.add, r=[("mi", 7), ("mi", 8)], w=["t1"])
            p.tt(t2[:], t1[:], t1[:], ALU.mult, r=["t1"], w=["t2"])
            for k in range(4):
                ps, pk = nps()
                p.mm(ps[:, 0:P_], onesN[:], t2[:, k, :], True, True, r=["ones", "t2"], w=[pk])
                p.ts(cg[:], ps[:, 0:P_], 1.0 / 128, 1e-6, ALU.mult, ALU.add, r=[pk], w=["cg"])
                p.act(cg[:], cg[:], AF.Sqrt, r=["cg"], w=["cg"])
                p.op("dve", lambda eng: eng.reciprocal(cg[:], cg[:]), r=["cg"], w=["cg"])
                p.stt(t1[:, k, :], t1[:, k, :], vec[:, V_GLAW:V_GLAW + 1], cg[:], ALU.mult, ALU.mult,
                      r=["t1", "vec", "cg"], w=["t1"])
            p.act(t2[:], mi[:, 36:40, :], AF.Silu, r=[("mi", 9), "t2"], w=["t2"])
            p.tt(R(mix[:, 12:16, :]), t1[:], t2[:], ALU.mult, r=["t1", "t2"], w=[("mix", 3)])
        p.store(dbg_d[s].rearrange("(k p) n -> p k n", p=128), mix[:], r=[("mix", i) for i in range(4)])
        wov = wo_d.rearrange("(k p) m -> p k m", p=128)
        for g in range(4):
            wbt, wk = nwb()
            wview = wbt[:, :].rearrange("p (k m) -> p k m", k=KC)
            p.load(R(wview), R(wov[:, :, g * 512:(g + 1) * 512]), w=[wk])
            for mm_ in range(4):
                m = g * 4 + mm_
                ps, pk = nps()
                for k in range(KC):
                    p.mm(ps[:, 0:P_], R(wview[:, k, mm_ * 128:(mm_ + 1) * 128]), R(mix[:, k, :]), k == 0, k == KC - 1,
                         r=[wk, ("mix", k // 4)], w=[pk])
                residual(ps, m, 0 * KC, 1 * KC)
        ln_affine(V_LN, V_LN + 16)
        stats("x")
        for k in range(KC):
            p.tt(R(mix[:, k, :]), x[:, k, :], mean[:], ALU.subtract, r=[("x", k), "mean"], w=[("mix", k // 4)], e="pool")
            p.tt(R(mix[:, k, :]), mix[:, k, :], rstd[:], ALU.mult, r=["rstd", ("mix", k // 4)], w=[("mix", k // 4)])
            p.act(R(mix[:, k, 0:NCXC]), mix[:, k, 0:NCXC], AF.Identity, r=["sc1", "mod", ("mix", k // 4)],
                  w=[("mix", k // 4)], scale=sc1[:, KC + k:KC + k + 1], bias=mod[:, 4 * KC + k:4 * KC + k + 1])
            p.act(R(mix[:, k, NCXC:P_]), mix[:, k, NCXC:P_], AF.Identity, r=["sc1", "mod", ("mix", k // 4)],
                  w=[("mix", k // 4)], scale=sc1[:, k:k + 1], bias=mod[:, 2 * KC + k:2 * KC + k + 1])
        for hi, col in enumerate((0, NCXC - 1, NCXC, P_ - 1)):
            p.ts(R(mix[:, :, col:col + 1]), mix[:, :, col:col + 1], hm[:, hi:hi + 1], None, ALU.mult, None,
                 r=["hm"] + [("mix", i) for i in range(4)], w=[("mix", i) for i in range(4)])
        upv = up_d.rearrange("(k p) m -> p k m", p=128)
        p.copy(R(hff[:, :, 0:1]), zt[:], r=["zt"], w=["hff"] + [("mi", i) for i in range(10)])
        p.copy(R(hff[:, :, P_ - 1:P_]), zt[:], r=["zt"], w=["hff"])
        for j in range(43):
            wbt, wk = nwb()
            wview = wbt[:, 0:4096].rearrange("p (a k m) -> p a k m", a=2, k=KC)
            p.load(R(wview[:, 0]), R(upv[:, :, j * 128:(j + 1) * 128]), w=[wk])
            p.load(R(wview[:, 1]), R(upv[:, :, DFF + j * 128:DFF + (j + 1) * 128]), w=[wk])
            outs = []
            for a in range(2):
                ps, pk = nps()
                for k in range(KC):
                    p.mm(ps[:, 0:P_], R(wview[:, a, k, :]), R(mix[:, k, :]), k == 0, k == KC - 1,
                         r=[wk, ("mix", k // 4)], w=[pk])
                outs.append((ps, pk))
            for a, (ps, pk) in enumerate(outs):
                dst, dk = (ca, "ca") if a == 0 else (cg, "cg")
                c = V_CW + (a * 43 + j) * 4
                p.ts(dst[:, 1:P_ - 1], ps[:, 1:P_ - 1], vec[:, c + 1:c + 2], vec[:, c + 3:c + 4], ALU.mult, ALU.add,
                     r=[pk, "vec"], w=[dk])
                p.stt(dst[:, 1:P_ - 1], ps[:, 0:P_ - 2], vec[:, c:c + 1], dst[:, 1:P_ - 1], ALU.mult, ALU.add,
                      r=[pk, "vec", dk], w=[dk])
                p.stt(dst[:, 1:P_ - 1], ps[:, 2:P_], vec[:, c + 2:c + 3], dst[:, 1:P_ - 1], ALU.mult, ALU.add,
                      r=[pk, "vec", dk], w=[dk])
            p.act(cg[:, 1:P_ - 1], cg[:, 1:P_ - 1], AF.Silu, r=["cg"], w=["cg"])
            p.tt(R(hff[:, j, 1:P_ - 1]), ca[:, 1:P_ - 1], cg[:, 1:P_ - 1], ALU.mult, r=["ca", "cg"], w=["hff"], e="pool")
        dnv = dn_d.rearrange("(j p) m -> p j m", p=128)
        for m in range(KC):
            wbt, wk = nwb()
            wview = wbt[:, 0:43 * 128].rearrange("p (j m) -> p j m", j=43)
            p.load(R(wview), R(dnv[:, :, m * 128:(m + 1) * 128]), w=[wk])
            ps, pk = nps()
            for j in range(43):
                p.mm(ps[:, 0:P_], R(wview[:, j, :]), R(hff[:, j, :]), j == 0, j == 42, r=[wk, "hff"], w=[pk])
            residual(ps, m, 6 * KC, 7 * KC)
        ln_affine(V_LN + 32, V_LN + 48)
        p.store(out_d[s].rearrange("(k p) n -> p k n", p=128), x[:], r=[("x", k) for k in range(KC)])
    p.finish()
    p.emit()
    return nc


def chunkcols(v):
    v = np.asarray(v, np.float32)
    return np.ascontiguousarray(v.reshape(-1, 128).T)


def _pass_rows(i, s):
    c0 = 32 * i + 8 * s
    l0 = 1024 * i + 256 * s
    idx = np.empty(PC, np.int64)
    cr = np.arange(c0 - 1, c0 + 9)
    cr = np.where((cr >= 0) & (cr < CTX), cr, -1)
    lr = np.arange(l0 - 1, l0 + 257)
    lr = np.where((lr >= 0) & (lr < SEQ), lr + CTX, -1)
    idx[:NCXC] = cr
    idx[NCXC:] = lr
    return idx


def _gather_T(full, idx):
    out = full[np.maximum(idx, 0)].T.copy()
    out[:, idx < 0] = 0
    return np.ascontiguousarray(out, np.float32)


def phase_c_inputs(l, xfull, mixfull, pxsel, mod_x, mod_c, P, inject=False):
    mx = [mod_x[j * D:(j + 1) * D] for j in range(6)]
    mc = [mod_c[j * D:(j + 1) * D] for j in range(6)]
    modC = np.concatenate([chunkcols(v) for v in (mx[2], mc[2], mx[3], mx[4], mc[3], mc[4], mx[5], mc[5])], 1)
    vec = np.zeros((128, NV_C), np.float32)
    vec[:, 0:4] = chunkcols(P["s5_d"][l])
    vec[:, 4:8] = chunkcols(P["s5_glu_b"][l])
    vec[:, 8:12] = chunkcols(P["ssd_norm_w"][l])
    vec[:, 12:13] = chunkcols(P["gla_norm_w"][l])
    vec[:, 13:29] = chunkcols(P["ln_g"][l, 0])
    vec[:, 29:45] = chunkcols(P["ln_b"][l, 0])
    vec[:, 45:61] = chunkcols(P["ln_g"][l, 1])
    vec[:, 61:77] = chunkcols(P["ln_b"][l, 1])
    cw = P["ffn_conv_w"][l]
    cb = P["ffn_conv_b"][l]
    cv = np.stack([chunkcols(cw[0]), chunkcols(cw[1]), chunkcols(cw[2]), chunkcols(cb)], 2)
    vec[:, 77:] = cv.reshape(128, 86 * 4)
    in_maps = []
    for i in range(NCORES):
        xs, ms, hs = [], [], []
        for s in range(NPASS):
            idx = _pass_rows(i, s)
            xs.append(_gather_T(xfull, idx))
            m = np.zeros((40 * 128, PC), np.float32)
            mt = _gather_T(mixfull, idx)
            m[:mt.shape[0]] = mt
            ms.append(m)
            flags = (idx[[0, NCXC - 1, NCXC, PC - 1]] >= 0).astype(np.float32)
            hs.append(np.tile(flags[None, :], (128, 1)))
        in_maps.append({
            "xT": np.stack(xs), "mixin": np.stack(ms), "modC": modC, "vecs": vec, "hmask": np.stack(hs),
            "glu_w": np.ascontiguousarray(P["s5_glu_w"][l]), "w_out": np.ascontiguousarray(P["w_out"][l]),
            "ffn_up": np.ascontiguousarray(P["ffn_up"][l]), "ffn_down": np.ascontiguousarray(P["ffn_down"][l]),
        })
    return in_maps


def phase_c_gather(res, name):
    out = np.zeros((CTX + SEQ, D), np.float32)
    for i in range(NCORES):
        o = res[i][name]
        for s in range(NPASS):
            c0 = 32 * i + 8 * s
            l0 = 1024 * i + 256 * s
            out[c0:c0 + 8] = o[s][:, 1:9].T
            out[CTX + l0:CTX + l0 + 256] = o[s][:, NCXC + 1:PC - 1].T
    return out


def build_mod():
    nc = bass.Bass("TRN2", target_bir_lowering=False)
    NCOL = 6144
    w_d = nc.dram_tensor("w", [D, NCOL], F32, kind="ExternalInput").ap()
    c_d = nc.dram_tensor("cv", [128, KC, 2], F32, kind="ExternalInput").ap()
    b_d = nc.dram_tensor("b", [128, 48], F32, kind="ExternalInput").ap()
    o_d = nc.dram_tensor("mod", [128, 48, 2], F32, kind="ExternalOutput").ap()
    p = Prog(nc)
    cv = p.sb([128, KC, 2])
    bb = p.sb([128, 48])
    ob = p.sb([128, 48, 2])
    wb = [p.sb([128, KC, 512]) for _ in range(2)]
    pss = [p.ps([128, 512]) for _ in range(2)]
    cv0 = p.sb([128, KC, 2])
    p.load(cv0[:], c_d, w=["cv0"])
    p.load(bb[:], b_d, w=["b"])
    p.act(R(cv[:]), cv0[:], AF.Silu, r=["cv0"], w=["cv"])
    wv = w_d.rearrange("(k p) m -> p k m", p=128)
    for g in range(12):
        wt = wb[g % 2]
        for k in range(KC):
            p.load(R(wt[:, k, :]), R(wv[:, k, g * 512:(g + 1) * 512]), w=[("w", g % 2, k)])
        for mm_ in range(4):
            j = g * 4 + mm_
            ps = pss[j % 2]
            for k in range(KC):
                p.mm(ps[:, 0:2], R(wt[:, k, mm_ * 128:(mm_ + 1) * 128]), R(cv[:, k, :]), k == 0, k == KC - 1,
                     r=[("w", g % 2, k), "cv"], w=[("ps", j % 2)])
            p.ts(ob[:, j, :], ps[:, 0:2], bb[:, j:j + 1], None, ALU.add, None, r=[("ps", j % 2), "b"], w=["ob"])
    p.store(o_d, ob[:], r=["ob"])
    p.finish()
    p.emit()
    return nc


def run_mod(c, c_ctx, w_mod, b_mod):
    nc = build_mod()
    cvh = np.stack([chunkcols(c.reshape(-1)), chunkcols(c_ctx.reshape(-1))], 2)
    in_maps = []
    for i in range(NCORES):
        l, h = i // 2, i % 2
        in_maps.append({"w": np.ascontiguousarray(w_mod[l][:, h * 6144:(h + 1) * 6144]), "cv": cvh,
                        "b": chunkcols(b_mod[l][h * 6144:(h + 1) * 6144])})
    res = run_spmd(nc, in_maps)
    mod_x = np.zeros((DEPTH, 6 * D), np.float32)
    mod_c = np.zeros((DEPTH, 6 * D), np.float32)
    for i in range(NCORES):
        l, h = i // 2, i % 2
        o = res[i]["mod"]
        mod_x[l, h * 6144:(h + 1) * 6144] = o[:, :, 0].T.reshape(-1)
        mod_c[l, h * 6144:(h + 1) * 6144] = o[:, :, 1].T.reshape(-1)
    return mod_x, mod_c


TT = CTX + SEQ
I32 = mybir.dt.int32
TWO_PI_HI = 6.28125
TWO_PI_LO = 2.0 * np.pi - 6.28125


def trig(p, x, n, tmp, ki, cos_o, sin_o, tag):
    a, b, c = tmp
    kx = [tag + "a", tag + "b", tag + "c", tag + "k"]
    p.ts(a, x, 1.0 / (2.0 * np.pi), None, ALU.mult, None, r=[tag + "x"], w=[kx[0]])
    p.copy(ki, a, r=[kx[0]], w=[kx[3]])
    p.copy(a, ki, r=[kx[3]], w=[kx[0]])
    p.stt(b, a, -TWO_PI_HI, x, ALU.mult, ALU.add, r=[kx[0], tag + "x"], w=[kx[1]])
    p.stt(b, a, -TWO_PI_LO, b, ALU.mult, ALU.add, r=[kx[0], kx[1]], w=[kx[1]])
    p.act(a, b, AF.Sin, r=[kx[1]], w=[kx[0]], scale=0.25)
    p.ts(b, b, 0.25, float(np.pi / 2), ALU.mult, ALU.add, r=[kx[1]], w=[kx[1]])
    p.act(b, b, AF.Sin, r=[kx[1]], w=[kx[1]])
    for it in range(2):
        p.tt(c, a, b, ALU.mult, r=[kx[0], kx[1]], w=[kx[2]])
        p.tt(b, b, b, ALU.mult, r=[kx[1]], w=[kx[1]])
        p.tt(a, a, a, ALU.mult, r=[kx[0]], w=[kx[0]])
        p.tt(b, b, a, ALU.subtract, r=[kx[0], kx[1]], w=[kx[1]])
        p.ts(a, c, 2.0, None, ALU.mult, None, r=[kx[2]], w=[kx[0]])
    p.copy(cos_o, b, r=[kx[1]], w=[tag + "cos"])
    p.copy(sin_o, a, r=[kx[0]], w=[tag + "sin"])


def build_s5():
    nc = bass.Bass("TRN2", target_bir_lowering=False)
    u_d = nc.dram_tensor("u", [2, 64, TT], F32, kind="ExternalInput").ap()
    lp_d = nc.dram_tensor("lanep", [2, 2, 128, 3], F32, kind="ExternalInput").ap()
    bre_d = nc.dram_tensor("bre", [2, 2, 128, 16], F32, kind="ExternalInput").ap()
    bim_d = nc.dram_tensor("bim", [2, 2, 128, 16], F32, kind="ExternalInput").ap()
    cre_d = nc.dram_tensor("cre", [2, 2, 128, 64], F32, kind="ExternalInput").ap()
    cim_d = nc.dram_tensor("cim", [2, 2, 128, 64], F32, kind="ExternalInput").ap()
    tau_d = nc.dram_tensor("tau1", [128, 128], F32, kind="ExternalInput").ap()
    id_d = nc.dram_tensor("ident", [128, 128], F32, kind="ExternalInput").ap()
    y_d = nc.dram_tensor("y", [2, 64, TT], F32, kind="ExternalOutput").ap()
    p = Prog(nc)
    BL = 512
    u = p.sb([64, TT])
    yb = [p.sb([64, BL]) for _ in range(2)]
    tau = p.sb([128, 128])
    ident = p.sb([128, 128])
    lp = p.sb([128, 3])
    sc = p.sb([128, 16])
    braw = p.sb([128, 2, 16])
    bbf = p.sb([128, 2, 64])
    bbT = [[p.sb([64, 2, 128]) for _ in range(2)] for _ in range(2)]
    cc = [[p.sb([128, 2, 64]) for _ in range(2)] for _ in range(2)]
    cosT = [[p.sb([128, BL]) for _ in range(2)] for _ in range(2)]
    sinT = [[p.sb([128, BL]) for _ in range(2)] for _ in range(2)]
    rhoT = [[p.sb([128, 128]) for _ in range(2)] for _ in range(2)]
    cq = [[p.sb([128, 2]) for _ in range(2)] for _ in range(2)]
    tmp = [p.sb([128, 128]) for _ in range(3)]
    tki = p.sb([128, 128], I32)
    ang = p.sb([128, 128])
    b_re = p.sb([128, BL]); b_im = p.sb([128, BL])
    v_re = p.sb([128, BL]); v_im = p.sb([128, BL])
    m1 = p.sb([128, BL]); m2 = p.sb([128, BL])
    w_re = p.sb([128, BL]); w_im = p.sb([128, BL])
    s_re = p.sb([128, BL]); s_im = p.sb([128, BL])
    car = [p.sb([128, 2]) for _ in range(2)]
    ct = p.sb([128, 2])
    psb = [p.ps([128, 512]) for _ in range(4)]
    psy = [p.ps([128, 512]) for _ in range(2)]
    pst = p.ps([128, 512])

    p.load(tau[:], tau_d, w=["tau"])
    p.load(ident[:], id_d, w=["ident"])
    for d in range(2):
        for t in range(2):
            tg = f"s{d}{t}"
            p.load(lp[:], lp_d[d, t], w=["lp"])
            p.load(braw[:, 0, :], bre_d[d, t], w=["braw"])
            p.load(braw[:, 1, :], bim_d[d, t], w=["braw"])
            p.load(R(cc[d][t][:, 0, :]), R(cre_d[d, t]), w=[("cc", d, t)])
            p.load(R(cc[d][t][:, 1, :]), R(cim_d[d, t]), w=[("cc", d, t)])
            p.act(sc[:, 0:1], lp[:, 2:3], AF.Exp, r=["lp"], w=["sc"])
            p.tt(sc[:, 1:2], lp[:, 0:1], sc[:, 0:1], ALU.mult, r=["lp", "sc"], w=["sc"])
            p.act(sc[:, 2:3], sc[:, 1:2], AF.Exp, r=["sc"], w=["sc"])
            p.tt(sc[:, 3:4], lp[:, 1:2], sc[:, 0:1], ALU.mult, r=["lp", "sc"], w=["sc"])
            p.ts(ang[:], tau[:], sc[:, 3:4], None, ALU.mult, None, r=["tau", "sc"], w=[tg + "x"])
            trig(p, ang[:], 128, [tmp[0][:], tmp[1][:], tmp[2][:]], tki[:], cosT[d][t][:, 0:128], sinT[d][t][:, 0:128], tg)
            for rep in range(1, 4):
                p.copy(cosT[d][t][:, rep * 128:(rep + 1) * 128], cosT[d][t][:, 0:128], r=[tg + "cos"], w=[tg + "cos"], e="pool")
                p.copy(sinT[d][t][:, rep * 128:(rep + 1) * 128], sinT[d][t][:, 0:128], r=[tg + "sin"], w=[tg + "sin"], e="pool")
            p.copy(cq[d][t][:, 0:1], cosT[d][t][:, 127:128], r=[tg + "cos"], w=[("cq", d, t)])
            p.copy(cq[d][t][:, 1:2], sinT[d][t][:, 127:128], r=[tg + "sin"], w=[("cq", d, t)])
            p.memset(rhoT[d][t][:], 1.0, w=[("rho", d, t)])
            p.ts(rhoT[d][t][:], rhoT[d][t][:], sc[:, 2:3], None, ALU.mult, None, r=["sc", ("rho", d, t)], w=[("rho", d, t)])
            p.tt(sc[:, 4:5], sc[:, 2:3], cosT[d][t][:, 0:1], ALU.mult, r=["sc", tg + "cos"], w=["sc"])
            p.ts(sc[:, 4:5], sc[:, 4:5], -1.0, None, ALU.add, None, r=["sc"], w=["sc"])
            p.tt(sc[:, 5:6], sc[:, 2:3], sinT[d][t][:, 0:1], ALU.mult, r=["sc", tg + "sin"], w=["sc"])
            p.tt(sc[:, 6:7], lp[:, 0:1], lp[:, 0:1], ALU.mult, r=["lp"], w=["sc"])
            p.tt(sc[:, 9:10], lp[:, 1:2], lp[:, 1:2], ALU.mult, r=["lp"], w=["sc"])
            p.tt(sc[:, 6:7], sc[:, 6:7], sc[:, 9:10], ALU.add, r=["sc"], w=["sc"])
            p.op("dve", lambda eng: eng.reciprocal(sc[:, 6:7], sc[:, 6:7]), r=["sc"], w=["sc"])
            p.tt(sc[:, 7:8], sc[:, 4:5], lp[:, 0:1], ALU.mult, r=["sc", "lp"], w=["sc"])
            p.tt(sc[:, 9:10], sc[:, 5:6], lp[:, 1:2], ALU.mult, r=["sc", "lp"], w=["sc"])
            p.tt(sc[:, 7:8], sc[:, 7:8], sc[:, 9:10], ALU.add, r=["sc"], w=["sc"])
            p.tt(sc[:, 7:8], sc[:, 7:8], sc[:, 6:7], ALU.mult, r=["sc"], w=["sc"])
            p.tt(sc[:, 8:9], sc[:, 5:6], lp[:, 0:1], ALU.mult, r=["sc", "lp"], w=["sc"])
            p.tt(sc[:, 9:10], sc[:, 4:5], lp[:, 1:2], ALU.mult, r=["sc", "lp"], w=["sc"])
            p.tt(sc[:, 8:9], sc[:, 8:9], sc[:, 9:10], ALU.subtract, r=["sc"], w=["sc"])
            p.tt(sc[:, 8:9], sc[:, 8:9], sc[:, 6:7], ALU.mult, r=["sc"], w=["sc"])
            p.memset(bbf[:], 0.0, w=["bbf"])
            for half in range(2):
                rows = slice(64 * half, 64 * half + 64)
                co = 16 * (2 * t + half)
                p.ts(bbf[rows, 0, co:co + 16], braw[rows, 1, :], sc[rows, 8:9], -1.0, ALU.mult, ALU.mult,
                     r=["braw", "sc"], w=["bbf"])
                p.stt(bbf[rows, 0, co:co + 16], braw[rows, 0, :], sc[rows, 7:8], bbf[rows, 0, co:co + 16], ALU.mult, ALU.add,
                      r=["braw", "sc", "bbf"], w=["bbf"])
                p.ts(bbf[rows, 1, co:co + 16], braw[rows, 0, :], sc[rows, 8:9], None, ALU.mult, None,
                     r=["braw", "sc"], w=["bbf"])
                p.stt(bbf[rows, 1, co:co + 16], braw[rows, 1, :], sc[rows, 7:8], bbf[rows, 1, co:co + 16], ALU.mult, ALU.add,
                      r=["braw", "sc", "bbf"], w=["bbf"])
            for c2 in range(2):
                p.op("pe", lambda eng, o=pst[0:64, c2 * 128:(c2 + 1) * 128], i_=bbf[:, c2, :]: eng.transpose(o, i_, ident[:]),
                     r=["bbf", "ident"], w=["pst"])
            p.copy(R(bbT[d][t][:, :, :]), pst[0:64, 0:256].rearrange("p (a m) -> p a m", a=2), r=["pst"], w=[("bbT", d, t)], e="act")
            p.ts(R(cc[d][t][:, 1, :]), cc[d][t][:, 1, :], -1.0, None, ALU.mult, None, r=[("cc", d, t)], w=[("cc", d, t)])
            p.copy(R(cc[d][t][:, 0, :]), cc[d][t][:, 0, :], r=[("cc", d, t)], w=[("cc", d, t)])
    nblk = (TT + BL - 1) // BL
    for d in range(2):
        for k0 in range(0, TT, 2112):
            p.load(R(u[:, k0:k0 + 2112]), R(u_d[d][:, k0:k0 + 2112]), w=[("u", k0)])
        for t in range(2):
            p.memset(car[t][:], 0.0, w=[("car", t)])
        for bi in range(nblk):
            c0 = bi * BL
            cn = min(BL, TT - c0)
            uk = ("u", (c0 // 2112) * 2112)
            py = psy[bi % 2]
            pyk = ("psy", bi % 2)
            for t in range(2):
                pr, pim = psb[2 * t], psb[2 * t + 1]
                p.mm(pr[:, 0:cn], R(bbT[d][t][:, 0, :]), R(u[:, c0:c0 + cn]), True, True, r=[("bbT", d, t), uk], w=[("psb", 2 * t)])
                p.mm(pim[:, 0:cn], R(bbT[d][t][:, 1, :]), R(u[:, c0:c0 + cn]), True, True, r=[("bbT", d, t), uk], w=[("psb", 2 * t + 1)])
                p.copy(b_re[:, 0:cn], pr[:, 0:cn], r=[("psb", 2 * t)], w=["b_re"], e="act")
                p.copy(b_im[:, 0:cn], pim[:, 0:cn], r=[("psb", 2 * t + 1)], w=["b_im"], e="act")
                C_, S_ = cosT[d][t], sinT[d][t]
                tgc, tgs = f"s{d}{t}cos", f"s{d}{t}sin"
                p.tt(m1[:, 0:cn], b_re[:, 0:cn], C_[:, 0:cn], ALU.mult, r=["b_re", tgc], w=["m1"], e="pool")
                p.tt(m2[:, 0:cn], b_im[:, 0:cn], S_[:, 0:cn], ALU.mult, r=["b_im", tgs], w=["m2"], e="pool")
                p.tt(v_re[:, 0:cn], m1[:, 0:cn], m2[:, 0:cn], ALU.add, r=["m1", "m2"], w=["v_re"], e="pool")
                p.tt(m1[:, 0:cn], b_im[:, 0:cn], C_[:, 0:cn], ALU.mult, r=["b_im", tgc], w=["m1"], e="pool")
                p.tt(m2[:, 0:cn], b_re[:, 0:cn], S_[:, 0:cn], ALU.mult, r=["b_re", tgs], w=["m2"], e="pool")
                p.tt(v_im[:, 0:cn], m1[:, 0:cn], m2[:, 0:cn], ALU.subtract, r=["m1", "m2"], w=["v_im"], e="pool")
                for q0 in range(0, cn, 128):
                    sl = slice(q0, q0 + 128)
                    p.op("dve", lambda eng, o=w_re[:, sl], a=rhoT[d][t][:], b=v_re[:, sl], i_=car[t][:, 0:1]:
                         eng.tensor_tensor_scan(o, a, b, i_, ALU.mult, ALU.add),
                         r=[("rho", d, t), "v_re", ("car", t)], w=["w_re"])
                    p.op("dve", lambda eng, o=w_im[:, sl], a=rhoT[d][t][:], b=v_im[:, sl], i_=car[t][:, 1:2]:
                         eng.tensor_tensor_scan(o, a, b, i_, ALU.mult, ALU.add),
                         r=[("rho", d, t), "v_im", ("car", t)], w=["w_im"])
                    last = q0 + 127
                    cqt = cq[d][t]
                    p.ts(ct[:, 0:1], w_im[:, last:last + 1], cqt[:, 1:2], None, ALU.mult, None, r=["w_im", ("cq", d, t)], w=["ct"])
                    p.ts(ct[:, 1:2], w_re[:, last:last + 1], cqt[:, 1:2], None, ALU.mult, None, r=["w_re", ("cq", d, t)], w=["ct"])
                    p.stt(car[t][:, 0:1], w_re[:, last:last + 1], cqt[:, 0:1], ct[:, 0:1], ALU.mult, ALU.subtract,
                          r=["w_re", ("cq", d, t), "ct"], w=[("car", t)])
                    p.stt(car[t][:, 1:2], w_im[:, last:last + 1], cqt[:, 0:1], ct[:, 1:2], ALU.mult, ALU.add,
                          r=["w_im", ("cq", d, t), "ct"], w=[("car", t)])
                p.tt(m1[:, 0:cn], w_re[:, 0:cn], C_[:, 0:cn], ALU.mult, r=["w_re", tgc], w=["m1"], e="pool")
                p.tt(m2[:, 0:cn], w_im[:, 0:cn], S_[:, 0:cn], ALU.mult, r=["w_im", tgs], w=["m2"], e="pool")
                p.tt(R(s_re[:, 0:cn]), m1[:, 0:cn], m2[:, 0:cn], ALU.subtract, r=["m1", "m2"], w=["s_re"])
                p.tt(m1[:, 0:cn], w_im[:, 0:cn], C_[:, 0:cn], ALU.mult, r=["w_im", tgc], w=["m1"], e="pool")
                p.tt(m2[:, 0:cn], w_re[:, 0:cn], S_[:, 0:cn], ALU.mult, r=["w_re", tgs], w=["m2"], e="pool")
                p.tt(R(s_im[:, 0:cn]), m1[:, 0:cn], m2[:, 0:cn], ALU.add, r=["m1", "m2"], w=["s_im"])
                p.mm(py[0:64, 0:cn], R(cc[d][t][:, 0, :]), R(s_re[:, 0:cn]), t == 0, False, r=[("cc", d, t), "s_re"], w=[pyk])
                p.mm(py[0:64, 0:cn], R(cc[d][t][:, 1, :]), R(s_im[:, 0:cn]), False, t == 1, r=[("cc", d, t), "s_im"], w=[pyk])
            ybt = yb[bi % 2]
            p.copy(ybt[:, 0:cn], py[0:64, 0:cn], r=[pyk], w=[("yb", bi % 2)], e="act")
            p.store(y_d[d][:, c0:c0 + cn], ybt[:, 0:cn], r=[("yb", bi % 2)])
    p.finish()
    p.emit()
    return nc


def s5_inputs(l, u_full, P):
    tau1 = np.tile(np.arange(1, 129, dtype=np.float32)[None, :], (128, 1))
    ident = np.eye(128, dtype=np.float32)
    in_maps = []
    for i in range(NCORES):
        uc = u_full[:, 64 * i:64 * i + 64]
        uf = uc.T
        ub = np.concatenate([uc[:CTX][::-1], uc[CTX:][::-1]], 0).T
        lanep = np.zeros((2, 2, 128, 3), np.float32)
        bre = np.zeros((2, 2, 128, 16), np.float32)
        bim = np.zeros((2, 2, 128, 16), np.float32)
        cre = np.zeros((2, 2, 128, 64), np.float32)
        cim = np.zeros((2, 2, 128, 64), np.float32)
        for d in range(2):
            for t in range(2):
                for h in range(2):
                    gl = 2 * t + h
                    g = 4 * i + gl
                    rows = slice(64 * h, 64 * h + 64)
                    lanep[d, t, rows, 0] = P["s5_lam_re"][l, d, g]
                    lanep[d, t, rows, 1] = P["s5_lam_im"][l, d, g]
                    lanep[d, t, rows, 2] = P["s5_log_step"][l, d, g]
                    bre[d, t, rows] = P["s5_b_re"][l, d, g]
                    bim[d, t, rows] = P["s5_b_im"][l, d, g]
                    cre[d, t, rows, 16 * gl:16 * gl + 16] = P["s5_c_re"][l, d, g].T
                    cim[d, t, rows, 16 * gl:16 * gl + 16] = P["s5_c_im"][l, d, g].T
        in_maps.append({"u": np.ascontiguousarray(np.stack([uf, ub])), "lanep": lanep, "bre": bre, "bim": bim,
                        "cre": cre, "cim": cim, "tau1": tau1, "ident": ident})
    return in_maps


def unflip(y):
    yt = y.T
    return np.concatenate([yt[:CTX][::-1], yt[CTX:][::-1]], 0)


def s5_gather(res):
    yf = np.concatenate([res[i]["y"][0].T for i in range(NCORES)], 1)
    ybk = np.concatenate([unflip(res[i]["y"][1]) for i in range(NCORES)], 1)
    return yf, ybk


NEG = -30000.0


def na_cls(b):
    return 0 if b == 0 else 1 if b == 1 else 3 if b == 62 else 4 if b == 63 else 2


def na_kr0(b):
    return min(max(2 * b - 4, 0), 119)


def build_na():
    nc = bass.Bass("TRN2", target_bir_lowering=False)
    q_d = nc.dram_tensor("qT", [64, TT], F32, kind="ExternalInput").ap()
    k_d = nc.dram_tensor("kT", [64, TT], F32, kind="ExternalInput").ap()
    v_d = nc.dram_tensor("vt", [64, 132, 64], F32, kind="ExternalInput").ap()
    b_d = nc.dram_tensor("bias", [5, 128, 576], F32, kind="ExternalInput").ap()
    id_d = nc.dram_tensor("ident", [128, 128], F32, kind="ExternalInput").ap()
    o_d = nc.dram_tensor("oT", [64, TT], F32, kind="ExternalOutput").ap()
    p = Prog(nc)
    qT = p.sb([64, TT]); kT = p.sb([64, TT]); vt = p.sb([64, 132, 64]); oT = p.sb([64, TT])
    bias = p.sb([128, 5, 576])
    ident = p.sb([128, 128])
    S = p.sb([128, 832]); Pm = p.sb([128, 832])
    PTs = p.sb([64, 13, 128])
    dg = p.sb([128, 128])
    st = p.sb([128, 4])
    psA = p.ps([128, 512]); psB = p.ps([128, 512]); psC = p.ps([128, 512])
    psT = [p.ps([128, 512]) for _ in range(4)]
    pso = p.ps([128, 512])
    for k0 in range(0, TT, 2112):
        p.load(R(qT[:, k0:k0 + 2112]), R(q_d[:, k0:k0 + 2112]), w=["q"])
        p.load(R(kT[:, k0:k0 + 2112]), R(k_d[:, k0:k0 + 2112]), w=["k"])
    for r0 in range(0, 132, 33):
        p.load(R(vt[:, r0:r0 + 33, :]), R(v_d[:, r0:r0 + 33, :]), w=["v"])
    for c in range(5):
        p.load(bias[:, c, :], b_d[c], w=["bias"])
    p.load(ident[:], id_d, w=["ident"])

    def block(qc0, lat, b):
        nk = 832 if lat else 256
        if lat:
            kr0 = na_kr0(b)
            kc0 = CTX + 64 * kr0
            cls = na_cls(b)
            p.mm(psA[:, 0:288], R(qT[:, qc0:qc0 + 128]), R(kT[:, kc0:kc0 + 288]), True, True, r=["q", "k"], w=["psA"])
            p.mm(psB[:, 0:288], R(qT[:, qc0:qc0 + 128]), R(kT[:, kc0 + 288:kc0 + 576]), True, True, r=["q", "k"], w=["psB"])
            p.mm(psC[:, 0:256], R(qT[:, qc0:qc0 + 128]), R(kT[:, 0:256]), True, True, r=["q", "k"], w=["psC"])
            p.stt(S[:, 0:288], psA[:, 0:288], 0.125, bias[:, cls, 0:288], ALU.mult, ALU.add, r=["psA", "bias"], w=["S"])
            p.stt(S[:, 288:576], psB[:, 0:288], 0.125, bias[:, cls, 288:576], ALU.mult, ALU.add, r=["psB", "bias"], w=["S"])
            p.act(S[:, 576:832], psC[:, 0:256], AF.Copy, r=["psC"], w=["S"], scale=0.125)
        else:
            p.mm(psC[:, 0:256], R(qT[:, qc0:qc0 + 128]), R(kT[:, 0:256]), True, True, r=["q", "k"], w=["psC"])
            p.act(S[:, 0:256], psC[:, 0:256], AF.Copy, r=["psC"], w=["S"], scale=0.125)
        p.op("dve", lambda eng: eng.reduce_max(st[:, 0:1], S[:, 0:nk], AX.X), r=["S"], w=["st"])
        p.ts(st[:, 1:2], st[:, 0:1], -1.0, None, ALU.mult, None, r=["st"], w=["st"])
        p.act(R(Pm[:, 0:nk]), S[:, 0:nk], AF.Exp, r=["S", "st"], w=["P", "st2"], bias=st[:, 1:2], accum_out=st[:, 2:3])
        p.op("dve", lambda eng: eng.reciprocal(st[:, 3:4], st[:, 2:3]), r=["st2", "P"], w=["st3"])
        p.ts(R(dg[:]), ident[:], st[:, 3:4], None, ALU.mult, None, r=["ident", "st3"], w=["dg"])
        nt = nk // 64
        for kt in range(nt):
            bank = psT[kt // 4]
            p.mm(bank[0:64, (kt % 4) * 128:(kt % 4) * 128 + 128], R(Pm[:, kt * 64:(kt + 1) * 64]), R(dg[:]), True, True,
                 r=["P", "dg"], w=[("psT", kt // 4)])
        for bk in range((nt + 3) // 4):
            n4 = min(4, nt - 4 * bk)
            p.copy(R(PTs[:, 4 * bk:4 * bk + n4, :]), psT[bk][0:64, 0:n4 * 128].rearrange("p (a m) -> p a m", a=n4),
                   r=[("psT", bk)], w=["PTs"], e="act" if bk % 2 == 0 else "dve")
        for kt in range(nt):
            if lat:
                row = 4 + kr0 + kt if kt < 9 else kt - 9
            else:
                row = kt
            p.mm(pso[0:64, 0:128], R(vt[:, row, :]), R(PTs[:, kt, :]), kt == 0, kt == nt - 1, r=["v", "PTs"], w=["pso"])
        p.copy(oT[:, qc0:qc0 + 128], pso[0:64, 0:128], r=["pso"], w=["oT"], e="pool" if False else "act")

    for cb in range(2):
        block(128 * cb, False, cb)
    for b in range(64):
        block(CTX + 128 * b, True, b)
    for k0 in range(0, TT, 2112):
        p.store(o_d[:, k0:k0 + 2112], oT[:, k0:k0 + 2112], r=["oT"])
    p.finish()
    p.emit()
    return nc


def na_bias_tables(rpb_h):
    out = np.full((5, 128, 576), NEG, np.float32)
    for ci, b in enumerate((0, 1, 2, 62, 63)):
        kr0 = na_kr0(b)
        for q in range(128):
            r = 2 * b + q // 64
            c = q % 64
            rs = min(max(r - 4, 0), 120)
            cs = min(max(c - 8, 0), 48)
            for kr in range(rs, rs + 8):
                sl = (kr - kr0) * 64
                out[ci, q, sl + cs:sl + cs + 16] = rpb_h[kr - r + 7, cs - c + 15:cs - c + 31]
    return out


def na_inputs(l, q_full, k_full, v_full, P):
    ident = np.eye(128, dtype=np.float32)
    in_maps = []
    for i in range(NCORES):
        sl = slice(64 * i, 64 * i + 64)
        vt = v_full[:, sl].reshape(132, 64, 64).transpose(1, 0, 2)
        in_maps.append({"qT": np.ascontiguousarray(q_full[:, sl].T), "kT": np.ascontiguousarray(k_full[:, sl].T),
                        "vt": np.ascontiguousarray(vt), "bias": na_bias_tables(P["na_rpb"][l, i]), "ident": ident})
    return in_maps


def na_gather(res):
    return np.concatenate([res[i]["oT"].T for i in range(NCORES)], 1)


NCH = TT // 128


def build_ssd(nch=NCH, do_conv=True, do_setup=True):
    nc = bass.Bass("TRN2", target_bir_lowering=False)
    x_d = nc.dram_tensor("xbc", [2, 3, 128, TT], F32, kind="ExternalInput").ap()
    cw_d = nc.dram_tensor("cw", [2, 128, 3, 4], F32, kind="ExternalInput").ap()
    dt_d = nc.dram_tensor("dtm", [2, 128, NCH], F32, kind="ExternalInput").ap()
    sc_d = nc.dram_tensor("scal", [2, 128, 4], F32, kind="ExternalInput").ap()
    tri_d = nc.dram_tensor("tri", [128, 128], F32, kind="ExternalInput").ap()
    nm_d = nc.dram_tensor("negmask", [128, 128], F32, kind="ExternalInput").ap()
    id_d = nc.dram_tensor("ident", [128, 128], F32, kind="ExternalInput").ap()
    y_d = nc.dram_tensor("y", [2, 128, NCH, 64], F32, kind="ExternalOutput").ap()
    p = Prog(nc)
    raw = p.sb([128, TT])
    cv = [p.sb([128, TT]) for _ in range(3)]
    cw = p.sb([128, 3, 4]); scal = p.sb([128, 4])
    tri = p.sb([128, 128]); nm = p.sb([128, 128]); ident = p.sb([128, 128]); ones = p.sb([128, 128])
    zt = p.sb([128, 64])
    dt = p.sb([128, NCH]); dta = p.sb([128, NCH]); tA = p.sb([128, NCH]); tB = p.sb([128, NCH])
    nacs = p.sb([128, NCH]); wdec = p.sb([128, NCH]); dec = p.sb([128, NCH])
    ybuf = p.sb([128, NCH, 64])
    xdt = p.sb([128, 64]); Bw = p.sb([128, 128]); dtab = p.sb([128, 128])
    E = p.sb([128, 128]); CE = p.sb([128, 128]); Rm = p.sb([128, 128]); LT = p.sb([128, 128]); MT = p.sb([128, 128])
    hT = p.sb([128, 64])
    ps_x = p.ps([128, 512]); ps_B = p.ps([128, 512]); ps_R = p.ps([128, 512]); ps_CB = p.ps([128, 512])
    ps_y = p.ps([128, 512]); ps_h = p.ps([128, 512]); ps_s = p.ps([128, 512])
    p.load(tri[:], tri_d, w=["tri"]); p.load(nm[:], nm_d, w=["nm"]); p.load(ident[:], id_d, w=["ident"])
    p.memset(ones[:], 1.0, w=["ones"]); p.memset(zt[:], 0.0, w=["zt"])
    segs = [(0, CTX), (CTX, TT)]
    for d in range(2):
        p.load(cw[:], cw_d[d], w=["cw"]); p.load(scal[:], sc_d[d], w=["scal"]); p.load(dt[:], dt_d[d], w=["dt"])
        p.ts(dt[:], dt[:], scal[:, 0:1], None, ALU.add, None, r=["dt", "scal"], w=["dt"])
        p.ts(tA[:], dt[:], 0.0, None, ALU.max, None, r=["dt"], w=["tA"])
        p.ts(tB[:], dt[:], 0.0, None, ALU.min, None, r=["dt"], w=["tB"])
        p.tt(tB[:], tB[:], tA[:], ALU.subtract, r=["tA", "tB"], w=["tB"])
        p.act(tB[:], tB[:], AF.Exp, r=["tB"], w=["tB"])
        p.act(tB[:], tB[:], AF.Ln, r=["tB"], w=["tB"], bias=1.0)
        p.tt(dt[:], tA[:], tB[:], ALU.add, r=["tA", "tB"], w=["dt"])
        p.act(scal[:, 3:4], scal[:, 1:2], AF.Exp, r=["scal"], w=["scal"])
        p.ts(dta[:], dt[:], scal[:, 3:4], -1.0, ALU.mult, ALU.mult, r=["dt", "scal"], w=["dta"])
        p.mm(ps_s[:, 0:NCH], tri[:], dta[:], True, True, r=["tri", "dta"], w=["ps_s"])
        p.ts(nacs[:], ps_s[:, 0:NCH], -1.0, None, ALU.mult, None, r=["ps_s"], w=["nacs"])
        p.mm(ps_s[:, 0:NCH], ones[:], dta[:], True, True, r=["ones", "dta", "nacs"], w=["ps_s"])
        p.tt(wdec[:], ps_s[:, 0:NCH], nacs[:], ALU.add, r=["ps_s", "nacs"], w=["wdec"])
        p.act(wdec[:], wdec[:], AF.Exp, r=["wdec"], w=["wdec"])
        p.act(dec[:], ps_s[:, 0:NCH], AF.Exp, r=["ps_s"], w=["dec"])
        for ch in range(3 if do_conv else 0):
            np_ = 64 if ch == 0 else 128
            for k0 in range(0, TT, 2112):
                p.load(raw[0:np_, k0:k0 + 2112], x_d[d, ch, 0:np_, k0:k0 + 2112], w=["raw"])
            o = cv[ch]
            for (a, b) in segs:
                p.ts(R(o[0:np_, a:b]), raw[0:np_, a:b], cw[0:np_, ch, 1:2], cw[0:np_, ch, 3:4], ALU.mult, ALU.add,
                     r=["raw", "cw"], w=[("cv", ch)])
                p.stt(R(o[0:np_, a + 1:b]), raw[0:np_, a:b - 1], cw[0:np_, ch, 0:1], o[0:np_, a + 1:b], ALU.mult, ALU.add,
                      r=["raw", "cw", ("cv", ch)], w=[("cv", ch)])
                p.stt(R(o[0:np_, a:b - 1]), raw[0:np_, a + 1:b], cw[0:np_, ch, 2:3], o[0:np_, a:b - 1], ALU.mult, ALU.add,
                      r=["raw", "cw", ("cv", ch)], w=[("cv", ch)])
            p.act(R(o[0:np_, :]), o[0:np_, :], AF.Silu, r=[("cv", ch)], w=[("cv", ch)])
        xs, Bm, Cm = cv
        p.copy(R(hT[:]), zt[:], r=["zt"], w=["hT"])
        for c in range(nch):
            cols = slice(128 * c, 128 * c + 128)
            p.op("pe", lambda eng, i_=xs[0:64, cols]: eng.transpose(ps_x[:, 0:64], i_, ident[0:64, 0:64]),
                 r=[("cv", 0), "ident"], w=["ps_x"])
            p.op("pe", lambda eng, i_=Bm[:, cols]: eng.transpose(ps_B[:, 0:128], i_, ident[:]),
                 r=[("cv", 1), "ident"], w=["ps_B"])
            p.ts(R(xdt[:]), ps_x[:, 0:64], dt[:, c:c + 1], None, ALU.mult, None, r=["ps_x", "dt"], w=["xdt"])
            p.ts(R(Bw[:]), ps_B[:, 0:128], wdec[:, c:c + 1], None, ALU.mult, None, r=["ps_B", "wdec"], w=["Bw"])
            p.ts(dtab[:], ones[:], dta[:, c:c + 1], None, ALU.mult, None, r=["ones", "dta"], w=["dtab"])
            p.mm(ps_R[:, 0:128], dtab[:], tri[:], True, True, r=["dtab", "tri"], w=["ps_R"])
            p.act(E[:], ps_R[:, 0:128], AF.Exp, r=["ps_R"], w=["E"])
            p.tt(R(CE[:]), Cm[:, cols], E[:], ALU.mult, r=[("cv", 2), "E"], w=["CE"])
            p.tt(Rm[:], ps_R[:, 0:128], nm[:], ALU.add, r=["ps_R", "nm"], w=["Rm"])
            p.act(LT[:], Rm[:], AF.Exp, r=["Rm", "nacs"], w=["LT"], bias=nacs[:, c:c + 1])
            p.mm(ps_CB[:, 0:128], R(Bm[:, cols]), R(Cm[:, cols]), True, True, r=[("cv", 1), ("cv", 2)], w=["ps_CB"])
            p.tt(R(MT[:]), ps_CB[:, 0:128], LT[:], ALU.mult, r=["ps_CB", "LT"], w=["MT"])
            p.mm(ps_y[:, 0:64], R(MT[:]), R(xdt[:]), True, False, r=["MT", "xdt"], w=["ps_y"])
            p.mm(ps_y[:, 0:64], R(CE[:]), R(hT[:]), False, True, r=["CE", "hT"], w=["ps_y"])
            p.copy(ybuf[:, c, :], ps_y[:, 0:64], r=["ps_y"], w=[("yb", c)], e="act")
            if d == 0:
                p.stt(ybuf[:, c, :], ps_x[:, 0:64], scal[:, 2:3], ybuf[:, c, :], ALU.mult, ALU.add,
                      r=["ps_x", "scal", ("yb", c)], w=[("yb", c)])
            p.mm(ps_h[:, 0:64], R(Bw[:]), R(xdt[:]), True, True, r=["Bw", "xdt"], w=["ps_h"])
            p.ts(R(hT[:]), hT[:], dec[:, c:c + 1], None, ALU.mult, None, r=["hT", "dec"], w=["hT"])
            p.tt(R(hT[:]), hT[:], ps_h[:, 0:64], ALU.add, r=["hT", "ps_h"], w=["hT"])
        p.store(y_d[d], ybuf[:], r=[("yb", c) for c in range(NCH)])
    p.finish()
    p.emit()
    return nc


def flipseq(a):
    return np.concatenate([a[:CTX][::-1], a[CTX:][::-1]], 0)


def ssd_inputs(l, xbc_full, dt_full, P):
    jj = np.arange(128)
    tri = (jj[:, None] <= jj[None, :]).astype(np.float32)
    negmask = np.where(jj[:, None] <= jj[None, :], 0.0, NEG).astype(np.float32)
    ident = np.eye(128, dtype=np.float32)
    cwl, cbl = P["ssd_conv_w"][l], P["ssd_conv_b"][l]
    in_maps = []
    for i in range(NCORES):
        g = i // 4
        colsets = [np.arange(64 * i, 64 * i + 64), 512 + np.arange(128 * g, 128 * g + 128),
                   768 + np.arange(128 * g, 128 * g + 128)]
        xbc = np.zeros((2, 3, 128, TT), np.float32)
        cw = np.zeros((2, 128, 3, 4), np.float32)
        dtm = np.zeros((2, 128, NCH), np.float32)
        scal = np.zeros((2, 128, 4), np.float32)
        for d in range(2):
            for ch, cs in enumerate(colsets):
                a = xbc_full[:, cs]
                if d == 1:
                    a = flipseq(a)
                xbc[d, ch, :len(cs)] = a.T
                taps = cwl[:, cs] if d == 0 else cwl[::-1][:, cs]
                cw[d, :len(cs), ch, 0:3] = taps.T
                cw[d, :len(cs), ch, 3] = cbl[cs]
            dcol = dt_full[:, d * 8 + i]
            if d == 1:
                dcol = flipseq(dcol[:, None])[:, 0]
            dtm[d] = dcol.reshape(NCH, 128).T
            scal[d, :, 0] = P["ssd_dt_bias"][l, d, i]
            scal[d, :, 1] = P["ssd_a_log"][l, d, i]
            scal[d, :, 2] = P["ssd_d"][l, i]
        in_maps.append({"xbc": xbc, "cw": cw, "dtm": dtm, "scal": scal, "tri": tri, "negmask": negmask, "ident": ident})
    return in_maps


def tm_to_nat(y, rev):
    a = y.transpose(1, 0, 2).reshape(TT, -1)
    if rev:
        a = np.concatenate([a[:CTX][::-1], a[CTX:][::-1]], 0)
    return a


def ssd_gather(res):
    yf = np.concatenate([tm_to_nat(res[i]["y"][0], False) for i in range(NCORES)], 1)
    yb = np.concatenate([tm_to_nat(res[i]["y"][1], True) for i in range(NCORES)], 1)
    return yf, yb


NCG = TT // 64
LNK = float(np.log(64.0 ** -0.5))


def build_gla():
    nc = bass.Bass("TRN2", target_bir_lowering=False)
    qk_d = nc.dram_tensor("qk", [2, 2, 64, TT], F32, kind="ExternalInput").ap()
    v_d = nc.dram_tensor("vtm", [2, 64, NCG, 64], F32, kind="ExternalInput").ap()
    g_d = nc.dram_tensor("gT", [2, 16, TT], F32, kind="ExternalInput").ap()
    gw_d = nc.dram_tensor("gw", [2, 16, 64], F32, kind="ExternalInput").ap()
    gb_d = nc.dram_tensor("gb", [2, 64, 1], F32, kind="ExternalInput").ap()
    cs_d = nc.dram_tensor("cs", [2, 2, 64, TT], F32, kind="ExternalInput").ap()
    rot_d = nc.dram_tensor("rot", [64, 64], F32, kind="ExternalInput").ap()
    cm_d = nc.dram_tensor("cmask", [64, TT], F32, kind="ExternalInput").ap()
    um_d = nc.dram_tensor("umask", [64, 64], F32, kind="ExternalInput").ap()
    id_d = nc.dram_tensor("ident", [128, 128], F32, kind="ExternalInput").ap()
    o_d = nc.dram_tensor("o", [2, 64, NCG, 64], F32, kind="ExternalOutput").ap()
    p = Prog(nc)
    BL = 512
    Q = p.sb([64, TT]); Kt = p.sb([64, TT]); LA = p.sb([64, TT]); Bt = p.sb([64, TT])
    vtm = p.sb([64, NCG, 64])
    gT = vtm[:, :, :].rearrange("p c v -> p (c v)")[0:16, :]
    gw = p.sb([16, 64]); gb = p.sb([64, 1]); ngb = p.sb([64, 1])
    rot = p.sb([64, 64]); um = p.sb([64, 64]); ident = p.sb([128, 128])
    csb = [p.sb([64, 2, BL]) for _ in range(2)]
    t1 = p.sb([64, BL]); t2 = p.sb([64, BL])
    ebl = p.sb([64, NCG])
    attT = p.sb([64, 64]); ktm = p.sb([64, 64]); S = p.sb([64, 64]); zt = p.sb([64, 64])
    obuf = [p.sb([64, 8, 64]) for _ in range(2)]
    ps_l = p.ps([128, 512]); ps_r = p.ps([128, 512])
    ps_a = p.ps([128, 512]); ps_t = p.ps([128, 512]); ps_o = p.ps([128, 512]); ps_kv = p.ps([128, 512])
    p.load(R(rot[:]), R(rot_d), w=["rot"]); p.load(um[:], um_d, w=["um"]); p.load(ident[:], id_d, w=["ident"])
    p.memset(zt[:], 0.0, w=["zt"])
    nblk = (TT + BL - 1) // BL
    for d in range(2):
        for k0 in range(0, TT, 2112):
            p.load(R(Q[:, k0:k0 + 2112]), R(qk_d[d, 0][:, k0:k0 + 2112]), w=["Q"])
            p.load(R(Kt[:, k0:k0 + 2112]), R(qk_d[d, 1][:, k0:k0 + 2112]), w=["K"])
            p.load(R(gT[:, k0:k0 + 2112]), R(g_d[d][:, k0:k0 + 2112]), w=["v"])
            p.load(Bt[:, k0:k0 + 2112], cm_d[:, k0:k0 + 2112], w=["B"])
        p.load(R(gw[:]), R(gw_d[d]), w=["gw"]); p.load(gb[:], gb_d[d], w=["gb"])
        p.ts(ngb[:], gb[:], -1.0, None, ALU.mult, None, r=["gb"], w=["ngb"])
        for bi in range(nblk):
            c0 = bi * BL
            cn = min(BL, TT - c0)
            p.mm(ps_l[0:64, 0:cn], R(gw[:]), R(gT[:, c0:c0 + cn]), True, True, r=["gw", "v"], w=["ps_l"])
            p.act(t1[:, 0:cn], ps_l[0:64, 0:cn], AF.Exp, r=["ps_l", "ngb"], w=["t1"], scale=-1.0, bias=ngb[:, 0:1])
            p.act(t1[:, 0:cn], t1[:, 0:cn], AF.Ln, r=["t1"], w=["t1"], bias=1.0)
            p.ts(LA[:, c0:c0 + cn], t1[:, 0:cn], -1.0 / 16.0, None, ALU.mult, None, r=["t1"], w=["LA"])
        for c0 in range(0, NCG, 33):
            p.load(R(vtm[:, c0:c0 + 33, :]), R(v_d[d][:, c0:c0 + 33, :]), w=["v"])
        for h0 in range(0, TT, 2112):
            p.op("dve", lambda eng, o=Bt[:, h0:h0 + 2112], a=Bt[:, h0:h0 + 2112], b=LA[:, h0:h0 + 2112]:
                 eng.tensor_tensor_scan(o, a, b, 0.0, ALU.mult, ALU.add), r=["B", "LA"], w=["B"])
        p.act(LA[:], Bt[:], AF.Exp, r=["B"], w=["LA"])
        p.act(Bt[:], Bt[:], AF.Exp, r=["B"], w=["B"], scale=-1.0, bias=LNK)
        p.copy(ebl[:], LA[:, 63:TT:64], r=["LA"], w=["ebl"])
        for bi in range(nblk):
            c0 = bi * BL
            cn = min(BL, TT - c0)
            cb = csb[bi % 2]
            p.load(cb[:, 0, 0:cn], cs_d[d, 0][:, c0:c0 + cn], w=[("cs", bi % 2)])
            p.load(cb[:, 1, 0:cn], cs_d[d, 1][:, c0:c0 + cn], w=[("cs", bi % 2)])
            for X, xk, E, ek in ((Q, "Q", LA, "LA"), (Kt, "K", Bt, "B")):
                p.mm(ps_r[0:64, 0:cn], R(rot[:]), R(X[:, c0:c0 + cn]), True, True, r=["rot", xk], w=["ps_r"])
                p.tt(t1[:, 0:cn], ps_r[0:64, 0:cn], cb[:, 1, 0:cn], ALU.mult, r=["ps_r", ("cs", bi % 2)], w=["t1"])
                p.tt(t2[:, 0:cn], X[:, c0:c0 + cn], cb[:, 0, 0:cn], ALU.mult, r=[xk, ("cs", bi % 2)], w=["t2"], e="pool")
                p.tt(t1[:, 0:cn], t1[:, 0:cn], t2[:, 0:cn], ALU.add, r=["t1", "t2"], w=["t1"])
                p.tt(R(X[:, c0:c0 + cn]), t1[:, 0:cn], E[:, c0:c0 + cn], ALU.mult, r=["t1", ek], w=[xk])
        p.copy(R(S[:]), zt[:], r=["zt"], w=["S"])
        for c in range(NCG):
            cols = slice(64 * c, 64 * c + 64)
            ob = obuf[(c // 8) % 2]
            obk = ("ob", (c // 8) % 2)
            p.mm(ps_a[0:64, 0:64], R(Kt[:, cols]), R(Q[:, cols]), True, True, r=["K", "Q"], w=["ps_a"])
            p.tt(R(attT[:]), ps_a[0:64, 0:64], um[:], ALU.mult, r=["ps_a", "um"], w=["attT"])
            p.op("pe", lambda eng, i_=Kt[:, cols]: eng.transpose(ps_t[0:64, 0:64], i_, ident[0:64, 0:64]),
                 r=["K", "ident"], w=["ps_t"])
            p.copy(R(ktm[:]), ps_t[0:64, 0:64], r=["ps_t"], w=["ktm"], e="act")
            p.mm(ps_o[0:64, 0:64], R(attT[:]), R(vtm[:, c, :]), True, False, r=["attT", "v"], w=["ps_o"])
            p.mm(ps_o[0:64, 0:64], R(Q[:, cols]), R(S[:]), False, True, r=["Q", "S"], w=["ps_o"])
            p.copy(ob[:, c % 8, :], ps_o[0:64, 0:64], r=["ps_o"], w=[obk], e="act")
            p.mm(ps_kv[0:64, 0:64], R(ktm[:]), R(vtm[:, c, :]), True, True, r=["ktm", "v"], w=["ps_kv"])
            p.ts(R(S[:]), S[:], ebl[:, c:c + 1], None, ALU.mult, None, r=["S", "ebl"], w=["S"])
            p.stt(R(S[:]), ps_kv[0:64, 0:64], ebl[:, c:c + 1], S[:], ALU.mult, ALU.add, r=["ps_kv", "ebl", "S"], w=["S"])
            if c % 8 == 7 or c == NCG - 1:
                g0 = (c // 8) * 8
                p.store(o_d[d][:, g0:c + 1, :], ob[:, 0:c + 1 - g0, :], r=[obk])
    p.finish()
    p.emit()
    return nc


def rope_tables():
    pos = np.arange(SEQ)
    rows = (pos // 64).astype(np.float32)
    cols = (pos % 64).astype(np.float32)
    inv = (np.float32(10000.0) ** (-np.arange(16, dtype=np.float32) / np.float32(16))).astype(np.float32)
    ang = np.concatenate([rows[:, None] * inv, cols[:, None] * inv], -1).astype(np.float32)
    cos = np.cos(ang).astype(np.float32)
    sin = np.sin(ang).astype(np.float32)
    c2 = np.concatenate([np.ones((CTX, 64), np.float32), np.concatenate([cos, cos], 1)], 0)
    s2 = np.concatenate([np.zeros((CTX, 64), np.float32), np.concatenate([sin, sin], 1)], 0)
    return c2, s2


def gla_inputs(l, q_full, k_full, v_full, g_full, P):
    c2, s2 = rope_tables()
    rot = np.zeros((64, 64), np.float32)
    for m in range(32):
        rot[m + 32, m] = -1.0
        rot[m, m + 32] = 1.0
    cmask = np.ones((64, TT), np.float32)
    cmask[:, ::64] = 0.0
    jj = np.arange(64)
    umask = (jj[:, None] <= jj[None, :]).astype(np.float32)
    ident = np.eye(128, dtype=np.float32)
    tabs = []
    for d in range(2):
        a, b = (c2, s2) if d == 0 else (flipseq(c2), flipseq(s2))
        tabs.append(np.stack([a.T, b.T]))
    cs = np.ascontiguousarray(np.stack(tabs))
    in_maps = []
    for i in range(NCORES):
        hh, vh = i // 2, i % 2
        qk = np.zeros((2, 2, 64, TT), np.float32)
        vtm = np.zeros((2, 64, NCG, 64), np.float32)
        gT = np.zeros((2, 16, TT), np.float32)
        gw = np.zeros((2, 16, 64), np.float32)
        gb = np.zeros((2, 64, 1), np.float32)
        for d in range(2):
            f = (lambda a: a) if d == 0 else flipseq
            qk[d, 0] = f(q_full[:, 64 * hh:64 * hh + 64]).T
            qk[d, 1] = f(k_full[:, 64 * hh:64 * hh + 64]).T
            vv = f(v_full[:, 128 * hh + 64 * vh:128 * hh + 64 * vh + 64])
            vtm[d] = vv.reshape(NCG, 64, 64).transpose(1, 0, 2)
            gT[d] = f(g_full[:, 16 * d:16 * d + 16]).T
            gw[d] = P["gla_gate_w"][l, d][:, 64 * hh:64 * hh + 64]
            gb[d, :, 0] = P["gla_gate_b"][l, d][64 * hh:64 * hh + 64]
        in_maps.append({"qk": qk, "vtm": vtm, "gT": gT, "gw": gw, "gb": gb, "cs": cs, "rot": rot,
                        "cmask": cmask, "umask": umask, "ident": ident})
    return in_maps


def gla_tm_to_nat(o, rev):
    a = o.transpose(1, 0, 2).reshape(TT, -1)
    if rev:
        a = np.concatenate([a[:CTX][::-1], a[CTX:][::-1]], 0)
    return a


def gla_gather(res):
    of = np.concatenate([gla_tm_to_nat(res[i]["o"][0], False) for i in range(NCORES)], 1)
    ob = np.concatenate([gla_tm_to_nat(res[i]["o"][1], True) for i in range(NCORES)], 1)
    return of, ob


_PROGS = {}


def _prog(name, fn):
    if name not in _PROGS:
        _PROGS[name] = fn()
    return _PROGS[name]


def phase_a_run(l, xfull, mod_x, mod_c, w_in):
    mx = [mod_x[j * D:(j + 1) * D] for j in range(6)]
    mc = [mod_c[j * D:(j + 1) * D] for j in range(6)]
    modA = np.concatenate([chunkcols(v) for v in (mx[1], mx[0], mc[1], mc[0])], 1)
    nc = _prog("A", lambda: build_phase_a(1056, 32))
    in_maps = []
    for i in range(NCORES):
        xt = np.concatenate([xfull[32 * i:32 * i + 32], xfull[CTX + 1024 * i:CTX + 1024 * i + 1024]], 0).T
        in_maps.append({"xT": np.ascontiguousarray(xt), "modA": modA, "w_in": np.ascontiguousarray(w_in)})
    res = run_spmd(nc, in_maps)
    pf = np.zeros((TT, DPROJ), np.float32)
    for i in range(NCORES):
        o = res[i]["pxT"].T
        pf[32 * i:32 * i + 32] = o[:32]
        pf[CTX + 1024 * i:CTX + 1024 * i + 1024] = o[32:]
    return pf


def layer_forward(l, xfull, mod_x, mod_c, P):
    pf = phase_a_run(l, xfull, mod_x, mod_c, P["w_in"][l])
    u = pf[:, 0:512]
    naq, nak, nav = pf[:, 512:1024], pf[:, 1024:1536], pf[:, 1536:2048]
    z = pf[:, 2048:2560]
    xbc = pf[:, 2560:3584]
    dtc = pf[:, 3584:3600]
    gq, gk, gv, gr, gg = pf[:, 3600:3856], pf[:, 3856:4112], pf[:, 4112:4624], pf[:, 4624:5136], pf[:, 5136:5168]
    s5f, s5b = s5_gather(run_spmd(_prog("S5", build_s5), s5_inputs(l, u, P)))
    nao = na_gather(run_spmd(_prog("NA", build_na), na_inputs(l, naq, nak, nav, P)))
    ssf, ssb = ssd_gather(run_spmd(_prog("SSD", build_ssd), ssd_inputs(l, xbc, dtc, P)))
    glf, glb = gla_gather(run_spmd(_prog("GLA", build_gla), gla_inputs(l, gq, gk, gv, gg, P)))
    mixfull = np.concatenate([s5f, s5b, u, nao, ssf, ssb, z, glf, glb, gr], 1)
    res = run_spmd(_prog("C", build_phase_c), phase_c_inputs(l, xfull, mixfull, None, mod_x, mod_c, P))
    return phase_c_gather(res, "x2T"), res


def kernel(**inputs):
    P = {k: np.asarray(v, np.float32) for k, v in inputs.items()}
    mod_x, mod_c = run_mod(P["c"], P["c_ctx"], P["w_mod"], P["b_mod"])
    xfull = np.concatenate([P["ctx"][0], P["x"][0]], 0)
    for l in range(DEPTH):
        xfull, _ = layer_forward(l, xfull, mod_x[l], mod_c[l], P)
    return np.ascontiguousarray(xfull[CTX:][None]).astype(np.float32)
```

```python
import numpy as np
from contextlib import ExitStack
import concourse.bass as bass
import concourse.mybir as mybir
from concourse.bass_utils import run_bass_kernel_spmd

F32 = mybir.dt.float32
F32R = mybir.dt.float32r
BF16 = mybir.dt.bfloat16
AF = mybir.ActivationFunctionType
ALU = mybir.AluOpType
AX = mybir.AxisListType

NCORES = 8
D = 2048
KC = D // 128
DEPTH = 4
SEQ = 8192
CTX = 256
DPROJ = 5168
DFF = 5504


class Prog:
    ENG = ("pe", "act", "dve", "pool", "sp")

    def __init__(self, nc, ndma=24):
        self.nc = nc
        import os as _os
        self.maxops = int(_os.environ.get("PROG_MAXOPS", str(1 << 60)))
        nc.dge_precook = False
        self.stream = {e: [] for e in self.ENG}
        self.cnt = {e: 0 for e in self.ENG}
        self.clock = {e: {} for e in self.ENG}
        self.lastw = {}
        self.readers = {}
        self.ndma = ndma
        self.dma_uses = [0] * ndma
        self.dma_ev = [None] * ndma
        self.dma_rr = 0
        self.coll_extra = {}
        self.root = ExitStack()
        self.es = self.root
        self.nalloc = 0
        self.sems = None

    def sb(self, shape, dtype=F32, name=None):
        self.nalloc += 1
        name = name or f"sb{self.nalloc}"
        return self.es.enter_context(self.nc.sbuf_tensor(name, list(shape), dtype))

    def ps(self, shape, dtype=F32, name=None):
        self.nalloc += 1
        name = name or f"ps{self.nalloc}"
        return self.es.enter_context(self.nc.psum_tensor(name, list(shape), dtype))

    def _deps(self, e, reads, writes):
        evs = []
        for k in reads:
            ev = self.lastw.get(k)
            if ev is not None:
                evs.append(ev)
        for k in writes:
            ev = self.lastw.get(k)
            if ev is not None:
                evs.append(ev)
            evs.extend(self.readers.get(k, ()))
        clk = self.clock[e]
        waits = {}
        for (sk, v, snap) in evs:
            if e == "pe" and sk == "pe":
                continue
            if clk.get(sk, 0) >= v:
                continue
            waits[sk] = max(waits.get(sk, 0), v)
            for s2, v2 in snap.items():
                if clk.get(s2, 0) < v2:
                    clk[s2] = v2
            clk[sk] = v
        return list(waits.items())

    def _commit(self, ev, reads, writes):
        for k in writes:
            self.lastw[k] = ev
            self.readers[k] = []
        for k in reads:
            if k in writes:
                continue
            self.readers.setdefault(k, []).append(ev)

    def op(self, e, fn, r=(), w=()):
        self.nops = getattr(self, "nops", 0) + 1
        if self.nops > getattr(self, "maxops", 1 << 60):
            return
        waits = self._deps(e, r, w)
        self.cnt[e] += 1
        idx = self.cnt[e]
        ev = (e, idx, dict(self.clock[e]))
        self.stream[e].append((waits, fn, (e, 1)))
        self._commit(ev, r, w)

    def dma(self, fn, r=(), w=(), q="sp"):
        self.nops = getattr(self, "nops", 0) + 1
        if self.nops > getattr(self, "maxops", 1 << 60):
            return
        k = self.dma_rr
        self.dma_rr = (self.dma_rr + 1) % self.ndma
        waits = self._deps(q, r, w)
        prev = self.dma_ev[k]
        if prev is not None:
            sk, v, snap = prev
            if self.clock[q].get(sk, 0) < v:
                waits.append((sk, v))
                self.clock[q][sk] = v
        self.dma_uses[k] += 1
        sk = ("d", k)
        ev = (sk, 16 * self.dma_uses[k] + self.coll_extra.get(k, 0), dict(self.clock[q]))
        self.dma_ev[k] = ev
        self.stream[q].append((waits, fn, (sk, 16)))
        self._commit(ev, r, w)

    def coll(self, fn, r=(), w=()):
        self.nops = getattr(self, "nops", 0) + 1
        if self.nops > getattr(self, "maxops", 1 << 60):
            return
        k = self.dma_rr
        self.dma_rr = (self.dma_rr + 1) % self.ndma
        q = "pool"
        waits = self._deps(q, r, w)
        prev = self.dma_ev[k]
        if prev is not None:
            sk, v, snap = prev
            if self.clock[q].get(sk, 0) < v:
                waits.append((sk, v))
                self.clock[q][sk] = v
        sk = ("d", k)
        base = 16 * self.dma_uses[k] + self.coll_extra.get(k, 0)
        self.coll_extra[k] = self.coll_extra.get(k, 0) + 1
        ev = (sk, base + 1, dict(self.clock[q]))
        self.dma_ev[k] = ev
        self.stream[q].append((waits, fn, (sk, 1)))
        self._commit(ev, r, w)

    def finish(self):
        waits = []
        for ev in self.dma_ev:
            if ev is not None and self.clock["sp"].get(ev[0], 0) < ev[1]:
                waits.append((ev[0], ev[1]))
        self.stream["sp"].append((waits, None, None))

    def barrier(self):
        evs = [(e, self.cnt[e]) for e in self.ENG if self.cnt[e] > 0]
        evs += [(ev[0], ev[1]) for ev in self.dma_ev if ev is not None]
        for e in self.ENG:
            waits = []
            for sk, v in evs:
                if e == "pe" and sk == "pe":
                    continue
                if self.clock[e].get(sk, 0) < v:
                    waits.append((sk, v))
                    self.clock[e][sk] = v
            if waits:
                self.stream[e].append((waits, None, None))
        self.lastw.clear()
        self.readers.clear()

    def scope(self):
        prog = self

        class _Scope:
            def __enter__(self_):
                self_.old = prog.es
                prog.es = ExitStack()
                return prog

            def __exit__(self_, *a):
                if a[0] is None:
                    prog.barrier()
                    prog.flush()
                prog.es.close()
                prog.es = self_.old
                return False
        return _Scope()

    def emit(self):
        self.flush()
        self.root.close()

    def flush(self):
        nc = self.nc
        if self.sems is None:
            self.sems = {}
            for e in self.ENG:
                self.sems[e] = self.root.enter_context(nc.semaphore(f"s_{e}"))
            for k in range(self.ndma):
                self.sems[("d", k)] = self.root.enter_context(nc.semaphore(f"s_d{k}"))
        sems = self.sems
        streams = self.stream
        self.stream = {e: [] for e in self.ENG}

        def replay(e, eng):
            for waits, fn, inc in streams[e]:
                for sk, v in waits:
                    eng.wait_ge(sems[sk], v)
                if fn is None:
                    continue
                ins = fn(eng)
                ins.then_inc(sems[inc[0]], inc[1])

        with nc.Block() as block:
            @block.tensor
            def _(eng):
                replay("pe", eng)

            @block.scalar
            def _(eng):
                replay("act", eng)

            @block.vector
            def _(eng):
                replay("dve", eng)

            @block.gpsimd
            def _(eng):
                replay("pool", eng)

            @block.sync
            def _(eng):
                replay("sp", eng)

    def mm(self, out, lhsT, rhs, start, stop, r, w):
        self.op("pe", lambda eng: eng.matmul(out, lhsT, rhs, start=start, stop=stop), r, w)

    def act(self, out, in_, func, r, w, bias=None, scale=None, accum_out=None):
        kw = {}
        if bias is not None:
            kw["bias"] = bias
        if scale is not None:
            kw["scale"] = scale
        if accum_out is not None:
            kw["accum_out"] = accum_out
        self.op("act", lambda eng: eng.activation(out, in_, func, **kw), r, w)

    def tt(self, out, a, b, op, r, w, e="dve"):
        self.op(e, lambda eng: eng.tensor_tensor(out, a, b, op), r, w)

    def ts(self, out, a, s1, s2, op0, op1, r, w, e="dve"):
        if s2 is None:
            self.op(e, lambda eng: eng.tensor_scalar(out, a, s1, None, op0), r, w)
        else:
            self.op(e, lambda eng: eng.tensor_scalar(out, a, s1, s2, op0, op1), r, w)

    def stt(self, out, a, s, b, op0, op1, r, w, e="dve"):
        self.op(e, lambda eng: eng.scalar_tensor_tensor(out, a, s, b, op0, op1), r, w)

    def copy(self, out, in_, r, w, e="dve"):
        if e == "act":
            self.op(e, lambda eng: eng.copy(out, in_), r, w)
        else:
            self.op(e, lambda eng: eng.tensor_copy(out, in_), r, w)

    def memset(self, ap, val, w, e="dve"):
        self.op(e, lambda eng: eng.memset(ap, val), (), w)

    def load(self, out, in_, w, r=(), q="sp"):
        self.dma(lambda eng: eng.dma_start(out=out, in_=in_), r, w, q)

    def store(self, out, in_, r, w=(), q="sp"):
        self.dma(lambda eng: eng.dma_start(out=out, in_=in_), r, w, q)


def R(ap):
    return ap.bitcast(F32R)


def run_spmd(nc, in_maps):
    res = run_bass_kernel_spmd(nc, in_maps, core_ids=list(range(NCORES)))
    return res.results


def ln_stats(p, xT, ncols, tiles, onesN, sq, ps_m, ps_q, mean, rstd, tag):
    for (c0, cn) in tiles:
        for k in range(KC):
            p.act(sq[:, 0:cn], xT[:, k, c0:c0 + cn], AF.Square, r=[(tag, "x", k)], w=["sq"])
            p.mm(ps_m[:, 0:cn], onesN[:], xT[:, k, c0:c0 + cn], k == 0, k == KC - 1,
                 r=[(tag, "x", k), "ones"], w=["ps_m"])
            p.mm(ps_q[:, 0:cn], onesN[:], sq[:, 0:cn], k == 0, k == KC - 1,
                 r=["sq", "ones"], w=["ps_q"])
        p.copy(mean[:, c0:c0 + cn], ps_m[:, 0:cn], r=["ps_m"], w=[(tag, "mean")], e="act")
        p.tt(rstd[:, c0:c0 + cn], mean[:, c0:c0 + cn], mean[:, c0:c0 + cn], ALU.mult,
             r=[(tag, "mean")], w=[(tag, "rstd")])
        p.tt(rstd[:, c0:c0 + cn], ps_q[:, 0:cn], rstd[:, c0:c0 + cn], ALU.subtract,
             r=["ps_q", (tag, "rstd")], w=[(tag, "rstd")])
        p.ts(rstd[:, c0:c0 + cn], rstd[:, c0:c0 + cn], 1e-6, None, ALU.add, None,
             r=[(tag, "rstd")], w=[(tag, "rstd")])
        p.act(rstd[:, c0:c0 + cn], rstd[:, c0:c0 + cn], AF.Sqrt, r=[(tag, "rstd")], w=[(tag, "rstd")])
        p.op("dve", lambda eng, a=rstd[:, c0:c0 + cn]: eng.reciprocal(a, a),
             r=[(tag, "rstd")], w=[(tag, "rstd")])


def build_phase_a(NT, NCX):
    nc = bass.Bass("TRN2", target_bir_lowering=False)
    xT_d = nc.dram_tensor("xT", [D, NT], F32, kind="ExternalInput").ap()
    mod_d = nc.dram_tensor("modA", [128, 4 * KC], F32, kind="ExternalInput").ap()
    w_d = nc.dram_tensor("w_in", [D, DPROJ], F32, kind="ExternalInput").ap()
    out_d = nc.dram_tensor("pxT", [DPROJ, NT], F32, kind="ExternalOutput").ap()
    p = Prog(nc)
    xT = p.sb([128, KC, NT])
    hT = xT
    mod = p.sb([128, 4 * KC])
    onesN = p.sb([128, 128])
    sq = p.sb([128, 512])
    mean = p.sb([128, NT])
    rstd = p.sb([128, NT])
    GW = 512
    wbuf = [p.sb([128, KC, GW]) for _ in range(2)]
    obuf = [p.sb([128, NT]) for _ in range(2)]
    ps_m = p.ps([128, 512])
    ps_q = p.ps([128, 512])
    ps_o = [p.ps([128, 512]) for _ in range(4)]

    tiles = []
    c = 0
    while c < NT:
        cn = min(352, NT - c)
        tiles.append((c, cn))
        c += cn

    p.memset(onesN[:], 1.0 / D, w=["ones"])
    modr = p.sb([128, 4 * KC])
    p.load(modr[:], mod_d, w=["modr"])
    p.copy(mod[:], modr[:], r=["modr"], w=["mod"])
    p.ts(mod[:, 0:KC], modr[:, 0:KC], 1.0, None, ALU.add, None, r=["modr", "mod"], w=["mod"])
    p.ts(mod[:, 2 * KC:3 * KC], modr[:, 2 * KC:3 * KC], 1.0, None, ALU.add, None, r=["modr", "mod"], w=["mod"])
    xv = xT_d.rearrange("(k p) n -> p k n", p=128)
    for k in range(KC):
        p.load(R(xT[:, k, :]), R(xv[:, k, :]), w=[("A", "x", k)])
    ln_stats(p, xT, NT, tiles, onesN, sq, ps_m, ps_q, mean, rstd, "A")
    for k in range(KC):
        p.tt(R(hT[:, k, :]), xT[:, k, :], mean[:], ALU.subtract, r=[("A", "x", k), ("A", "mean")],
             w=[("h", k), ("A", "x", k)], e="pool")
        p.tt(R(hT[:, k, :]), hT[:, k, :], rstd[:], ALU.mult, r=[("h", k), ("A", "rstd")], w=[("h", k)])
        if NCX > 0:
            p.act(R(hT[:, k, 0:NCX]), hT[:, k, 0:NCX], AF.Identity, r=[("h", k), "mod"], w=[("h", k)],
                  scale=mod[:, 2 * KC + k:2 * KC + k + 1], bias=mod[:, 3 * KC + k:3 * KC + k + 1])
        p.act(R(hT[:, k, NCX:NT]), hT[:, k, NCX:NT], AF.Identity, r=[("h", k), "mod"], w=[("h", k)],
              scale=mod[:, k:k + 1], bias=mod[:, KC + k:KC + k + 1])
    wv = w_d.rearrange("(k p) m -> p k m", p=128)
    ngrp = (DPROJ + GW - 1) // GW
    oi = 0
    for g in range(ngrp):
        g0 = g * GW
        gw = min(GW, DPROJ - g0)
        wb = wbuf[g % 2]
        for k in range(KC):
            p.load(R(wb[:, k, 0:gw]), R(wv[:, k, g0:g0 + gw]), w=[("w", g % 2, k)])
        for m0 in range(0, gw, 128):
            mw = min(128, gw - m0)
            ob = obuf[oi % 2]
            for ti, (c0, cn) in enumerate(tiles):
                ps = ps_o[ti % 4]
                for k in range(KC):
                    p.mm(ps[0:mw, 0:cn], R(wb[:, k, m0:m0 + mw]), R(hT[:, k, c0:c0 + cn]), k == 0, k == KC - 1,
                         r=[("w", g % 2, k), ("h", k)], w=[("pso", ti % 4)])
                p.copy(ob[0:mw, c0:c0 + cn], ps[0:mw, 0:cn], r=[("pso", ti % 4)], w=[("ob", oi % 2)],
                       e="act" if ti % 2 == 0 else "dve")
            p.store(out_d[g0 + m0:g0 + m0 + mw, :], ob[0:mw, :], r=[("ob", oi % 2)])
            oi += 1
    p.finish()
    p.emit()
    return nc


NPASS = 4
PC = 268
NCXC = 10
ALPHA = float((2 * DEPTH) ** 0.25)
NV_C = 4 + 4 + 4 + 1 + 64 + 86 * 4


def build_phase_c(inject=False):
    nc = bass.Bass("TRN2", target_bir_lowering=False)
    P_ = PC
    xT_d = nc.dram_tensor("xT", [NPASS, D, P_], F32, kind="ExternalInput").ap()
    mi_d = nc.dram_tensor("mixin", [NPASS, 40 * 128, P_], F32, kind="ExternalInput").ap()
    mod_d = nc.dram_tensor("modC", [128, 8 * KC], F32, kind="ExternalInput").ap()
    vec_d = nc.dram_tensor("vecs", [128, NV_C], F32, kind="ExternalInput").ap()
    hm_d = nc.dram_tensor("hmask", [NPASS, 128, 4], F32, kind="ExternalInput").ap()
    glu_d = nc.dram_tensor("glu_w", [512, 512], F32, kind="ExternalInput").ap()
    wo_d = nc.dram_tensor("w_out", [D, D], F32, kind="ExternalInput").ap()
    up_d = nc.dram_tensor("ffn_up", [D, 2 * DFF], F32, kind="ExternalInput").ap()
    dn_d = nc.dram_tensor("ffn_down", [DFF, D], F32, kind="ExternalInput").ap()
    out_d = nc.dram_tensor("x2T", [NPASS, D, P_], F32, kind="ExternalOutput").ap()
    dbg_d = nc.dram_tensor("mixT", [NPASS, D, P_], F32, kind="ExternalOutput").ap()
    p = Prog(nc)
    x = p.sb([128, KC, P_])
    hff = p.sb([128, 43, P_])
    mi = hff
    mix = p.sb([128, KC, P_])
    wb = [p.sb([128, 8192]) for _ in range(2)]
    mod = p.sb([128, 8 * KC])
    sc1 = p.sb([128, 2 * KC])
    vec = p.sb([128, NV_C])
    hm = p.sb([128, 4])
    glu = p.sb([128, 4, 512])
    onesN = p.sb([128, 128])
    sq = p.sb([128, 512])
    mean = p.sb([128, P_])
    rstd = p.sb([128, P_])
    t1 = p.sb([128, 4, P_])
    t2 = p.sb([128, 4, P_])
    gR = p.sb([128, 4, P_])
    ca = p.sb([128, P_])
    cg = p.sb([128, P_])
    ps_m = p.ps([128, 512])
    ps_q = p.ps([128, 512])
    psr = [p.ps([128, 512]) for _ in range(6)]
    pi = [0]

    def nps():
        pi[0] = (pi[0] + 1) % 6
        return psr[pi[0]], ("psr", pi[0])

    V_S5D, V_GLUB, V_SSDW, V_GLAW, V_LN, V_CW = 0, 4, 8, 12, 13, 77
    p.memset(onesN[:], 1.0, w=["ones"])
    zt = p.sb([128, 43, 1])
    p.memset(zt[:], 0.0, w=["zt"])
    p.load(mod[:], mod_d, w=["mod"])
    p.load(vec[:], vec_d, w=["vec"])
    p.load(R(glu[:]), R(glu_d.rearrange("(k p) m -> p k m", p=128)), w=["glu"])
    p.ts(sc1[:, 0:KC], mod[:, 3 * KC:4 * KC], 1.0, None, ALU.add, None, r=["mod"], w=["sc1"])
    p.ts(sc1[:, KC:2 * KC], mod[:, 5 * KC:6 * KC], 1.0, None, ALU.add, None, r=["mod"], w=["sc1"])
    tiles = [(0, P_)]
    wi = [0]

    def nwb():
        wi[0] = (wi[0] + 1) % 2
        return wb[wi[0]], ("wb", wi[0])

    def stats(tag):
        for k in range(KC):
            p.act(sq[:, 0:P_], x[:, k, :], AF.Square, r=[("x", k)], w=["sq"])
            p.mm(ps_m[:, 0:P_], onesN[:], x[:, k, :], k == 0, k == KC - 1, r=[("x", k), "ones"], w=["ps_m"])
            p.mm(ps_q[:, 0:P_], onesN[:], sq[:, 0:P_], k == 0, k == KC - 1, r=["sq", "ones"], w=["ps_q"])
        p.ts(mean[:], ps_m[:, 0:P_], 1.0 / D, None, ALU.mult, None, r=["ps_m"], w=["mean"])
        p.tt(rstd[:], mean[:], mean[:], ALU.mult, r=["mean"], w=["rstd"])
        p.stt(rstd[:], ps_q[:, 0:P_], 1.0 / D, rstd[:], ALU.mult, ALU.subtract, r=["ps_q", "rstd"], w=["rstd"])
        p.ts(rstd[:], rstd[:], 1e-6, None, ALU.add, None, r=["rstd"], w=["rstd"])
        p.act(rstd[:], rstd[:], AF.Sqrt, r=["rstd"], w=["rstd"])
        p.op("dve", lambda eng: eng.reciprocal(rstd[:], rstd[:]), r=["rstd"], w=["rstd"])

    def ln_affine(gcol, bcol):
        stats("x")
        for k in range(KC):
            p.tt(x[:, k, :], x[:, k, :], mean[:], ALU.subtract, r=[("x", k), "mean"], w=[("x", k)], e="pool")
            p.tt(x[:, k, :], x[:, k, :], rstd[:], ALU.mult, r=[("x", k), "rstd"], w=[("x", k)])
            p.act(x[:, k, :], x[:, k, :], AF.Identity, r=[("x", k), "vec"], w=[("x", k)],
                  scale=vec[:, gcol + k:gcol + k + 1], bias=vec[:, bcol + k:bcol + k + 1])

    def residual(ps, m, gx, gc):
        p.ts(x[:, m, :], x[:, m, :], ALPHA, None, ALU.mult, None, r=[("x", m)], w=[("x", m)], e="pool")
        p.stt(x[:, m, 0:NCXC], ps[:, 0:NCXC], mod[:, gc + m:gc + m + 1], x[:, m, 0:NCXC], ALU.mult, ALU.add,
              r=[("x", m), "mod", pk], w=[("x", m)])
        p.stt(x[:, m, NCXC:P_], ps[:, NCXC:P_], mod[:, gx + m:gx + m + 1], x[:, m, NCXC:P_], ALU.mult, ALU.add,
              r=[("x", m), "mod", pk], w=[("x", m)])

    for s in range(NPASS):
        xv = xT_d[s].rearrange("(k p) n -> p k n", p=128)
        miv = mi_d[s].rearrange("(k p) n -> p k n", p=128)
        for k in range(KC):
            p.load(x[:, k, :], xv[:, k, :], w=[("x", k)])
        for k in range(40):
            p.load(R(mi[:, k, :]), R(miv[:, k, :]), w=[("mi", k // 4), "hff"])
        p.load(hm[:], hm_d[s], w=["hm"])
        if inject:
            for k in range(KC):
                p.copy(R(mix[:, k, :]), mi[:, k, :], r=[("mi", k // 4)], w=[("mix", k // 4)], e="pool")
        else:
            p.tt(t1[:], mi[:, 0:4, :], mi[:, 4:8, :], ALU.add, r=[("mi", 0), ("mi", 1)], w=["t1"])
            for k in range(4):
                p.stt(t1[:, k, :], mi[:, 8 + k, :], vec[:, V_S5D + k:V_S5D + k + 1], t1[:, k, :], ALU.mult, ALU.add,
                      r=[("mi", 2), "vec", "t1"], w=["t1"])
            p.tt(t2[:], t1[:], t1[:], ALU.mult, r=["t1"], w=["t2"])
            p.ts(t2[:], t2[:], 0.044715, 1.0, ALU.mult, ALU.add, r=["t2"], w=["t2"])
            p.tt(t2[:], t2[:], t1[:], ALU.mult, r=["t1", "t2"], w=["t2"])
            p.act(t2[:], t2[:], AF.Sigmoid, r=["t2"], w=["t2"], scale=1.5957691216057308)
            p.tt(R(gR[:]), t1[:], t2[:], ALU.mult, r=["t1", "t2"], w=["gR"])
            for m in range(4):
                ps, pk = nps()
                for k in range(4):
                    p.mm(ps[:, 0:P_], R(glu[:, k, m * 128:(m + 1) * 128]), R(gR[:, k, :]), k == 0, k == 3,
                         r=["glu", "gR"], w=[pk])
                p.act(t2[:, m, :], ps[:, 0:P_], AF.Sigmoid, r=[pk, "vec"], w=["t2"],
                      bias=vec[:, V_GLUB + m:V_GLUB + m + 1])
            p.tt(R(mix[:, 0:4, :]), gR[:], t2[:], ALU.mult, r=["gR", "t2"], w=[("mix", 0)])
            p.copy(R(mix[:, 4:8, :]), mi[:, 12:16, :], r=[("mi", 3)], w=[("mix", 1)], e="pool")
            p.tt(t1[:], mi[:, 16:20, :], mi[:, 20:24, :], ALU.add, r=[("mi", 4), ("mi", 5)], w=["t1"])
            p.act(t2[:], mi[:, 24:28, :], AF.Silu, r=[("mi", 6)], w=["t2"])
            p.tt(t1[:], t1[:], t2[:], ALU.mult, r=["t1", "t2"], w=["t1"])
            p.tt(t2[:], t1[:], t1[:], ALU.mult, r=["t1"], w=["t2"])
            ps, pk = nps()
            for k in range(4):
                p.mm(ps[:, 0:P_], onesN[:], t2[:, k, :], k == 0, k == 3, r=["ones", "t2"], w=[pk])
            p.ts(ca[:], ps[:, 0:P_], 1.0 / 512, 1e-6, ALU.mult, ALU.add, r=[pk], w=["ca"])
            p.act(ca[:], ca[:], AF.Sqrt, r=["ca"], w=["ca"])
            p.op("dve", lambda eng: eng.reciprocal(ca[:], ca[:]), r=["ca"], w=["ca"])
            for k in range(4):
                p.stt(R(mix[:, 8 + k, :]), t1[:, k, :], vec[:, V_SSDW + k:V_SSDW + k + 1], ca[:], ALU.mult, ALU.mult,
                      r=["t1", "vec", "ca"], w=[("mix", 2)])
            p.tt(t1[:], mi[:, 28:32, :], mi[:, 32:36, :], ALU.add, r=[("mi", 7), ("mi", 8)], w=["t1"])
            p.tt(t2[:], t1[:], t1[:], ALU.mult, r=["t1"], w=["t2"])
            for k in range(4):
                ps, pk = nps()
                p.mm(ps[:, 0:P_], onesN[:], t2[:, k, :], True, True, r=["ones", "t2"], w=[pk])
                p.ts(cg[:], ps[:, 0:P_], 1.0 / 128, 1e-6, ALU.mult, ALU.add, r=[pk], w=["cg"])
                p.act(cg[:], cg[:], AF.Sqrt, r=["cg"], w=["cg"])
                p.op("dve", lambda eng: eng.reciprocal(cg[:], cg[:]), r=["cg"], w=["cg"])
                p.stt(t1[:, k, :], t1[:, k, :], vec[:, V_GLAW:V_GLAW + 1], cg[:], ALU.mult, ALU.mult,
                      r=["t1", "vec", "cg"], w=["t1"])
            p.act(t2[:], mi[:, 36:40, :], AF.Silu, r=[("mi", 9), "t2"], w=["t2"])
            p.tt(R(mix[:, 12:16, :]), t1[:], t2[:], ALU.mult, r=["t1", "t2"], w=[("mix", 3)])
        p.store(dbg_d[s].rearrange("(k p) n -> p k n", p=128), mix[:], r=[("mix", i) for i in range(4)])
        wov = wo_d.rearrange("(k p) m -> p k m", p=128)
        for g in range(4):
            wbt, wk = nwb()
            wview = wbt[:, :].rearrange("p (k m) -> p k m", k=KC)
            p.load(R(wview), R(wov[:, :, g * 512:(g + 1) * 512]), w=[wk])
            for mm_ in range(4):
                m = g * 4 + mm_
                ps, pk = nps()
                for k in range(KC):
                    p.mm(ps[:, 0:P_], R(wview[:, k, mm_ * 128:(mm_ + 1) * 128]), R(mix[:, k, :]), k == 0, k == KC - 1,
                         r=[wk, ("mix", k // 4)], w=[pk])
                residual(ps, m, 0 * KC, 1 * KC)
        ln_affine(V_LN, V_LN + 16)
        stats("x")
        for k in range(KC):
            p.tt(R(mix[:, k, :]), x[:, k, :], mean[:], ALU.subtract, r=[("x", k), "mean"], w=[("mix", k // 4)], e="pool")
            p.tt(R(mix[:, k, :]), mix[:, k, :], rstd[:], ALU.mult, r=["rstd", ("mix", k // 4)], w=[("mix", k // 4)])
            p.act(R(mix[:, k, 0:NCXC]), mix[:, k, 0:NCXC], AF.Identity, r=["sc1", "mod", ("mix", k // 4)],
                  w=[("mix", k // 4)], scale=sc1[:, KC + k:KC + k + 1], bias=mod[:, 4 * KC + k:4 * KC + k + 1])
            p.act(R(mix[:, k, NCXC:P_]), mix[:, k, NCXC:P_], AF.Identity, r=["sc1", "mod", ("mix", k // 4)],
                  w=[("mix", k // 4)], scale=sc1[:, k:k + 1], bias=mod[:, 2 * KC + k:2 * KC + k + 1])
        for hi, col in enumerate((0, NCXC - 1, NCXC, P_ - 1)):
            p.ts(R(mix[:, :, col:col + 1]), mix[:, :, col:col + 1], hm[:, hi:hi + 1], None, ALU.mult, None,
                 r=["hm"] + [("mix", i) for i in range(4)], w=[("mix", i) for i in range(4)])
        upv = up_d.rearrange("(k p) m -> p k m", p=128)
        p.copy(R(hff[:, :, 0:1]), zt[:], r=["zt"], w=["hff"] + [("mi", i) for i in range(10)])
        p.copy(R(hff[:, :, P_ - 1:P_]), zt[:], r=["zt"], w=["hff"])
        for j0 in range(0, 43, 2):
            nj = min(2, 43 - j0)
            wj = 128 * nj
            wbt, wk = nwb()
            wview = wbt[:, 0:2 * KC * wj].rearrange("p (a k m) -> p a k m", a=2, k=KC)
            p.load(R(wview[:, 0]), R(upv[:, :, j0 * 128:j0 * 128 + wj]), w=[wk])
            p.load(R(wview[:, 1]), R(upv[:, :, DFF + j0 * 128:DFF + j0 * 128 + wj]), w=[wk])
            for j in range(j0, j0 + nj):
                jo = (j - j0) * 128
                outs = []
                for a in range(2):
                    ps, pk = nps()
                    for k in range(KC):
                        p.mm(ps[:, 0:P_], R(wview[:, a, k, jo:jo + 128]), R(mix[:, k, :]), k == 0, k == KC - 1,
                             r=[wk, ("mix", k // 4)], w=[pk])
                    outs.append((ps, pk))
                for a, (ps, pk) in enumerate(outs):
                    dst, dk = (ca, "ca") if a == 0 else (cg, "cg")
                    c = V_CW + (a * 43 + j) * 4
                    p.ts(dst[:, 1:P_ - 1], ps[:, 1:P_ - 1], vec[:, c + 1:c + 2], vec[:, c + 3:c + 4], ALU.mult, ALU.add,
                         r=[pk, "vec"], w=[dk])
                    p.stt(dst[:, 1:P_ - 1], ps[:, 0:P_ - 2], vec[:, c:c + 1], dst[:, 1:P_ - 1], ALU.mult, ALU.add,
                          r=[pk, "vec", dk], w=[dk])
                    p.stt(dst[:, 1:P_ - 1], ps[:, 2:P_], vec[:, c + 2:c + 3], dst[:, 1:P_ - 1], ALU.mult, ALU.add,
                          r=[pk, "vec", dk], w=[dk])
                p.act(cg[:, 1:P_ - 1], cg[:, 1:P_ - 1], AF.Silu, r=["cg"], w=["cg"])
                p.tt(R(hff[:, j, 1:P_ - 1]), ca[:, 1:P_ - 1], cg[:, 1:P_ - 1], ALU.mult, r=["ca", "cg"], w=["hff"], e="pool")
        dnv = dn_d.rearrange("(j p) m -> p j m", p=128)
        for mg in range(4):
            accs = [nps() for _ in range(4)]
            for (ja, jb) in ((0, 11), (11, 22), (22, 33), (33, 43)):
                wbt, wk = nwb()
                wview = wbt[:, 0:(jb - ja) * 512].rearrange("p (j m) -> p j m", j=jb - ja)
                p.load(R(wview), R(dnv[:, ja:jb, mg * 512:(mg + 1) * 512]), w=[wk])
                for mm_ in range(4):
                    ps, pk = accs[mm_]
                    for j in range(ja, jb):
                        p.mm(ps[:, 0:P_], R(wview[:, j - ja, mm_ * 128:(mm_ + 1) * 128]), R(hff[:, j, :]), j == 0, j == 42,
                             r=[wk, "hff"], w=[pk])
            for mm_ in range(4):
                ps, pk = accs[mm_]
                residual(ps, mg * 4 + mm_, 6 * KC, 7 * KC)
        ln_affine(V_LN + 32, V_LN + 48)
        p.store(out_d[s].rearrange("(k p) n -> p k n", p=128), x[:], r=[("x", k) for k in range(KC)])
    p.finish()
    p.emit()
    return nc


def chunkcols(v):
    v = np.asarray(v, np.float32)
    return np.ascontiguousarray(v.reshape(-1, 128).T)


def _pass_rows(i, s):
    c0 = 32 * i + 8 * s
    l0 = 1024 * i + 256 * s
    idx = np.empty(PC, np.int64)
    cr = np.arange(c0 - 1, c0 + 9)
    cr = np.where((cr >= 0) & (cr < CTX), cr, -1)
    lr = np.arange(l0 - 1, l0 + 257)
    lr = np.where((lr >= 0) & (lr < SEQ), lr + CTX, -1)
    idx[:NCXC] = cr
    idx[NCXC:] = lr
    return idx


def _gather_T(full, idx):
    out = full[np.maximum(idx, 0)].T.copy()
    out[:, idx < 0] = 0
    return np.ascontiguousarray(out, np.float32)


def phase_c_inputs(l, xfull, mixfull, pxsel, mod_x, mod_c, P, inject=False):
    mx = [mod_x[j * D:(j + 1) * D] for j in range(6)]
    mc = [mod_c[j * D:(j + 1) * D] for j in range(6)]
    modC = np.concatenate([chunkcols(v) for v in (mx[2], mc[2], mx[3], mx[4], mc[3], mc[4], mx[5], mc[5])], 1)
    vec = np.zeros((128, NV_C), np.float32)
    vec[:, 0:4] = chunkcols(P["s5_d"][l])
    vec[:, 4:8] = chunkcols(P["s5_glu_b"][l])
    vec[:, 8:12] = chunkcols(P["ssd_norm_w"][l])
    vec[:, 12:13] = chunkcols(P["gla_norm_w"][l])
    vec[:, 13:29] = chunkcols(P["ln_g"][l, 0])
    vec[:, 29:45] = chunkcols(P["ln_b"][l, 0])
    vec[:, 45:61] = chunkcols(P["ln_g"][l, 1])
    vec[:, 61:77] = chunkcols(P["ln_b"][l, 1])
    cw = P["ffn_conv_w"][l]
    cb = P["ffn_conv_b"][l]
    cv = np.stack([chunkcols(cw[0]), chunkcols(cw[1]), chunkcols(cw[2]), chunkcols(cb)], 2)
    vec[:, 77:] = cv.reshape(128, 86 * 4)
    in_maps = []
    for i in range(NCORES):
        xs, ms, hs = [], [], []
        for s in range(NPASS):
            idx = _pass_rows(i, s)
            xs.append(_gather_T(xfull, idx))
            m = np.zeros((40 * 128, PC), np.float32)
            mt = _gather_T(mixfull, idx)
            m[:mt.shape[0]] = mt
            ms.append(m)
            flags = (idx[[0, NCXC - 1, NCXC, PC - 1]] >= 0).astype(np.float32)
            hs.append(np.tile(flags[None, :], (128, 1)))
        in_maps.append({
            "xT": np.stack(xs), "mixin": np.stack(ms), "modC": modC, "vecs": vec, "hmask": np.stack(hs),
            "glu_w": np.ascontiguousarray(P["s5_glu_w"][l]), "w_out": np.ascontiguousarray(P["w_out"][l]),
            "ffn_up": np.ascontiguousarray(P["ffn_up"][l]), "ffn_down": np.ascontiguousarray(P["ffn_down"][l]),
        })
    return in_maps


def phase_c_gather(res, name):
    out = np.zeros((CTX + SEQ, D), np.float32)
    for i in range(NCORES):
        o = res[i][name]
        for s in range(NPASS):
            c0 = 32 * i + 8 * s
            l0 = 1024 * i + 256 * s
            out[c0:c0 + 8] = o[s][:, 1:9].T
            out[CTX + l0:CTX + l0 + 256] = o[s][:, NCXC + 1:PC - 1].T
    return out


def build_mod():
    nc = bass.Bass("TRN2", target_bir_lowering=False)
    NCOL = 6144
    w_d = nc.dram_tensor("w", [D, NCOL], F32, kind="ExternalInput").ap()
    c_d = nc.dram_tensor("cv", [128, KC, 2], F32, kind="ExternalInput").ap()
    b_d = nc.dram_tensor("b", [128, 48], F32, kind="ExternalInput").ap()
    o_d = nc.dram_tensor("mod", [128, 48, 2], F32, kind="ExternalOutput").ap()
    p = Prog(nc)
    cv = p.sb([128, KC, 2])
    bb = p.sb([128, 48])
    ob = p.sb([128, 48, 2])
    wb = [p.sb([128, KC, 512]) for _ in range(2)]
    pss = [p.ps([128, 512]) for _ in range(2)]
    cv0 = p.sb([128, KC, 2])
    p.load(cv0[:], c_d, w=["cv0"])
    p.load(bb[:], b_d, w=["b"])
    p.act(R(cv[:]), cv0[:], AF.Silu, r=["cv0"], w=["cv"])
    wv = w_d.rearrange("(k p) m -> p k m", p=128)
    for g in range(12):
        wt = wb[g % 2]
        for k in range(KC):
            p.load(R(wt[:, k, :]), R(wv[:, k, g * 512:(g + 1) * 512]), w=[("w", g % 2, k)])
        for mm_ in range(4):
            j = g * 4 + mm_
            ps = pss[j % 2]
            for k in range(KC):
                p.mm(ps[:, 0:2], R(wt[:, k, mm_ * 128:(mm_ + 1) * 128]), R(cv[:, k, :]), k == 0, k == KC - 1,
                     r=[("w", g % 2, k), "cv"], w=[("ps", j % 2)])
            p.ts(ob[:, j, :], ps[:, 0:2], bb[:, j:j + 1], None, ALU.add, None, r=[("ps", j % 2), "b"], w=["ob"])
    p.store(o_d, ob[:], r=["ob"])
    p.finish()
    p.emit()
    return nc


def run_mod(c, c_ctx, w_mod, b_mod):
    nc = build_mod()
    cvh = np.stack([chunkcols(c.reshape(-1)), chunkcols(c_ctx.reshape(-1))], 2)
    in_maps = []
    for i in range(NCORES):
        l, h = i // 2, i % 2
        in_maps.append({"w": np.ascontiguousarray(w_mod[l][:, h * 6144:(h + 1) * 6144]), "cv": cvh,
                        "b": chunkcols(b_mod[l][h * 6144:(h + 1) * 6144])})
    res = run_spmd(nc, in_maps)
    mod_x = np.zeros((DEPTH, 6 * D), np.float32)
    mod_c = np.zeros((DEPTH, 6 * D), np.float32)
    for i in range(NCORES):
        l, h = i // 2, i % 2
        o = res[i]["mod"]
        mod_x[l, h * 6144:(h + 1) * 6144] = o[:, :, 0].T.reshape(-1)
        mod_c[l, h * 6144:(h + 1) * 6144] = o[:, :, 1].T.reshape(-1)
    return mod_x, mod_c


TT = CTX + SEQ
I32 = mybir.dt.int32
TWO_PI_HI = 6.28125
TWO_PI_LO = 2.0 * np.pi - 6.28125


def trig(p, x, n, tmp, ki, cos_o, sin_o, tag):
    a, b, c = tmp
    kx = [tag + "a", tag + "b", tag + "c", tag + "k"]
    p.ts(a, x, 1.0 / (2.0 * np.pi), None, ALU.mult, None, r=[tag + "x"], w=[kx[0]])
    p.copy(ki, a, r=[kx[0]], w=[kx[3]])
    p.copy(a, ki, r=[kx[3]], w=[kx[0]])
    p.stt(b, a, -TWO_PI_HI, x, ALU.mult, ALU.add, r=[kx[0], tag + "x"], w=[kx[1]])
    p.stt(b, a, -TWO_PI_LO, b, ALU.mult, ALU.add, r=[kx[0], kx[1]], w=[kx[1]])
    p.act(a, b, AF.Sin, r=[kx[1]], w=[kx[0]], scale=0.25)
    p.ts(b, b, 0.25, float(np.pi / 2), ALU.mult, ALU.add, r=[kx[1]], w=[kx[1]])
    p.act(b, b, AF.Sin, r=[kx[1]], w=[kx[1]])
    for it in range(2):
        p.tt(c, a, b, ALU.mult, r=[kx[0], kx[1]], w=[kx[2]])
        p.tt(b, b, b, ALU.mult, r=[kx[1]], w=[kx[1]])
        p.tt(a, a, a, ALU.mult, r=[kx[0]], w=[kx[0]])
        p.tt(b, b, a, ALU.subtract, r=[kx[0], kx[1]], w=[kx[1]])
        p.ts(a, c, 2.0, None, ALU.mult, None, r=[kx[2]], w=[kx[0]])
    p.copy(cos_o, b, r=[kx[1]], w=[tag + "cos"])
    p.copy(sin_o, a, r=[kx[0]], w=[tag + "sin"])


def build_s5():
    nc = bass.Bass("TRN2", target_bir_lowering=False)
    u_d = nc.dram_tensor("u", [2, 64, TT], F32, kind="ExternalInput").ap()
    lp_d = nc.dram_tensor("lanep", [2, 2, 128, 3], F32, kind="ExternalInput").ap()
    bre_d = nc.dram_tensor("bre", [2, 2, 128, 16], F32, kind="ExternalInput").ap()
    bim_d = nc.dram_tensor("bim", [2, 2, 128, 16], F32, kind="ExternalInput").ap()
    cre_d = nc.dram_tensor("cre", [2, 2, 128, 64], F32, kind="ExternalInput").ap()
    cim_d = nc.dram_tensor("cim", [2, 2, 128, 64], F32, kind="ExternalInput").ap()
    tau_d = nc.dram_tensor("tau1", [128, 128], F32, kind="ExternalInput").ap()
    id_d = nc.dram_tensor("ident", [128, 128], F32, kind="ExternalInput").ap()
    y_d = nc.dram_tensor("y", [2, 64, TT], F32, kind="ExternalOutput").ap()
    p = Prog(nc)
    BL = 512
    u = p.sb([64, TT])
    yb = [p.sb([64, BL]) for _ in range(2)]
    tau = p.sb([128, 128])
    ident = p.sb([128, 128])
    lp = p.sb([128, 3])
    sc = p.sb([128, 16])
    braw = p.sb([128, 2, 16])
    bbf = p.sb([128, 2, 64])
    bbT = [[p.sb([64, 2, 128]) for _ in range(2)] for _ in range(2)]
    cc = [[p.sb([128, 2, 64]) for _ in range(2)] for _ in range(2)]
    cosT = [[p.sb([128, BL]) for _ in range(2)] for _ in range(2)]
    sinT = [[p.sb([128, BL]) for _ in range(2)] for _ in range(2)]
    rhoT = [[p.sb([128, 128]) for _ in range(2)] for _ in range(2)]
    cq = [[p.sb([128, 2]) for _ in range(2)] for _ in range(2)]
    tmp = [p.sb([128, 128]) for _ in range(3)]
    tki = p.sb([128, 128], I32)
    ang = p.sb([128, 128])
    b_re = p.sb([128, BL]); b_im = p.sb([128, BL])
    v_re = p.sb([128, BL]); v_im = p.sb([128, BL])
    m1 = p.sb([128, BL]); m2 = p.sb([128, BL])
    w_re = p.sb([128, BL]); w_im = p.sb([128, BL])
    s_re = p.sb([128, BL]); s_im = p.sb([128, BL])
    car = [p.sb([128, 2]) for _ in range(2)]
    ct = p.sb([128, 2])
    psb = [p.ps([128, 512]) for _ in range(4)]
    psy = [p.ps([128, 512]) for _ in range(2)]
    pst = p.ps([128, 512])

    p.load(tau[:], tau_d, w=["tau"])
    p.load(ident[:], id_d, w=["ident"])
    for d in range(2):
        for t in range(2):
            tg = f"s{d}{t}"
            p.load(lp[:], lp_d[d, t], w=["lp"])
            p.load(braw[:, 0, :], bre_d[d, t], w=["braw"])
            p.load(braw[:, 1, :], bim_d[d, t], w=["braw"])
            p.load(R(cc[d][t][:, 0, :]), R(cre_d[d, t]), w=[("cc", d, t)])
            p.load(R(cc[d][t][:, 1, :]), R(cim_d[d, t]), w=[("cc", d, t)])
            p.act(sc[:, 0:1], lp[:, 2:3], AF.Exp, r=["lp"], w=["sc"])
            p.tt(sc[:, 1:2], lp[:, 0:1], sc[:, 0:1], ALU.mult, r=["lp", "sc"], w=["sc"])
            p.act(sc[:, 2:3], sc[:, 1:2], AF.Exp, r=["sc"], w=["sc"])
            p.tt(sc[:, 3:4], lp[:, 1:2], sc[:, 0:1], ALU.mult, r=["lp", "sc"], w=["sc"])
            p.ts(ang[:], tau[:], sc[:, 3:4], None, ALU.mult, None, r=["tau", "sc"], w=[tg + "x"])
            trig(p, ang[:], 128, [tmp[0][:], tmp[1][:], tmp[2][:]], tki[:], cosT[d][t][:, 0:128], sinT[d][t][:, 0:128], tg)
            for rep in range(1, 4):
                p.copy(cosT[d][t][:, rep * 128:(rep + 1) * 128], cosT[d][t][:, 0:128], r=[tg + "cos"], w=[tg + "cos"], e="pool")
                p.copy(sinT[d][t][:, rep * 128:(rep + 1) * 128], sinT[d][t][:, 0:128], r=[tg + "sin"], w=[tg + "sin"], e="pool")
            p.copy(cq[d][t][:, 0:1], cosT[d][t][:, 127:128], r=[tg + "cos"], w=[("cq", d, t)])
            p.copy(cq[d][t][:, 1:2], sinT[d][t][:, 127:128], r=[tg + "sin"], w=[("cq", d, t)])
            p.memset(rhoT[d][t][:], 1.0, w=[("rho", d, t)])
            p.ts(rhoT[d][t][:], rhoT[d][t][:], sc[:, 2:3], None, ALU.mult, None, r=["sc", ("rho", d, t)], w=[("rho", d, t)])
            p.tt(sc[:, 4:5], sc[:, 2:3], cosT[d][t][:, 0:1], ALU.mult, r=["sc", tg + "cos"], w=["sc"])
            p.ts(sc[:, 4:5], sc[:, 4:5], -1.0, None, ALU.add, None, r=["sc"], w=["sc"])
            p.tt(sc[:, 5:6], sc[:, 2:3], sinT[d][t][:, 0:1], ALU.mult, r=["sc", tg + "sin"], w=["sc"])
            p.tt(sc[:, 6:7], lp[:, 0:1], lp[:, 0:1], ALU.mult, r=["lp"], w=["sc"])
            p.tt(sc[:, 9:10], lp[:, 1:2], lp[:, 1:2], ALU.mult, r=["lp"], w=["sc"])
            p.tt(sc[:, 6:7], sc[:, 6:7], sc[:, 9:10], ALU.add, r=["sc"], w=["sc"])
            p.op("dve", lambda eng: eng.reciprocal(sc[:, 6:7], sc[:, 6:7]), r=["sc"], w=["sc"])
            p.tt(sc[:, 7:8], sc[:, 4:5], lp[:, 0:1], ALU.mult, r=["sc", "lp"], w=["sc"])
            p.tt(sc[:, 9:10], sc[:, 5:6], lp[:, 1:2], ALU.mult, r=["sc", "lp"], w=["sc"])
            p.tt(sc[:, 7:8], sc[:, 7:8], sc[:, 9:10], ALU.add, r=["sc"], w=["sc"])
            p.tt(sc[:, 7:8], sc[:, 7:8], sc[:, 6:7], ALU.mult, r=["sc"], w=["sc"])
            p.tt(sc[:, 8:9], sc[:, 5:6], lp[:, 0:1], ALU.mult, r=["sc", "lp"], w=["sc"])
            p.tt(sc[:, 9:10], sc[:, 4:5], lp[:, 1:2], ALU.mult, r=["sc", "lp"], w=["sc"])
            p.tt(sc[:, 8:9], sc[:, 8:9], sc[:, 9:10], ALU.subtract, r=["sc"], w=["sc"])
            p.tt(sc[:, 8:9], sc[:, 8:9], sc[:, 6:7], ALU.mult, r=["sc"], w=["sc"])
            p.memset(bbf[:], 0.0, w=["bbf"])
            for half in range(2):
                rows = slice(64 * half, 64 * half + 64)
                co = 16 * (2 * t + half)
                p.ts(bbf[rows, 0, co:co + 16], braw[rows, 1, :], sc[rows, 8:9], -1.0, ALU.mult, ALU.mult,
                     r=["braw", "sc"], w=["bbf"])
                p.stt(bbf[rows, 0, co:co + 16], braw[rows, 0, :], sc[rows, 7:8], bbf[rows, 0, co:co + 16], ALU.mult, ALU.add,
                      r=["braw", "sc", "bbf"], w=["bbf"])
                p.ts(bbf[rows, 1, co:co + 16], braw[rows, 0, :], sc[rows, 8:9], None, ALU.mult, None,
                     r=["braw", "sc"], w=["bbf"])
                p.stt(bbf[rows, 1, co:co + 16], braw[rows, 1, :], sc[rows, 7:8], bbf[rows, 1, co:co + 16], ALU.mult, ALU.add,
                      r=["braw", "sc", "bbf"], w=["bbf"])
            for c2 in range(2):
                p.op("pe", lambda eng, o=pst[0:64, c2 * 128:(c2 + 1) * 128], i_=bbf[:, c2, :]: eng.transpose(o, i_, ident[:]),
                     r=["bbf", "ident"], w=["pst"])
            p.copy(R(bbT[d][t][:, :, :]), pst[0:64, 0:256].rearrange("p (a m) -> p a m", a=2), r=["pst"], w=[("bbT", d, t)], e="act")
            p.ts(R(cc[d][t][:, 1, :]), cc[d][t][:, 1, :], -1.0, None, ALU.mult, None, r=[("cc", d, t)], w=[("cc", d, t)])
            p.copy(R(cc[d][t][:, 0, :]), cc[d][t][:, 0, :], r=[("cc", d, t)], w=[("cc", d, t)])
    nblk = (TT + BL - 1) // BL
    for d in range(2):
        for k0 in range(0, TT, 2112):
            p.load(R(u[:, k0:k0 + 2112]), R(u_d[d][:, k0:k0 + 2112]), w=[("u", k0)])
        for t in range(2):
            p.memset(car[t][:], 0.0, w=[("car", t)])
        for bi in range(nblk):
            c0 = bi * BL
            cn = min(BL, TT - c0)
            uk = ("u", (c0 // 2112) * 2112)
            py = psy[bi % 2]
            pyk = ("psy", bi % 2)
            for t in range(2):
                pr, pim = psb[2 * t], psb[2 * t + 1]
                p.mm(pr[:, 0:cn], R(bbT[d][t][:, 0, :]), R(u[:, c0:c0 + cn]), True, True, r=[("bbT", d, t), uk], w=[("psb", 2 * t)])
                p.mm(pim[:, 0:cn], R(bbT[d][t][:, 1, :]), R(u[:, c0:c0 + cn]), True, True, r=[("bbT", d, t), uk], w=[("psb", 2 * t + 1)])
                p.copy(b_re[:, 0:cn], pr[:, 0:cn], r=[("psb", 2 * t)], w=["b_re"], e="act")
                p.copy(b_im[:, 0:cn], pim[:, 0:cn], r=[("psb", 2 * t + 1)], w=["b_im"], e="act")
                C_, S_ = cosT[d][t], sinT[d][t]
                tgc, tgs = f"s{d}{t}cos", f"s{d}{t}sin"
                p.tt(m1[:, 0:cn], b_re[:, 0:cn], C_[:, 0:cn], ALU.mult, r=["b_re", tgc], w=["m1"], e="pool")
                p.tt(m2[:, 0:cn], b_im[:, 0:cn], S_[:, 0:cn], ALU.mult, r=["b_im", tgs], w=["m2"], e="pool")
                p.tt(v_re[:, 0:cn], m1[:, 0:cn], m2[:, 0:cn], ALU.add, r=["m1", "m2"], w=["v_re"], e="pool")
                p.tt(m1[:, 0:cn], b_im[:, 0:cn], C_[:, 0:cn], ALU.mult, r=["b_im", tgc], w=["m1"], e="pool")
                p.tt(m2[:, 0:cn], b_re[:, 0:cn], S_[:, 0:cn], ALU.mult, r=["b_re", tgs], w=["m2"], e="pool")
                p.tt(v_im[:, 0:cn], m1[:, 0:cn], m2[:, 0:cn], ALU.subtract, r=["m1", "m2"], w=["v_im"], e="pool")
                for q0 in range(0, cn, 128):
                    sl = slice(q0, q0 + 128)
                    p.op("dve", lambda eng, o=w_re[:, sl], a=rhoT[d][t][:], b=v_re[:, sl], i_=car[t][:, 0:1]:
                         eng.tensor_tensor_scan(o, a, b, i_, ALU.mult, ALU.add),
                         r=[("rho", d, t), "v_re", ("car", t)], w=["w_re"])
                    p.op("dve", lambda eng, o=w_im[:, sl], a=rhoT[d][t][:], b=v_im[:, sl], i_=car[t][:, 1:2]:
                         eng.tensor_tensor_scan(o, a, b, i_, ALU.mult, ALU.add),
                         r=[("rho", d, t), "v_im", ("car", t)], w=["w_im"])
                    last = q0 + 127
                    cqt = cq[d][t]
                    p.ts(ct[:, 0:1], w_im[:, last:last + 1], cqt[:, 1:2], None, ALU.mult, None, r=["w_im", ("cq", d, t)], w=["ct"])
                    p.ts(ct[:, 1:2], w_re[:, last:last + 1], cqt[:, 1:2], None, ALU.mult, None, r=["w_re", ("cq", d, t)], w=["ct"])
                    p.stt(car[t][:, 0:1], w_re[:, last:last + 1], cqt[:, 0:1], ct[:, 0:1], ALU.mult, ALU.subtract,
                          r=["w_re", ("cq", d, t), "ct"], w=[("car", t)])
                    p.stt(car[t][:, 1:2], w_im[:, last:last + 1], cqt[:, 0:1], ct[:, 1:2], ALU.mult, ALU.add,
                          r=["w_im", ("cq", d, t), "ct"], w=[("car", t)])
                p.tt(m1[:, 0:cn], w_re[:, 0:cn], C_[:, 0:cn], ALU.mult, r=["w_re", tgc], w=["m1"], e="pool")
                p.tt(m2[:, 0:cn], w_im[:, 0:cn], S_[:, 0:cn], ALU.mult, r=["w_im", tgs], w=["m2"], e="pool")
                p.tt(R(s_re[:, 0:cn]), m1[:, 0:cn], m2[:, 0:cn], ALU.subtract, r=["m1", "m2"], w=["s_re"])
                p.tt(m1[:, 0:cn], w_im[:, 0:cn], C_[:, 0:cn], ALU.mult, r=["w_im", tgc], w=["m1"], e="pool")
                p.tt(m2[:, 0:cn], w_re[:, 0:cn], S_[:, 0:cn], ALU.mult, r=["w_re", tgs], w=["m2"], e="pool")
                p.tt(R(s_im[:, 0:cn]), m1[:, 0:cn], m2[:, 0:cn], ALU.add, r=["m1", "m2"], w=["s_im"])
                p.mm(py[0:64, 0:cn], R(cc[d][t][:, 0, :]), R(s_re[:, 0:cn]), t == 0, False, r=[("cc", d, t), "s_re"], w=[pyk])
                p.mm(py[0:64, 0:cn], R(cc[d][t][:, 1, :]), R(s_im[:, 0:cn]), False, t == 1, r=[("cc", d, t), "s_im"], w=[pyk])
            ybt = yb[bi % 2]
            p.copy(ybt[:, 0:cn], py[0:64, 0:cn], r=[pyk], w=[("yb", bi % 2)], e="act")
            p.store(y_d[d][:, c0:c0 + cn], ybt[:, 0:cn], r=[("yb", bi % 2)])
    p.finish()
    p.emit()
    return nc


def s5_inputs(l, u_full, P):
    tau1 = np.tile(np.arange(1, 129, dtype=np.float32)[None, :], (128, 1))
    ident = np.eye(128, dtype=np.float32)
    in_maps = []
    for i in range(NCORES):
        uc = u_full[:, 64 * i:64 * i + 64]
        uf = uc.T
        ub = np.concatenate([uc[:CTX][::-1], uc[CTX:][::-1]], 0).T
        lanep = np.zeros((2, 2, 128, 3), np.float32)
        bre = np.zeros((2, 2, 128, 16), np.float32)
        bim = np.zeros((2, 2, 128, 16), np.float32)
        cre = np.zeros((2, 2, 128, 64), np.float32)
        cim = np.zeros((2, 2, 128, 64), np.float32)
        for d in range(2):
            for t in range(2):
                for h in range(2):
                    gl = 2 * t + h
                    g = 4 * i + gl
                    rows = slice(64 * h, 64 * h + 64)
                    lanep[d, t, rows, 0] = P["s5_lam_re"][l, d, g]
                    lanep[d, t, rows, 1] = P["s5_lam_im"][l, d, g]
                    lanep[d, t, rows, 2] = P["s5_log_step"][l, d, g]
                    bre[d, t, rows] = P["s5_b_re"][l, d, g]
                    bim[d, t, rows] = P["s5_b_im"][l, d, g]
                    cre[d, t, rows, 16 * gl:16 * gl + 16] = P["s5_c_re"][l, d, g].T
                    cim[d, t, rows, 16 * gl:16 * gl + 16] = P["s5_c_im"][l, d, g].T
        in_maps.append({"u": np.ascontiguousarray(np.stack([uf, ub])), "lanep": lanep, "bre": bre, "bim": bim,
                        "cre": cre, "cim": cim, "tau1": tau1, "ident": ident})
    return in_maps


def unflip(y):
    yt = y.T
    return np.concatenate([yt[:CTX][::-1], yt[CTX:][::-1]], 0)


def s5_gather(res):
    yf = np.concatenate([res[i]["y"][0].T for i in range(NCORES)], 1)
    ybk = np.concatenate([unflip(res[i]["y"][1]) for i in range(NCORES)], 1)
    return yf, ybk


NEG = -30000.0


def na_cls(b):
    return 0 if b == 0 else 1 if b == 1 else 3 if b == 62 else 4 if b == 63 else 2


def na_kr0(b):
    return min(max(2 * b - 4, 0), 119)


def build_na():
    nc = bass.Bass("TRN2", target_bir_lowering=False)
    q_d = nc.dram_tensor("qT", [64, TT], F32, kind="ExternalInput").ap()
    k_d = nc.dram_tensor("kT", [64, TT], F32, kind="ExternalInput").ap()
    v_d = nc.dram_tensor("vt", [64, 132, 64], F32, kind="ExternalInput").ap()
    b_d = nc.dram_tensor("bias", [5, 128, 576], F32, kind="ExternalInput").ap()
    id_d = nc.dram_tensor("ident", [128, 128], F32, kind="ExternalInput").ap()
    o_d = nc.dram_tensor("oT", [64, TT], F32, kind="ExternalOutput").ap()
    p = Prog(nc)
    qT = p.sb([64, TT]); kT = p.sb([64, TT]); vt = p.sb([64, 132, 64]); oT = p.sb([64, TT])
    bias = p.sb([128, 5, 576])
    ident = p.sb([128, 128])
    S = p.sb([128, 832]); Pm = p.sb([128, 832])
    PTs = p.sb([64, 13, 128])
    dg = p.sb([128, 128])
    st = p.sb([128, 4])
    psA = p.ps([128, 512]); psB = p.ps([128, 512]); psC = p.ps([128, 512])
    psT = [p.ps([128, 512]) for _ in range(4)]
    pso = p.ps([128, 512])
    for k0 in range(0, TT, 2112):
        p.load(R(qT[:, k0:k0 + 2112]), R(q_d[:, k0:k0 + 2112]), w=["q"])
        p.load(R(kT[:, k0:k0 + 2112]), R(k_d[:, k0:k0 + 2112]), w=["k"])
    for r0 in range(0, 132, 33):
        p.load(R(vt[:, r0:r0 + 33, :]), R(v_d[:, r0:r0 + 33, :]), w=["v"])
    for c in range(5):
        p.load(bias[:, c, :], b_d[c], w=["bias"])
    p.load(ident[:], id_d, w=["ident"])

    def block(qc0, lat, b):
        nk = 832 if lat else 256
        if lat:
            kr0 = na_kr0(b)
            kc0 = CTX + 64 * kr0
            cls = na_cls(b)
            p.mm(psA[:, 0:288], R(qT[:, qc0:qc0 + 128]), R(kT[:, kc0:kc0 + 288]), True, True, r=["q", "k"], w=["psA"])
            p.mm(psB[:, 0:288], R(qT[:, qc0:qc0 + 128]), R(kT[:, kc0 + 288:kc0 + 576]), True, True, r=["q", "k"], w=["psB"])
            p.mm(psC[:, 0:256], R(qT[:, qc0:qc0 + 128]), R(kT[:, 0:256]), True, True, r=["q", "k"], w=["psC"])
            p.stt(S[:, 0:288], psA[:, 0:288], 0.125, bias[:, cls, 0:288], ALU.mult, ALU.add, r=["psA", "bias"], w=["S"])
            p.stt(S[:, 288:576], psB[:, 0:288], 0.125, bias[:, cls, 288:576], ALU.mult, ALU.add, r=["psB", "bias"], w=["S"])
            p.act(S[:, 576:832], psC[:, 0:256], AF.Copy, r=["psC"], w=["S"], scale=0.125)
        else:
            p.mm(psC[:, 0:256], R(qT[:, qc0:qc0 + 128]), R(kT[:, 0:256]), True, True, r=["q", "k"], w=["psC"])
            p.act(S[:, 0:256], psC[:, 0:256], AF.Copy, r=["psC"], w=["S"], scale=0.125)
        p.op("dve", lambda eng: eng.reduce_max(st[:, 0:1], S[:, 0:nk], AX.X), r=["S"], w=["st"])
        p.ts(st[:, 1:2], st[:, 0:1], -1.0, None, ALU.mult, None, r=["st"], w=["st"])
        p.act(R(Pm[:, 0:nk]), S[:, 0:nk], AF.Exp, r=["S", "st"], w=["P", "st2"], bias=st[:, 1:2], accum_out=st[:, 2:3])
        p.op("dve", lambda eng: eng.reciprocal(st[:, 3:4], st[:, 2:3]), r=["st2", "P"], w=["st3"])
        p.ts(R(dg[:]), ident[:], st[:, 3:4], None, ALU.mult, None, r=["ident", "st3"], w=["dg"])
        nt = nk // 64
        for kt in range(nt):
            bank = psT[kt // 4]
            p.mm(bank[0:64, (kt % 4) * 128:(kt % 4) * 128 + 128], R(Pm[:, kt * 64:(kt + 1) * 64]), R(dg[:]), True, True,
                 r=["P", "dg"], w=[("psT", kt // 4)])
        for bk in range((nt + 3) // 4):
            n4 = min(4, nt - 4 * bk)
            p.copy(R(PTs[:, 4 * bk:4 * bk + n4, :]), psT[bk][0:64, 0:n4 * 128].rearrange("p (a m) -> p a m", a=n4),
                   r=[("psT", bk)], w=["PTs"], e="act" if bk % 2 == 0 else "dve")
        for kt in range(nt):
            if lat:
                row = 4 + kr0 + kt if kt < 9 else kt - 9
            else:
                row = kt
            p.mm(pso[0:64, 0:128], R(vt[:, row, :]), R(PTs[:, kt, :]), kt == 0, kt == nt - 1, r=["v", "PTs"], w=["pso"])
        p.copy(oT[:, qc0:qc0 + 128], pso[0:64, 0:128], r=["pso"], w=["oT"], e="pool" if False else "act")

    for cb in range(2):
        block(128 * cb, False, cb)
    for b in range(64):
        block(CTX + 128 * b, True, b)
    for k0 in range(0, TT, 2112):
        p.store(o_d[:, k0:k0 + 2112], oT[:, k0:k0 + 2112], r=["oT"])
    p.finish()
    p.emit()
    return nc


def na_bias_tables(rpb_h):
    out = np.full((5, 128, 576), NEG, np.float32)
    for ci, b in enumerate((0, 1, 2, 62, 63)):
        kr0 = na_kr0(b)
        for q in range(128):
            r = 2 * b + q // 64
            c = q % 64
            rs = min(max(r - 4, 0), 120)
            cs = min(max(c - 8, 0), 48)
            for kr in range(rs, rs + 8):
                sl = (kr - kr0) * 64
                out[ci, q, sl + cs:sl + cs + 16] = rpb_h[kr - r + 7, cs - c + 15:cs - c + 31]
    return out


def na_inputs(l, q_full, k_full, v_full, P):
    ident = np.eye(128, dtype=np.float32)
    in_maps = []
    for i in range(NCORES):
        sl = slice(64 * i, 64 * i + 64)
        vt = v_full[:, sl].reshape(132, 64, 64).transpose(1, 0, 2)
        in_maps.append({"qT": np.ascontiguousarray(q_full[:, sl].T), "kT": np.ascontiguousarray(k_full[:, sl].T),
                        "vt": np.ascontiguousarray(vt), "bias": na_bias_tables(P["na_rpb"][l, i]), "ident": ident})
    return in_maps


def na_gather(res):
    return np.concatenate([res[i]["oT"].T for i in range(NCORES)], 1)


NCH = TT // 128


def build_ssd(nch=NCH, do_conv=True, do_setup=True):
    nc = bass.Bass("TRN2", target_bir_lowering=False)
    x_d = nc.dram_tensor("xbc", [2, 3, 128, TT], F32, kind="ExternalInput").ap()
    cw_d = nc.dram_tensor("cw", [2, 128, 3, 4], F32, kind="ExternalInput").ap()
    dt_d = nc.dram_tensor("dtm", [2, 128, NCH], F32, kind="ExternalInput").ap()
    sc_d = nc.dram_tensor("scal", [2, 128, 4], F32, kind="ExternalInput").ap()
    tri_d = nc.dram_tensor("tri", [128, 128], F32, kind="ExternalInput").ap()
    nm_d = nc.dram_tensor("negmask", [128, 128], F32, kind="ExternalInput").ap()
    id_d = nc.dram_tensor("ident", [128, 128], F32, kind="ExternalInput").ap()
    y_d = nc.dram_tensor("y", [2, 128, NCH, 64], F32, kind="ExternalOutput").ap()
    p = Prog(nc)
    raw = p.sb([128, TT])
    cv = [p.sb([128, TT]) for _ in range(3)]
    cw = p.sb([128, 3, 4]); scal = p.sb([128, 4])
    tri = p.sb([128, 128]); nm = p.sb([128, 128]); ident = p.sb([128, 128]); ones = p.sb([128, 128])
    zt = p.sb([128, 64])
    dt = p.sb([128, NCH]); dta = p.sb([128, NCH]); tA = p.sb([128, NCH]); tB = p.sb([128, NCH])
    nacs = p.sb([128, NCH]); wdec = p.sb([128, NCH]); dec = p.sb([128, NCH])
    ybuf = p.sb([128, NCH, 64])
    xdt = p.sb([128, 64]); Bw = p.sb([128, 128]); dtab = p.sb([128, 128])
    E = p.sb([128, 128]); CE = p.sb([128, 128]); Rm = p.sb([128, 128]); LT = p.sb([128, 128]); MT = p.sb([128, 128])
    hT = p.sb([128, 64])
    ps_x = p.ps([128, 512]); ps_B = p.ps([128, 512]); ps_R = p.ps([128, 512]); ps_CB = p.ps([128, 512])
    ps_y = p.ps([128, 512]); ps_h = p.ps([128, 512]); ps_s = p.ps([128, 512])
    p.load(tri[:], tri_d, w=["tri"]); p.load(nm[:], nm_d, w=["nm"]); p.load(ident[:], id_d, w=["ident"])
    p.memset(ones[:], 1.0, w=["ones"]); p.memset(zt[:], 0.0, w=["zt"])
    segs = [(0, CTX), (CTX, TT)]
    for d in range(2):
        p.load(cw[:], cw_d[d], w=["cw"]); p.load(scal[:], sc_d[d], w=["scal"]); p.load(dt[:], dt_d[d], w=["dt"])
        p.ts(dt[:], dt[:], scal[:, 0:1], None, ALU.add, None, r=["dt", "scal"], w=["dt"])
        p.ts(tA[:], dt[:], 0.0, None, ALU.max, None, r=["dt"], w=["tA"])
        p.ts(tB[:], dt[:], 0.0, None, ALU.min, None, r=["dt"], w=["tB"])
        p.tt(tB[:], tB[:], tA[:], ALU.subtract, r=["tA", "tB"], w=["tB"])
        p.act(tB[:], tB[:], AF.Exp, r=["tB"], w=["tB"])
        p.act(tB[:], tB[:], AF.Ln, r=["tB"], w=["tB"], bias=1.0)
        p.tt(dt[:], tA[:], tB[:], ALU.add, r=["tA", "tB"], w=["dt"])
        p.act(scal[:, 3:4], scal[:, 1:2], AF.Exp, r=["scal"], w=["scal"])
        p.ts(dta[:], dt[:], scal[:, 3:4], -1.0, ALU.mult, ALU.mult, r=["dt", "scal"], w=["dta"])
        p.mm(ps_s[:, 0:NCH], tri[:], dta[:], True, True, r=["tri", "dta"], w=["ps_s"])
        p.ts(nacs[:], ps_s[:, 0:NCH], -1.0, None, ALU.mult, None, r=["ps_s"], w=["nacs"])
        p.mm(ps_s[:, 0:NCH], ones[:], dta[:], True, True, r=["ones", "dta", "nacs"], w=["ps_s"])
        p.tt(wdec[:], ps_s[:, 0:NCH], nacs[:], ALU.add, r=["ps_s", "nacs"], w=["wdec"])
        p.act(wdec[:], wdec[:], AF.Exp, r=["wdec"], w=["wdec"])
        p.act(dec[:], ps_s[:, 0:NCH], AF.Exp, r=["ps_s"], w=["dec"])
        for ch in range(3 if do_conv else 0):
            np_ = 64 if ch == 0 else 128
            for k0 in range(0, TT, 2112):
                p.load(raw[0:np_, k0:k0 + 2112], x_d[d, ch, 0:np_, k0:k0 + 2112], w=["raw"])
            o = cv[ch]
            for (a, b) in segs:
                p.ts(R(o[0:np_, a:b]), raw[0:np_, a:b], cw[0:np_, ch, 1:2], cw[0:np_, ch, 3:4], ALU.mult, ALU.add,
                     r=["raw", "cw"], w=[("cv", ch)])
                p.stt(R(o[0:np_, a + 1:b]), raw[0:np_, a:b - 1], cw[0:np_, ch, 0:1], o[0:np_, a + 1:b], ALU.mult, ALU.add,
                      r=["raw", "cw", ("cv", ch)], w=[("cv", ch)])
                p.stt(R(o[0:np_, a:b - 1]), raw[0:np_, a + 1:b], cw[0:np_, ch, 2:3], o[0:np_, a:b - 1], ALU.mult, ALU.add,
                      r=["raw", "cw", ("cv", ch)], w=[("cv", ch)])
            p.act(R(o[0:np_, :]), o[0:np_, :], AF.Silu, r=[("cv", ch)], w=[("cv", ch)])
        xs, Bm, Cm = cv
        p.copy(R(hT[:]), zt[:], r=["zt"], w=["hT"])
        for c in range(nch):
            cols = slice(128 * c, 128 * c + 128)
            p.op("pe", lambda eng, i_=xs[0:64, cols]: eng.transpose(ps_x[:, 0:64], i_, ident[0:64, 0:64]),
                 r=[("cv", 0), "ident"], w=["ps_x"])
            p.op("pe", lambda eng, i_=Bm[:, cols]: eng.transpose(ps_B[:, 0:128], i_, ident[:]),
                 r=[("cv", 1), "ident"], w=["ps_B"])
            p.ts(R(xdt[:]), ps_x[:, 0:64], dt[:, c:c + 1], None, ALU.mult, None, r=["ps_x", "dt"], w=["xdt"])
            p.ts(R(Bw[:]), ps_B[:, 0:128], wdec[:, c:c + 1], None, ALU.mult, None, r=["ps_B", "wdec"], w=["Bw"])
            p.ts(dtab[:], ones[:], dta[:, c:c + 1], None, ALU.mult, None, r=["ones", "dta"], w=["dtab"])
            p.mm(ps_R[:, 0:128], dtab[:], tri[:], True, True, r=["dtab", "tri"], w=["ps_R"])
            p.act(E[:], ps_R[:, 0:128], AF.Exp, r=["ps_R"], w=["E"])
            p.tt(R(CE[:]), Cm[:, cols], E[:], ALU.mult, r=[("cv", 2), "E"], w=["CE"])
            p.tt(Rm[:], ps_R[:, 0:128], nm[:], ALU.add, r=["ps_R", "nm"], w=["Rm"])
            p.act(LT[:], Rm[:], AF.Exp, r=["Rm", "nacs"], w=["LT"], bias=nacs[:, c:c + 1])
            p.mm(ps_CB[:, 0:128], R(Bm[:, cols]), R(Cm[:, cols]), True, True, r=[("cv", 1), ("cv", 2)], w=["ps_CB"])
            p.tt(R(MT[:]), ps_CB[:, 0:128], LT[:], ALU.mult, r=["ps_CB", "LT"], w=["MT"])
            p.mm(ps_y[:, 0:64], R(MT[:]), R(xdt[:]), True, False, r=["MT", "xdt"], w=["ps_y"])
            p.mm(ps_y[:, 0:64], R(CE[:]), R(hT[:]), False, True, r=["CE", "hT"], w=["ps_y"])
            p.copy(ybuf[:, c, :], ps_y[:, 0:64], r=["ps_y"], w=[("yb", c)], e="act")
            if d == 0:
                p.stt(ybuf[:, c, :], ps_x[:, 0:64], scal[:, 2:3], ybuf[:, c, :], ALU.mult, ALU.add,
                      r=["ps_x", "scal", ("yb", c)], w=[("yb", c)])
            p.mm(ps_h[:, 0:64], R(Bw[:]), R(xdt[:]), True, True, r=["Bw", "xdt"], w=["ps_h"])
            p.ts(R(hT[:]), hT[:], dec[:, c:c + 1], None, ALU.mult, None, r=["hT", "dec"], w=["hT"])
            p.tt(R(hT[:]), hT[:], ps_h[:, 0:64], ALU.add, r=["hT", "ps_h"], w=["hT"])
        p.store(y_d[d], ybuf[:], r=[("yb", c) for c in range(NCH)])
    p.finish()
    p.emit()
    return nc


def flipseq(a):
    return np.concatenate([a[:CTX][::-1], a[CTX:][::-1]], 0)


def ssd_inputs(l, xbc_full, dt_full, P):
    jj = np.arange(128)
    tri = (jj[:, None] <= jj[None, :]).astype(np.float32)
    negmask = np.where(jj[:, None] <= jj[None, :], 0.0, NEG).astype(np.float32)
    ident = np.eye(128, dtype=np.float32)
    cwl, cbl = P["ssd_conv_w"][l], P["ssd_conv_b"][l]
    in_maps = []
    for i in range(NCORES):
        g = i // 4
        colsets = [np.arange(64 * i, 64 * i + 64), 512 + np.arange(128 * g, 128 * g + 128),
                   768 + np.arange(128 * g, 128 * g + 128)]
        xbc = np.zeros((2, 3, 128, TT), np.float32)
        cw = np.zeros((2, 128, 3, 4), np.float32)
        dtm = np.zeros((2, 128, NCH), np.float32)
        scal = np.zeros((2, 128, 4), np.float32)
        for d in range(2):
            for ch, cs in enumerate(colsets):
                a = xbc_full[:, cs]
                if d == 1:
                    a = flipseq(a)
                xbc[d, ch, :len(cs)] = a.T
                taps = cwl[:, cs] if d == 0 else cwl[::-1][:, cs]
                cw[d, :len(cs), ch, 0:3] = taps.T
                cw[d, :len(cs), ch, 3] = cbl[cs]
            dcol = dt_full[:, d * 8 + i]
            if d == 1:
                dcol = flipseq(dcol[:, None])[:, 0]
            dtm[d] = dcol.reshape(NCH, 128).T
            scal[d, :, 0] = P["ssd_dt_bias"][l, d, i]
            scal[d, :, 1] = P["ssd_a_log"][l, d, i]
            scal[d, :, 2] = P["ssd_d"][l, i]
        in_maps.append({"xbc": xbc, "cw": cw, "dtm": dtm, "scal": scal, "tri": tri, "negmask": negmask, "ident": ident})
    return in_maps


def tm_to_nat(y, rev):
    a = y.transpose(1, 0, 2).reshape(TT, -1)
    if rev:
        a = np.concatenate([a[:CTX][::-1], a[CTX:][::-1]], 0)
    return a


def ssd_gather(res):
    yf = np.concatenate([tm_to_nat(res[i]["y"][0], False) for i in range(NCORES)], 1)
    yb = np.concatenate([tm_to_nat(res[i]["y"][1], True) for i in range(NCORES)], 1)
    return yf, yb


NCG = TT // 64
LNK = float(np.log(64.0 ** -0.5))


def build_gla():
    nc = bass.Bass("TRN2", target_bir_lowering=False)
    qk_d = nc.dram_tensor("qk", [2, 2, 64, TT], F32, kind="ExternalInput").ap()
    v_d = nc.dram_tensor("vtm", [2, 64, NCG, 64], F32, kind="ExternalInput").ap()
    g_d = nc.dram_tensor("gT", [2, 16, TT], F32, kind="ExternalInput").ap()
    gw_d = nc.dram_tensor("gw", [2, 16, 64], F32, kind="ExternalInput").ap()
    gb_d = nc.dram_tensor("gb", [2, 64, 1], F32, kind="ExternalInput").ap()
    cs_d = nc.dram_tensor("cs", [2, 2, 64, TT], F32, kind="ExternalInput").ap()
    rot_d = nc.dram_tensor("rot", [64, 64], F32, kind="ExternalInput").ap()
    cm_d = nc.dram_tensor("cmask", [64, TT], F32, kind="ExternalInput").ap()
    um_d = nc.dram_tensor("umask", [64, 64], F32, kind="ExternalInput").ap()
    id_d = nc.dram_tensor("ident", [128, 128], F32, kind="ExternalInput").ap()
    o_d = nc.dram_tensor("o", [2, 64, NCG, 64], F32, kind="ExternalOutput").ap()
    p = Prog(nc)
    BL = 512
    Q = p.sb([64, TT]); Kt = p.sb([64, TT]); LA = p.sb([64, TT]); Bt = p.sb([64, TT])
    vtm = p.sb([64, NCG, 64])
    gT = vtm[:, :, :].rearrange("p c v -> p (c v)")[0:16, :]
    gw = p.sb([16, 64]); gb = p.sb([64, 1]); ngb = p.sb([64, 1])
    rot = p.sb([64, 64]); um = p.sb([64, 64]); ident = p.sb([128, 128])
    csb = [p.sb([64, 2, BL]) for _ in range(2)]
    t1 = p.sb([64, BL]); t2 = p.sb([64, BL])
    ebl = p.sb([64, NCG])
    attT = p.sb([64, 64]); ktm = p.sb([64, 64]); S = p.sb([64, 64]); zt = p.sb([64, 64])
    obuf = [p.sb([64, 8, 64]) for _ in range(2)]
    ps_l = p.ps([128, 512]); ps_r = p.ps([128, 512])
    ps_a = p.ps([128, 512]); ps_t = p.ps([128, 512]); ps_o = p.ps([128, 512]); ps_kv = p.ps([128, 512])
    p.load(R(rot[:]), R(rot_d), w=["rot"]); p.load(um[:], um_d, w=["um"]); p.load(ident[:], id_d, w=["ident"])
    p.memset(zt[:], 0.0, w=["zt"])
    nblk = (TT + BL - 1) // BL
    for d in range(2):
        for k0 in range(0, TT, 2112):
            p.load(R(Q[:, k0:k0 + 2112]), R(qk_d[d, 0][:, k0:k0 + 2112]), w=["Q"])
            p.load(R(Kt[:, k0:k0 + 2112]), R(qk_d[d, 1][:, k0:k0 + 2112]), w=["K"])
            p.load(R(gT[:, k0:k0 + 2112]), R(g_d[d][:, k0:k0 + 2112]), w=["v"])
            p.load(Bt[:, k0:k0 + 2112], cm_d[:, k0:k0 + 2112], w=["B"])
        p.load(R(gw[:]), R(gw_d[d]), w=["gw"]); p.load(gb[:], gb_d[d], w=["gb"])
        p.ts(ngb[:], gb[:], -1.0, None, ALU.mult, None, r=["gb"], w=["ngb"])
        for bi in range(nblk):
            c0 = bi * BL
            cn = min(BL, TT - c0)
            p.mm(ps_l[0:64, 0:cn], R(gw[:]), R(gT[:, c0:c0 + cn]), True, True, r=["gw", "v"], w=["ps_l"])
            p.act(t1[:, 0:cn], ps_l[0:64, 0:cn], AF.Exp, r=["ps_l", "ngb"], w=["t1"], scale=-1.0, bias=ngb[:, 0:1])
            p.act(t1[:, 0:cn], t1[:, 0:cn], AF.Ln, r=["t1"], w=["t1"], bias=1.0)
            p.ts(LA[:, c0:c0 + cn], t1[:, 0:cn], -1.0 / 16.0, None, ALU.mult, None, r=["t1"], w=["LA"])
        for c0 in range(0, NCG, 33):
            p.load(R(vtm[:, c0:c0 + 33, :]), R(v_d[d][:, c0:c0 + 33, :]), w=["v"])
        for h0 in range(0, TT, 2112):
            p.op("dve", lambda eng, o=Bt[:, h0:h0 + 2112], a=Bt[:, h0:h0 + 2112], b=LA[:, h0:h0 + 2112]:
                 eng.tensor_tensor_scan(o, a, b, 0.0, ALU.mult, ALU.add), r=["B", "LA"], w=["B"])
        p.act(LA[:], Bt[:], AF.Exp, r=["B"], w=["LA"])
        p.act(Bt[:], Bt[:], AF.Exp, r=["B"], w=["B"], scale=-1.0, bias=LNK)
        p.copy(ebl[:], LA[:, 63:TT:64], r=["LA"], w=["ebl"])
        for bi in range(nblk):
            c0 = bi * BL
            cn = min(BL, TT - c0)
            cb = csb[bi % 2]
            p.load(cb[:, 0, 0:cn], cs_d[d, 0][:, c0:c0 + cn], w=[("cs", bi % 2)])
            p.load(cb[:, 1, 0:cn], cs_d[d, 1][:, c0:c0 + cn], w=[("cs", bi % 2)])
            for X, xk, E, ek in ((Q, "Q", LA, "LA"), (Kt, "K", Bt, "B")):
                p.mm(ps_r[0:64, 0:cn], R(rot[:]), R(X[:, c0:c0 + cn]), True, True, r=["rot", xk], w=["ps_r"])
                p.tt(t1[:, 0:cn], ps_r[0:64, 0:cn], cb[:, 1, 0:cn], ALU.mult, r=["ps_r", ("cs", bi % 2)], w=["t1"])
                p.tt(t2[:, 0:cn], X[:, c0:c0 + cn], cb[:, 0, 0:cn], ALU.mult, r=[xk, ("cs", bi % 2)], w=["t2"], e="pool")
                p.tt(t1[:, 0:cn], t1[:, 0:cn], t2[:, 0:cn], ALU.add, r=["t1", "t2"], w=["t1"])
                p.tt(R(X[:, c0:c0 + cn]), t1[:, 0:cn], E[:, c0:c0 + cn], ALU.mult, r=["t1", ek], w=[xk])
        p.copy(R(S[:]), zt[:], r=["zt"], w=["S"])
        for c in range(NCG):
            cols = slice(64 * c, 64 * c + 64)
            ob = obuf[(c // 8) % 2]
            obk = ("ob", (c // 8) % 2)
            p.mm(ps_a[0:64, 0:64], R(Kt[:, cols]), R(Q[:, cols]), True, True, r=["K", "Q"], w=["ps_a"])
            p.tt(R(attT[:]), ps_a[0:64, 0:64], um[:], ALU.mult, r=["ps_a", "um"], w=["attT"])
            p.op("pe", lambda eng, i_=Kt[:, cols]: eng.transpose(ps_t[0:64, 0:64], i_, ident[0:64, 0:64]),
                 r=["K", "ident"], w=["ps_t"])
            p.copy(R(ktm[:]), ps_t[0:64, 0:64], r=["ps_t"], w=["ktm"], e="act")
            p.mm(ps_o[0:64, 0:64], R(attT[:]), R(vtm[:, c, :]), True, False, r=["attT", "v"], w=["ps_o"])
            p.mm(ps_o[0:64, 0:64], R(Q[:, cols]), R(S[:]), False, True, r=["Q", "S"], w=["ps_o"])
            p.copy(ob[:, c % 8, :], ps_o[0:64, 0:64], r=["ps_o"], w=[obk], e="act")
            p.mm(ps_kv[0:64, 0:64], R(ktm[:]), R(vtm[:, c, :]), True, True, r=["ktm", "v"], w=["ps_kv"])
            p.ts(R(S[:]), S[:], ebl[:, c:c + 1], None, ALU.mult, None, r=["S", "ebl"], w=["S"])
            p.stt(R(S[:]), ps_kv[0:64, 0:64], ebl[:, c:c + 1], S[:], ALU.mult, ALU.add, r=["ps_kv", "ebl", "S"], w=["S"])
            if c % 8 == 7 or c == NCG - 1:
                g0 = (c // 8) * 8
                p.store(o_d[d][:, g0:c + 1, :], ob[:, 0:c + 1 - g0, :], r=[obk])
    p.finish()
    p.emit()
    return nc


def rope_tables():
    pos = np.arange(SEQ)
    rows = (pos // 64).astype(np.float32)
    cols = (pos % 64).astype(np.float32)
    inv = (np.float32(10000.0) ** (-np.arange(16, dtype=np.float32) / np.float32(16))).astype(np.float32)
    ang = np.concatenate([rows[:, None] * inv, cols[:, None] * inv], -1).astype(np.float32)
    cos = np.cos(ang).astype(np.float32)
    sin = np.sin(ang).astype(np.float32)
    c2 = np.concatenate([np.ones((CTX, 64), np.float32), np.concatenate([cos, cos], 1)], 0)
    s2 = np.concatenate([np.zeros((CTX, 64), np.float32), np.concatenate([sin, sin], 1)], 0)
    return c2, s2


def gla_inputs(l, q_full, k_full, v_full, g_full, P):
    c2, s2 = rope_tables()
    rot = np.zeros((64, 64), np.float32)
    for m in range(32):
        rot[m + 32, m] = -1.0
        rot[m, m + 32] = 1.0
    cmask = np.ones((64, TT), np.float32)
    cmask[:, ::64] = 0.0
    jj = np.arange(64)
    umask = (jj[:, None] <= jj[None, :]).astype(np.float32)
    ident = np.eye(128, dtype=np.float32)
    tabs = []
    for d in range(2):
        a, b = (c2, s2) if d == 0 else (flipseq(c2), flipseq(s2))
        tabs.append(np.stack([a.T, b.T]))
    cs = np.ascontiguousarray(np.stack(tabs))
    in_maps = []
    for i in range(NCORES):
        hh, vh = i // 2, i % 2
        qk = np.zeros((2, 2, 64, TT), np.float32)
        vtm = np.zeros((2, 64, NCG, 64), np.float32)
        gT = np.zeros((2, 16, TT), np.float32)
        gw = np.zeros((2, 16, 64), np.float32)
        gb = np.zeros((2, 64, 1), np.float32)
        for d in range(2):
            f = (lambda a: a) if d == 0 else flipseq
            qk[d, 0] = f(q_full[:, 64 * hh:64 * hh + 64]).T
            qk[d, 1] = f(k_full[:, 64 * hh:64 * hh + 64]).T
            vv = f(v_full[:, 128 * hh + 64 * vh:128 * hh + 64 * vh + 64])
            vtm[d] = vv.reshape(NCG, 64, 64).transpose(1, 0, 2)
            gT[d] = f(g_full[:, 16 * d:16 * d + 16]).T
            gw[d] = P["gla_gate_w"][l, d][:, 64 * hh:64 * hh + 64]
            gb[d, :, 0] = P["gla_gate_b"][l, d][64 * hh:64 * hh + 64]
        in_maps.append({"qk": qk, "vtm": vtm, "gT": gT, "gw": gw, "gb": gb, "cs": cs, "rot": rot,
                        "cmask": cmask, "umask": umask, "ident": ident})
    return in_maps


def gla_tm_to_nat(o, rev):
    a = o.transpose(1, 0, 2).reshape(TT, -1)
    if rev:
        a = np.concatenate([a[:CTX][::-1], a[CTX:][::-1]], 0)
    return a


def gla_gather(res):
    of = np.concatenate([gla_tm_to_nat(res[i]["o"][0], False) for i in range(NCORES)], 1)
    ob = np.concatenate([gla_tm_to_nat(res[i]["o"][1], True) for i in range(NCORES)], 1)
    return of, ob


_PROGS = {}


def _prog(name, fn):
    if name not in _PROGS:
        _PROGS[name] = fn()
    return _PROGS[name]


def phase_a_run(l, xfull, mod_x, mod_c, w_in):
    mx = [mod_x[j * D:(j + 1) * D] for j in range(6)]
    mc = [mod_c[j * D:(j + 1) * D] for j in range(6)]
    modA = np.concatenate([chunkcols(v) for v in (mx[1], mx[0], mc[1], mc[0])], 1)
    nc = _prog("A", lambda: build_phase_a(1056, 32))
    in_maps = []
    for i in range(NCORES):
        xt = np.concatenate([xfull[32 * i:32 * i + 32], xfull[CTX + 1024 * i:CTX + 1024 * i + 1024]], 0).T
        in_maps.append({"xT": np.ascontiguousarray(xt), "modA": modA, "w_in": np.ascontiguousarray(w_in)})
    res = run_spmd(nc, in_maps)
    pf = np.zeros((TT, DPROJ), np.float32)
    for i in range(NCORES):
        o = res[i]["pxT"].T
        pf[32 * i:32 * i + 32] = o[:32]
        pf[CTX + 1024 * i:CTX + 1024 * i + 1024] = o[32:]
    return pf


def layer_forward(l, xfull, mod_x, mod_c, P):
    pf = phase_a_run(l, xfull, mod_x, mod_c, P["w_in"][l])
    u = pf[:, 0:512]
    naq, nak, nav = pf[:, 512:1024], pf[:, 1024:1536], pf[:, 1536:2048]
    z = pf[:, 2048:2560]
    xbc = pf[:, 2560:3584]
    dtc = pf[:, 3584:3600]
    gq, gk, gv, gr, gg = pf[:, 3600:3856], pf[:, 3856:4112], pf[:, 4112:4624], pf[:, 4624:5136], pf[:, 5136:5168]
    s5f, s5b = s5_gather(run_spmd(_prog("S5", build_s5), s5_inputs(l, u, P)))
    nao = na_gather(run_spmd(_prog("NA", build_na), na_inputs(l, naq, nak, nav, P)))
    ssf, ssb = ssd_gather(run_spmd(_prog("SSD", build_ssd), ssd_inputs(l, xbc, dtc, P)))
    glf, glb = gla_gather(run_spmd(_prog("GLA", build_gla), gla_inputs(l, gq, gk, gv, gg, P)))
    mixfull = np.concatenate([s5f, s5b, u, nao, ssf, ssb, z, glf, glb, gr], 1)
    res = run_spmd(_prog("C", build_phase_c), phase_c_inputs(l, xfull, mixfull, None, mod_x, mod_c, P))
    return phase_c_gather(res, "x2T"), res


def kernel(**inputs):
    P = {k: np.asarray(v, np.float32) for k, v in inputs.items()}
    mod_x, mod_c = run_mod(P["c"], P["c_ctx"], P["w_mod"], P["b_mod"])
    xfull = np.concatenate([P["ctx"][0], P["x"][0]], 0)
    for l in range(DEPTH):
        xfull, _ = layer_forward(l, xfull, mod_x[l], mod_c[l], P)
    return np.ascontiguousarray(xfull[CTX:][None]).astype(np.float32)


from concourse.bass import IndirectOffsetOnAxis
U32 = mybir.dt.uint32
NTC = 1072
NCXF = 40
SROWS = 4144
BROWS = 448
PW = 270
PWIN = (0, 268, 536, 802)
NGT = 13


def rv(tile_ap, a, b, rev):
    if not rev:
        return tile_ap[:, a:b]
    if a == 0:
        return tile_ap[:, b - 1::-1]
    return tile_ap[:, b - 1:a - 1:-1]


def build_fused(nlayers=DEPTH, dbg=False):
    nc = bass.Bass("TRN2", target_bir_lowering=False)
    L = nlayers
    EI = lambda name, shape, dt=F32: nc.dram_tensor(name, list(shape), dt, kind="ExternalInput").ap()
    x0_d = EI("x0T", [D, NTC])
    cmk_d = EI("colmask", [128, 16])
    idxA_d = EI("idxA", [128, NGT * 8], U32)
    idxB_d = EI("idxB", [128, 28 * 4], U32)
    wmod_d = EI("wmod", [D, 6144]); bmod_d = EI("bmod", [128, 48]); cv_d = EI("cv", [128, KC, 2])
    win_d = EI("w_in", [L, D, DPROJ]); wout_d = EI("w_out", [L, D, D])
    up_d = EI("ffn_up", [L, D, 2 * DFF]); dn_d = EI("ffn_down", [L, DFF, D]); glu_d = EI("glu_w", [L, 512, 512])
    vec_d = EI("vecs", [L, 128, NV_C])
    lp_d = EI("lanep", [L, 2, 2, 128, 3]); bre_d = EI("bre", [L, 2, 2, 128, 16]); bim_d = EI("bim", [L, 2, 2, 128, 16])
    cre_d = EI("cre", [L, 2, 2, 128, 64]); cim_d = EI("cim", [L, 2, 2, 128, 64])
    nab_d = EI("nabias", [L, 5, 128, 576])
    scw_d = EI("ssd_cw", [L, 128, 3, 4]); ssc_d = EI("ssd_scal", [L, 2, 128, 4])
    ggw_d = EI("gla_gw", [L, 2, 16, 64]); ggb_d = EI("gla_gb", [L, 2, 64, 1]); gcs_d = EI("gla_cs", [2, 64, TT])
    tau_d = EI("tau1", [128, 128]); id_d = EI("ident", [128, 128]); dtsel_d = EI("dtsel", [16, 2])
    tri_d = EI("tri2", [2, 128, 128]); nm_d = EI("negmask2", [2, 128, 128])
    rot_d = EI("rot", [64, 64]); gcm_d = EI("cmask", [64, TT]); um_d = EI("umask2", [2, 64, 64])
    out_d = nc.dram_tensor("outT", [D, 1024], F32, kind="ExternalOutput").ap()
    IT = lambda name, shape: nc.dram_tensor(name, list(shape), F32).ap()
    xres = [IT("xres0", [D, NTC]), IT("xres1", [D, NTC])]
    pxloc = IT("pxloc", [1536, NTC])
    sA_lat = IT("sA_lat", [SROWS, 1024]); sA_ctx = IT("sA_ctx", [SROWS, 32])
    gA_lat = IT("gA_lat", [8 * SROWS, 1024]); gA_ctx = IT("gA_ctx", [8 * SROWS, 32])
    sB = IT("sB", [BROWS * 32, PW]); gB = IT("gB", [8 * BROWS * 32, PW])
    msend = IT("msend", [128, 96]); mg = IT("mg", [8 * 128, 96])
    dtscr = IT("dtscr", [2, TT])
    dbg_o = {}
    if dbg:
        dbg_o["px_send"] = nc.dram_tensor("dbg_sA", [SROWS, 1024], F32, kind="ExternalOutput").ap()
        dbg_o["sB"] = nc.dram_tensor("dbg_sB", [BROWS * 32, PW], F32, kind="ExternalOutput").ap()
        dbg_o["x1"] = nc.dram_tensor("dbg_x", [D, NTC], F32, kind="ExternalOutput").ap()
    rg = [list(range(NCORES))]
    p = Prog(nc)
    modsb = p.sb([128, 8, 96])
    idxA = p.sb([128, NGT * 8], U32); idxB = p.sb([128, 28 * 4], U32)
    cmk = p.sb([128, 16]); ident = p.sb([128, 128])
    modL = p.sb([128, 12, KC])
    p.load(idxA[:], idxA_d, w=["idxA"]); p.load(idxB[:], idxB_d, w=["idxB"])
    p.load(cmk[:], cmk_d, w=["cmk"]); p.load(ident[:], id_d, w=["ident"])

    def allgather(src, dst, rk, wk):
        p.coll(lambda eng: eng.collective_compute("AllGather", ALU.bypass, replica_groups=rg, ins=[src.opt()], outs=[dst.opt()]),
               r=[rk], w=[wk])

    def gatherA(out_lat, out_ctx, npart, t, r, wkeys):
        col = t * 8 + r
        p.dma(lambda eng: eng.indirect_dma_start(out=out_lat, out_offset=None, in_=gA_lat,
                                                 in_offset=IndirectOffsetOnAxis(idxA[0:npart, col:col + 1], 0)),
              r=["gA", "idxA"], w=wkeys, q="pool")
        p.dma(lambda eng: eng.indirect_dma_start(out=out_ctx, out_offset=None, in_=gA_ctx,
                                                 in_offset=IndirectOffsetOnAxis(idxA[0:npart, col:col + 1], 0)),
              r=["gA", "idxA"], w=wkeys, q="pool")

    def gather_seq(tile, npart, t, key, rr=False):
        for r in range(8):
            ol = tile[0:npart, CTX + 1024 * r:CTX + 1024 * r + 1024]
            oc = tile[0:npart, 32 * r:32 * r + 32]
            gatherA(R(ol) if rr else ol, R(oc) if rr else oc, npart, t, r, [key])

    sBv = sB.rearrange("(q b) c -> q b c", b=32)

    def send_rows(tile, rowbase, key, nat0=0, nlen=TT):
        for j in range(8):
            for s in range(4):
                w0 = PWIN[s]
                pieces = []
                ca, cb = max(w0, 0), min(w0 + PW, NCXF)
                if ca < cb:
                    t0 = 32 * j - 4 + ca
                    t1 = 32 * j - 4 + cb
                    lo, hi = max(t0, 0), min(t1, CTX)
                    if lo < hi:
                        pieces.append((ca - w0 + (lo - t0), lo, hi - lo))
                la, lb = max(w0, NCXF), min(w0 + PW, NTC)
                if la < lb:
                    t0 = 1024 * j - 4 + (la - NCXF)
                    t1 = 1024 * j - 4 + (lb - NCXF)
                    lo, hi = max(t0, 0), min(t1, SEQ)
                    if lo < hi:
                        pieces.append((la - w0 + (lo - t0), CTX + lo, hi - lo))
                for (off, nat, ln) in pieces:
                    lo, hi = max(nat, nat0), min(nat + ln, nat0 + nlen)
                    if lo >= hi:
                        continue
                    p.store(sBv[rowbase:rowbase + 64, j * 4 + s, off + (lo - nat):off + (hi - nat)],
                            tile[0:64, lo - nat0:hi - nat0], r=[key], w=["sB"])

    with p.scope():
        z = p.sb([128, 14 * PW])
        p.memset(z[:], 0.0, w=["z"])
        sBz = sB.rearrange("(a p b) c -> a p (b c)", p=128, b=14)
        for a in range(8):
            p.store(sBz[a], z[:], r=["z"], w=["sB"])
    with p.scope():
        cv0 = p.sb([128, KC, 2]); cv = p.sb([128, KC, 2]); bb = p.sb([128, 48]); ob = p.sb([128, 48, 2])
        wb = [p.sb([128, KC, 512]) for _ in range(2)]
        pss = [p.ps([128, 512]) for _ in range(2)]
        p.load(cv0[:], cv_d, w=["cv0"]); p.load(bb[:], bmod_d, w=["b"])
        p.act(R(cv[:]), cv0[:], AF.Silu, r=["cv0"], w=["cv"])
        wv = wmod_d.rearrange("(k p) m -> p k m", p=128)
        for g in range(12):
            wt = wb[g % 2]
            for k in range(KC):
                p.load(R(wt[:, k, :]), R(wv[:, k, g * 512:(g + 1) * 512]), w=[("w", g % 2, k)])
            for mm_ in range(4):
                j = g * 4 + mm_
                ps = pss[j % 2]
                for k in range(KC):
                    p.mm(ps[:, 0:2], R(wt[:, k, mm_ * 128:(mm_ + 1) * 128]), R(cv[:, k, :]), k == 0, k == KC - 1,
                         r=[("w", g % 2, k), "cv"], w=[("ps", j % 2)])
                p.ts(ob[:, j, :], ps[:, 0:2], bb[:, j:j + 1], None, ALU.add, None, r=[("ps", j % 2), "b"], w=["ob"])
        p.store(msend, ob[:].rearrange("p j t -> p (j t)"), r=["ob"], w=["msend"])
        allgather(msend, mg, "msend", "mg")
        p.load(modsb[:], mg.rearrange("(r p) n -> p r n", p=128), r=["mg"], w=["modsb"])
    p.dma(lambda eng: eng.dma_start(out=xres[0], in_=x0_d), r=[], w=["xres0"])
    p.barrier()

    for l in range(nlayers):
        xin, xout = xres[l % 2], xres[(l + 1) % 2]
        for v in range(6):
            for wh in range(2):
                src = modsb[:, 2 * l + v // 3, (16 * (v % 3)) * 2 + wh:(16 * (v % 3) + 16) * 2:2]
                if v in (1, 4):
                    p.ts(modL[:, 2 * v + wh, :], src, 1.0, None, ALU.add, None, r=["modsb"], w=["modL"])
                else:
                    p.copy(modL[:, 2 * v + wh, :], src, r=["modsb"], w=["modL"])
        MC = lambda v, wh, k: modL[:, 2 * v + wh, k:k + 1]
        with p.scope():
            NT = NTC
            xT = p.sb([128, KC, NT]); hT = xT
            onesN = p.sb([128, 128]); sq = p.sb([128, 512]); mean = p.sb([128, NT]); rstd = p.sb([128, NT])
            GW = 512
            wbuf = [p.sb([128, KC, GW]) for _ in range(2)]
            obuf = [p.sb([128, NT]) for _ in range(2)]
            ps_m = p.ps([128, 512]); ps_q = p.ps([128, 512]); ps_o = [p.ps([128, 512]) for _ in range(4)]
            tiles = [(0, 268), (268, 268), (536, 268), (804, 268)]
            p.memset(onesN[:], 1.0 / D, w=["ones"])
            xv = xin.rearrange("(k p) n -> p k n", p=128)
            for k in range(KC):
                p.load(R(xT[:, k, :]), R(xv[:, k, :]), r=[f"xres{l % 2}"], w=[("A", "x", k)])
            ln_stats(p, xT, NT, tiles, onesN, sq, ps_m, ps_q, mean, rstd, "A")
            for k in range(KC):
                p.tt(R(hT[:, k, :]), xT[:, k, :], mean[:], ALU.subtract, r=[("A", "x", k), ("A", "mean")],
                     w=[("h", k), ("A", "x", k)], e="pool")
                p.tt(R(hT[:, k, :]), hT[:, k, :], rstd[:], ALU.mult, r=[("h", k), ("A", "rstd")], w=[("h", k)])
                p.act(R(hT[:, k, 0:NCXF]), hT[:, k, 0:NCXF], AF.Identity, r=[("h", k), "modL"], w=[("h", k)],
                      scale=MC(1, 1, k), bias=MC(0, 1, k))
                p.act(R(hT[:, k, NCXF:NT]), hT[:, k, NCXF:NT], AF.Identity, r=[("h", k), "modL"], w=[("h", k)],
                      scale=MC(1, 0, k), bias=MC(0, 0, k))
            wv = win_d[l].rearrange("(k p) m -> p k m", p=128)
            ngrp = (DPROJ + GW - 1) // GW
            oi = 0
            segs = [(0, 512, "su", 0), (512, 2048, "s", 0), (2048, 2560, "l", 512 - 2048), (2560, 4624, "s", -512),
                    (4624, 5136, "l", 1024 - 4624), (5136, 5168, "s", -1024)]
            for g in range(ngrp):
                g0 = g * GW
                gw_ = min(GW, DPROJ - g0)
                wbt = wbuf[g % 2]
                for k in range(KC):
                    p.load(R(wbt[:, k, 0:gw_]), R(wv[:, k, g0:g0 + gw_]), w=[("w", g % 2, k)])
                for m0 in range(0, gw_, 128):
                    mw = min(128, gw_ - m0)
                    obt = obuf[oi % 2]
                    for ti, (c0, cn) in enumerate(tiles):
                        ps = ps_o[ti % 4]
                        for k in range(KC):
                            p.mm(ps[0:mw, 0:cn], R(wbt[:, k, m0:m0 + mw]), R(hT[:, k, c0:c0 + cn]), k == 0, k == KC - 1,
                                 r=[("w", g % 2, k), ("h", k)], w=[("pso", ti % 4)])
                        p.copy(obt[0:mw, c0:c0 + cn], ps[0:mw, 0:cn], r=[("pso", ti % 4)], w=[("ob", oi % 2)],
                               e="act" if ti % 2 == 0 else "dve")
                    ca0 = g0 + m0
                    for (a, b, kind, sh) in segs:
                        lo, hi = max(a, ca0), min(b, ca0 + mw)
                        if lo >= hi:
                            continue
                        pr = slice(lo - ca0, hi - ca0)
                        if kind in ("s", "su"):
                            ro = lo + (sh if kind == "s" else 0)
                            p.store(sA_lat[ro:ro + hi - lo, :], obt[pr, 44:1068], r=[("ob", oi % 2)], w=["sA"])
                            p.store(sA_ctx[ro:ro + hi - lo, :], obt[pr, 4:36], r=[("ob", oi % 2)], w=["sA"])
                        if kind in ("l", "su"):
                            ro = lo + (sh if kind == "l" else 0)
                            p.store(pxloc[ro:ro + hi - lo, :], obt[pr, :], r=[("ob", oi % 2)], w=["pxloc"])
                    oi += 1
        allgather(sA_lat, gA_lat, "sA", "gA")
        allgather(sA_ctx, gA_ctx, "sA", "gA")
        if dbg and l == 0:
            p.dma(lambda eng: eng.dma_start(out=dbg_o["px_send"], in_=sA_lat), r=["sA"], w=["dbgA"])
        p.barrier()
        import os as _os
        _skip = _os.environ.get("FUSED_SKIP", "").split(",")
        with p.scope():
          if "S5" not in _skip:
            BL = 512
            u = p.sb([64, TT]); yo = [p.sb([64, TT]) for _ in range(2)]
            tau = p.sb([128, 128]); lp = p.sb([128, 3]); sc = p.sb([128, 16])
            braw = p.sb([128, 2, 16]); bbf = p.sb([128, 2, 64])
            bbT = [[p.sb([64, 2, 128]) for _ in range(2)] for _ in range(2)]
            cc = [[p.sb([128, 2, 64]) for _ in range(2)] for _ in range(2)]
            cosT = [[p.sb([128, BL]) for _ in range(2)] for _ in range(2)]
            sinT = [[p.sb([128, BL]) for _ in range(2)] for _ in range(2)]
            rhoT = [[p.sb([128, 128]) for _ in range(2)] for _ in range(2)]
            cq = [[p.sb([128, 2]) for _ in range(2)] for _ in range(2)]
            tmp = [p.sb([128, 128]) for _ in range(3)]
            tki = p.sb([128, 128], I32); ang = p.sb([128, 128])
            b_re = p.sb([128, BL]); b_im = p.sb([128, BL]); v_re = p.sb([128, BL]); v_im = p.sb([128, BL])
            m1 = p.sb([128, BL]); m2 = p.sb([128, BL]); w_re = p.sb([128, BL]); w_im = p.sb([128, BL])
            s_re = p.sb([128, BL]); s_im = p.sb([128, BL])
            car = [p.sb([128, 2]) for _ in range(2)]; ct = p.sb([128, 2])
            psb = [p.ps([128, 512]) for _ in range(4)]; psy = [p.ps([128, 512]) for _ in range(2)]; pst = p.ps([128, 512])
            p.load(tau[:], tau_d, w=["tau"])
            gather_seq(u, 64, 0, "u", rr=True)
            for d in range(2):
                for t in range(2):
                    tg = f"s{d}{t}"
                    p.load(lp[:], lp_d[l, d, t], w=["lp"])
                    p.load(braw[:, 0, :], bre_d[l, d, t], w=["braw"]); p.load(braw[:, 1, :], bim_d[l, d, t], w=["braw"])
                    p.load(R(cc[d][t][:, 0, :]), R(cre_d[l, d, t]), w=[("cc", d, t)])
                    p.load(R(cc[d][t][:, 1, :]), R(cim_d[l, d, t]), w=[("cc", d, t)])
                    p.act(sc[:, 0:1], lp[:, 2:3], AF.Exp, r=["lp"], w=["sc"])
                    p.tt(sc[:, 1:2], lp[:, 0:1], sc[:, 0:1], ALU.mult, r=["lp", "sc"], w=["sc"])
                    p.act(sc[:, 2:3], sc[:, 1:2], AF.Exp, r=["sc"], w=["sc"])
                    p.tt(sc[:, 3:4], lp[:, 1:2], sc[:, 0:1], ALU.mult, r=["lp", "sc"], w=["sc"])
                    p.ts(ang[:], tau[:], sc[:, 3:4], None, ALU.mult, None, r=["tau", "sc"], w=[tg + "x"])
                    trig(p, ang[:], 128, [tmp[0][:], tmp[1][:], tmp[2][:]], tki[:], cosT[d][t][:, 0:128], sinT[d][t][:, 0:128], tg)
                    for rep in range(1, 4):
                        p.copy(cosT[d][t][:, rep * 128:(rep + 1) * 128], cosT[d][t][:, 0:128], r=[tg + "cos"], w=[tg + "cos"])
                        p.copy(sinT[d][t][:, rep * 128:(rep + 1) * 128], sinT[d][t][:, 0:128], r=[tg + "sin"], w=[tg + "sin"])
                    p.copy(cq[d][t][:, 0:1], cosT[d][t][:, 127:128], r=[tg + "cos"], w=[("cq", d, t)])
                    p.copy(cq[d][t][:, 1:2], sinT[d][t][:, 127:128], r=[tg + "sin"], w=[("cq", d, t)])
                    p.memset(rhoT[d][t][:], 1.0, w=[("rho", d, t)])
                    p.ts(rhoT[d][t][:], rhoT[d][t][:], sc[:, 2:3], None, ALU.mult, None, r=["sc", ("rho", d, t)], w=[("rho", d, t)])
                    p.tt(sc[:, 4:5], sc[:, 2:3], cosT[d][t][:, 0:1], ALU.mult, r=["sc", tg + "cos"], w=["sc"])
                    p.ts(sc[:, 4:5], sc[:, 4:5], -1.0, None, ALU.add, None, r=["sc"], w=["sc"])
                    p.tt(sc[:, 5:6], sc[:, 2:3], sinT[d][t][:, 0:1], ALU.mult, r=["sc", tg + "sin"], w=["sc"])
                    p.tt(sc[:, 6:7], lp[:, 0:1], lp[:, 0:1], ALU.mult, r=["lp"], w=["sc"])
                    p.tt(sc[:, 9:10], lp[:, 1:2], lp[:, 1:2], ALU.mult, r=["lp"], w=["sc"])
                    p.tt(sc[:, 6:7], sc[:, 6:7], sc[:, 9:10], ALU.add, r=["sc"], w=["sc"])
                    p.op("dve", lambda eng: eng.reciprocal(sc[:, 6:7], sc[:, 6:7]), r=["sc"], w=["sc"])
                    p.tt(sc[:, 7:8], sc[:, 4:5], lp[:, 0:1], ALU.mult, r=["sc", "lp"], w=["sc"])
                    p.tt(sc[:, 9:10], sc[:, 5:6], lp[:, 1:2], ALU.mult, r=["sc", "lp"], w=["sc"])
                    p.tt(sc[:, 7:8], sc[:, 7:8], sc[:, 9:10], ALU.add, r=["sc"], w=["sc"])
                    p.tt(sc[:, 7:8], sc[:, 7:8], sc[:, 6:7], ALU.mult, r=["sc"], w=["sc"])
                    p.tt(sc[:, 8:9], sc[:, 5:6], lp[:, 0:1], ALU.mult, r=["sc", "lp"], w=["sc"])
                    p.tt(sc[:, 9:10], sc[:, 4:5], lp[:, 1:2], ALU.mult, r=["sc", "lp"], w=["sc"])
                    p.tt(sc[:, 8:9], sc[:, 8:9], sc[:, 9:10], ALU.subtract, r=["sc"], w=["sc"])
                    p.tt(sc[:, 8:9], sc[:, 8:9], sc[:, 6:7], ALU.mult, r=["sc"], w=["sc"])
                    p.memset(bbf[:], 0.0, w=["bbf"])
                    for half in range(2):
                        rows = slice(64 * half, 64 * half + 64)
                        co = 16 * (2 * t + half)
                        p.ts(bbf[rows, 0, co:co + 16], braw[rows, 1, :], sc[rows, 8:9], -1.0, ALU.mult, ALU.mult,
                             r=["braw", "sc"], w=["bbf"])
                        p.stt(bbf[rows, 0, co:co + 16], braw[rows, 0, :], sc[rows, 7:8], bbf[rows, 0, co:co + 16], ALU.mult, ALU.add,
                              r=["braw", "sc", "bbf"], w=["bbf"])
                        p.ts(bbf[rows, 1, co:co + 16], braw[rows, 0, :], sc[rows, 8:9], None, ALU.mult, None,
                             r=["braw", "sc"], w=["bbf"])
                        p.stt(bbf[rows, 1, co:co + 16], braw[rows, 1, :], sc[rows, 7:8], bbf[rows, 1, co:co + 16], ALU.mult, ALU.add,
                              r=["braw", "sc", "bbf"], w=["bbf"])
                    for c2 in range(2):
                        p.op("pe", lambda eng, o=pst[0:64, c2 * 128:(c2 + 1) * 128], i_=bbf[:, c2, :]: eng.transpose(o, i_, ident[:]),
                             r=["bbf", "ident"], w=["pst"])
                    p.copy(R(bbT[d][t][:, :, :]), pst[0:64, 0:256].rearrange("p (a m) -> p a m", a=2), r=["pst"], w=[("bbT", d, t)], e="act")
                    p.ts(R(cc[d][t][:, 1, :]), cc[d][t][:, 1, :], -1.0, None, ALU.mult, None, r=[("cc", d, t)], w=[("cc", d, t)])
            blocks = [(0, 256)] + [(CTX + BL * k, BL) for k in range(16)]
            for d in range(2):
                rev = d == 1
                order = blocks if not rev else [blocks[0]] + blocks[:0:-1]
                for t in range(2):
                    p.memset(car[t][:], 0.0, w=[("car", t)])
                for bi, (c0, cn) in enumerate(order):
                    py = psy[bi % 2]
                    pyk = ("psy", bi % 2)
                    for t in range(2):
                        pr, pim = psb[2 * t], psb[2 * t + 1]
                        p.mm(pr[:, 0:cn], R(bbT[d][t][:, 0, :]), R(u[:, c0:c0 + cn]), True, True, r=[("bbT", d, t), "u"], w=[("psb", 2 * t)])
                        p.mm(pim[:, 0:cn], R(bbT[d][t][:, 1, :]), R(u[:, c0:c0 + cn]), True, True, r=[("bbT", d, t), "u"], w=[("psb", 2 * t + 1)])
                        p.copy(b_re[:, 0:cn], pr[:, 0:cn], r=[("psb", 2 * t)], w=["b_re"], e="act")
                        p.copy(b_im[:, 0:cn], pim[:, 0:cn], r=[("psb", 2 * t + 1)], w=["b_im"], e="act")
                        Cv = rv(cosT[d][t][:], 0, cn, rev); Sv = rv(sinT[d][t][:], 0, cn, rev)
                        tgc, tgs = f"s{d}{t}cos", f"s{d}{t}sin"
                        e1 = "dve" if rev else "pool"
                        p.tt(m1[:, 0:cn], b_re[:, 0:cn], Cv, ALU.mult, r=["b_re", tgc], w=["m1"], e=e1)
                        p.tt(m2[:, 0:cn], b_im[:, 0:cn], Sv, ALU.mult, r=["b_im", tgs], w=["m2"], e=e1)
                        p.tt(v_re[:, 0:cn], m1[:, 0:cn], m2[:, 0:cn], ALU.add, r=["m1", "m2"], w=["v_re"], e=e1)
                        p.tt(m1[:, 0:cn], b_im[:, 0:cn], Cv, ALU.mult, r=["b_im", tgc], w=["m1"], e=e1)
                        p.tt(m2[:, 0:cn], b_re[:, 0:cn], Sv, ALU.mult, r=["b_re", tgs], w=["m2"], e=e1)
                        p.tt(v_im[:, 0:cn], m1[:, 0:cn], m2[:, 0:cn], ALU.subtract, r=["m1", "m2"], w=["v_im"], e=e1)
                        qs = list(range(0, cn, 128))
                        if rev:
                            qs = qs[::-1]
                        for q0 in qs:
                            p.op("dve", lambda eng, o=rv(w_re[:], q0, q0 + 128, rev), a=rhoT[d][t][:], b=rv(v_re[:], q0, q0 + 128, rev),
                                 i_=car[t][:, 0:1]: eng.tensor_tensor_scan(o, a, b, i_, ALU.mult, ALU.add),
                                 r=[("rho", d, t), "v_re", ("car", t)], w=["w_re"])
                            p.op("dve", lambda eng, o=rv(w_im[:], q0, q0 + 128, rev), a=rhoT[d][t][:], b=rv(v_im[:], q0, q0 + 128, rev),
                                 i_=car[t][:, 1:2]: eng.tensor_tensor_scan(o, a, b, i_, ALU.mult, ALU.add),
                                 r=[("rho", d, t), "v_im", ("car", t)], w=["w_im"])
                            last = q0 if rev else q0 + 127
                            cqt = cq[d][t]
                            p.ts(ct[:, 0:1], w_im[:, last:last + 1], cqt[:, 1:2], None, ALU.mult, None, r=["w_im", ("cq", d, t)], w=["ct"])
                            p.ts(ct[:, 1:2], w_re[:, last:last + 1], cqt[:, 1:2], None, ALU.mult, None, r=["w_re", ("cq", d, t)], w=["ct"])
                            p.stt(car[t][:, 0:1], w_re[:, last:last + 1], cqt[:, 0:1], ct[:, 0:1], ALU.mult, ALU.subtract,
                                  r=["w_re", ("cq", d, t), "ct"], w=[("car", t)])
                            p.stt(car[t][:, 1:2], w_im[:, last:last + 1], cqt[:, 0:1], ct[:, 1:2], ALU.mult, ALU.add,
                                  r=["w_im", ("cq", d, t), "ct"], w=[("car", t)])
                        p.tt(m1[:, 0:cn], w_re[:, 0:cn], Cv, ALU.mult, r=["w_re", tgc], w=["m1"], e=e1)
                        p.tt(m2[:, 0:cn], w_im[:, 0:cn], Sv, ALU.mult, r=["w_im", tgs], w=["m2"], e=e1)
                        p.tt(R(s_re[:, 0:cn]), m1[:, 0:cn], m2[:, 0:cn], ALU.subtract, r=["m1", "m2"], w=["s_re"])
                        p.tt(m1[:, 0:cn], w_im[:, 0:cn], Cv, ALU.mult, r=["w_im", tgc], w=["m1"], e=e1)
                        p.tt(m2[:, 0:cn], w_re[:, 0:cn], Sv, ALU.mult, r=["w_re", tgs], w=["m2"], e=e1)
                        p.tt(R(s_im[:, 0:cn]), m1[:, 0:cn], m2[:, 0:cn], ALU.add, r=["m1", "m2"], w=["s_im"])
                        p.mm(py[0:64, 0:cn], R(cc[d][t][:, 0, :]), R(s_re[:, 0:cn]), t == 0, False, r=[("cc", d, t), "s_re"], w=[pyk])
                        p.mm(py[0:64, 0:cn], R(cc[d][t][:, 1, :]), R(s_im[:, 0:cn]), False, t == 1, r=[("cc", d, t), "s_im"], w=[pyk])
                    p.copy(yo[d][:, c0:c0 + cn], py[0:64, 0:cn], r=[pyk], w=[("yo", d)], e="act")
            send_rows(yo[0], 0, ("yo", 0))
            send_rows(yo[1], 64, ("yo", 1))
        with p.scope():
          if "NA" not in _skip:
            qT = p.sb([64, TT]); kT = p.sb([64, TT]); vt = p.sb([64, 132, 64]); oT = p.sb([64, TT])
            bias = p.sb([128, 5, 576])
            S = p.sb([128, 832]); Pm = p.sb([128, 832]); PTs = p.sb([64, 13, 128]); dg = p.sb([128, 128]); st = p.sb([128, 4])
            psA = p.ps([128, 512]); psB = p.ps([128, 512]); psC = p.ps([128, 512])
            psT = [p.ps([128, 512]) for _ in range(4)]; pso = p.ps([128, 512])
            gather_seq(qT, 64, 1, "q", rr=True)
            gather_seq(kT, 64, 2, "k", rr=True)
            gather_seq(oT, 64, 3, "oT")
            for c in range(5):
                p.load(bias[:, c, :], nab_d[l, c], w=["bias"])
            for r0 in range(0, 132, 8):
                n8 = min(8, 132 - r0)
                bank = psT[(r0 // 8) % 4]
                bk = ("psT", (r0 // 8) % 4)
                for a in range(n8):
                    r_ = r0 + a
                    p.op("pe", lambda eng, o=bank[0:64, a * 64:(a + 1) * 64], i_=oT[:, 64 * r_:64 * r_ + 64]:
                         eng.transpose(o, i_, ident[0:64, 0:64]), r=["oT", "ident"], w=[bk])
                p.copy(R(vt[:, r0:r0 + n8, :]), bank[0:64, 0:n8 * 64].rearrange("p (a m) -> p a m", a=n8), r=[bk], w=["v"],
                       e="act" if (r0 // 8) % 2 == 0 else "dve")

            def block(qc0, lat, b):
                nk = 832 if lat else 256
                if lat:
                    kr0 = na_kr0(b)
                    kc0 = CTX + 64 * kr0
                    cls = na_cls(b)
                    p.mm(psA[:, 0:288], R(qT[:, qc0:qc0 + 128]), R(kT[:, kc0:kc0 + 288]), True, True, r=["q", "k"], w=["psA"])
                    p.mm(psB[:, 0:288], R(qT[:, qc0:qc0 + 128]), R(kT[:, kc0 + 288:kc0 + 576]), True, True, r=["q", "k"], w=["psB"])
                    p.mm(psC[:, 0:256], R(qT[:, qc0:qc0 + 128]), R(kT[:, 0:256]), True, True, r=["q", "k"], w=["psC"])
                    p.stt(S[:, 0:288], psA[:, 0:288], 0.125, bias[:, cls, 0:288], ALU.mult, ALU.add, r=["psA", "bias"], w=["S"])
                    p.stt(S[:, 288:576], psB[:, 0:288], 0.125, bias[:, cls, 288:576], ALU.mult, ALU.add, r=["psB", "bias"], w=["S"])
                    p.act(S[:, 576:832], psC[:, 0:256], AF.Copy, r=["psC"], w=["S"], scale=0.125)
                else:
                    p.mm(psC[:, 0:256], R(qT[:, qc0:qc0 + 128]), R(kT[:, 0:256]), True, True, r=["q", "k"], w=["psC"])
                    p.act(S[:, 0:256], psC[:, 0:256], AF.Copy, r=["psC"], w=["S"], scale=0.125)
                p.op("dve", lambda eng: eng.reduce_max(st[:, 0:1], S[:, 0:nk], AX.X), r=["S"], w=["st"])
                p.ts(st[:, 1:2], st[:, 0:1], -1.0, None, ALU.mult, None, r=["st"], w=["st"])
                p.act(R(Pm[:, 0:nk]), S[:, 0:nk], AF.Exp, r=["S", "st"], w=["P", "st2"], bias=st[:, 1:2], accum_out=st[:, 2:3])
                p.op("dve", lambda eng: eng.reciprocal(st[:, 3:4], st[:, 2:3]), r=["st2", "P"], w=["st3"])
                p.ts(R(dg[:]), ident[:], st[:, 3:4], None, ALU.mult, None, r=["ident", "st3"], w=["dg"])
                nt = nk // 64
                for kt in range(nt):
                    bank = psT[kt // 4]
                    p.mm(bank[0:64, (kt % 4) * 128:(kt % 4) * 128 + 128], R(Pm[:, kt * 64:(kt + 1) * 64]), R(dg[:]), True, True,
                         r=["P", "dg"], w=[("psT", kt // 4)])
                for bk_ in range((nt + 3) // 4):
                    n4 = min(4, nt - 4 * bk_)
                    p.copy(R(PTs[:, 4 * bk_:4 * bk_ + n4, :]), psT[bk_][0:64, 0:n4 * 128].rearrange("p (a m) -> p a m", a=n4),
                           r=[("psT", bk_)], w=["PTs"], e="act" if bk_ % 2 == 0 else "dve")
                for kt in range(nt):
                    if lat:
                        row = 4 + kr0 + kt if kt < 9 else kt - 9
                    else:
                        row = kt
                    p.mm(pso[0:64, 0:128], R(vt[:, row, :]), R(PTs[:, kt, :]), kt == 0, kt == nt - 1, r=["v", "PTs"], w=["pso"])
                p.copy(oT[:, qc0:qc0 + 128], pso[0:64, 0:128], r=["pso"], w=["oT"], e="act")

            for cb in range(2):
                block(128 * cb, False, cb)
            for b in range(64):
                block(CTX + 128 * b, True, b)
            send_rows(oT, 128, "oT")
        with p.scope():
            raw = p.sb([128, TT])
            cv_ = [p.sb([128, TT]) for _ in range(3)]
            cw = p.sb([128, 3, 4]); scal = p.sb([128, 4])
            tri = p.sb([128, 128]); nm = p.sb([128, 128]); ones = p.sb([128, 128]); zt = p.sb([128, 64])
            dtsel = p.sb([16, 2]); dtall = p.sb([128, NCH, 2])
            dt = p.sb([128, NCH]); dta = p.sb([128, NCH]); tA = p.sb([128, NCH]); tB = p.sb([128, NCH])
            nacs = p.sb([128, NCH]); wdec = p.sb([128, NCH]); dec = p.sb([128, NCH])
            yT = raw[0:64, :]
            xdt = p.sb([128, 64]); Bw = p.sb([128, 128]); dtab = p.sb([128, 128])
            E = p.sb([128, 128]); CE = p.sb([128, 128]); Rm = p.sb([128, 128]); LT = p.sb([128, 128]); MT = p.sb([128, 128])
            hT = p.sb([128, 64])
            ps_x = p.ps([128, 512]); ps_B = p.ps([128, 512]); ps_R = p.ps([128, 512]); ps_CB = p.ps([128, 512])
            ps_y = p.ps([128, 512]); ps_h = p.ps([128, 512]); ps_s = p.ps([128, 512])
            p.memset(ones[:], 1.0, w=["ones"]); p.memset(zt[:], 0.0, w=["zt"])
            p.load(cw[:], scw_d[l], w=["cw"])
            dt16 = raw[0:16, :]
            for r in range(8):
                p.load(dt16[:, CTX + 1024 * r:CTX + 1024 * r + 1024], gA_lat[r * SROWS + 3072:r * SROWS + 3088, :], r=["gA"], w=["raw"])
                p.load(dt16[:, 32 * r:32 * r + 32], gA_ctx[r * SROWS + 3072:r * SROWS + 3088, :], r=["gA"], w=["raw"])
            p.load(dtsel[:], dtsel_d, w=["dtsel"])
            for c in range(NCH):
                p.mm(ps_s[:, 2 * c:2 * c + 2], dt16[:, 128 * c:128 * c + 128], dtsel[:], True, True, r=["raw", "dtsel"], w=["ps_s"])
            p.copy(dtall[:], ps_s[:, 0:2 * NCH].rearrange("p (c d) -> p c d", d=2), r=["ps_s"], w=["dtall"])
            p.barrier()
            segs_ = [(0, CTX), (CTX, TT)]
            for ch in range(3):
                np_ = 64 if ch == 0 else 128
                gather_seq(raw, np_, 4 + ch, "raw")
                o = cv_[ch]
                for (a, b) in segs_:
                    p.ts(R(o[0:np_, a:b]), raw[0:np_, a:b], cw[0:np_, ch, 1:2], cw[0:np_, ch, 3:4], ALU.mult, ALU.add,
                         r=["raw", "cw"], w=[("cv", ch)])
                    p.stt(R(o[0:np_, a + 1:b]), raw[0:np_, a:b - 1], cw[0:np_, ch, 0:1], o[0:np_, a + 1:b], ALU.mult, ALU.add,
                          r=["raw", "cw", ("cv", ch)], w=[("cv", ch)])
                    p.stt(R(o[0:np_, a:b - 1]), raw[0:np_, a + 1:b], cw[0:np_, ch, 2:3], o[0:np_, a:b - 1], ALU.mult, ALU.add,
                          r=["raw", "cw", ("cv", ch)], w=[("cv", ch)])
                p.act(R(o[0:np_, :]), o[0:np_, :], AF.Silu, r=[("cv", ch)], w=[("cv", ch)])
            xs, Bm, Cm = cv_
            p.barrier()
            for d in range(2):
                rev = d == 1
                p.load(scal[:], ssc_d[l, d], w=["scal"])
                p.load(tri[:], tri_d[d], w=["tri"]); p.load(nm[:], nm_d[d], w=["nm"])
                p.ts(dt[:], dtall[:, :, d], scal[:, 0:1], None, ALU.add, None, r=["dtall", "scal"], w=["dt"])
                p.ts(tA[:], dt[:], 0.0, None, ALU.max, None, r=["dt"], w=["tA"])
                p.ts(tB[:], dt[:], 0.0, None, ALU.min, None, r=["dt"], w=["tB"])
                p.tt(tB[:], tB[:], tA[:], ALU.subtract, r=["tA", "tB"], w=["tB"])
                p.act(tB[:], tB[:], AF.Exp, r=["tB"], w=["tB"])
                p.act(tB[:], tB[:], AF.Ln, r=["tB"], w=["tB"], bias=1.0)
                p.tt(dt[:], tA[:], tB[:], ALU.add, r=["tA", "tB"], w=["dt"])
                p.act(scal[:, 3:4], scal[:, 1:2], AF.Exp, r=["scal"], w=["scal"])
                p.ts(dta[:], dt[:], scal[:, 3:4], -1.0, ALU.mult, ALU.mult, r=["dt", "scal"], w=["dta"])
                p.mm(ps_s[:, 0:NCH], tri[:], dta[:], True, True, r=["tri", "dta", "dt"], w=["ps_s"])
                p.ts(nacs[:], ps_s[:, 0:NCH], -1.0, None, ALU.mult, None, r=["ps_s"], w=["nacs"])
                p.mm(ps_s[:, 0:NCH], ones[:], dta[:], True, True, r=["ones", "dta", "nacs"], w=["ps_s"])
                p.tt(wdec[:], ps_s[:, 0:NCH], nacs[:], ALU.add, r=["ps_s", "nacs"], w=["wdec"])
                p.act(wdec[:], wdec[:], AF.Exp, r=["wdec"], w=["wdec"])
                p.act(dec[:], ps_s[:, 0:NCH], AF.Exp, r=["ps_s"], w=["dec"])
                p.copy(R(hT[:]), zt[:], r=["zt"], w=["hT"])
                corder = list(range(NCH)) if not rev else [1, 0] + list(range(NCH - 1, 1, -1))
                for c in corder:
                    cols = slice(128 * c, 128 * c + 128)
                    p.op("pe", lambda eng, i_=xs[0:64, cols]: eng.transpose(ps_x[:, 0:64], i_, ident[0:64, 0:64]),
                         r=[("cv", 0), "ident"], w=["ps_x"])
                    p.op("pe", lambda eng, i_=Bm[:, cols]: eng.transpose(ps_B[:, 0:128], i_, ident[:]),
                         r=[("cv", 1), "ident"], w=["ps_B"])
                    p.ts(R(xdt[:]), ps_x[:, 0:64], dt[:, c:c + 1], None, ALU.mult, None, r=["ps_x", "dt"], w=["xdt"])
                    p.ts(R(Bw[:]), ps_B[:, 0:128], wdec[:, c:c + 1], None, ALU.mult, None, r=["ps_B", "wdec"], w=["Bw"])
                    p.ts(dtab[:], ones[:], dta[:, c:c + 1], None, ALU.mult, None, r=["ones", "dta"], w=["dtab"])
                    p.mm(ps_R[:, 0:128], dtab[:], tri[:], True, True, r=["dtab", "tri"], w=["ps_R"])
                    p.act(E[:], ps_R[:, 0:128], AF.Exp, r=["ps_R"], w=["E"])
                    p.tt(R(CE[:]), Cm[:, cols], E[:], ALU.mult, r=[("cv", 2), "E"], w=["CE"])
                    p.tt(Rm[:], ps_R[:, 0:128], nm[:], ALU.add, r=["ps_R", "nm"], w=["Rm"])
                    p.act(LT[:], Rm[:], AF.Exp, r=["Rm", "nacs"], w=["LT"], bias=nacs[:, c:c + 1])
                    p.mm(ps_CB[:, 0:128], R(Bm[:, cols]), R(Cm[:, cols]), True, True, r=[("cv", 1), ("cv", 2)], w=["ps_CB"])
                    p.tt(R(MT[:]), ps_CB[:, 0:128], LT[:], ALU.mult, r=["ps_CB", "LT"], w=["MT"])
                    p.mm(ps_y[0:64, 0:128], R(xdt[:]), R(MT[:]), True, False, r=["MT", "xdt"], w=["ps_y"])
                    p.mm(ps_y[0:64, 0:128], R(hT[:]), R(CE[:]), False, True, r=["CE", "hT"], w=["ps_y"])
                    p.copy(yT[:, cols], ps_y[0:64, 0:128], r=["ps_y"], w=["yT"], e="act")
                    if d == 0:
                        p.stt(yT[:, cols], xs[0:64, cols], scal[0:64, 2:3], yT[:, cols], ALU.mult, ALU.add,
                              r=[("cv", 0), "scal", "yT"], w=["yT"])
                    p.mm(ps_h[:, 0:64], R(Bw[:]), R(xdt[:]), True, True, r=["Bw", "xdt"], w=["ps_h"])
                    p.ts(R(hT[:]), hT[:], dec[:, c:c + 1], None, ALU.mult, None, r=["hT", "dec"], w=["hT"])
                    p.tt(R(hT[:]), hT[:], ps_h[:, 0:64], ALU.add, r=["hT", "ps_h"], w=["hT"])
                send_rows(yT, 192 + 64 * d, "yT")
                p.barrier()
        with p.scope():
            BL = 512
            Q = p.sb([64, TT]); Kt = p.sb([64, TT]); LA = p.sb([64, TT]); Bt = p.sb([64, TT])
            vtm = p.sb([64, NCG, 64])
            vflat = vtm[:, :, :].rearrange("p c v -> p (c v)")
            gw = p.sb([16, 64]); gb = p.sb([64, 1]); ngb = p.sb([64, 1])
            rot = p.sb([64, 64]); um = p.sb([64, 64]); cmt = p.sb([64, 2112])
            obg = [p.sb([64, 512]) for _ in range(2)]
            csb = [p.sb([64, 2, BL]) for _ in range(2)]
            t1 = p.sb([64, BL]); t2 = p.sb([64, BL])
            ebl = p.sb([64, NCG])
            attT = p.sb([64, 64]); ktm = p.sb([64, 64]); S_ = p.sb([64, 64]); zt = p.sb([64, 64])
            qd = p.sb([64, 64]); kd = p.sb([64, 64])
            ps_l = p.ps([128, 512]); ps_r = p.ps([128, 512])
            ps_a = p.ps([128, 512]); ps_t = p.ps([128, 512]); ps_o = p.ps([128, 512]); ps_kv = p.ps([128, 512])
            nblk = (TT + BL - 1) // BL
            p.load(R(rot[:]), R(rot_d), w=["rot"]); p.memset(zt[:], 0.0, w=["zt"])
            p.load(cmt[:], gcm_d[:, 0:2112], w=["cmt"])
            gather_seq(Q, 64, 8, "Q", rr=True)
            gather_seq(Kt, 64, 9, "K", rr=True)
            for bi in range(nblk):
                c0 = bi * BL
                cn = min(BL, TT - c0)
                cb = csb[bi % 2]
                p.load(cb[:, 0, 0:cn], gcs_d[0][:, c0:c0 + cn], w=[("cs", bi % 2)])
                p.load(cb[:, 1, 0:cn], gcs_d[1][:, c0:c0 + cn], w=[("cs", bi % 2)])
                for X, xk in ((Q, "Q"), (Kt, "K")):
                    p.mm(ps_r[0:64, 0:cn], R(rot[:]), R(X[:, c0:c0 + cn]), True, True, r=["rot", xk], w=["ps_r"])
                    p.tt(t1[:, 0:cn], ps_r[0:64, 0:cn], cb[:, 1, 0:cn], ALU.mult, r=["ps_r", ("cs", bi % 2)], w=["t1"])
                    p.tt(t2[:, 0:cn], X[:, c0:c0 + cn], cb[:, 0, 0:cn], ALU.mult, r=[xk, ("cs", bi % 2)], w=["t2"])
                    p.tt(R(X[:, c0:c0 + cn]), t1[:, 0:cn], t2[:, 0:cn], ALU.add, r=["t1", "t2"], w=[xk])
            gather_seq(LA, 64, 10, "LA")
            for r0 in range(0, NCG, 8):
                n8 = min(8, NCG - r0)
                for a in range(n8):
                    r_ = r0 + a
                    p.op("pe", lambda eng, o=ps_t[0:64, a * 64:(a + 1) * 64], i_=LA[:, 64 * r_:64 * r_ + 64]:
                         eng.transpose(o, i_, ident[0:64, 0:64]), r=["LA", "ident"], w=["ps_t"])
                p.copy(R(vtm[:, r0:r0 + n8, :]), ps_t[0:64, 0:n8 * 64].rearrange("p (a m) -> p a m", a=n8), r=["ps_t"], w=["v"],
                       e="act" if (r0 // 8) % 2 == 0 else "dve")
            for d in range(2):
                rev = d == 1
                gather_seq(Bt, 16, 11 + d, "B")
                p.load(gw[:], ggw_d[l, d], w=["gw"]); p.load(gb[:], ggb_d[l, d], w=["gb"])
                p.load(um[:], um_d[d], w=["um"])
                p.ts(ngb[:], gb[:], -1.0, None, ALU.mult, None, r=["gb"], w=["ngb"])
                for bi in range(nblk):
                    c0 = bi * BL
                    cn = min(BL, TT - c0)
                    p.mm(ps_l[0:64, 0:cn], gw[:], Bt[0:16, c0:c0 + cn], True, True, r=["gw", "B"], w=["ps_l"])
                    p.act(t1[:, 0:cn], ps_l[0:64, 0:cn], AF.Exp, r=["ps_l", "ngb"], w=["t1"], scale=-1.0, bias=ngb[:, 0:1])
                    p.act(t1[:, 0:cn], t1[:, 0:cn], AF.Ln, r=["t1"], w=["t1"], bias=1.0)
                    p.ts(LA[:, c0:c0 + cn], t1[:, 0:cn], -1.0 / 16.0, None, ALU.mult, None, r=["t1"], w=["LA"])
                for h0 in range(0, TT, 2112):
                    p.op("dve", lambda eng, o=rv(Bt[:], h0, h0 + 2112, rev), a=rv(cmt[:], 0, 2112, rev),
                         b=rv(LA[:], h0, h0 + 2112, rev): eng.tensor_tensor_scan(o, a, b, 0.0, ALU.mult, ALU.add),
                         r=["cmt", "LA", "B"], w=["B"])
                p.act(LA[:], Bt[:], AF.Exp, r=["B"], w=["LA"])
                p.act(Bt[:], Bt[:], AF.Exp, r=["B"], w=["B"], scale=-1.0, bias=LNK)
                if not rev:
                    p.copy(ebl[:], LA[:, 63:TT:64], r=["LA"], w=["ebl"])
                else:
                    p.copy(ebl[:], LA[:, 0:TT:64], r=["LA"], w=["ebl"])
                p.copy(R(S_[:]), zt[:], r=["zt"], w=["S"])
                if not rev:
                    groups = [list(range(g, min(g + 8, NCG))) for g in range(0, NCG, 8)]
                else:
                    groups = [[3, 2, 1, 0]] + [list(range(g + 7, g - 1, -1)) for g in range(124, 3, -8)]
                for gi_, grp_ in enumerate(groups):
                  gmin = min(grp_)
                  ob = obg[gi_ % 2]
                  obk = ("obg", gi_ % 2)
                  for c in grp_:
                    cols = slice(64 * c, 64 * c + 64)
                    p.tt(R(qd[:]), Q[:, cols], LA[:, cols], ALU.mult, r=["Q", "LA"], w=["qd"])
                    p.tt(R(kd[:]), Kt[:, cols], Bt[:, cols], ALU.mult, r=["K", "B"], w=["kd"])
                    p.mm(ps_a[0:64, 0:64], R(kd[:]), R(qd[:]), True, True, r=["kd", "qd"], w=["ps_a"])
                    p.tt(R(attT[:]), ps_a[0:64, 0:64], um[:], ALU.mult, r=["ps_a", "um"], w=["attT"])
                    p.op("pe", lambda eng: eng.transpose(ps_t[0:64, 0:64], kd[:], ident[0:64, 0:64]),
                         r=["kd", "ident"], w=["ps_t"])
                    p.copy(R(ktm[:]), ps_t[0:64, 0:64], r=["ps_t"], w=["ktm"], e="act")
                    p.mm(ps_o[0:64, 0:64], R(vtm[:, c, :]), R(attT[:]), True, False, r=["attT", "v"], w=["ps_o"])
                    p.mm(ps_o[0:64, 0:64], R(S_[:]), R(qd[:]), False, True, r=["qd", "S"], w=["ps_o"])
                    p.copy(ob[:, (c - gmin) * 64:(c - gmin) * 64 + 64], ps_o[0:64, 0:64], r=["ps_o"], w=[obk], e="act")
                    p.mm(ps_kv[0:64, 0:64], R(ktm[:]), R(vtm[:, c, :]), True, True, r=["ktm", "v"], w=["ps_kv"])
                    p.ts(R(S_[:]), S_[:], ebl[:, c:c + 1], None, ALU.mult, None, r=["S", "ebl"], w=["S"])
                    p.stt(R(S_[:]), ps_kv[0:64, 0:64], ebl[:, c:c + 1], S_[:], ALU.mult, ALU.add, r=["ps_kv", "ebl", "S"], w=["S"])
                  send_rows(ob, 320 + 64 * d, obk, nat0=64 * gmin, nlen=64 * len(grp_))
        allgather(sB, gB, "sB", "gB")
        p.barrier()
        with p.scope():
            P_ = PW
            x = p.sb([128, KC, P_])
            hff = p.sb([128, 43, P_]); mi = hff
            mix = p.sb([128, KC, P_])
            wb = [p.sb([128, 8192]) for _ in range(2)]
            sc1 = None
            vec = p.sb([128, NV_C]); glu = p.sb([128, 4, 512]); onesN = p.sb([128, 128]); sq = p.sb([128, 512])
            mean = p.sb([128, P_]); rstd = p.sb([128, P_])
            t1 = p.sb([128, 4, P_]); t2 = p.sb([128, 4, P_]); gR = p.sb([128, 4, P_])
            ca = p.sb([128, P_]); cg = p.sb([128, P_]); zt = p.sb([128, 43, 1])
            ps_m = p.ps([128, 512]); ps_q = p.ps([128, 512]); psr = [p.ps([128, 512]) for _ in range(6)]
            pi = [0]

            def nps():
                pi[0] = (pi[0] + 1) % 6
                return psr[pi[0]], ("psr", pi[0])

            V_S5D, V_GLUB, V_SSDW, V_GLAW, V_LN, V_CW = 0, 4, 8, 12, 13, 77
            p.memset(onesN[:], 1.0, w=["ones"]); p.memset(zt[:], 0.0, w=["zt"])
            p.load(vec[:], vec_d[l], w=["vec"])
            p.load(R(glu[:]), R(glu_d[l].rearrange("(k p) m -> p k m", p=128)), w=["glu"])
            wi = [0]

            def nwb():
                wi[0] = (wi[0] + 1) % 2
                return wb[wi[0]], ("wb", wi[0])

            def stats():
                for k in range(KC):
                    p.act(sq[:, 0:P_], x[:, k, :], AF.Square, r=[("x", k)], w=["sq"])
                    p.mm(ps_m[:, 0:P_], onesN[:], x[:, k, :], k == 0, k == KC - 1, r=[("x", k), "ones"], w=["ps_m"])
                    p.mm(ps_q[:, 0:P_], onesN[:], sq[:, 0:P_], k == 0, k == KC - 1, r=["sq", "ones"], w=["ps_q"])
                p.ts(mean[:], ps_m[:, 0:P_], 1.0 / D, None, ALU.mult, None, r=["ps_m"], w=["mean"])
                p.tt(rstd[:], mean[:], mean[:], ALU.mult, r=["mean"], w=["rstd"])
                p.stt(rstd[:], ps_q[:, 0:P_], 1.0 / D, rstd[:], ALU.mult, ALU.subtract, r=["ps_q", "rstd"], w=["rstd"])
                p.ts(rstd[:], rstd[:], 1e-6, None, ALU.add, None, r=["rstd"], w=["rstd"])
                p.act(rstd[:], rstd[:], AF.Sqrt, r=["rstd"], w=["rstd"])
                p.op("dve", lambda eng: eng.reciprocal(rstd[:], rstd[:]), r=["rstd"], w=["rstd"])

            def ln_affine(gcol, bcol):
                stats()
                for k in range(KC):
                    p.tt(x[:, k, :], x[:, k, :], mean[:], ALU.subtract, r=[("x", k), "mean"], w=[("x", k)], e="pool")
                    p.tt(x[:, k, :], x[:, k, :], rstd[:], ALU.mult, r=[("x", k), "rstd"], w=[("x", k)])
                    p.act(x[:, k, :], x[:, k, :], AF.Identity, r=[("x", k), "vec"], w=[("x", k)],
                          scale=vec[:, gcol + k:gcol + k + 1], bias=vec[:, bcol + k:bcol + k + 1])

            for s in range(NPASS):
                w0 = PWIN[s]
                ncx = max(0, min(NCXF - w0, P_))

                def residual(ps, pk, m, v):
                    p.ts(x[:, m, :], x[:, m, :], ALPHA, None, ALU.mult, None, r=[("x", m)], w=[("x", m)], e="pool")
                    if ncx > 0:
                        p.stt(x[:, m, 0:ncx], ps[:, 0:ncx], MC(v, 1, m), x[:, m, 0:ncx], ALU.mult, ALU.add,
                              r=[("x", m), "modL", pk], w=[("x", m)])
                    p.stt(x[:, m, ncx:P_], ps[:, ncx:P_], MC(v, 0, m), x[:, m, ncx:P_], ALU.mult, ALU.add,
                          r=[("x", m), "modL", pk], w=[("x", m)])

                xv = xin.rearrange("(k p) n -> p k n", p=128)
                for k in range(KC):
                    p.load(x[:, k, :], xv[:, k, w0:w0 + P_], r=[f"xres{l % 2}"], w=[("x", k)])
                gch = {}
                order = [0, 1, None, 2, 3, 4, None, 5, 6, None]
                gi = 0
                for grp in range(10):
                    for kk in range(4):
                        k = grp * 4 + kk
                        if order[grp] is None:
                            lrow = {2: 0, 6: 512, 9: 1024}[grp] + 128 * kk
                            p.load(R(mi[:, k, :]), R(pxloc[lrow:lrow + 128, w0:w0 + P_]), r=["pxloc"], w=[("mi", grp), "hff"])
                        else:
                            col = (order[grp] * 4 + kk) * 4 + s
                            p.dma(lambda eng, o=R(mi[:, k, :]), col=col: eng.indirect_dma_start(
                                out=o, out_offset=None, in_=R(gB), in_offset=IndirectOffsetOnAxis(idxB[:, col:col + 1], 0)),
                                r=["gB", "idxB"], w=[("mi", grp), "hff"], q="pool")
                p.tt(t1[:], mi[:, 0:4, :], mi[:, 4:8, :], ALU.add, r=[("mi", 0), ("mi", 1)], w=["t1"])
                for k in range(4):
                    p.stt(t1[:, k, :], mi[:, 8 + k, :], vec[:, V_S5D + k:V_S5D + k + 1], t1[:, k, :], ALU.mult, ALU.add,
                          r=[("mi", 2), "vec", "t1"], w=["t1"])
                p.tt(t2[:], t1[:], t1[:], ALU.mult, r=["t1"], w=["t2"])
                p.ts(t2[:], t2[:], 0.044715, 1.0, ALU.mult, ALU.add, r=["t2"], w=["t2"])
                p.tt(t2[:], t2[:], t1[:], ALU.mult, r=["t1", "t2"], w=["t2"])
                p.act(t2[:], t2[:], AF.Sigmoid, r=["t2"], w=["t2"], scale=1.5957691216057308)
                p.tt(R(gR[:]), t1[:], t2[:], ALU.mult, r=["t1", "t2"], w=["gR"])
                for m in range(4):
                    ps, pk = nps()
                    for k in range(4):
                        p.mm(ps[:, 0:P_], R(glu[:, k, m * 128:(m + 1) * 128]), R(gR[:, k, :]), k == 0, k == 3,
                             r=["glu", "gR"], w=[pk])
                    p.act(t2[:, m, :], ps[:, 0:P_], AF.Sigmoid, r=[pk, "vec"], w=["t2"], bias=vec[:, V_GLUB + m:V_GLUB + m + 1])
                p.tt(R(mix[:, 0:4, :]), gR[:], t2[:], ALU.mult, r=["gR", "t2"], w=[("mix", 0)])
                p.copy(R(mix[:, 4:8, :]), mi[:, 12:16, :], r=[("mi", 3)], w=[("mix", 1)], e="pool")
                p.tt(t1[:], mi[:, 16:20, :], mi[:, 20:24, :], ALU.add, r=[("mi", 4), ("mi", 5)], w=["t1"])
                p.act(t2[:], mi[:, 24:28, :], AF.Silu, r=[("mi", 6)], w=["t2"])
                p.tt(t1[:], t1[:], t2[:], ALU.mult, r=["t1", "t2"], w=["t1"])
                p.tt(t2[:], t1[:], t1[:], ALU.mult, r=["t1"], w=["t2"])
                ps, pk = nps()
                for k in range(4):
                    p.mm(ps[:, 0:P_], onesN[:], t2[:, k, :], k == 0, k == 3, r=["ones", "t2"], w=[pk])
                p.ts(ca[:], ps[:, 0:P_], 1.0 / 512, 1e-6, ALU.mult, ALU.add, r=[pk], w=["ca"])
                p.act(ca[:], ca[:], AF.Sqrt, r=["ca"], w=["ca"])
                p.op("dve", lambda eng: eng.reciprocal(ca[:], ca[:]), r=["ca"], w=["ca"])
                for k in range(4):
                    p.stt(R(mix[:, 8 + k, :]), t1[:, k, :], vec[:, V_SSDW + k:V_SSDW + k + 1], ca[:], ALU.mult, ALU.mult,
                          r=["t1", "vec", "ca"], w=[("mix", 2)])
                p.tt(t1[:], mi[:, 28:32, :], mi[:, 32:36, :], ALU.add, r=[("mi", 7), ("mi", 8)], w=["t1"])
                p.tt(t2[:], t1[:], t1[:], ALU.mult, r=["t1"], w=["t2"])
                for k in range(4):
                    ps, pk = nps()
                    p.mm(ps[:, 0:P_], onesN[:], t2[:, k, :], True, True, r=["ones", "t2"], w=[pk])
                    p.ts(cg[:], ps[:, 0:P_], 1.0 / 128, 1e-6, ALU.mult, ALU.add, r=[pk], w=["cg"])
                    p.act(cg[:], cg[:], AF.Sqrt, r=["cg"], w=["cg"])
                    p.op("dve", lambda eng: eng.reciprocal(cg[:], cg[:]), r=["cg"], w=["cg"])
                    p.stt(t1[:, k, :], t1[:, k, :], vec[:, V_GLAW:V_GLAW + 1], cg[:], ALU.mult, ALU.mult,
                          r=["t1", "vec", "cg"], w=["t1"])
                p.act(t2[:], mi[:, 36:40, :], AF.Silu, r=[("mi", 9), "t2"], w=["t2"])
                p.tt(R(mix[:, 12:16, :]), t1[:], t2[:], ALU.mult, r=["t1", "t2"], w=[("mix", 3)])
                wov = wout_d[l].rearrange("(k p) m -> p k m", p=128)
                for g in range(4):
                    wbt, wk = nwb()
                    wview = wbt[:, :].rearrange("p (k m) -> p k m", k=KC)
                    p.load(R(wview), R(wov[:, :, g * 512:(g + 1) * 512]), w=[wk])
                    for mm_ in range(4):
                        m = g * 4 + mm_
                        ps, pk = nps()
                        for k in range(KC):
                            p.mm(ps[:, 0:P_], R(wview[:, k, mm_ * 128:(mm_ + 1) * 128]), R(mix[:, k, :]), k == 0, k == KC - 1,
                                 r=[wk, ("mix", k // 4)], w=[pk])
                        residual(ps, pk, m, 2)
                ln_affine(V_LN, V_LN + 16)
                if dbg and l == 0 and s == 0:
                    pass
                stats()
                for k in range(KC):
                    p.tt(R(mix[:, k, :]), x[:, k, :], mean[:], ALU.subtract, r=[("x", k), "mean"], w=[("mix", k // 4)], e="pool")
                    p.tt(R(mix[:, k, :]), mix[:, k, :], rstd[:], ALU.mult, r=["rstd", ("mix", k // 4)], w=[("mix", k // 4)])
                    if ncx > 0:
                        p.act(R(mix[:, k, 0:ncx]), mix[:, k, 0:ncx], AF.Identity, r=["modL", ("mix", k // 4)],
                              w=[("mix", k // 4)], scale=MC(4, 1, k), bias=MC(3, 1, k))
                    p.act(R(mix[:, k, ncx:P_]), mix[:, k, ncx:P_], AF.Identity, r=["modL", ("mix", k // 4)],
                          w=[("mix", k // 4)], scale=MC(4, 0, k), bias=MC(3, 0, k))
                edges = [(0, 0), (36, 4), (40, 8), (1068, 12)]
                for (gc, mcol) in edges:
                    a = gc - w0
                    if a < 0 or a + 4 > P_:
                        continue
                    for k in range(KC):
                        p.tt(R(mix[:, k, a:a + 4]), mix[:, k, a:a + 4], cmk[:, mcol:mcol + 4], ALU.mult,
                             r=["cmk", ("mix", k // 4)], w=[("mix", k // 4)])
                upv = up_d[l].rearrange("(k p) m -> p k m", p=128)
                p.copy(R(hff[:, :, 0:1]), zt[:], r=["zt"], w=["hff"] + [("mi", i) for i in range(10)])
                p.copy(R(hff[:, :, P_ - 1:P_]), zt[:], r=["zt"], w=["hff"])
                for j in range(43):
                    wbt, wk = nwb()
                    wview = wbt[:, 0:4096].rearrange("p (a k m) -> p a k m", a=2, k=KC)
                    p.load(R(wview[:, 0]), R(upv[:, :, j * 128:(j + 1) * 128]), w=[wk])
                    p.load(R(wview[:, 1]), R(upv[:, :, DFF + j * 128:DFF + (j + 1) * 128]), w=[wk])
                    outs = []
                    for a in range(2):
                        ps, pk = nps()
                        for k in range(KC):
                            p.mm(ps[:, 0:P_], R(wview[:, a, k, :]), R(mix[:, k, :]), k == 0, k == KC - 1,
                                 r=[wk, ("mix", k // 4)], w=[pk])
                        outs.append((ps, pk))
                    for a, (ps, pk) in enumerate(outs):
                        dst, dk = (ca, "ca") if a == 0 else (cg, "cg")
                        c = V_CW + (a * 43 + j) * 4
                        p.ts(dst[:, 1:P_ - 1], ps[:, 1:P_ - 1], vec[:, c + 1:c + 2], vec[:, c + 3:c + 4], ALU.mult, ALU.add,
                             r=[pk, "vec"], w=[dk])
                        p.stt(dst[:, 1:P_ - 1], ps[:, 0:P_ - 2], vec[:, c:c + 1], dst[:, 1:P_ - 1], ALU.mult, ALU.add,
                              r=[pk, "vec", dk], w=[dk])
                        p.stt(dst[:, 1:P_ - 1], ps[:, 2:P_], vec[:, c + 2:c + 3], dst[:, 1:P_ - 1], ALU.mult, ALU.add,
                              r=[pk, "vec", dk], w=[dk])
                    p.act(cg[:, 1:P_ - 1], cg[:, 1:P_ - 1], AF.Silu, r=["cg"], w=["cg"])
                    p.tt(R(hff[:, j, 1:P_ - 1]), ca[:, 1:P_ - 1], cg[:, 1:P_ - 1], ALU.mult, r=["ca", "cg"], w=["hff"], e="pool")
                dnv = dn_d[l].rearrange("(j p) m -> p j m", p=128)
                for m in range(KC):
                    wbt, wk = nwb()
                    wview = wbt[:, 0:43 * 128].rearrange("p (j m) -> p j m", j=43)
                    p.load(R(wview), R(dnv[:, :, m * 128:(m + 1) * 128]), w=[wk])
                    ps, pk = nps()
                    for j in range(43):
                        p.mm(ps[:, 0:P_], R(wview[:, j, :]), R(hff[:, j, :]), j == 0, j == 42, r=[wk, "hff"], w=[pk])
                    residual(ps, pk, m, 5)
                ln_affine(V_LN + 32, V_LN + 48)
                xo = xout.rearrange("(k p) n -> p k n", p=128)
                p.store(xo[:, :, w0 + 1:w0 + P_ - 1], x[:, :, 1:P_ - 1], r=[("x", k) for k in range(KC)], w=[f"xres{(l + 1) % 2}"])
        p.barrier()
    p.maxops = 1 << 60
    p.barrier()
    if dbg:
        p.dma(lambda eng: eng.dma_start(out=dbg_o["sB"], in_=sB), r=["sB"], w=["dbgB"])
    fin = xres[nlayers % 2]
    p.dma(lambda eng: eng.dma_start(out=out_d, in_=fin[:, 44:1068]), r=[f"xres{nlayers % 2}"], w=["out"])
    if dbg:
        p.dma(lambda eng: eng.dma_start(out=dbg_o["x1"], in_=fin), r=[f"xres{nlayers % 2}"], w=["dbgx"])
    p.finish()
    p.emit()
    return nc


def fused_inputs(P, nlayers=DEPTH):
    NL = nlayers
    xfull = np.concatenate([P["ctx"][0], P["x"][0]], 0)
    jj = np.arange(128)
    tri2 = np.stack([(jj[:, None] <= jj[None, :]), (jj[:, None] >= jj[None, :])]).astype(np.float32)
    negmask2 = np.where(tri2 > 0, 0.0, NEG).astype(np.float32)
    j6 = np.arange(64)
    umask2 = np.stack([(j6[:, None] <= j6[None, :]), (j6[:, None] >= j6[None, :])]).astype(np.float32)
    rot = np.zeros((64, 64), np.float32)
    for m in range(32):
        rot[m + 32, m] = -1.0
        rot[m, m + 32] = 1.0
    cmask = np.ones((64, TT), np.float32)
    cmask[:, ::64] = 0.0
    c2, s2 = rope_tables()
    gcs = np.ascontiguousarray(np.stack([c2.T, s2.T]))
    tau1 = np.tile(np.arange(1, 129, dtype=np.float32)[None, :], (128, 1))
    ident = np.eye(128, dtype=np.float32)
    cvh = np.stack([chunkcols(P["c"].reshape(-1)), chunkcols(P["c_ctx"].reshape(-1))], 2)
    vecs = np.zeros((DEPTH, 128, NV_C), np.float32)
    for l in range(DEPTH):
        vec = vecs[l]
        vec[:, 0:4] = chunkcols(P["s5_d"][l]); vec[:, 4:8] = chunkcols(P["s5_glu_b"][l])
        vec[:, 8:12] = chunkcols(P["ssd_norm_w"][l]); vec[:, 12:13] = chunkcols(P["gla_norm_w"][l])
        vec[:, 13:29] = chunkcols(P["ln_g"][l, 0]); vec[:, 29:45] = chunkcols(P["ln_b"][l, 0])
        vec[:, 45:61] = chunkcols(P["ln_g"][l, 1]); vec[:, 61:77] = chunkcols(P["ln_b"][l, 1])
        cw = P["ffn_conv_w"][l]
        cvv = np.stack([chunkcols(cw[0]), chunkcols(cw[1]), chunkcols(cw[2]), chunkcols(P["ffn_conv_b"][l])], 2)
        vec[:, 77:] = cvv.reshape(128, 86 * 4)
    shared = {"w_in": P["w_in"][:NL], "w_out": P["w_out"][:NL], "ffn_up": P["ffn_up"][:NL], "ffn_down": P["ffn_down"][:NL],
              "glu_w": P["s5_glu_w"][:NL], "vecs": vecs[:NL], "gla_cs": gcs, "tau1": tau1, "ident": ident, "tri2": tri2,
              "negmask2": negmask2, "rot": rot, "cmask": cmask, "umask2": umask2, "cv": cvh}
    in_maps = []
    pp = np.arange(128)
    for i in range(NCORES):
        g = i // 4
        hh, vh = i // 2, i % 2
        m = dict(shared)
        cr = np.arange(32 * i - 4, 32 * i + 36)
        lr = np.arange(1024 * i - 4, 1024 * i + 1028)
        idx = np.concatenate([np.where((cr >= 0) & (cr < CTX), cr, -1), np.where((lr >= 0) & (lr < SEQ), lr + CTX, -1)])
        m["x0T"] = _gather_T(xfull, idx)
        ex = (idx >= 0).astype(np.float32)
        flags = np.concatenate([ex[0:4], ex[36:40], ex[40:44], ex[1068:1072]])
        m["colmask"] = np.tile(flags[None, :], (128, 1)).astype(np.float32)
        bases = [64 * i, 512 + 64 * i, 1024 + 64 * i, 1536 + 64 * i, 2048 + 64 * i, 2560 + 128 * g, 2816 + 128 * g,
                 None, 3088 + 64 * hh, 3344 + 64 * hh, 3600 + 128 * hh + 64 * vh, 4112, 4128]
        idxA = np.zeros((128, NGT * 8), np.uint32)
        for t, b in enumerate(bases):
            for r in range(8):
                if t == 7:
                    col = np.zeros(128, np.int64)
                    col[0] = 3072 + i
                    col[1] = 3080 + i
                else:
                    col = b + pp
                idxA[:, t * 8 + r] = (r * SROWS + np.minimum(col, SROWS - 1)).astype(np.uint32)
        m["idxA"] = idxA
        idxB = np.zeros((128, 28 * 4), np.uint32)
        for gq in range(7):
            for kk in range(4):
                for s in range(4):
                    r = 2 * kk + pp // 64
                    rowid = 64 * gq + pp % 64
                    idxB[:, (gq * 4 + kk) * 4 + s] = (r * (BROWS * 32) + rowid * 32 + i * 4 + s).astype(np.uint32)
        m["idxB"] = idxB
        dtsel = np.zeros((16, 2), np.float32)
        dtsel[i, 0] = 1.0
        dtsel[8 + i, 1] = 1.0
        m["dtsel"] = dtsel
        l_, h_ = i // 2, i % 2
        m["wmod"] = np.ascontiguousarray(P["w_mod"][l_][:, h_ * 6144:(h_ + 1) * 6144])
        m["bmod"] = chunkcols(P["b_mod"][l_][h_ * 6144:(h_ + 1) * 6144])
        lanep = np.zeros((DEPTH, 2, 2, 128, 3), np.float32)
        bre = np.zeros((DEPTH, 2, 2, 128, 16), np.float32); bim = np.zeros((DEPTH, 2, 2, 128, 16), np.float32)
        cre = np.zeros((DEPTH, 2, 2, 128, 64), np.float32); cim = np.zeros((DEPTH, 2, 2, 128, 64), np.float32)
        nab = np.zeros((DEPTH, 5, 128, 576), np.float32)
        scw = np.zeros((DEPTH, 128, 3, 4), np.float32); ssc = np.zeros((DEPTH, 2, 128, 4), np.float32)
        ggw = np.zeros((DEPTH, 2, 16, 64), np.float32); ggb = np.zeros((DEPTH, 2, 64, 1), np.float32)
        colsets = [np.arange(64 * i, 64 * i + 64), 512 + np.arange(128 * g, 128 * g + 128), 768 + np.arange(128 * g, 128 * g + 128)]
        for l in range(DEPTH):
            for d in range(2):
                for t in range(2):
                    for h in range(2):
                        gl = 2 * t + h
                        gg_ = 4 * i + gl
                        rows = slice(64 * h, 64 * h + 64)
                        lanep[l, d, t, rows, 0] = P["s5_lam_re"][l, d, gg_]
                        lanep[l, d, t, rows, 1] = P["s5_lam_im"][l, d, gg_]
                        lanep[l, d, t, rows, 2] = P["s5_log_step"][l, d, gg_]
                        bre[l, d, t, rows] = P["s5_b_re"][l, d, gg_]; bim[l, d, t, rows] = P["s5_b_im"][l, d, gg_]
                        cre[l, d, t, rows, 16 * gl:16 * gl + 16] = P["s5_c_re"][l, d, gg_].T
                        cim[l, d, t, rows, 16 * gl:16 * gl + 16] = P["s5_c_im"][l, d, gg_].T
                ssc[l, d, :, 0] = P["ssd_dt_bias"][l, d, i]; ssc[l, d, :, 1] = P["ssd_a_log"][l, d, i]; ssc[l, d, :, 2] = P["ssd_d"][l, i]
                ggw[l, d] = P["gla_gate_w"][l, d][:, 64 * hh:64 * hh + 64]
                ggb[l, d, :, 0] = P["gla_gate_b"][l, d][64 * hh:64 * hh + 64]
            nab[l] = na_bias_tables(P["na_rpb"][l, i])
            for ch, cs_ in enumerate(colsets):
                scw[l, :len(cs_), ch, 0:3] = P["ssd_conv_w"][l][:, cs_].T
                scw[l, :len(cs_), ch, 3] = P["ssd_conv_b"][l][cs_]
        m.update({"lanep": lanep[:NL], "bre": bre[:NL], "bim": bim[:NL], "cre": cre[:NL], "cim": cim[:NL], "nabias": nab[:NL],
                  "ssd_cw": scw[:NL], "ssd_scal": ssc[:NL], "gla_gw": ggw[:NL], "gla_gb": ggb[:NL]})
        in_maps.append(m)
    return in_maps


def kernel_fused(nlayers=DEPTH, dbg=False, **inputs):
    P = {k: np.asarray(v, np.float32) for k, v in inputs.items()}
    nc = _prog(("F", nlayers, dbg), lambda: build_fused(nlayers, dbg))
    res = run_spmd(nc, fused_inputs(P, nlayers))
    out = np.concatenate([res[i]["outT"].T for i in range(NCORES)], 0)
    return np.ascontiguousarray(out[None]).astype(np.float32), res


def _pass_rows_f(j, s):
    cr = np.arange(32 * j - 4, 32 * j + 36)
    lr = np.arange(1024 * j - 4, 1024 * j + 1028)
    idx = np.concatenate([np.where((cr >= 0) & (cr < CTX), cr, -1), np.where((lr >= 0) & (lr < SEQ), lr + CTX, -1)])
    return idx[PWIN[s]:PWIN[s] + PW]
```

```python
import numpy as np
from contextlib import ExitStack
import concourse.bass as bass
import concourse.mybir as mybir
from concourse.bass_utils import run_bass_kernel_spmd

F32 = mybir.dt.float32
F32R = mybir.dt.float32r
BF16 = mybir.dt.bfloat16
AF = mybir.ActivationFunctionType
ALU = mybir.AluOpType
AX = mybir.AxisListType

NCORES = 8
D = 2048
KC = D // 128
DEPTH = 4
SEQ = 8192
CTX = 256
DPROJ = 5168
DFF = 5504


class Prog:
    ENG = ("pe", "act", "dve", "pool", "sp")

    def __init__(self, nc, ndma=24):
        self.nc = nc
        import os as _os
        self.maxops = int(_os.environ.get("PROG_MAXOPS", str(1 << 60)))
        nc.dge_precook = False
        self.stream = {e: [] for e in self.ENG}
        self.cnt = {e: 0 for e in self.ENG}
        self.clock = {e: {} for e in self.ENG}
        self.lastw = {}
        self.readers = {}
        self.ndma = ndma
        self.dma_uses = [0] * ndma
        self.dma_ev = [None] * ndma
        self.dma_rr = 0
        self.coll_extra = {}
        self.root = ExitStack()
        self.es = self.root
        self.nalloc = 0
        self.sems = None

    def sb(self, shape, dtype=F32, name=None):
        self.nalloc += 1
        name = name or f"sb{self.nalloc}"
        return self.es.enter_context(self.nc.sbuf_tensor(name, list(shape), dtype))

    def ps(self, shape, dtype=F32, name=None):
        self.nalloc += 1
        name = name or f"ps{self.nalloc}"
        return self.es.enter_context(self.nc.psum_tensor(name, list(shape), dtype))

    def _deps(self, e, reads, writes):
        evs = []
        for k in reads:
            ev = self.lastw.get(k)
            if ev is not None:
                evs.append(ev)
        for k in writes:
            ev = self.lastw.get(k)
            if ev is not None:
                evs.append(ev)
            evs.extend(self.readers.get(k, ()))
        clk = self.clock[e]
        waits = {}
        for (sk, v, snap) in evs:
            if e == "pe" and sk == "pe":
                continue
            if clk.get(sk, 0) >= v:
                continue
            waits[sk] = max(waits.get(sk, 0), v)
            for s2, v2 in snap.items():
                if clk.get(s2, 0) < v2:
                    clk[s2] = v2
            clk[sk] = v
        return list(waits.items())

    def _commit(self, ev, reads, writes):
        for k in writes:
            self.lastw[k] = ev
            self.readers[k] = []
        for k in reads:
            if k in writes:
                continue
            self.readers.setdefault(k, []).append(ev)

    def op(self, e, fn, r=(), w=()):
        self.nops = getattr(self, "nops", 0) + 1
        if self.nops > getattr(self, "maxops", 1 << 60):
            return
        waits = self._deps(e, r, w)
        self.cnt[e] += 1
        idx = self.cnt[e]
        ev = (e, idx, dict(self.clock[e]))
        self.stream[e].append((waits, fn, (e, 1)))
        self._commit(ev, r, w)

    def dma(self, fn, r=(), w=(), q="sp"):
        self.nops = getattr(self, "nops", 0) + 1
        if self.nops > getattr(self, "maxops", 1 << 60):
            return
        k = self.dma_rr
        self.dma_rr = (self.dma_rr + 1) % self.ndma
        waits = self._deps(q, r, w)
        prev = self.dma_ev[k]
        if prev is not None:
            sk, v, snap = prev
            if self.clock[q].get(sk, 0) < v:
                waits.append((sk, v))
                self.clock[q][sk] = v
        self.dma_uses[k] += 1
        sk = ("d", k)
        ev = (sk, 16 * self.dma_uses[k] + self.coll_extra.get(k, 0), dict(self.clock[q]))
        self.dma_ev[k] = ev
        self.stream[q].append((waits, fn, (sk, 16)))
        self._commit(ev, r, w)

    def coll(self, fn, r=(), w=()):
        self.nops = getattr(self, "nops", 0) + 1
        if self.nops > getattr(self, "maxops", 1 << 60):
            return
        k = self.dma_rr
        self.dma_rr = (self.dma_rr + 1) % self.ndma
        q = "pool"
        waits = self._deps(q, r, w)
        prev = self.dma_ev[k]
        if prev is not None:
            sk, v, snap = prev
            if self.clock[q].get(sk, 0) < v:
                waits.append((sk, v))
                self.clock[q][sk] = v
        sk = ("d", k)
        base = 16 * self.dma_uses[k] + self.coll_extra.get(k, 0)
        self.coll_extra[k] = self.coll_extra.get(k, 0) + 1
        ev = (sk, base + 1, dict(self.clock[q]))
        self.dma_ev[k] = ev
        self.stream[q].append((waits, fn, (sk, 1)))
        self._commit(ev, r, w)

    def finish(self):
        waits = []
        for ev in self.dma_ev:
            if ev is not None and self.clock["sp"].get(ev[0], 0) < ev[1]:
                waits.append((ev[0], ev[1]))
        self.stream["sp"].append((waits, None, None))

    def barrier(self):
        evs = [(e, self.cnt[e]) for e in self.ENG if self.cnt[e] > 0]
        evs += [(ev[0], ev[1]) for ev in self.dma_ev if ev is not None]
        for e in self.ENG:
            waits = []
            for sk, v in evs:
                if e == "pe" and sk == "pe":
                    continue
                if self.clock[e].get(sk, 0) < v:
                    waits.append((sk, v))
                    self.clock[e][sk] = v
            if waits:
                self.stream[e].append((waits, None, None))
        self.lastw.clear()
        self.readers.clear()

    def scope(self):
        prog = self

        class _Scope:
            def __enter__(self_):
                self_.old = prog.es
                prog.es = ExitStack()
                return prog

            def __exit__(self_, *a):
                if a[0] is None:
                    prog.barrier()
                    prog.flush()
                prog.es.close()
                prog.es = self_.old
                return False
        return _Scope()

    def emit(self):
        self.flush()
        self.root.close()

    def flush(self):
        nc = self.nc
        if self.sems is None:
            self.sems = {}
            for e in self.ENG:
                self.sems[e] = self.root.enter_context(nc.semaphore(f"s_{e}"))
            for k in range(self.ndma):
                self.sems[("d", k)] = self.root.enter_context(nc.semaphore(f"s_d{k}"))
        sems = self.sems
        streams = self.stream
        self.stream = {e: [] for e in self.ENG}

        def replay(e, eng):
            for waits, fn, inc in streams[e]:
                for sk, v in waits:
                    eng.wait_ge(sems[sk], v)
                if fn is None:
                    continue
                ins = fn(eng)
                ins.then_inc(sems[inc[0]], inc[1])

        with nc.Block() as block:
            @block.tensor
            def _(eng):
                replay("pe", eng)

            @block.scalar
            def _(eng):
                replay("act", eng)

            @block.vector
            def _(eng):
                replay("dve", eng)

            @block.gpsimd
            def _(eng):
                replay("pool", eng)

            @block.sync
            def _(eng):
                replay("sp", eng)

    def mm(self, out, lhsT, rhs, start, stop, r, w):
        self.op("pe", lambda eng: eng.matmul(out, lhsT, rhs, start=start, stop=stop), r, w)

    def act(self, out, in_, func, r, w, bias=None, scale=None, accum_out=None):
        kw = {}
        if bias is not None:
            kw["bias"] = bias
        if scale is not None:
            kw["scale"] = scale
        if accum_out is not None:
            kw["accum_out"] = accum_out
        self.op("act", lambda eng: eng.activation(out, in_, func, **kw), r, w)

    def tt(self, out, a, b, op, r, w, e="dve"):
        self.op(e, lambda eng: eng.tensor_tensor(out, a, b, op), r, w)

    def ts(self, out, a, s1, s2, op0, op1, r, w, e="dve"):
        if s2 is None:
            self.op(e, lambda eng: eng.tensor_scalar(out, a, s1, None, op0), r, w)
        else:
            self.op(e, lambda eng: eng.tensor_scalar(out, a, s1, s2, op0, op1), r, w)

    def stt(self, out, a, s, b, op0, op1, r, w, e="dve"):
        self.op(e, lambda eng: eng.scalar_tensor_tensor(out, a, s, b, op0, op1), r, w)

    def copy(self, out, in_, r, w, e="dve"):
        if e == "act":
            self.op(e, lambda eng: eng.copy(out, in_), r, w)
        else:
            self.op(e, lambda eng: eng.tensor_copy(out, in_), r, w)

    def memset(self, ap, val, w, e="dve"):
        self.op(e, lambda eng: eng.memset(ap, val), (), w)

    def load(self, out, in_, w, r=(), q="sp"):
        self.dma(lambda eng: eng.dma_start(out=out, in_=in_), r, w, q)

    def store(self, out, in_, r, w=(), q="sp"):
        self.dma(lambda eng: eng.dma_start(out=out, in_=in_), r, w, q)


def R(ap):
    return ap.bitcast(F32R)


def run_spmd(nc, in_maps):
    res = run_bass_kernel_spmd(nc, in_maps, core_ids=list(range(NCORES)))
    return res.results


def ln_stats(p, xT, ncols, tiles, onesN, sq, ps_m, ps_q, mean, rstd, tag):
    for (c0, cn) in tiles:
        for k in range(KC):
            p.act(sq[:, 0:cn], xT[:, k, c0:c0 + cn], AF.Square, r=[(tag, "x", k)], w=["sq"])
            p.mm(ps_m[:, 0:cn], onesN[:], xT[:, k, c0:c0 + cn], k == 0, k == KC - 1,
                 r=[(tag, "x", k), "ones"], w=["ps_m"])
            p.mm(ps_q[:, 0:cn], onesN[:], sq[:, 0:cn], k == 0, k == KC - 1,
                 r=["sq", "ones"], w=["ps_q"])
        p.copy(mean[:, c0:c0 + cn], ps_m[:, 0:cn], r=["ps_m"], w=[(tag, "mean")], e="act")
        p.tt(rstd[:, c0:c0 + cn], mean[:, c0:c0 + cn], mean[:, c0:c0 + cn], ALU.mult,
             r=[(tag, "mean")], w=[(tag, "rstd")])
        p.tt(rstd[:, c0:c0 + cn], ps_q[:, 0:cn], rstd[:, c0:c0 + cn], ALU.subtract,
             r=["ps_q", (tag, "rstd")], w=[(tag, "rstd")])
        p.ts(rstd[:, c0:c0 + cn], rstd[:, c0:c0 + cn], 1e-6, None, ALU.add, None,
             r=[(tag, "rstd")], w=[(tag, "rstd")])
        p.act(rstd[:, c0:c0 + cn], rstd[:, c0:c0 + cn], AF.Sqrt, r=[(tag, "rstd")], w=[(tag, "rstd")])
        p.op("dve", lambda eng, a=rstd[:, c0:c0 + cn]: eng.reciprocal(a, a),
             r=[(tag, "rstd")], w=[(tag, "rstd")])


def build_phase_a(NT, NCX):
    nc = bass.Bass("TRN2", target_bir_lowering=False)
    xT_d = nc.dram_tensor("xT", [D, NT], F32, kind="ExternalInput").ap()
    mod_d = nc.dram_tensor("modA", [128, 4 * KC], F32, kind="ExternalInput").ap()
    w_d = nc.dram_tensor("w_in", [D, DPROJ], F32, kind="ExternalInput").ap()
    out_d = nc.dram_tensor("pxT", [DPROJ, NT], F32, kind="ExternalOutput").ap()
    p = Prog(nc)
    xT = p.sb([128, KC, NT])
    hT = xT
    mod = p.sb([128, 4 * KC])
    onesN = p.sb([128, 128])
    sq = p.sb([128, 512])
    mean = p.sb([128, NT])
    rstd = p.sb([128, NT])
    GW = 512
    wbuf = [p.sb([128, KC, GW]) for _ in range(2)]
    obuf = [p.sb([128, NT]) for _ in range(2)]
    ps_m = p.ps([128, 512])
    ps_q = p.ps([128, 512])
    ps_o = [p.ps([128, 512]) for _ in range(4)]

    tiles = []
    c = 0
    while c < NT:
        cn = min(352, NT - c)
        tiles.append((c, cn))
        c += cn

    p.memset(onesN[:], 1.0 / D, w=["ones"])
    modr = p.sb([128, 4 * KC])
    p.load(modr[:], mod_d, w=["modr"])
    p.copy(mod[:], modr[:], r=["modr"], w=["mod"])
    p.ts(mod[:, 0:KC], modr[:, 0:KC], 1.0, None, ALU.add, None, r=["modr", "mod"], w=["mod"])
    p.ts(mod[:, 2 * KC:3 * KC], modr[:, 2 * KC:3 * KC], 1.0, None, ALU.add, None, r=["modr", "mod"], w=["mod"])
    xv = xT_d.rearrange("(k p) n -> p k n", p=128)
    for k in range(KC):
        p.load(R(xT[:, k, :]), R(xv[:, k, :]), w=[("A", "x", k)])
    ln_stats(p, xT, NT, tiles, onesN, sq, ps_m, ps_q, mean, rstd, "A")
    for k in range(KC):
        p.tt(R(hT[:, k, :]), xT[:, k, :], mean[:], ALU.subtract, r=[("A", "x", k), ("A", "mean")],
             w=[("h", k), ("A", "x", k)])
        p.tt(R(hT[:, k, :]), hT[:, k, :], rstd[:], ALU.mult, r=[("h", k), ("A", "rstd")], w=[("h", k)])
        if NCX > 0:
            p.act(R(hT[:, k, 0:NCX]), hT[:, k, 0:NCX], AF.Identity, r=[("h", k), "mod"], w=[("h", k)],
                  scale=mod[:, 2 * KC + k:2 * KC + k + 1], bias=mod[:, 3 * KC + k:3 * KC + k + 1])
        p.act(R(hT[:, k, NCX:NT]), hT[:, k, NCX:NT], AF.Identity, r=[("h", k), "mod"], w=[("h", k)],
              scale=mod[:, k:k + 1], bias=mod[:, KC + k:KC + k + 1])
    wv = w_d.rearrange("(k p) m -> p k m", p=128)
    ngrp = (DPROJ + GW - 1) // GW
    oi = 0
    for g in range(ngrp):
        g0 = g * GW
        gw = min(GW, DPROJ - g0)
        wb = wbuf[g % 2]
        for k in range(KC):
            p.load(R(wb[:, k, 0:gw]), R(wv[:, k, g0:g0 + gw]), w=[("w", g % 2, k)])
        for m0 in range(0, gw, 128):
            mw = min(128, gw - m0)
            ob = obuf[oi % 2]
            for ti, (c0, cn) in enumerate(tiles):
                ps = ps_o[ti % 4]
                for k in range(KC):
                    p.mm(ps[0:mw, 0:cn], R(wb[:, k, m0:m0 + mw]), R(hT[:, k, c0:c0 + cn]), k == 0, k == KC - 1,
                         r=[("w", g % 2, k), ("h", k)], w=[("pso", ti % 4)])
                p.copy(ob[0:mw, c0:c0 + cn], ps[0:mw, 0:cn], r=[("pso", ti % 4)], w=[("ob", oi % 2)],
                       e="act" if ti % 2 == 0 else "dve")
            p.store(out_d[g0 + m0:g0 + m0 + mw, :], ob[0:mw, :], r=[("ob", oi % 2)])
            oi += 1
    p.finish()
    p.emit()
    return nc


NPASS = 4
PC = 268
NCXC = 10
ALPHA = float((2 * DEPTH) ** 0.25)
NV_C = 4 + 4 + 4 + 1 + 64 + 86 * 4


def build_phase_c(inject=False):
    nc = bass.Bass("TRN2", target_bir_lowering=False)
    P_ = PC
    xT_d = nc.dram_tensor("xT", [NPASS, D, P_], F32, kind="ExternalInput").ap()
    mi_d = nc.dram_tensor("mixin", [NPASS, 40 * 128, P_], F32, kind="ExternalInput").ap()
    mod_d = nc.dram_tensor("modC", [128, 8 * KC], F32, kind="ExternalInput").ap()
    vec_d = nc.dram_tensor("vecs", [128, NV_C], F32, kind="ExternalInput").ap()
    hm_d = nc.dram_tensor("hmask", [NPASS, 128, 4], F32, kind="ExternalInput").ap()
    glu_d = nc.dram_tensor("glu_w", [512, 512], F32, kind="ExternalInput").ap()
    wo_d = nc.dram_tensor("w_out", [D, D], F32, kind="ExternalInput").ap()
    up_d = nc.dram_tensor("ffn_up", [D, 2 * DFF], F32, kind="ExternalInput").ap()
    dn_d = nc.dram_tensor("ffn_down", [DFF, D], F32, kind="ExternalInput").ap()
    out_d = nc.dram_tensor("x2T", [NPASS, D, P_], F32, kind="ExternalOutput").ap()
    dbg_d = nc.dram_tensor("mixT", [NPASS, D, P_], F32, kind="ExternalOutput").ap()
    p = Prog(nc)
    x = p.sb([128, KC, P_])
    hff = p.sb([128, 43, P_])
    mi = hff
    mix = p.sb([128, KC, P_])
    wb = [p.sb([128, 8192]) for _ in range(2)]
    mod = p.sb([128, 8 * KC])
    sc1 = p.sb([128, 2 * KC])
    vec = p.sb([128, NV_C])
    hm = p.sb([128, 4])
    glu = p.sb([128, 4, 512])
    onesN = p.sb([128, 128])
    sq = p.sb([128, 512])
    mean = p.sb([128, P_])
    rstd = p.sb([128, P_])
    t1 = p.sb([128, 4, P_])
    t2 = p.sb([128, 4, P_])
    gR = p.sb([128, 4, P_])
    ca = p.sb([128, P_])
    cg = p.sb([128, P_])
    ps_m = p.ps([128, 512])
    ps_q = p.ps([128, 512])
    psr = [p.ps([128, 512]) for _ in range(6)]
    pi = [0]

    def nps():
        pi[0] = (pi[0] + 1) % 6
        return psr[pi[0]], ("psr", pi[0])

    V_S5D, V_GLUB, V_SSDW, V_GLAW, V_LN, V_CW = 0, 4, 8, 12, 13, 77
    p.memset(onesN[:], 1.0, w=["ones"])
    zt = p.sb([128, 43, 1])
    p.memset(zt[:], 0.0, w=["zt"])
    p.load(mod[:], mod_d, w=["mod"])
    p.load(vec[:], vec_d, w=["vec"])
    p.load(R(glu[:]), R(glu_d.rearrange("(k p) m -> p k m", p=128)), w=["glu"])
    p.ts(sc1[:, 0:KC], mod[:, 3 * KC:4 * KC], 1.0, None, ALU.add, None, r=["mod"], w=["sc1"])
    p.ts(sc1[:, KC:2 * KC], mod[:, 5 * KC:6 * KC], 1.0, None, ALU.add, None, r=["mod"], w=["sc1"])
    tiles = [(0, P_)]
    wi = [0]

    def nwb():
        wi[0] = (wi[0] + 1) % 2
        return wb[wi[0]], ("wb", wi[0])

    def stats(tag):
        for k in range(KC):
            p.act(sq[:, 0:P_], x[:, k, :], AF.Square, r=[("x", k)], w=["sq"])
            p.mm(ps_m[:, 0:P_], onesN[:], x[:, k, :], k == 0, k == KC - 1, r=[("x", k), "ones"], w=["ps_m"])
            p.mm(ps_q[:, 0:P_], onesN[:], sq[:, 0:P_], k == 0, k == KC - 1, r=["sq", "ones"], w=["ps_q"])
        p.ts(mean[:], ps_m[:, 0:P_], 1.0 / D, None, ALU.mult, None, r=["ps_m"], w=["mean"])
        p.tt(rstd[:], mean[:], mean[:], ALU.mult, r=["mean"], w=["rstd"])
        p.stt(rstd[:], ps_q[:, 0:P_], 1.0 / D, rstd[:], ALU.mult, ALU.subtract, r=["ps_q", "rstd"], w=["rstd"])
        p.ts(rstd[:], rstd[:], 1e-6, None, ALU.add, None, r=["rstd"], w=["rstd"])
        p.act(rstd[:], rstd[:], AF.Sqrt, r=["rstd"], w=["rstd"])
        p.op("dve", lambda eng: eng.reciprocal(rstd[:], rstd[:]), r=["rstd"], w=["rstd"])

    def ln_affine(gcol, bcol):
        stats("x")
        for k in range(KC):
            p.tt(x[:, k, :], x[:, k, :], mean[:], ALU.subtract, r=[("x", k), "mean"], w=[("x", k)])
            p.tt(x[:, k, :], x[:, k, :], rstd[:], ALU.mult, r=[("x", k), "rstd"], w=[("x", k)])
            p.act(x[:, k, :], x[:, k, :], AF.Identity, r=[("x", k), "vec"], w=[("x", k)],
                  scale=vec[:, gcol + k:gcol + k + 1], bias=vec[:, bcol + k:bcol + k + 1])

    def residual(ps, m, gx, gc):
        p.ts(x[:, m, :], x[:, m, :], ALPHA, None, ALU.mult, None, r=[("x", m)], w=[("x", m)], e="pool")
        p.stt(x[:, m, 0:NCXC], ps[:, 0:NCXC], mod[:, gc + m:gc + m + 1], x[:, m, 0:NCXC], ALU.mult, ALU.add,
              r=[("x", m), "mod", pk], w=[("x", m)])
        p.stt(x[:, m, NCXC:P_], ps[:, NCXC:P_], mod[:, gx + m:gx + m + 1], x[:, m, NCXC:P_], ALU.mult, ALU.add,
              r=[("x", m), "mod", pk], w=[("x", m)])

    for s in range(NPASS):
        xv = xT_d[s].rearrange("(k p) n -> p k n", p=128)
        miv = mi_d[s].rearrange("(k p) n -> p k n", p=128)
        for k in range(KC):
            p.load(x[:, k, :], xv[:, k, :], w=[("x", k)])
        for k in range(40):
            p.load(R(mi[:, k, :]), R(miv[:, k, :]), w=[("mi", k // 4), "hff"])
        p.load(hm[:], hm_d[s], w=["hm"])
        if inject:
            for k in range(KC):
                p.copy(R(mix[:, k, :]), mi[:, k, :], r=[("mi", k // 4)], w=[("mix", k // 4)], e="pool")
        else:
            p.tt(t1[:], mi[:, 0:4, :], mi[:, 4:8, :], ALU.add, r=[("mi", 0), ("mi", 1)], w=["t1"])
            for k in range(4):
                p.stt(t1[:, k, :], mi[:, 8 + k, :], vec[:, V_S5D + k:V_S5D + k + 1], t1[:, k, :], ALU.mult, ALU.add,
                      r=[("mi", 2), "vec", "t1"], w=["t1"])
            p.tt(t2[:], t1[:], t1[:], ALU.mult, r=["t1"], w=["t2"])
            p.ts(t2[:], t2[:], 0.044715, 1.0, ALU.mult, ALU.add, r=["t2"], w=["t2"])
            p.tt(t2[:], t2[:], t1[:], ALU.mult, r=["t1", "t2"], w=["t2"])
            p.act(t2[:], t2[:], AF.Sigmoid, r=["t2"], w=["t2"], scale=1.5957691216057308)
            p.tt(R(gR[:]), t1[:], t2[:], ALU.mult, r=["t1", "t2"], w=["gR"])
            for m in range(4):
                ps, pk = nps()
                for k in range(4):
                    p.mm(ps[:, 0:P_], R(glu[:, k, m * 128:(m + 1) * 128]), R(gR[:, k, :]), k == 0, k == 3,
                         r=["glu", "gR"], w=[pk])
                p.act(t2[:, m, :], ps[:, 0:P_], AF.Sigmoid, r=[pk, "vec"], w=["t2"],
                      bias=vec[:, V_GLUB + m:V_GLUB + m + 1])
            p.tt(R(mix[:, 0:4, :]), gR[:], t2[:], ALU.mult, r=["gR", "t2"], w=[("mix", 0)])
            p.copy(R(mix[:, 4:8, :]), mi[:, 12:16, :], r=[("mi", 3)], w=[("mix", 1)], e="pool")
            p.tt(t1[:], mi[:, 16:20, :], mi[:, 20:24, :], ALU.add, r=[("mi", 4), ("mi", 5)], w=["t1"])
            p.act(t2[:], mi[:, 24:28, :], AF.Silu, r=[("mi", 6)], w=["t2"])
            p.tt(t1[:], t1[:], t2[:], ALU.mult, r=["t1", "t2"], w=["t1"])
            p.tt(t2[:], t1[:], t1[:], ALU.mult, r=["t1"], w=["t2"])
            ps, pk = nps()
            for k in range(4):
                p.mm(ps[:, 0:P_], onesN[:], t2[:, k, :], k == 0, k == 3, r=["ones", "t2"], w=[pk])
            p.ts(ca[:], ps[:, 0:P_], 1.0 / 512, 1e-6, ALU.mult, ALU.add, r=[pk], w=["ca"])
            p.act(ca[:], ca[:], AF.Sqrt, r=["ca"], w=["ca"])
            p.op("dve", lambda eng: eng.reciprocal(ca[:], ca[:]), r=["ca"], w=["ca"])
            for k in range(4):
                p.stt(R(mix[:, 8 + k, :]), t1[:, k, :], vec[:, V_SSDW + k:V_SSDW + k + 1], ca[:], ALU.mult, ALU.mult,
                      r=["t1", "vec", "ca"], w=[("mix", 2)])
            p.tt(t1[:], mi[:, 28:32, :], mi[:, 32:36, :], ALU.add, r=[("mi", 7), ("mi", 8)], w=["t1"])
            p.tt(t2[:], t1[:], t1[:], ALU.mult, r=["t1"], w=["t2"])
            for k in range(4):
                ps, pk = nps()
                p.mm(ps[:, 0:P_], onesN[:], t2[:, k, :], True, True, r=["ones", "t2"], w=[pk])
                p.ts(cg[:], ps[:, 0:P_], 1.0 / 128, 1e-6, ALU.mult, ALU.add, r=[pk], w=["cg"])
                p.act(cg[:], cg[:], AF.Sqrt, r=["cg"], w=["cg"])
                p.op("dve", lambda eng: eng.reciprocal(cg[:], cg[:]), r=["cg"], w=["cg"])
                p.stt(t1[:, k, :], t1[:, k, :], vec[:, V_GLAW:V_GLAW + 1], cg[:], ALU.mult, ALU.mult,
                      r=["t1", "vec", "cg"], w=["t1"])
            p.act(t2[:], mi[:, 36:40, :], AF.Silu, r=[("mi", 9), "t2"], w=["t2"])
            p.tt(R(mix[:, 12:16, :]), t1[:], t2[:], ALU.mult, r=["t1", "t2"], w=[("mix", 3)])
        p.store(dbg_d[s].rearrange("(k p) n -> p k n", p=128), mix[:], r=[("mix", i) for i in range(4)])
        wov = wo_d.rearrange("(k p) m -> p k m", p=128)
        for g in range(4):
            wbt, wk = nwb()
            wview = wbt[:, :].rearrange("p (k m) -> p k m", k=KC)
            p.load(R(wview), R(wov[:, :, g * 512:(g + 1) * 512]), w=[wk])
            for mm_ in range(4):
                m = g * 4 + mm_
                ps, pk = nps()
                for k in range(KC):
                    p.mm(ps[:, 0:P_], R(wview[:, k, mm_ * 128:(mm_ + 1) * 128]), R(mix[:, k, :]), k == 0, k == KC - 1,
                         r=[wk, ("mix", k // 4)], w=[pk])
                residual(ps, m, 0 * KC, 1 * KC)
        ln_affine(V_LN, V_LN + 16)
        stats("x")
        for k in range(KC):
            p.tt(R(mix[:, k, :]), x[:, k, :], mean[:], ALU.subtract, r=[("x", k), "mean"], w=[("mix", k // 4)])
            p.tt(R(mix[:, k, :]), mix[:, k, :], rstd[:], ALU.mult, r=["rstd", ("mix", k // 4)], w=[("mix", k // 4)])
            p.act(R(mix[:, k, 0:NCXC]), mix[:, k, 0:NCXC], AF.Identity, r=["sc1", "mod", ("mix", k // 4)],
                  w=[("mix", k // 4)], scale=sc1[:, KC + k:KC + k + 1], bias=mod[:, 4 * KC + k:4 * KC + k + 1])
            p.act(R(mix[:, k, NCXC:P_]), mix[:, k, NCXC:P_], AF.Identity, r=["sc1", "mod", ("mix", k // 4)],
                  w=[("mix", k // 4)], scale=sc1[:, k:k + 1], bias=mod[:, 2 * KC + k:2 * KC + k + 1])
        for hi, col in enumerate((0, NCXC - 1, NCXC, P_ - 1)):
            p.ts(R(mix[:, :, col:col + 1]), mix[:, :, col:col + 1], hm[:, hi:hi + 1], None, ALU.mult, None,
                 r=["hm"] + [("mix", i) for i in range(4)], w=[("mix", i) for i in range(4)])
        upv = up_d.rearrange("(k p) m -> p k m", p=128)
        p.copy(R(hff[:, :, 0:1]), zt[:], r=["zt"], w=["hff"] + [("mi", i) for i in range(10)])
        p.copy(R(hff[:, :, P_ - 1:P_]), zt[:], r=["zt"], w=["hff"])
        for j0 in range(0, 43, 2):
            nj = min(2, 43 - j0)
            wj = 128 * nj
            wbt, wk = nwb()
            wview = wbt[:, 0:2 * KC * wj].rearrange("p (a k m) -> p a k m", a=2, k=KC)
            p.load(R(wview[:, 0]), R(upv[:, :, j0 * 128:j0 * 128 + wj]), w=[wk])
            p.load(R(wview[:, 1]), R(upv[:, :, DFF + j0 * 128:DFF + j0 * 128 + wj]), w=[wk])
            for j in range(j0, j0 + nj):
                jo = (j - j0) * 128
                outs = []
                for a in range(2):
                    ps, pk = nps()
                    for k in range(KC):
                        p.mm(ps[:, 0:P_], R(wview[:, a, k, jo:jo + 128]), R(mix[:, k, :]), k == 0, k == KC - 1,
                             r=[wk, ("mix", k // 4)], w=[pk])
                    outs.append((ps, pk))
                for a, (ps, pk) in enumerate(outs):
                    dst, dk = (ca, "ca") if a == 0 else (cg, "cg")
                    c = V_CW + (a * 43 + j) * 4
                    p.ts(dst[:, 1:P_ - 1], ps[:, 1:P_ - 1], vec[:, c + 1:c + 2], vec[:, c + 3:c + 4], ALU.mult, ALU.add,
                         r=[pk, "vec"], w=[dk])
                    p.stt(dst[:, 1:P_ - 1], ps[:, 0:P_ - 2], vec[:, c:c + 1], dst[:, 1:P_ - 1], ALU.mult, ALU.add,
                          r=[pk, "vec", dk], w=[dk])
                    p.stt(dst[:, 1:P_ - 1], ps[:, 2:P_], vec[:, c + 2:c + 3], dst[:, 1:P_ - 1], ALU.mult, ALU.add,
                          r=[pk, "vec", dk], w=[dk])
                p.act(cg[:, 1:P_ - 1], cg[:, 1:P_ - 1], AF.Silu, r=["cg"], w=["cg"])
                p.tt(R(hff[:, j, 1:P_ - 1]), ca[:, 1:P_ - 1], cg[:, 1:P_ - 1], ALU.mult, r=["ca", "cg"], w=["hff"], e="pool")
        dnv = dn_d.rearrange("(j p) m -> p j m", p=128)
        for mg in range(4):
            accs = [nps() for _ in range(4)]
            for (ja, jb) in ((0, 11), (11, 22), (22, 33), (33, 43)):
                wbt, wk = nwb()
                wview = wbt[:, 0:(jb - ja) * 512].rearrange("p (j m) -> p j m", j=jb - ja)
                p.load(R(wview), R(dnv[:, ja:jb, mg * 512:(mg + 1) * 512]), w=[wk])
                for mm_ in range(4):
                    ps, pk = accs[mm_]
                    for j in range(ja, jb):
                        p.mm(ps[:, 0:P_], R(wview[:, j - ja, mm_ * 128:(mm_ + 1) * 128]), R(hff[:, j, :]), j == 0, j == 42,
                             r=[wk, "hff"], w=[pk])
            for mm_ in range(4):
                ps, pk = accs[mm_]
                residual(ps, mg * 4 + mm_, 6 * KC, 7 * KC)
        ln_affine(V_LN + 32, V_LN + 48)
        p.store(out_d[s].rearrange("(k p) n -> p k n", p=128), x[:], r=[("x", k) for k in range(KC)])
    p.finish()
    p.emit()
    return nc


def chunkcols(v):
    v = np.asarray(v, np.float32)
    return np.ascontiguousarray(v.reshape(-1, 128).T)


def _pass_rows(i, s):
    c0 = 32 * i + 8 * s
    l0 = 1024 * i + 256 * s
    idx = np.empty(PC, np.int64)
    cr = np.arange(c0 - 1, c0 + 9)
    cr = np.where((cr >= 0) & (cr < CTX), cr, -1)
    lr = np.arange(l0 - 1, l0 + 257)
    lr = np.where((lr >= 0) & (lr < SEQ), lr + CTX, -1)
    idx[:NCXC] = cr
    idx[NCXC:] = lr
    return idx


def _gather_T(full, idx):
    out = full[np.maximum(idx, 0)].T.copy()
    out[:, idx < 0] = 0
    return np.ascontiguousarray(out, np.float32)


def phase_c_inputs(l, xfull, mixfull, pxsel, mod_x, mod_c, P, inject=False):
    mx = [mod_x[j * D:(j + 1) * D] for j in range(6)]
    mc = [mod_c[j * D:(j + 1) * D] for j in range(6)]
    modC = np.concatenate([chunkcols(v) for v in (mx[2], mc[2], mx[3], mx[4], mc[3], mc[4], mx[5], mc[5])], 1)
    vec = np.zeros((128, NV_C), np.float32)
    vec[:, 0:4] = chunkcols(P["s5_d"][l])
    vec[:, 4:8] = chunkcols(P["s5_glu_b"][l])
    vec[:, 8:12] = chunkcols(P["ssd_norm_w"][l])
    vec[:, 12:13] = chunkcols(P["gla_norm_w"][l])
    vec[:, 13:29] = chunkcols(P["ln_g"][l, 0])
    vec[:, 29:45] = chunkcols(P["ln_b"][l, 0])
    vec[:, 45:61] = chunkcols(P["ln_g"][l, 1])
    vec[:, 61:77] = chunkcols(P["ln_b"][l, 1])
    cw = P["ffn_conv_w"][l]
    cb = P["ffn_conv_b"][l]
    cv = np.stack([chunkcols(cw[0]), chunkcols(cw[1]), chunkcols(cw[2]), chunkcols(cb)], 2)
    vec[:, 77:] = cv.reshape(128, 86 * 4)
    in_maps = []
    for i in range(NCORES):
        xs, ms, hs = [], [], []
        for s in range(NPASS):
            idx = _pass_rows(i, s)
            xs.append(_gather_T(xfull, idx))
            m = np.zeros((40 * 128, PC), np.float32)
            mt = _gather_T(mixfull, idx)
            m[:mt.shape[0]] = mt
            ms.append(m)
            flags = (idx[[0, NCXC - 1, NCXC, PC - 1]] >= 0).astype(np.float32)
            hs.append(np.tile(flags[None, :], (128, 1)))
        in_maps.append({
            "xT": np.stack(xs), "mixin": np.stack(ms), "modC": modC, "vecs": vec, "hmask": np.stack(hs),
            "glu_w": np.ascontiguousarray(P["s5_glu_w"][l]), "w_out": np.ascontiguousarray(P["w_out"][l]),
            "ffn_up": np.ascontiguousarray(P["ffn_up"][l]), "ffn_down": np.ascontiguousarray(P["ffn_down"][l]),
        })
    return in_maps


def phase_c_gather(res, name):
    out = np.zeros((CTX + SEQ, D), np.float32)
    for i in range(NCORES):
        o = res[i][name]
        for s in range(NPASS):
            c0 = 32 * i + 8 * s
            l0 = 1024 * i + 256 * s
            out[c0:c0 + 8] = o[s][:, 1:9].T
            out[CTX + l0:CTX + l0 + 256] = o[s][:, NCXC + 1:PC - 1].T
    return out


def build_mod():
    nc = bass.Bass("TRN2", target_bir_lowering=False)
    NCOL = 6144
    w_d = nc.dram_tensor("w", [D, NCOL], F32, kind="ExternalInput").ap()
    c_d = nc.dram_tensor("cv", [128, KC, 2], F32, kind="ExternalInput").ap()
    b_d = nc.dram_tensor("b", [128, 48], F32, kind="ExternalInput").ap()
    o_d = nc.dram_tensor("mod", [128, 48, 2], F32, kind="ExternalOutput").ap()
    p = Prog(nc)
    cv = p.sb([128, KC, 2])
    bb = p.sb([128, 48])
    ob = p.sb([128, 48, 2])
    wb = [p.sb([128, KC, 512]) for _ in range(2)]
    pss = [p.ps([128, 512]) for _ in range(2)]
    cv0 = p.sb([128, KC, 2])
    p.load(cv0[:], c_d, w=["cv0"])
    p.load(bb[:], b_d, w=["b"])
    p.act(R(cv[:]), cv0[:], AF.Silu, r=["cv0"], w=["cv"])
    wv = w_d.rearrange("(k p) m -> p k m", p=128)
    for g in range(12):
        wt = wb[g % 2]
        for k in range(KC):
            p.load(R(wt[:, k, :]), R(wv[:, k, g * 512:(g + 1) * 512]), w=[("w", g % 2, k)])
        for mm_ in range(4):
            j = g * 4 + mm_
            ps = pss[j % 2]
            for k in range(KC):
                p.mm(ps[:, 0:2], R(wt[:, k, mm_ * 128:(mm_ + 1) * 128]), R(cv[:, k, :]), k == 0, k == KC - 1,
                     r=[("w", g % 2, k), "cv"], w=[("ps", j % 2)])
            p.ts(ob[:, j, :], ps[:, 0:2], bb[:, j:j + 1], None, ALU.add, None, r=[("ps", j % 2), "b"], w=["ob"])
    p.store(o_d, ob[:], r=["ob"])
    p.finish()
    p.emit()
    return nc


def run_mod(c, c_ctx, w_mod, b_mod):
    nc = build_mod()
    cvh = np.stack([chunkcols(c.reshape(-1)), chunkcols(c_ctx.reshape(-1))], 2)
    in_maps = []
    for i in range(NCORES):
        l, h = i // 2, i % 2
        in_maps.append({"w": np.ascontiguousarray(w_mod[l][:, h * 6144:(h + 1) * 6144]), "cv": cvh,
                        "b": chunkcols(b_mod[l][h * 6144:(h + 1) * 6144])})
    res = run_spmd(nc, in_maps)
    mod_x = np.zeros((DEPTH, 6 * D), np.float32)
    mod_c = np.zeros((DEPTH, 6 * D), np.float32)
    for i in range(NCORES):
        l, h = i // 2, i % 2
        o = res[i]["mod"]
        mod_x[l, h * 6144:(h + 1) * 6144] = o[:, :, 0].T.reshape(-1)
        mod_c[l, h * 6144:(h + 1) * 6144] = o[:, :, 1].T.reshape(-1)
    return mod_x, mod_c


TT = CTX + SEQ
I32 = mybir.dt.int32
TWO_PI_HI = 6.28125
TWO_PI_LO = 2.0 * np.pi - 6.28125


def trig(p, x, n, tmp, ki, cos_o, sin_o, tag):
    a, b, c = tmp
    kx = [tag + "a", tag + "b", tag + "c", tag + "k"]
    p.ts(a, x, 1.0 / (2.0 * np.pi), None, ALU.mult, None, r=[tag + "x"], w=[kx[0]])
    p.copy(ki, a, r=[kx[0]], w=[kx[3]])
    p.copy(a, ki, r=[kx[3]], w=[kx[0]])
    p.stt(b, a, -TWO_PI_HI, x, ALU.mult, ALU.add, r=[kx[0], tag + "x"], w=[kx[1]])
    p.stt(b, a, -TWO_PI_LO, b, ALU.mult, ALU.add, r=[kx[0], kx[1]], w=[kx[1]])
    p.act(a, b, AF.Sin, r=[kx[1]], w=[kx[0]], scale=0.25)
    p.ts(b, b, 0.25, float(np.pi / 2), ALU.mult, ALU.add, r=[kx[1]], w=[kx[1]])
    p.act(b, b, AF.Sin, r=[kx[1]], w=[kx[1]])
    for it in range(2):
        p.tt(c, a, b, ALU.mult, r=[kx[0], kx[1]], w=[kx[2]])
        p.tt(b, b, b, ALU.mult, r=[kx[1]], w=[kx[1]])
        p.tt(a, a, a, ALU.mult, r=[kx[0]], w=[kx[0]])
        p.tt(b, b, a, ALU.subtract, r=[kx[0], kx[1]], w=[kx[1]])
        p.ts(a, c, 2.0, None, ALU.mult, None, r=[kx[2]], w=[kx[0]])
    p.copy(cos_o, b, r=[kx[1]], w=[tag + "cos"])
    p.copy(sin_o, a, r=[kx[0]], w=[tag + "sin"])


def build_s5():
    nc = bass.Bass("TRN2", target_bir_lowering=False)
    u_d = nc.dram_tensor("u", [2, 64, TT], F32, kind="ExternalInput").ap()
    lp_d = nc.dram_tensor("lanep", [2, 2, 128, 3], F32, kind="ExternalInput").ap()
    bre_d = nc.dram_tensor("bre", [2, 2, 128, 16], F32, kind="ExternalInput").ap()
    bim_d = nc.dram_tensor("bim", [2, 2, 128, 16], F32, kind="ExternalInput").ap()
    cre_d = nc.dram_tensor("cre", [2, 2, 128, 64], F32, kind="ExternalInput").ap()
    cim_d = nc.dram_tensor("cim", [2, 2, 128, 64], F32, kind="ExternalInput").ap()
    tau_d = nc.dram_tensor("tau1", [128, 128], F32, kind="ExternalInput").ap()
    id_d = nc.dram_tensor("ident", [128, 128], F32, kind="ExternalInput").ap()
    y_d = nc.dram_tensor("y", [2, 64, TT], F32, kind="ExternalOutput").ap()
    p = Prog(nc)
    BL = 512
    u = p.sb([64, TT])
    yb = [p.sb([64, BL]) for _ in range(2)]
    tau = p.sb([128, 128])
    ident = p.sb([128, 128])
    lp = p.sb([128, 3])
    sc = p.sb([128, 16])
    braw = p.sb([128, 2, 16])
    bbf = p.sb([128, 2, 64])
    bbT = [[p.sb([64, 2, 128]) for _ in range(2)] for _ in range(2)]
    cc = [[p.sb([128, 2, 64]) for _ in range(2)] for _ in range(2)]
    cosT = [[p.sb([128, BL]) for _ in range(2)] for _ in range(2)]
    sinT = [[p.sb([128, BL]) for _ in range(2)] for _ in range(2)]
    rhoT = [[p.sb([128, 128]) for _ in range(2)] for _ in range(2)]
    cq = [[p.sb([128, 2]) for _ in range(2)] for _ in range(2)]
    tmp = [p.sb([128, 128]) for _ in range(3)]
    tki = p.sb([128, 128], I32)
    ang = p.sb([128, 128])
    b_re = p.sb([128, BL]); b_im = p.sb([128, BL])
    v_re = p.sb([128, BL]); v_im = p.sb([128, BL])
    m1 = p.sb([128, BL]); m2 = p.sb([128, BL])
    w_re = p.sb([128, BL]); w_im = p.sb([128, BL])
    s_re = p.sb([128, BL]); s_im = p.sb([128, BL])
    car = [p.sb([128, 2]) for _ in range(2)]
    ct = p.sb([128, 2])
    psb = [p.ps([128, 512]) for _ in range(4)]
    psy = [p.ps([128, 512]) for _ in range(2)]
    pst = p.ps([128, 512])

    p.load(tau[:], tau_d, w=["tau"])
    p.load(ident[:], id_d, w=["ident"])
    for d in range(2):
        for t in range(2):
            tg = f"s{d}{t}"
            p.load(lp[:], lp_d[d, t], w=["lp"])
            p.load(braw[:, 0, :], bre_d[d, t], w=["braw"])
            p.load(braw[:, 1, :], bim_d[d, t], w=["braw"])
            p.load(R(cc[d][t][:, 0, :]), R(cre_d[d, t]), w=[("cc", d, t)])
            p.load(R(cc[d][t][:, 1, :]), R(cim_d[d, t]), w=[("cc", d, t)])
            p.act(sc[:, 0:1], lp[:, 2:3], AF.Exp, r=["lp"], w=["sc"])
            p.tt(sc[:, 1:2], lp[:, 0:1], sc[:, 0:1], ALU.mult, r=["lp", "sc"], w=["sc"])
            p.act(sc[:, 2:3], sc[:, 1:2], AF.Exp, r=["sc"], w=["sc"])
            p.tt(sc[:, 3:4], lp[:, 1:2], sc[:, 0:1], ALU.mult, r=["lp", "sc"], w=["sc"])
            p.ts(ang[:], tau[:], sc[:, 3:4], None, ALU.mult, None, r=["tau", "sc"], w=[tg + "x"])
            trig(p, ang[:], 128, [tmp[0][:], tmp[1][:], tmp[2][:]], tki[:], cosT[d][t][:, 0:128], sinT[d][t][:, 0:128], tg)
            for rep in range(1, 4):
                p.copy(cosT[d][t][:, rep * 128:(rep + 1) * 128], cosT[d][t][:, 0:128], r=[tg + "cos"], w=[tg + "cos"], e="pool")
                p.copy(sinT[d][t][:, rep * 128:(rep + 1) * 128], sinT[d][t][:, 0:128], r=[tg + "sin"], w=[tg + "sin"], e="pool")
            p.copy(cq[d][t][:, 0:1], cosT[d][t][:, 127:128], r=[tg + "cos"], w=[("cq", d, t)])
            p.copy(cq[d][t][:, 1:2], sinT[d][t][:, 127:128], r=[tg + "sin"], w=[("cq", d, t)])
            p.memset(rhoT[d][t][:], 1.0, w=[("rho", d, t)])
            p.ts(rhoT[d][t][:], rhoT[d][t][:], sc[:, 2:3], None, ALU.mult, None, r=["sc", ("rho", d, t)], w=[("rho", d, t)])
            p.tt(sc[:, 4:5], sc[:, 2:3], cosT[d][t][:, 0:1], ALU.mult, r=["sc", tg + "cos"], w=["sc"])
            p.ts(sc[:, 4:5], sc[:, 4:5], -1.0, None, ALU.add, None, r=["sc"], w=["sc"])
            p.tt(sc[:, 5:6], sc[:, 2:3], sinT[d][t][:, 0:1], ALU.mult, r=["sc", tg + "sin"], w=["sc"])
            p.tt(sc[:, 6:7], lp[:, 0:1], lp[:, 0:1], ALU.mult, r=["lp"], w=["sc"])
            p.tt(sc[:, 9:10], lp[:, 1:2], lp[:, 1:2], ALU.mult, r=["lp"], w=["sc"])
            p.tt(sc[:, 6:7], sc[:, 6:7], sc[:, 9:10], ALU.add, r=["sc"], w=["sc"])
            p.op("dve", lambda eng: eng.reciprocal(sc[:, 6:7], sc[:, 6:7]), r=["sc"], w=["sc"])
            p.tt(sc[:, 7:8], sc[:, 4:5], lp[:, 0:1], ALU.mult, r=["sc", "lp"], w=["sc"])
            p.tt(sc[:, 9:10], sc[:, 5:6], lp[:, 1:2], ALU.mult, r=["sc", "lp"], w=["sc"])
            p.tt(sc[:, 7:8], sc[:, 7:8], sc[:, 9:10], ALU.add, r=["sc"], w=["sc"])
            p.tt(sc[:, 7:8], sc[:, 7:8], sc[:, 6:7], ALU.mult, r=["sc"], w=["sc"])
            p.tt(sc[:, 8:9], sc[:, 5:6], lp[:, 0:1], ALU.mult, r=["sc", "lp"], w=["sc"])
            p.tt(sc[:, 9:10], sc[:, 4:5], lp[:, 1:2], ALU.mult, r=["sc", "lp"], w=["sc"])
            p.tt(sc[:, 8:9], sc[:, 8:9], sc[:, 9:10], ALU.subtract, r=["sc"], w=["sc"])
            p.tt(sc[:, 8:9], sc[:, 8:9], sc[:, 6:7], ALU.mult, r=["sc"], w=["sc"])
            p.memset(bbf[:], 0.0, w=["bbf"])
            for half in range(2):
                rows = slice(64 * half, 64 * half + 64)
                co = 16 * (2 * t + half)
                p.ts(bbf[rows, 0, co:co + 16], braw[rows, 1, :], sc[rows, 8:9], -1.0, ALU.mult, ALU.mult,
                     r=["braw", "sc"], w=["bbf"])
                p.stt(bbf[rows, 0, co:co + 16], braw[rows, 0, :], sc[rows, 7:8], bbf[rows, 0, co:co + 16], ALU.mult, ALU.add,
                      r=["braw", "sc", "bbf"], w=["bbf"])
                p.ts(bbf[rows, 1, co:co + 16], braw[rows, 0, :], sc[rows, 8:9], None, ALU.mult, None,
                     r=["braw", "sc"], w=["bbf"])
                p.stt(bbf[rows, 1, co:co + 16], braw[rows, 1, :], sc[rows, 7:8], bbf[rows, 1, co:co + 16], ALU.mult, ALU.add,
                      r=["braw", "sc", "bbf"], w=["bbf"])
            for c2 in range(2):
                p.op("pe", lambda eng, o=pst[0:64, c2 * 128:(c2 + 1) * 128], i_=bbf[:, c2, :]: eng.transpose(o, i_, ident[:]),
                     r=["bbf", "ident"], w=["pst"])
            p.copy(R(bbT[d][t][:, :, :]), pst[0:64, 0:256].rearrange("p (a m) -> p a m", a=2), r=["pst"], w=[("bbT", d, t)], e="act")
            p.ts(R(cc[d][t][:, 1, :]), cc[d][t][:, 1, :], -1.0, None, ALU.mult, None, r=[("cc", d, t)], w=[("cc", d, t)])
            p.copy(R(cc[d][t][:, 0, :]), cc[d][t][:, 0, :], r=[("cc", d, t)], w=[("cc", d, t)])
    nblk = (TT + BL - 1) // BL
    for d in range(2):
        for k0 in range(0, TT, 2112):
            p.load(R(u[:, k0:k0 + 2112]), R(u_d[d][:, k0:k0 + 2112]), w=[("u", k0)])
        for t in range(2):
            p.memset(car[t][:], 0.0, w=[("car", t)])
        for bi in range(nblk):
            c0 = bi * BL
            cn = min(BL, TT - c0)
            uk = ("u", (c0 // 2112) * 2112)
            py = psy[bi % 2]
            pyk = ("psy", bi % 2)
            for t in range(2):
                pr, pim = psb[2 * t], psb[2 * t + 1]
                p.mm(pr[:, 0:cn], R(bbT[d][t][:, 0, :]), R(u[:, c0:c0 + cn]), True, True, r=[("bbT", d, t), uk], w=[("psb", 2 * t)])
                p.mm(pim[:, 0:cn], R(bbT[d][t][:, 1, :]), R(u[:, c0:c0 + cn]), True, True, r=[("bbT", d, t), uk], w=[("psb", 2 * t + 1)])
                p.copy(b_re[:, 0:cn], pr[:, 0:cn], r=[("psb", 2 * t)], w=["b_re"], e="act")
                p.copy(b_im[:, 0:cn], pim[:, 0:cn], r=[("psb", 2 * t + 1)], w=["b_im"], e="act")
                C_, S_ = cosT[d][t], sinT[d][t]
                tgc, tgs = f"s{d}{t}cos", f"s{d}{t}sin"
                p.tt(m1[:, 0:cn], b_re[:, 0:cn], C_[:, 0:cn], ALU.mult, r=["b_re", tgc], w=["m1"])
                p.tt(m2[:, 0:cn], b_im[:, 0:cn], S_[:, 0:cn], ALU.mult, r=["b_im", tgs], w=["m2"])
                p.tt(v_re[:, 0:cn], m1[:, 0:cn], m2[:, 0:cn], ALU.add, r=["m1", "m2"], w=["v_re"])
                p.tt(m1[:, 0:cn], b_im[:, 0:cn], C_[:, 0:cn], ALU.mult, r=["b_im", tgc], w=["m1"])
                p.tt(m2[:, 0:cn], b_re[:, 0:cn], S_[:, 0:cn], ALU.mult, r=["b_re", tgs], w=["m2"])
                p.tt(v_im[:, 0:cn], m1[:, 0:cn], m2[:, 0:cn], ALU.subtract, r=["m1", "m2"], w=["v_im"])
                for q0 in range(0, cn, 128):
                    sl = slice(q0, q0 + 128)
                    p.op("dve", lambda eng, o=w_re[:, sl], a=rhoT[d][t][:], b=v_re[:, sl], i_=car[t][:, 0:1]:
                         eng.tensor_tensor_scan(o, a, b, i_, ALU.mult, ALU.add),
                         r=[("rho", d, t), "v_re", ("car", t)], w=["w_re"])
                    p.op("dve", lambda eng, o=w_im[:, sl], a=rhoT[d][t][:], b=v_im[:, sl], i_=car[t][:, 1:2]:
                         eng.tensor_tensor_scan(o, a, b, i_, ALU.mult, ALU.add),
                         r=[("rho", d, t), "v_im", ("car", t)], w=["w_im"])
                    last = q0 + 127
                    cqt = cq[d][t]
                    p.ts(ct[:, 0:1], w_im[:, last:last + 1], cqt[:, 1:2], None, ALU.mult, None, r=["w_im", ("cq", d, t)], w=["ct"])
                    p.ts(ct[:, 1:2], w_re[:, last:last + 1], cqt[:, 1:2], None, ALU.mult, None, r=["w_re", ("cq", d, t)], w=["ct"])
                    p.stt(car[t][:, 0:1], w_re[:, last:last + 1], cqt[:, 0:1], ct[:, 0:1], ALU.mult, ALU.subtract,
                          r=["w_re", ("cq", d, t), "ct"], w=[("car", t)])
                    p.stt(car[t][:, 1:2], w_im[:, last:last + 1], cqt[:, 0:1], ct[:, 1:2], ALU.mult, ALU.add,
                          r=["w_im", ("cq", d, t), "ct"], w=[("car", t)])
                p.tt(m1[:, 0:cn], w_re[:, 0:cn], C_[:, 0:cn], ALU.mult, r=["w_re", tgc], w=["m1"])
                p.tt(m2[:, 0:cn], w_im[:, 0:cn], S_[:, 0:cn], ALU.mult, r=["w_im", tgs], w=["m2"])
                p.tt(R(s_re[:, 0:cn]), m1[:, 0:cn], m2[:, 0:cn], ALU.subtract, r=["m1", "m2"], w=["s_re"])
                p.tt(m1[:, 0:cn], w_im[:, 0:cn], C_[:, 0:cn], ALU.mult, r=["w_im", tgc], w=["m1"])
                p.tt(m2[:, 0:cn], w_re[:, 0:cn], S_[:, 0:cn], ALU.mult, r=["w_re", tgs], w=["m2"])
                p.tt(R(s_im[:, 0:cn]), m1[:, 0:cn], m2[:, 0:cn], ALU.add, r=["m1", "m2"], w=["s_im"])
                p.mm(py[0:64, 0:cn], R(cc[d][t][:, 0, :]), R(s_re[:, 0:cn]), t == 0, False, r=[("cc", d, t), "s_re"], w=[pyk])
                p.mm(py[0:64, 0:cn], R(cc[d][t][:, 1, :]), R(s_im[:, 0:cn]), False, t == 1, r=[("cc", d, t), "s_im"], w=[pyk])
            ybt = yb[bi % 2]
            p.copy(ybt[:, 0:cn], py[0:64, 0:cn], r=[pyk], w=[("yb", bi % 2)], e="act")
            p.store(y_d[d][:, c0:c0 + cn], ybt[:, 0:cn], r=[("yb", bi % 2)])
    p.finish()
    p.emit()
    return nc


def s5_inputs(l, u_full, P):
    tau1 = np.tile(np.arange(1, 129, dtype=np.float32)[None, :], (128, 1))
    ident = np.eye(128, dtype=np.float32)
    in_maps = []
    for i in range(NCORES):
        uc = u_full[:, 64 * i:64 * i + 64]
        uf = uc.T
        ub = np.concatenate([uc[:CTX][::-1], uc[CTX:][::-1]], 0).T
        lanep = np.zeros((2, 2, 128, 3), np.float32)
        bre = np.zeros((2, 2, 128, 16), np.float32)
        bim = np.zeros((2, 2, 128, 16), np.float32)
        cre = np.zeros((2, 2, 128, 64), np.float32)
        cim = np.zeros((2, 2, 128, 64), np.float32)
        for d in range(2):
            for t in range(2):
                for h in range(2):
                    gl = 2 * t + h
                    g = 4 * i + gl
                    rows = slice(64 * h, 64 * h + 64)
                    lanep[d, t, rows, 0] = P["s5_lam_re"][l, d, g]
                    lanep[d, t, rows, 1] = P["s5_lam_im"][l, d, g]
                    lanep[d, t, rows, 2] = P["s5_log_step"][l, d, g]
                    bre[d, t, rows] = P["s5_b_re"][l, d, g]
                    bim[d, t, rows] = P["s5_b_im"][l, d, g]
                    cre[d, t, rows, 16 * gl:16 * gl + 16] = P["s5_c_re"][l, d, g].T
                    cim[d, t, rows, 16 * gl:16 * gl + 16] = P["s5_c_im"][l, d, g].T
        in_maps.append({"u": np.ascontiguousarray(np.stack([uf, ub])), "lanep": lanep, "bre": bre, "bim": bim,
                        "cre": cre, "cim": cim, "tau1": tau1, "ident": ident})
    return in_maps


def unflip(y):
    yt = y.T
    return np.concatenate([yt[:CTX][::-1], yt[CTX:][::-1]], 0)


def s5_gather(res):
    yf = np.concatenate([res[i]["y"][0].T for i in range(NCORES)], 1)
    ybk = np.concatenate([unflip(res[i]["y"][1]) for i in range(NCORES)], 1)
    return yf, ybk


NEG = -30000.0


def na_cls(b):
    return 0 if b == 0 else 1 if b == 1 else 3 if b == 62 else 4 if b == 63 else 2


def na_kr0(b):
    return min(max(2 * b - 4, 0), 119)


def build_na():
    nc = bass.Bass("TRN2", target_bir_lowering=False)
    q_d = nc.dram_tensor("qT", [64, TT], F32, kind="ExternalInput").ap()
    k_d = nc.dram_tensor("kT", [64, TT], F32, kind="ExternalInput").ap()
    v_d = nc.dram_tensor("vt", [64, 132, 64], F32, kind="ExternalInput").ap()
    b_d = nc.dram_tensor("bias", [5, 128, 576], F32, kind="ExternalInput").ap()
    id_d = nc.dram_tensor("ident", [128, 128], F32, kind="ExternalInput").ap()
    o_d = nc.dram_tensor("oT", [64, TT], F32, kind="ExternalOutput").ap()
    p = Prog(nc)
    qT = p.sb([64, TT]); kT = p.sb([64, TT]); vt = p.sb([64, 132, 64]); oT = p.sb([64, TT])
    bias = p.sb([128, 5, 576])
    ident = p.sb([128, 128])
    S = p.sb([128, 832]); Pm = p.sb([128, 832])
    PTs = p.sb([64, 13, 128])
    dg = p.sb([128, 128])
    st = p.sb([128, 4])
    psA = p.ps([128, 512]); psB = p.ps([128, 512]); psC = p.ps([128, 512])
    psT = [p.ps([128, 512]) for _ in range(4)]
    pso = p.ps([128, 512])
    for k0 in range(0, TT, 2112):
        p.load(R(qT[:, k0:k0 + 2112]), R(q_d[:, k0:k0 + 2112]), w=["q"])
        p.load(R(kT[:, k0:k0 + 2112]), R(k_d[:, k0:k0 + 2112]), w=["k"])
    for r0 in range(0, 132, 33):
        p.load(R(vt[:, r0:r0 + 33, :]), R(v_d[:, r0:r0 + 33, :]), w=["v"])
    for c in range(5):
        p.load(bias[:, c, :], b_d[c], w=["bias"])
    p.load(ident[:], id_d, w=["ident"])

    def block(qc0, lat, b):
        nk = 832 if lat else 256
        if lat:
            kr0 = na_kr0(b)
            kc0 = CTX + 64 * kr0
            cls = na_cls(b)
            p.mm(psA[:, 0:288], R(qT[:, qc0:qc0 + 128]), R(kT[:, kc0:kc0 + 288]), True, True, r=["q", "k"], w=["psA"])
            p.mm(psB[:, 0:288], R(qT[:, qc0:qc0 + 128]), R(kT[:, kc0 + 288:kc0 + 576]), True, True, r=["q", "k"], w=["psB"])
            p.mm(psC[:, 0:256], R(qT[:, qc0:qc0 + 128]), R(kT[:, 0:256]), True, True, r=["q", "k"], w=["psC"])
            p.stt(S[:, 0:288], psA[:, 0:288], 0.125, bias[:, cls, 0:288], ALU.mult, ALU.add, r=["psA", "bias"], w=["S"])
            p.stt(S[:, 288:576], psB[:, 0:288], 0.125, bias[:, cls, 288:576], ALU.mult, ALU.add, r=["psB", "bias"], w=["S"])
            p.act(S[:, 576:832], psC[:, 0:256], AF.Copy, r=["psC"], w=["S"], scale=0.125)
        else:
            p.mm(psC[:, 0:256], R(qT[:, qc0:qc0 + 128]), R(kT[:, 0:256]), True, True, r=["q", "k"], w=["psC"])
            p.act(S[:, 0:256], psC[:, 0:256], AF.Copy, r=["psC"], w=["S"], scale=0.125)
        p.op("dve", lambda eng: eng.reduce_max(st[:, 0:1], S[:, 0:nk], AX.X), r=["S"], w=["st"])
        p.ts(st[:, 1:2], st[:, 0:1], -1.0, None, ALU.mult, None, r=["st"], w=["st"])
        p.act(R(Pm[:, 0:nk]), S[:, 0:nk], AF.Exp, r=["S", "st"], w=["P", "st2"], bias=st[:, 1:2], accum_out=st[:, 2:3])
        p.op("dve", lambda eng: eng.reciprocal(st[:, 3:4], st[:, 2:3]), r=["st2", "P"], w=["st3"])
        p.ts(R(dg[:]), ident[:], st[:, 3:4], None, ALU.mult, None, r=["ident", "st3"], w=["dg"])
        nt = nk // 64
        for kt in range(nt):
            bank = psT[kt // 4]
            p.mm(bank[0:64, (kt % 4) * 128:(kt % 4) * 128 + 128], R(Pm[:, kt * 64:(kt + 1) * 64]), R(dg[:]), True, True,
                 r=["P", "dg"], w=[("psT", kt // 4)])
        for bk in range((nt + 3) // 4):
            n4 = min(4, nt - 4 * bk)
            p.copy(R(PTs[:, 4 * bk:4 * bk + n4, :]), psT[bk][0:64, 0:n4 * 128].rearrange("p (a m) -> p a m", a=n4),
                   r=[("psT", bk)], w=["PTs"], e="act" if bk % 2 == 0 else "dve")
        for kt in range(nt):
            if lat:
                row = 4 + kr0 + kt if kt < 9 else kt - 9
            else:
                row = kt
            p.mm(pso[0:64, 0:128], R(vt[:, row, :]), R(PTs[:, kt, :]), kt == 0, kt == nt - 1, r=["v", "PTs"], w=["pso"])
        p.copy(oT[:, qc0:qc0 + 128], pso[0:64, 0:128], r=["pso"], w=["oT"], e="pool" if False else "act")

    for cb in range(2):
        block(128 * cb, False, cb)
    for b in range(64):
        block(CTX + 128 * b, True, b)
    for k0 in range(0, TT, 2112):
        p.store(o_d[:, k0:k0 + 2112], oT[:, k0:k0 + 2112], r=["oT"])
    p.finish()
    p.emit()
    return nc


def na_bias_tables(rpb_h):
    out = np.full((5, 128, 576), NEG, np.float32)
    for ci, b in enumerate((0, 1, 2, 62, 63)):
        kr0 = na_kr0(b)
        for q in range(128):
            r = 2 * b + q // 64
            c = q % 64
            rs = min(max(r - 4, 0), 120)
            cs = min(max(c - 8, 0), 48)
            for kr in range(rs, rs + 8):
                sl = (kr - kr0) * 64
                out[ci, q, sl + cs:sl + cs + 16] = rpb_h[kr - r + 7, cs - c + 15:cs - c + 31]
    return out


def na_inputs(l, q_full, k_full, v_full, P):
    ident = np.eye(128, dtype=np.float32)
    in_maps = []
    for i in range(NCORES):
        sl = slice(64 * i, 64 * i + 64)
        vt = v_full[:, sl].reshape(132, 64, 64).transpose(1, 0, 2)
        in_maps.append({"qT": np.ascontiguousarray(q_full[:, sl].T), "kT": np.ascontiguousarray(k_full[:, sl].T),
                        "vt": np.ascontiguousarray(vt), "bias": na_bias_tables(P["na_rpb"][l, i]), "ident": ident})
    return in_maps


def na_gather(res):
    return np.concatenate([res[i]["oT"].T for i in range(NCORES)], 1)


NCH = TT // 128


def build_ssd(nch=NCH, do_conv=True, do_setup=True):
    nc = bass.Bass("TRN2", target_bir_lowering=False)
    x_d = nc.dram_tensor("xbc", [2, 3, 128, TT], F32, kind="ExternalInput").ap()
    cw_d = nc.dram_tensor("cw", [2, 128, 3, 4], F32, kind="ExternalInput").ap()
    dt_d = nc.dram_tensor("dtm", [2, 128, NCH], F32, kind="ExternalInput").ap()
    sc_d = nc.dram_tensor("scal", [2, 128, 4], F32, kind="ExternalInput").ap()
    tri_d = nc.dram_tensor("tri", [128, 128], F32, kind="ExternalInput").ap()
    nm_d = nc.dram_tensor("negmask", [128, 128], F32, kind="ExternalInput").ap()
    id_d = nc.dram_tensor("ident", [128, 128], F32, kind="ExternalInput").ap()
    y_d = nc.dram_tensor("y", [2, 128, NCH, 64], F32, kind="ExternalOutput").ap()
    p = Prog(nc)
    raw = p.sb([128, TT])
    cv = [p.sb([128, TT]) for _ in range(3)]
    cw = p.sb([128, 3, 4]); scal = p.sb([128, 4])
    tri = p.sb([128, 128]); nm = p.sb([128, 128]); ident = p.sb([128, 128]); ones = p.sb([128, 128])
    zt = p.sb([128, 64])
    dt = p.sb([128, NCH]); dta = p.sb([128, NCH]); tA = p.sb([128, NCH]); tB = p.sb([128, NCH])
    nacs = p.sb([128, NCH]); wdec = p.sb([128, NCH]); dec = p.sb([128, NCH])
    ybuf = p.sb([128, NCH, 64])
    xdt = p.sb([128, 64]); Bw = p.sb([128, 128]); dtab = p.sb([128, 128])
    E = p.sb([128, 128]); CE = p.sb([128, 128]); Rm = p.sb([128, 128]); LT = p.sb([128, 128]); MT = p.sb([128, 128])
    hT = p.sb([128, 64])
    ps_x = p.ps([128, 512]); ps_B = p.ps([128, 512]); ps_R = p.ps([128, 512]); ps_CB = p.ps([128, 512])
    ps_y = p.ps([128, 512]); ps_h = p.ps([128, 512]); ps_s = p.ps([128, 512])
    p.load(tri[:], tri_d, w=["tri"]); p.load(nm[:], nm_d, w=["nm"]); p.load(ident[:], id_d, w=["ident"])
    p.memset(ones[:], 1.0, w=["ones"]); p.memset(zt[:], 0.0, w=["zt"])
    segs = [(0, CTX), (CTX, TT)]
    for d in range(2):
        p.load(cw[:], cw_d[d], w=["cw"]); p.load(scal[:], sc_d[d], w=["scal"]); p.load(dt[:], dt_d[d], w=["dt"])
        p.ts(dt[:], dt[:], scal[:, 0:1], None, ALU.add, None, r=["dt", "scal"], w=["dt"])
        p.ts(tA[:], dt[:], 0.0, None, ALU.max, None, r=["dt"], w=["tA"])
        p.ts(tB[:], dt[:], 0.0, None, ALU.min, None, r=["dt"], w=["tB"])
        p.tt(tB[:], tB[:], tA[:], ALU.subtract, r=["tA", "tB"], w=["tB"])
        p.act(tB[:], tB[:], AF.Exp, r=["tB"], w=["tB"])
        p.act(tB[:], tB[:], AF.Ln, r=["tB"], w=["tB"], bias=1.0)
        p.tt(dt[:], tA[:], tB[:], ALU.add, r=["tA", "tB"], w=["dt"])
        p.act(scal[:, 3:4], scal[:, 1:2], AF.Exp, r=["scal"], w=["scal"])
        p.ts(dta[:], dt[:], scal[:, 3:4], -1.0, ALU.mult, ALU.mult, r=["dt", "scal"], w=["dta"])
        p.mm(ps_s[:, 0:NCH], tri[:], dta[:], True, True, r=["tri", "dta"], w=["ps_s"])
        p.ts(nacs[:], ps_s[:, 0:NCH], -1.0, None, ALU.mult, None, r=["ps_s"], w=["nacs"])
        p.mm(ps_s[:, 0:NCH], ones[:], dta[:], True, True, r=["ones", "dta", "nacs"], w=["ps_s"])
        p.tt(wdec[:], ps_s[:, 0:NCH], nacs[:], ALU.add, r=["ps_s", "nacs"], w=["wdec"])
        p.act(wdec[:], wdec[:], AF.Exp, r=["wdec"], w=["wdec"])
        p.act(dec[:], ps_s[:, 0:NCH], AF.Exp, r=["ps_s"], w=["dec"])
        for ch in range(3 if do_conv else 0):
            np_ = 64 if ch == 0 else 128
            for k0 in range(0, TT, 2112):
                p.load(raw[0:np_, k0:k0 + 2112], x_d[d, ch, 0:np_, k0:k0 + 2112], w=["raw"])
            o = cv[ch]
            for (a, b) in segs:
                p.ts(R(o[0:np_, a:b]), raw[0:np_, a:b], cw[0:np_, ch, 1:2], cw[0:np_, ch, 3:4], ALU.mult, ALU.add,
                     r=["raw", "cw"], w=[("cv", ch)])
                p.stt(R(o[0:np_, a + 1:b]), raw[0:np_, a:b - 1], cw[0:np_, ch, 0:1], o[0:np_, a + 1:b], ALU.mult, ALU.add,
                      r=["raw", "cw", ("cv", ch)], w=[("cv", ch)])
                p.stt(R(o[0:np_, a:b - 1]), raw[0:np_, a + 1:b], cw[0:np_, ch, 2:3], o[0:np_, a:b - 1], ALU.mult, ALU.add,
                      r=["raw", "cw", ("cv", ch)], w=[("cv", ch)])
            p.act(R(o[0:np_, :]), o[0:np_, :], AF.Silu, r=[("cv", ch)], w=[("cv", ch)])
        xs, Bm, Cm = cv
        p.copy(R(hT[:]), zt[:], r=["zt"], w=["hT"])
        for c in range(nch):
            cols = slice(128 * c, 128 * c + 128)
            p.op("pe", lambda eng, i_=xs[0:64, cols]: eng.transpose(ps_x[:, 0:64], i_, ident[0:64, 0:64]),
                 r=[("cv", 0), "ident"], w=["ps_x"])
            p.op("pe", lambda eng, i_=Bm[:, cols]: eng.transpose(ps_B[:, 0:128], i_, ident[:]),
                 r=[("cv", 1), "ident"], w=["ps_B"])
            p.ts(R(xdt[:]), ps_x[:, 0:64], dt[:, c:c + 1], None, ALU.mult, None, r=["ps_x", "dt"], w=["xdt"])
            p.ts(R(Bw[:]), ps_B[:, 0:128], wdec[:, c:c + 1], None, ALU.mult, None, r=["ps_B", "wdec"], w=["Bw"])
            p.ts(dtab[:], ones[:], dta[:, c:c + 1], None, ALU.mult, None, r=["ones", "dta"], w=["dtab"])
            p.mm(ps_R[:, 0:128], dtab[:], tri[:], True, True, r=["dtab", "tri"], w=["ps_R"])
            p.act(E[:], ps_R[:, 0:128], AF.Exp, r=["ps_R"], w=["E"])
            p.tt(R(CE[:]), Cm[:, cols], E[:], ALU.mult, r=[("cv", 2), "E"], w=["CE"])
            p.tt(Rm[:], ps_R[:, 0:128], nm[:], ALU.add, r=["ps_R", "nm"], w=["Rm"])
            p.act(LT[:], Rm[:], AF.Exp, r=["Rm", "nacs"], w=["LT"], bias=nacs[:, c:c + 1])
            p.mm(ps_CB[:, 0:128], R(Bm[:, cols]), R(Cm[:, cols]), True, True, r=[("cv", 1), ("cv", 2)], w=["ps_CB"])
            p.tt(R(MT[:]), ps_CB[:, 0:128], LT[:], ALU.mult, r=["ps_CB", "LT"], w=["MT"])
            p.mm(ps_y[:, 0:64], R(MT[:]), R(xdt[:]), True, False, r=["MT", "xdt"], w=["ps_y"])
            p.mm(ps_y[:, 0:64], R(CE[:]), R(hT[:]), False, True, r=["CE", "hT"], w=["ps_y"])
            p.copy(ybuf[:, c, :], ps_y[:, 0:64], r=["ps_y"], w=[("yb", c)], e="act")
            if d == 0:
                p.stt(ybuf[:, c, :], ps_x[:, 0:64], scal[:, 2:3], ybuf[:, c, :], ALU.mult, ALU.add,
                      r=["ps_x", "scal", ("yb", c)], w=[("yb", c)])
            p.mm(ps_h[:, 0:64], R(Bw[:]), R(xdt[:]), True, True, r=["Bw", "xdt"], w=["ps_h"])
            p.ts(R(hT[:]), hT[:], dec[:, c:c + 1], None, ALU.mult, None, r=["hT", "dec"], w=["hT"])
            p.tt(R(hT[:]), hT[:], ps_h[:, 0:64], ALU.add, r=["hT", "ps_h"], w=["hT"])
        p.store(y_d[d], ybuf[:], r=[("yb", c) for c in range(NCH)])
    p.finish()
    p.emit()
    return nc


def flipseq(a):
    return np.concatenate([a[:CTX][::-1], a[CTX:][::-1]], 0)


def ssd_inputs(l, xbc_full, dt_full, P):
    jj = np.arange(128)
    tri = (jj[:, None] <= jj[None, :]).astype(np.float32)
    negmask = np.where(jj[:, None] <= jj[None, :], 0.0, NEG).astype(np.float32)
    ident = np.eye(128, dtype=np.float32)
    cwl, cbl = P["ssd_conv_w"][l], P["ssd_conv_b"][l]
    in_maps = []
    for i in range(NCORES):
        g = i // 4
        colsets = [np.arange(64 * i, 64 * i + 64), 512 + np.arange(128 * g, 128 * g + 128),
                   768 + np.arange(128 * g, 128 * g + 128)]
        xbc = np.zeros((2, 3, 128, TT), np.float32)
        cw = np.zeros((2, 128, 3, 4), np.float32)
        dtm = np.zeros((2, 128, NCH), np.float32)
        scal = np.zeros((2, 128, 4), np.float32)
        for d in range(2):
            for ch, cs in enumerate(colsets):
                a = xbc_full[:, cs]
                if d == 1:
                    a = flipseq(a)
                xbc[d, ch, :len(cs)] = a.T
                taps = cwl[:, cs] if d == 0 else cwl[::-1][:, cs]
                cw[d, :len(cs), ch, 0:3] = taps.T
                cw[d, :len(cs), ch, 3] = cbl[cs]
            dcol = dt_full[:, d * 8 + i]
            if d == 1:
                dcol = flipseq(dcol[:, None])[:, 0]
            dtm[d] = dcol.reshape(NCH, 128).T
            scal[d, :, 0] = P["ssd_dt_bias"][l, d, i]
            scal[d, :, 1] = P["ssd_a_log"][l, d, i]
            scal[d, :, 2] = P["ssd_d"][l, i]
        in_maps.append({"xbc": xbc, "cw": cw, "dtm": dtm, "scal": scal, "tri": tri, "negmask": negmask, "ident": ident})
    return in_maps


def tm_to_nat(y, rev):
    a = y.transpose(1, 0, 2).reshape(TT, -1)
    if rev:
        a = np.concatenate([a[:CTX][::-1], a[CTX:][::-1]], 0)
    return a


def ssd_gather(res):
    yf = np.concatenate([tm_to_nat(res[i]["y"][0], False) for i in range(NCORES)], 1)
    yb = np.concatenate([tm_to_nat(res[i]["y"][1], True) for i in range(NCORES)], 1)
    return yf, yb


NCG = TT // 64
LNK = float(np.log(64.0 ** -0.5))


def build_gla():
    nc = bass.Bass("TRN2", target_bir_lowering=False)
    qk_d = nc.dram_tensor("qk", [2, 2, 64, TT], F32, kind="ExternalInput").ap()
    v_d = nc.dram_tensor("vtm", [2, 64, NCG, 64], F32, kind="ExternalInput").ap()
    g_d = nc.dram_tensor("gT", [2, 16, TT], F32, kind="ExternalInput").ap()
    gw_d = nc.dram_tensor("gw", [2, 16, 64], F32, kind="ExternalInput").ap()
    gb_d = nc.dram_tensor("gb", [2, 64, 1], F32, kind="ExternalInput").ap()
    cs_d = nc.dram_tensor("cs", [2, 2, 64, TT], F32, kind="ExternalInput").ap()
    rot_d = nc.dram_tensor("rot", [64, 64], F32, kind="ExternalInput").ap()
    cm_d = nc.dram_tensor("cmask", [64, TT], F32, kind="ExternalInput").ap()
    um_d = nc.dram_tensor("umask", [64, 64], F32, kind="ExternalInput").ap()
    id_d = nc.dram_tensor("ident", [128, 128], F32, kind="ExternalInput").ap()
    o_d = nc.dram_tensor("o", [2, 64, NCG, 64], F32, kind="ExternalOutput").ap()
    p = Prog(nc)
    BL = 512
    Q = p.sb([64, TT]); Kt = p.sb([64, TT]); LA = p.sb([64, TT]); Bt = p.sb([64, TT])
    vtm = p.sb([64, NCG, 64])
    gT = vtm[:, :, :].rearrange("p c v -> p (c v)")[0:16, :]
    gw = p.sb([16, 64]); gb = p.sb([64, 1]); ngb = p.sb([64, 1])
    rot = p.sb([64, 64]); um = p.sb([64, 64]); ident = p.sb([128, 128])
    csb = [p.sb([64, 2, BL]) for _ in range(2)]
    t1 = p.sb([64, BL]); t2 = p.sb([64, BL])
    ebl = p.sb([64, NCG])
    attT = p.sb([64, 64]); ktm = p.sb([64, 64]); S = p.sb([64, 64]); zt = p.sb([64, 64])
    obuf = [p.sb([64, 8, 64]) for _ in range(2)]
    ps_l = p.ps([128, 512]); ps_r = p.ps([128, 512])
    ps_a = p.ps([128, 512]); ps_t = p.ps([128, 512]); ps_o = p.ps([128, 512]); ps_kv = p.ps([128, 512])
    p.load(R(rot[:]), R(rot_d), w=["rot"]); p.load(um[:], um_d, w=["um"]); p.load(ident[:], id_d, w=["ident"])
    p.memset(zt[:], 0.0, w=["zt"])
    nblk = (TT + BL - 1) // BL
    for d in range(2):
        for k0 in range(0, TT, 2112):
            p.load(R(Q[:, k0:k0 + 2112]), R(qk_d[d, 0][:, k0:k0 + 2112]), w=["Q"])
            p.load(R(Kt[:, k0:k0 + 2112]), R(qk_d[d, 1][:, k0:k0 + 2112]), w=["K"])
            p.load(R(gT[:, k0:k0 + 2112]), R(g_d[d][:, k0:k0 + 2112]), w=["v"])
            p.load(Bt[:, k0:k0 + 2112], cm_d[:, k0:k0 + 2112], w=["B"])
        p.load(R(gw[:]), R(gw_d[d]), w=["gw"]); p.load(gb[:], gb_d[d], w=["gb"])
        p.ts(ngb[:], gb[:], -1.0, None, ALU.mult, None, r=["gb"], w=["ngb"])
        for bi in range(nblk):
            c0 = bi * BL
            cn = min(BL, TT - c0)
            p.mm(ps_l[0:64, 0:cn], R(gw[:]), R(gT[:, c0:c0 + cn]), True, True, r=["gw", "v"], w=["ps_l"])
            p.act(t1[:, 0:cn], ps_l[0:64, 0:cn], AF.Exp, r=["ps_l", "ngb"], w=["t1"], scale=-1.0, bias=ngb[:, 0:1])
            p.act(t1[:, 0:cn], t1[:, 0:cn], AF.Ln, r=["t1"], w=["t1"], bias=1.0)
            p.ts(LA[:, c0:c0 + cn], t1[:, 0:cn], -1.0 / 16.0, None, ALU.mult, None, r=["t1"], w=["LA"])
        for c0 in range(0, NCG, 33):
            p.load(R(vtm[:, c0:c0 + 33, :]), R(v_d[d][:, c0:c0 + 33, :]), w=["v"])
        for h0 in range(0, TT, 2112):
            p.op("dve", lambda eng, o=Bt[:, h0:h0 + 2112], a=Bt[:, h0:h0 + 2112], b=LA[:, h0:h0 + 2112]:
                 eng.tensor_tensor_scan(o, a, b, 0.0, ALU.mult, ALU.add), r=["B", "LA"], w=["B"])
        p.act(LA[:], Bt[:], AF.Exp, r=["B"], w=["LA"])
        p.act(Bt[:], Bt[:], AF.Exp, r=["B"], w=["B"], scale=-1.0, bias=LNK)
        p.copy(ebl[:], LA[:, 63:TT:64], r=["LA"], w=["ebl"])
        for bi in range(nblk):
            c0 = bi * BL
            cn = min(BL, TT - c0)
            cb = csb[bi % 2]
            p.load(cb[:, 0, 0:cn], cs_d[d, 0][:, c0:c0 + cn], w=[("cs", bi % 2)])
            p.load(cb[:, 1, 0:cn], cs_d[d, 1][:, c0:c0 + cn], w=[("cs", bi % 2)])
            for X, xk, E, ek in ((Q, "Q", LA, "LA"), (Kt, "K", Bt, "B")):
                p.mm(ps_r[0:64, 0:cn], R(rot[:]), R(X[:, c0:c0 + cn]), True, True, r=["rot", xk], w=["ps_r"])
                p.tt(t1[:, 0:cn], ps_r[0:64, 0:cn], cb[:, 1, 0:cn], ALU.mult, r=["ps_r", ("cs", bi % 2)], w=["t1"])
                p.tt(t2[:, 0:cn], X[:, c0:c0 + cn], cb[:, 0, 0:cn], ALU.mult, r=[xk, ("cs", bi % 2)], w=["t2"], e="pool")
                p.tt(t1[:, 0:cn], t1[:, 0:cn], t2[:, 0:cn], ALU.add, r=["t1", "t2"], w=["t1"])
                p.tt(R(X[:, c0:c0 + cn]), t1[:, 0:cn], E[:, c0:c0 + cn], ALU.mult, r=["t1", ek], w=[xk])
        p.copy(R(S[:]), zt[:], r=["zt"], w=["S"])
        for c in range(NCG):
            cols = slice(64 * c, 64 * c + 64)
            ob = obuf[(c // 8) % 2]
            obk = ("ob", (c // 8) % 2)
            p.mm(ps_a[0:64, 0:64], R(Kt[:, cols]), R(Q[:, cols]), True, True, r=["K", "Q"], w=["ps_a"])
            p.tt(R(attT[:]), ps_a[0:64, 0:64], um[:], ALU.mult, r=["ps_a", "um"], w=["attT"])
            p.op("pe", lambda eng, i_=Kt[:, cols]: eng.transpose(ps_t[0:64, 0:64], i_, ident[0:64, 0:64]),
                 r=["K", "ident"], w=["ps_t"])
            p.copy(R(ktm[:]), ps_t[0:64, 0:64], r=["ps_t"], w=["ktm"], e="act")
            p.mm(ps_o[0:64, 0:64], R(attT[:]), R(vtm[:, c, :]), True, False, r=["attT", "v"], w=["ps_o"])
            p.mm(ps_o[0:64, 0:64], R(Q[:, cols]), R(S[:]), False, True, r=["Q", "S"], w=["ps_o"])
            p.copy(ob[:, c % 8, :], ps_o[0:64, 0:64], r=["ps_o"], w=[obk], e="act")
            p.mm(ps_kv[0:64, 0:64], R(ktm[:]), R(vtm[:, c, :]), True, True, r=["ktm", "v"], w=["ps_kv"])
            p.ts(R(S[:]), S[:], ebl[:, c:c + 1], None, ALU.mult, None, r=["S", "ebl"], w=["S"])
            p.stt(R(S[:]), ps_kv[0:64, 0:64], ebl[:, c:c + 1], S[:], ALU.mult, ALU.add, r=["ps_kv", "ebl", "S"], w=["S"])
            if c % 8 == 7 or c == NCG - 1:
                g0 = (c // 8) * 8
                p.store(o_d[d][:, g0:c + 1, :], ob[:, 0:c + 1 - g0, :], r=[obk])
    p.finish()
    p.emit()
    return nc


def rope_tables():
    pos = np.arange(SEQ)
    rows = (pos // 64).astype(np.float32)
    cols = (pos % 64).astype(np.float32)
    inv = (np.float32(10000.0) ** (-np.arange(16, dtype=np.float32) / np.float32(16))).astype(np.float32)
    ang = np.concatenate([rows[:, None] * inv, cols[:, None] * inv], -1).astype(np.float32)
    cos = np.cos(ang).astype(np.float32)
    sin = np.sin(ang).astype(np.float32)
    c2 = np.concatenate([np.ones((CTX, 64), np.float32), np.concatenate([cos, cos], 1)], 0)
    s2 = np.concatenate([np.zeros((CTX, 64), np.float32), np.concatenate([sin, sin], 1)], 0)
    return c2, s2


def gla_inputs(l, q_full, k_full, v_full, g_full, P):
    c2, s2 = rope_tables()
    rot = np.zeros((64, 64), np.float32)
    for m in range(32):
        rot[m + 32, m] = -1.0
        rot[m, m + 32] = 1.0
    cmask = np.ones((64, TT), np.float32)
    cmask[:, ::64] = 0.0
    jj = np.arange(64)
    umask = (jj[:, None] <= jj[None, :]).astype(np.float32)
    ident = np.eye(128, dtype=np.float32)
    tabs = []
    for d in range(2):
        a, b = (c2, s2) if d == 0 else (flipseq(c2), flipseq(s2))
        tabs.append(np.stack([a.T, b.T]))
    cs = np.ascontiguousarray(np.stack(tabs))
    in_maps = []
    for i in range(NCORES):
        hh, vh = i // 2, i % 2
        qk = np.zeros((2, 2, 64, TT), np.float32)
        vtm = np.zeros((2, 64, NCG, 64), np.float32)
        gT = np.zeros((2, 16, TT), np.float32)
        gw = np.zeros((2, 16, 64), np.float32)
        gb = np.zeros((2, 64, 1), np.float32)
        for d in range(2):
            f = (lambda a: a) if d == 0 else flipseq
            qk[d, 0] = f(q_full[:, 64 * hh:64 * hh + 64]).T
            qk[d, 1] = f(k_full[:, 64 * hh:64 * hh + 64]).T
            vv = f(v_full[:, 128 * hh + 64 * vh:128 * hh + 64 * vh + 64])
            vtm[d] = vv.reshape(NCG, 64, 64).transpose(1, 0, 2)
            gT[d] = f(g_full[:, 16 * d:16 * d + 16]).T
            gw[d] = P["gla_gate_w"][l, d][:, 64 * hh:64 * hh + 64]
            gb[d, :, 0] = P["gla_gate_b"][l, d][64 * hh:64 * hh + 64]
        in_maps.append({"qk": qk, "vtm": vtm, "gT": gT, "gw": gw, "gb": gb, "cs": cs, "rot": rot,
                        "cmask": cmask, "umask": umask, "ident": ident})
    return in_maps


def gla_tm_to_nat(o, rev):
    a = o.transpose(1, 0, 2).reshape(TT, -1)
    if rev:
        a = np.concatenate([a[:CTX][::-1], a[CTX:][::-1]], 0)
    return a


def gla_gather(res):
    of = np.concatenate([gla_tm_to_nat(res[i]["o"][0], False) for i in range(NCORES)], 1)
    ob = np.concatenate([gla_tm_to_nat(res[i]["o"][1], True) for i in range(NCORES)], 1)
    return of, ob


_PROGS = {}


def _prog(name, fn):
    if name not in _PROGS:
        _PROGS[name] = fn()
    return _PROGS[name]


def phase_a_run(l, xfull, mod_x, mod_c, w_in):
    mx = [mod_x[j * D:(j + 1) * D] for j in range(6)]
    mc = [mod_c[j * D:(j + 1) * D] for j in range(6)]
    modA = np.concatenate([chunkcols(v) for v in (mx[1], mx[0], mc[1], mc[0])], 1)
    nc = _prog("A", lambda: build_phase_a(1056, 32))
    in_maps = []
    for i in range(NCORES):
        xt = np.concatenate([xfull[32 * i:32 * i + 32], xfull[CTX + 1024 * i:CTX + 1024 * i + 1024]], 0).T
        in_maps.append({"xT": np.ascontiguousarray(xt), "modA": modA, "w_in": np.ascontiguousarray(w_in)})
    res = run_spmd(nc, in_maps)
    pf = np.zeros((TT, DPROJ), np.float32)
    for i in range(NCORES):
        o = res[i]["pxT"].T
        pf[32 * i:32 * i + 32] = o[:32]
        pf[CTX + 1024 * i:CTX + 1024 * i + 1024] = o[32:]
    return pf


def layer_forward(l, xfull, mod_x, mod_c, P):
    pf = phase_a_run(l, xfull, mod_x, mod_c, P["w_in"][l])
    u = pf[:, 0:512]
    naq, nak, nav = pf[:, 512:1024], pf[:, 1024:1536], pf[:, 1536:2048]
    z = pf[:, 2048:2560]
    xbc = pf[:, 2560:3584]
    dtc = pf[:, 3584:3600]
    gq, gk, gv, gr, gg = pf[:, 3600:3856], pf[:, 3856:4112], pf[:, 4112:4624], pf[:, 4624:5136], pf[:, 5136:5168]
    s5f, s5b = s5_gather(run_spmd(_prog("S5", build_s5), s5_inputs(l, u, P)))
    nao = na_gather(run_spmd(_prog("NA", build_na), na_inputs(l, naq, nak, nav, P)))
    ssf, ssb = ssd_gather(run_spmd(_prog("SSD", build_ssd), ssd_inputs(l, xbc, dtc, P)))
    glf, glb = gla_gather(run_spmd(_prog("GLA", build_gla), gla_inputs(l, gq, gk, gv, gg, P)))
    mixfull = np.concatenate([s5f, s5b, u, nao, ssf, ssb, z, glf, glb, gr], 1)
    res = run_spmd(_prog("C", build_phase_c), phase_c_inputs(l, xfull, mixfull, None, mod_x, mod_c, P))
    return phase_c_gather(res, "x2T"), res


def kernel(**inputs):
    P = {k: np.asarray(v, np.float32) for k, v in inputs.items()}
    mod_x, mod_c = run_mod(P["c"], P["c_ctx"], P["w_mod"], P["b_mod"])
    xfull = np.concatenate([P["ctx"][0], P["x"][0]], 0)
    for l in range(DEPTH):
        xfull, _ = layer_forward(l, xfull, mod_x[l], mod_c[l], P)
    return np.ascontiguousarray(xfull[CTX:][None]).astype(np.float32)


from concourse.bass import IndirectOffsetOnAxis
U32 = mybir.dt.uint32
NTC = 1072
NCXF = 40
SROWS = 4144
BROWS = 448
PW = 270
PWIN = (0, 268, 536, 802)
NGT = 13


def rv(tile_ap, a, b, rev):
    if not rev:
        return tile_ap[:, a:b]
    if a == 0:
        return tile_ap[:, b - 1::-1]
    return tile_ap[:, b - 1:a - 1:-1]


def build_fused(nlayers=DEPTH, dbg=False):
    nc = bass.Bass("TRN2", target_bir_lowering=False)
    L = nlayers
    EI = lambda name, shape, dt=F32: nc.dram_tensor(name, list(shape), dt, kind="ExternalInput").ap()
    x0_d = EI("x0T", [D, NTC])
    cmk_d = EI("colmask", [128, 16])
    idxA_d = EI("idxA", [128, NGT * 8], U32)
    idxB_d = EI("idxB", [128, 28 * 4], U32)
    wmod_d = EI("wmod", [D, 6144]); bmod_d = EI("bmod", [128, 48]); cv_d = EI("cv", [128, KC, 2])
    win_d = EI("w_in", [L, D, DPROJ]); wout_d = EI("w_out", [L, D, D])
    up_d = EI("ffn_up", [L, D, 2 * DFF]); dn_d = EI("ffn_down", [L, DFF, D]); glu_d = EI("glu_w", [L, 512, 512])
    vec_d = EI("vecs", [L, 128, NV_C])
    lp_d = EI("lanep", [L, 2, 2, 128, 3]); bre_d = EI("bre", [L, 2, 2, 128, 16]); bim_d = EI("bim", [L, 2, 2, 128, 16])
    cre_d = EI("cre", [L, 2, 2, 128, 64]); cim_d = EI("cim", [L, 2, 2, 128, 64])
    nab_d = EI("nabias", [L, 5, 128, 576])
    scw_d = EI("ssd_cw", [L, 128, 3, 4]); ssc_d = EI("ssd_scal", [L, 2, 128, 4])
    ggw_d = EI("gla_gw", [L, 2, 16, 64]); ggb_d = EI("gla_gb", [L, 2, 64, 1]); gcs_d = EI("gla_cs", [2, 64, TT])
    tau_d = EI("tau1", [128, 128]); id_d = EI("ident", [128, 128]); dtsel_d = EI("dtsel", [16, 2])
    tri_d = EI("tri2", [2, 128, 128]); nm_d = EI("negmask2", [2, 128, 128])
    rot_d = EI("rot", [64, 64]); gcm_d = EI("cmask", [64, TT]); um_d = EI("umask2", [2, 64, 64])
    out_d = nc.dram_tensor("outT", [D, 1024], F32, kind="ExternalOutput").ap()
    IT = lambda name, shape: nc.dram_tensor(name, list(shape), F32).ap()
    xres = [IT("xres0", [D, NTC]), IT("xres1", [D, NTC])]
    pxloc = IT("pxloc", [1536, NTC])
    sA_lat = IT("sA_lat", [SROWS, 1024]); sA_ctx = IT("sA_ctx", [SROWS, 32])
    gA_lat = IT("gA_lat", [8 * SROWS, 1024]); gA_ctx = IT("gA_ctx", [8 * SROWS, 32])
    sB = IT("sB", [BROWS * 32, PW]); gB = IT("gB", [8 * BROWS * 32, PW])
    msend = IT("msend", [128, 96]); mg = IT("mg", [8 * 128, 96])
    dtscr = IT("dtscr", [2, TT])
    dbg_o = {}
    if dbg:
        dbg_o["px_send"] = nc.dram_tensor("dbg_sA", [SROWS, 1024], F32, kind="ExternalOutput").ap()
        dbg_o["sB"] = nc.dram_tensor("dbg_sB", [BROWS * 32, PW], F32, kind="ExternalOutput").ap()
        dbg_o["x1"] = nc.dram_tensor("dbg_x", [D, NTC], F32, kind="ExternalOutput").ap()
    rg = [list(range(NCORES))]
    p = Prog(nc)
    modsb = p.sb([128, 8, 96])
    idxA = p.sb([128, NGT * 8], U32); idxB = p.sb([128, 28 * 4], U32)
    cmk = p.sb([128, 16]); ident = p.sb([128, 128])
    modL = p.sb([128, 12, KC])
    p.load(idxA[:], idxA_d, w=["idxA"]); p.load(idxB[:], idxB_d, w=["idxB"])
    p.load(cmk[:], cmk_d, w=["cmk"]); p.load(ident[:], id_d, w=["ident"])

    def allgather(src, dst, rk, wk):
        p.coll(lambda eng: eng.collective_compute("AllGather", ALU.bypass, replica_groups=rg, ins=[src.opt()], outs=[dst.opt()]),
               r=[rk], w=[wk])

    def gatherA(out_lat, out_ctx, npart, t, r, wkeys):
        col = t * 8 + r
        p.dma(lambda eng: eng.indirect_dma_start(out=out_lat, out_offset=None, in_=gA_lat,
                                                 in_offset=IndirectOffsetOnAxis(idxA[0:npart, col:col + 1], 0)),
              r=["gA", "idxA"], w=wkeys, q="pool")
        p.dma(lambda eng: eng.indirect_dma_start(out=out_ctx, out_offset=None, in_=gA_ctx,
                                                 in_offset=IndirectOffsetOnAxis(idxA[0:npart, col:col + 1], 0)),
              r=["gA", "idxA"], w=wkeys, q="pool")

    def gather_seq(tile, npart, t, key, rr=False):
        for r in range(8):
            ol = tile[0:npart, CTX + 1024 * r:CTX + 1024 * r + 1024]
            oc = tile[0:npart, 32 * r:32 * r + 32]
            gatherA(R(ol) if rr else ol, R(oc) if rr else oc, npart, t, r, [key])

    sBv = sB.rearrange("(q b) c -> q b c", b=32)

    def send_rows(tile, rowbase, key, nat0=0, nlen=TT):
        for j in range(8):
            for s in range(4):
                w0 = PWIN[s]
                pieces = []
                ca, cb = max(w0, 0), min(w0 + PW, NCXF)
                if ca < cb:
                    t0 = 32 * j - 4 + ca
                    t1 = 32 * j - 4 + cb
                    lo, hi = max(t0, 0), min(t1, CTX)
                    if lo < hi:
                        pieces.append((ca - w0 + (lo - t0), lo, hi - lo))
                la, lb = max(w0, NCXF), min(w0 + PW, NTC)
                if la < lb:
                    t0 = 1024 * j - 4 + (la - NCXF)
                    t1 = 1024 * j - 4 + (lb - NCXF)
                    lo, hi = max(t0, 0), min(t1, SEQ)
                    if lo < hi:
                        pieces.append((la - w0 + (lo - t0), CTX + lo, hi - lo))
                for (off, nat, ln) in pieces:
                    lo, hi = max(nat, nat0), min(nat + ln, nat0 + nlen)
                    if lo >= hi:
                        continue
                    p.store(sBv[rowbase:rowbase + 64, j * 4 + s, off + (lo - nat):off + (hi - nat)],
                            tile[0:64, lo - nat0:hi - nat0], r=[key], w=["sB"])

    with p.scope():
        z = p.sb([128, 14 * PW])
        p.memset(z[:], 0.0, w=["z"])
        sBz = sB.rearrange("(a p b) c -> a p (b c)", p=128, b=14)
        for a in range(8):
            p.store(sBz[a], z[:], r=["z"], w=["sB"])
    with p.scope():
        cv0 = p.sb([128, KC, 2]); cv = p.sb([128, KC, 2]); bb = p.sb([128, 48]); ob = p.sb([128, 48, 2])
        wb = [p.sb([128, KC, 512]) for _ in range(2)]
        pss = [p.ps([128, 512]) for _ in range(2)]
        p.load(cv0[:], cv_d, w=["cv0"]); p.load(bb[:], bmod_d, w=["b"])
        p.act(R(cv[:]), cv0[:], AF.Silu, r=["cv0"], w=["cv"])
        wv = wmod_d.rearrange("(k p) m -> p k m", p=128)
        for g in range(12):
            wt = wb[g % 2]
            for k in range(KC):
                p.load(R(wt[:, k, :]), R(wv[:, k, g * 512:(g + 1) * 512]), w=[("w", g % 2, k)])
            for mm_ in range(4):
                j = g * 4 + mm_
                ps = pss[j % 2]
                for k in range(KC):
                    p.mm(ps[:, 0:2], R(wt[:, k, mm_ * 128:(mm_ + 1) * 128]), R(cv[:, k, :]), k == 0, k == KC - 1,
                         r=[("w", g % 2, k), "cv"], w=[("ps", j % 2)])
                p.ts(ob[:, j, :], ps[:, 0:2], bb[:, j:j + 1], None, ALU.add, None, r=[("ps", j % 2), "b"], w=["ob"])
        p.store(msend, ob[:].rearrange("p j t -> p (j t)"), r=["ob"], w=["msend"])
        allgather(msend, mg, "msend", "mg")
        p.load(modsb[:], mg.rearrange("(r p) n -> p r n", p=128), r=["mg"], w=["modsb"])
    p.dma(lambda eng: eng.dma_start(out=xres[0], in_=x0_d), r=[], w=["xres0"])
    p.barrier()

    for l in range(nlayers):
        xin, xout = xres[l % 2], xres[(l + 1) % 2]
        for v in range(6):
            for wh in range(2):
                src = modsb[:, 2 * l + v // 3, (16 * (v % 3)) * 2 + wh:(16 * (v % 3) + 16) * 2:2]
                if v in (1, 4):
                    p.ts(modL[:, 2 * v + wh, :], src, 1.0, None, ALU.add, None, r=["modsb"], w=["modL"])
                else:
                    p.copy(modL[:, 2 * v + wh, :], src, r=["modsb"], w=["modL"])
        MC = lambda v, wh, k: modL[:, 2 * v + wh, k:k + 1]
        with p.scope():
            NT = NTC
            xT = p.sb([128, KC, NT]); hT = xT
            onesN = p.sb([128, 128]); sq = p.sb([128, 512]); mean = p.sb([128, NT]); rstd = p.sb([128, NT])
            GW = 512
            wbuf = [p.sb([128, KC, GW]) for _ in range(2)]
            obuf = [p.sb([128, NT]) for _ in range(2)]
            ps_m = p.ps([128, 512]); ps_q = p.ps([128, 512]); ps_o = [p.ps([128, 512]) for _ in range(4)]
            tiles = [(0, 268), (268, 268), (536, 268), (804, 268)]
            p.memset(onesN[:], 1.0 / D, w=["ones"])
            xv = xin.rearrange("(k p) n -> p k n", p=128)
            for k in range(KC):
                p.load(R(xT[:, k, :]), R(xv[:, k, :]), r=[f"xres{l % 2}"], w=[("A", "x", k)])
            ln_stats(p, xT, NT, tiles, onesN, sq, ps_m, ps_q, mean, rstd, "A")
            for k in range(KC):
                p.tt(R(hT[:, k, :]), xT[:, k, :], mean[:], ALU.subtract, r=[("A", "x", k), ("A", "mean")],
                     w=[("h", k), ("A", "x", k)], e="pool")
                p.tt(R(hT[:, k, :]), hT[:, k, :], rstd[:], ALU.mult, r=[("h", k), ("A", "rstd")], w=[("h", k)])
                p.act(R(hT[:, k, 0:NCXF]), hT[:, k, 0:NCXF], AF.Identity, r=[("h", k), "modL"], w=[("h", k)],
                      scale=MC(1, 1, k), bias=MC(0, 1, k))
                p.act(R(hT[:, k, NCXF:NT]), hT[:, k, NCXF:NT], AF.Identity, r=[("h", k), "modL"], w=[("h", k)],
                      scale=MC(1, 0, k), bias=MC(0, 0, k))
            wv = win_d[l].rearrange("(k p) m -> p k m", p=128)
            ngrp = (DPROJ + GW - 1) // GW
            oi = 0
            segs = [(0, 512, "su", 0), (512, 2048, "s", 0), (2048, 2560, "l", 512 - 2048), (2560, 4624, "s", -512),
                    (4624, 5136, "l", 1024 - 4624), (5136, 5168, "s", -1024)]
            for g in range(ngrp):
                g0 = g * GW
                gw_ = min(GW, DPROJ - g0)
                wbt = wbuf[g % 2]
                for k in range(KC):
                    p.load(R(wbt[:, k, 0:gw_]), R(wv[:, k, g0:g0 + gw_]), w=[("w", g % 2, k)])
                for m0 in range(0, gw_, 128):
                    mw = min(128, gw_ - m0)
                    obt = obuf[oi % 2]
                    for ti, (c0, cn) in enumerate(tiles):
                        ps = ps_o[ti % 4]
                        for k in range(KC):
                            p.mm(ps[0:mw, 0:cn], R(wbt[:, k, m0:m0 + mw]), R(hT[:, k, c0:c0 + cn]), k == 0, k == KC - 1,
                                 r=[("w", g % 2, k), ("h", k)], w=[("pso", ti % 4)])
                        p.copy(obt[0:mw, c0:c0 + cn], ps[0:mw, 0:cn], r=[("pso", ti % 4)], w=[("ob", oi % 2)],
                               e="act" if ti % 2 == 0 else "dve")
                    ca0 = g0 + m0
                    for (a, b, kind, sh) in segs:
                        lo, hi = max(a, ca0), min(b, ca0 + mw)
                        if lo >= hi:
                            continue
                        pr = slice(lo - ca0, hi - ca0)
                        if kind in ("s", "su"):
                            ro = lo + (sh if kind == "s" else 0)
                            p.store(sA_lat[ro:ro + hi - lo, :], obt[pr, 44:1068], r=[("ob", oi % 2)], w=["sA"])
                            p.store(sA_ctx[ro:ro + hi - lo, :], obt[pr, 4:36], r=[("ob", oi % 2)], w=["sA"])
                        if kind in ("l", "su"):
                            ro = lo + (sh if kind == "l" else 0)
                            p.store(pxloc[ro:ro + hi - lo, :], obt[pr, :], r=[("ob", oi % 2)], w=["pxloc"])
                    oi += 1
        allgather(sA_lat, gA_lat, "sA", "gA")
        allgather(sA_ctx, gA_ctx, "sA", "gA")
        if dbg and l == 0:
            p.dma(lambda eng: eng.dma_start(out=dbg_o["px_send"], in_=sA_lat), r=["sA"], w=["dbgA"])
        p.barrier()
        import os as _os
        _skip = _os.environ.get("FUSED_SKIP", "").split(",")
        with p.scope():
          if "S5" not in _skip:
            BL = 512
            u = p.sb([64, TT]); yo = [p.sb([64, TT]) for _ in range(2)]
            tau = p.sb([128, 128]); lp = p.sb([128, 3]); sc = p.sb([128, 16])
            braw = p.sb([128, 2, 16]); bbf = p.sb([128, 2, 64])
            bbT = [[p.sb([64, 2, 128]) for _ in range(2)] for _ in range(2)]
            cc = [[p.sb([128, 2, 64]) for _ in range(2)] for _ in range(2)]
            cosT = [[p.sb([128, BL]) for _ in range(2)] for _ in range(2)]
            sinT = [[p.sb([128, BL]) for _ in range(2)] for _ in range(2)]
            rhoT = [[p.sb([128, 128]) for _ in range(2)] for _ in range(2)]
            cq = [[p.sb([128, 2]) for _ in range(2)] for _ in range(2)]
            tmp = [p.sb([128, 128]) for _ in range(3)]
            tki = p.sb([128, 128], I32); ang = p.sb([128, 128])
            b_re = p.sb([128, BL]); b_im = p.sb([128, BL]); v_re = p.sb([128, BL]); v_im = p.sb([128, BL])
            m1 = p.sb([128, BL]); m2 = p.sb([128, BL]); w_re = p.sb([128, BL]); w_im = p.sb([128, BL])
            s_re = p.sb([128, BL]); s_im = p.sb([128, BL])
            car = [p.sb([128, 2]) for _ in range(2)]; ct = p.sb([128, 2])
            psb = [p.ps([128, 512]) for _ in range(4)]; psy = [p.ps([128, 512]) for _ in range(2)]; pst = p.ps([128, 512])
            p.load(tau[:], tau_d, w=["tau"])
            gather_seq(u, 64, 0, "u", rr=True)
            for d in range(2):
                for t in range(2):
                    tg = f"s{d}{t}"
                    p.load(lp[:], lp_d[l, d, t], w=["lp"])
                    p.load(braw[:, 0, :], bre_d[l, d, t], w=["braw"]); p.load(braw[:, 1, :], bim_d[l, d, t], w=["braw"])
                    p.load(R(cc[d][t][:, 0, :]), R(cre_d[l, d, t]), w=[("cc", d, t)])
                    p.load(R(cc[d][t][:, 1, :]), R(cim_d[l, d, t]), w=[("cc", d, t)])
                    p.act(sc[:, 0:1], lp[:, 2:3], AF.Exp, r=["lp"], w=["sc"])
                    p.tt(sc[:, 1:2], lp[:, 0:1], sc[:, 0:1], ALU.mult, r=["lp", "sc"], w=["sc"])
                    p.act(sc[:, 2:3], sc[:, 1:2], AF.Exp, r=["sc"], w=["sc"])
                    p.tt(sc[:, 3:4], lp[:, 1:2], sc[:, 0:1], ALU.mult, r=["lp", "sc"], w=["sc"])
                    p.ts(ang[:], tau[:], sc[:, 3:4], None, ALU.mult, None, r=["tau", "sc"], w=[tg + "x"])
                    trig(p, ang[:], 128, [tmp[0][:], tmp[1][:], tmp[2][:]], tki[:], cosT[d][t][:, 0:128], sinT[d][t][:, 0:128], tg)
                    for rep in range(1, 4):
                        p.copy(cosT[d][t][:, rep * 128:(rep + 1) * 128], cosT[d][t][:, 0:128], r=[tg + "cos"], w=[tg + "cos"])
                        p.copy(sinT[d][t][:, rep * 128:(rep + 1) * 128], sinT[d][t][:, 0:128], r=[tg + "sin"], w=[tg + "sin"])
                    p.copy(cq[d][t][:, 0:1], cosT[d][t][:, 127:128], r=[tg + "cos"], w=[("cq", d, t)])
                    p.copy(cq[d][t][:, 1:2], sinT[d][t][:, 127:128], r=[tg + "sin"], w=[("cq", d, t)])
                    p.memset(rhoT[d][t][:], 1.0, w=[("rho", d, t)])
                    p.ts(rhoT[d][t][:], rhoT[d][t][:], sc[:, 2:3], None, ALU.mult, None, r=["sc", ("rho", d, t)], w=[("rho", d, t)])
                    p.tt(sc[:, 4:5], sc[:, 2:3], cosT[d][t][:, 0:1], ALU.mult, r=["sc", tg + "cos"], w=["sc"])
                    p.ts(sc[:, 4:5], sc[:, 4:5], -1.0, None, ALU.add, None, r=["sc"], w=["sc"])
                    p.tt(sc[:, 5:6], sc[:, 2:3], sinT[d][t][:, 0:1], ALU.mult, r=["sc", tg + "sin"], w=["sc"])
                    p.tt(sc[:, 6:7], lp[:, 0:1], lp[:, 0:1], ALU.mult, r=["lp"], w=["sc"])
                    p.tt(sc[:, 9:10], lp[:, 1:2], lp[:, 1:2], ALU.mult, r=["lp"], w=["sc"])
                    p.tt(sc[:, 6:7], sc[:, 6:7], sc[:, 9:10], ALU.add, r=["sc"], w=["sc"])
                    p.op("dve", lambda eng: eng.reciprocal(sc[:, 6:7], sc[:, 6:7]), r=["sc"], w=["sc"])
                    p.tt(sc[:, 7:8], sc[:, 4:5], lp[:, 0:1], ALU.mult, r=["sc", "lp"], w=["sc"])
                    p.tt(sc[:, 9:10], sc[:, 5:6], lp[:, 1:2], ALU.mult, r=["sc", "lp"], w=["sc"])
                    p.tt(sc[:, 7:8], sc[:, 7:8], sc[:, 9:10], ALU.add, r=["sc"], w=["sc"])
                    p.tt(sc[:, 7:8], sc[:, 7:8], sc[:, 6:7], ALU.mult, r=["sc"], w=["sc"])
                    p.tt(sc[:, 8:9], sc[:, 5:6], lp[:, 0:1], ALU.mult, r=["sc", "lp"], w=["sc"])
                    p.tt(sc[:, 9:10], sc[:, 4:5], lp[:, 1:2], ALU.mult, r=["sc", "lp"], w=["sc"])
                    p.tt(sc[:, 8:9], sc[:, 8:9], sc[:, 9:10], ALU.subtract, r=["sc"], w=["sc"])
                    p.tt(sc[:, 8:9], sc[:, 8:9], sc[:, 6:7], ALU.mult, r=["sc"], w=["sc"])
                    p.memset(bbf[:], 0.0, w=["bbf"])
                    for half in range(2):
                        rows = slice(64 * half, 64 * half + 64)
                        co = 16 * (2 * t + half)
                        p.ts(bbf[rows, 0, co:co + 16], braw[rows, 1, :], sc[rows, 8:9], -1.0, ALU.mult, ALU.mult,
                             r=["braw", "sc"], w=["bbf"])
                        p.stt(bbf[rows, 0, co:co + 16], braw[rows, 0, :], sc[rows, 7:8], bbf[rows, 0, co:co + 16], ALU.mult, ALU.add,
                              r=["braw", "sc", "bbf"], w=["bbf"])
                        p.ts(bbf[rows, 1, co:co + 16], braw[rows, 0, :], sc[rows, 8:9], None, ALU.mult, None,
                             r=["braw", "sc"], w=["bbf"])
                        p.stt(bbf[rows, 1, co:co + 16], braw[rows, 1, :], sc[rows, 7:8], bbf[rows, 1, co:co + 16], ALU.mult, ALU.add,
                              r=["braw", "sc", "bbf"], w=["bbf"])
                    for c2 in range(2):
                        p.op("pe", lambda eng, o=pst[0:64, c2 * 128:(c2 + 1) * 128], i_=bbf[:, c2, :]: eng.transpose(o, i_, ident[:]),
                             r=["bbf", "ident"], w=["pst"])
                    p.copy(R(bbT[d][t][:, :, :]), pst[0:64, 0:256].rearrange("p (a m) -> p a m", a=2), r=["pst"], w=[("bbT", d, t)], e="act")
                    p.ts(R(cc[d][t][:, 1, :]), cc[d][t][:, 1, :], -1.0, None, ALU.mult, None, r=[("cc", d, t)], w=[("cc", d, t)])
            blocks = [(0, 256)] + [(CTX + BL * k, BL) for k in range(16)]
            for d in range(2):
                rev = d == 1
                order = blocks if not rev else [blocks[0]] + blocks[:0:-1]
                for t in range(2):
                    p.memset(car[t][:], 0.0, w=[("car", t)])
                for bi, (c0, cn) in enumerate(order):
                    py = psy[bi % 2]
                    pyk = ("psy", bi % 2)
                    for t in range(2):
                        pr, pim = psb[2 * t], psb[2 * t + 1]
                        p.mm(pr[:, 0:cn], R(bbT[d][t][:, 0, :]), R(u[:, c0:c0 + cn]), True, True, r=[("bbT", d, t), "u"], w=[("psb", 2 * t)])
                        p.mm(pim[:, 0:cn], R(bbT[d][t][:, 1, :]), R(u[:, c0:c0 + cn]), True, True, r=[("bbT", d, t), "u"], w=[("psb", 2 * t + 1)])
                        p.copy(b_re[:, 0:cn], pr[:, 0:cn], r=[("psb", 2 * t)], w=["b_re"], e="act")
                        p.copy(b_im[:, 0:cn], pim[:, 0:cn], r=[("psb", 2 * t + 1)], w=["b_im"], e="act")
                        Cv = rv(cosT[d][t][:], 0, cn, rev); Sv = rv(sinT[d][t][:], 0, cn, rev)
                        tgc, tgs = f"s{d}{t}cos", f"s{d}{t}sin"
                        e1 = "dve" if rev else "pool"
                        p.tt(m1[:, 0:cn], b_re[:, 0:cn], Cv, ALU.mult, r=["b_re", tgc], w=["m1"], e=e1)
                        p.tt(m2[:, 0:cn], b_im[:, 0:cn], Sv, ALU.mult, r=["b_im", tgs], w=["m2"], e=e1)
                        p.tt(v_re[:, 0:cn], m1[:, 0:cn], m2[:, 0:cn], ALU.add, r=["m1", "m2"], w=["v_re"], e=e1)
                        p.tt(m1[:, 0:cn], b_im[:, 0:cn], Cv, ALU.mult, r=["b_im", tgc], w=["m1"], e=e1)
                        p.tt(m2[:, 0:cn], b_re[:, 0:cn], Sv, ALU.mult, r=["b_re", tgs], w=["m2"], e=e1)
                        p.tt(v_im[:, 0:cn], m1[:, 0:cn], m2[:, 0:cn], ALU.subtract, r=["m1", "m2"], w=["v_im"], e=e1)
                        qs = list(range(0, cn, 128))
                        if rev:
                            qs = qs[::-1]
                        for q0 in qs:
                            p.op("dve", lambda eng, o=rv(w_re[:], q0, q0 + 128, rev), a=rhoT[d][t][:], b=rv(v_re[:], q0, q0 + 128, rev),
                                 i_=car[t][:, 0:1]: eng.tensor_tensor_scan(o, a, b, i_, ALU.mult, ALU.add),
                                 r=[("rho", d, t), "v_re", ("car", t)], w=["w_re"])
                            p.op("dve", lambda eng, o=rv(w_im[:], q0, q0 + 128, rev), a=rhoT[d][t][:], b=rv(v_im[:], q0, q0 + 128, rev),
                                 i_=car[t][:, 1:2]: eng.tensor_tensor_scan(o, a, b, i_, ALU.mult, ALU.add),
                                 r=[("rho", d, t), "v_im", ("car", t)], w=["w_im"])
                            last = q0 if rev else q0 + 127
                            cqt = cq[d][t]
                            p.ts(ct[:, 0:1], w_im[:, last:last + 1], cqt[:, 1:2], None, ALU.mult, None, r=["w_im", ("cq", d, t)], w=["ct"])
                            p.ts(ct[:, 1:2], w_re[:, last:last + 1], cqt[:, 1:2], None, ALU.mult, None, r=["w_re", ("cq", d, t)], w=["ct"])
                            p.stt(car[t][:, 0:1], w_re[:, last:last + 1], cqt[:, 0:1], ct[:, 0:1], ALU.mult, ALU.subtract,
                                  r=["w_re", ("cq", d, t), "ct"], w=[("car", t)])
                            p.stt(car[t][:, 1:2], w_im[:, last:last + 1], cqt[:, 0:1], ct[:, 1:2], ALU.mult, ALU.add,
                                  r=["w_im", ("cq", d, t), "ct"], w=[("car", t)])
                        p.tt(m1[:, 0:cn], w_re[:, 0:cn], Cv, ALU.mult, r=["w_re", tgc], w=["m1"], e=e1)
                        p.tt(m2[:, 0:cn], w_im[:, 0:cn], Sv, ALU.mult, r=["w_im", tgs], w=["m2"], e=e1)
                        p.tt(R(s_re[:, 0:cn]), m1[:, 0:cn], m2[:, 0:cn], ALU.subtract, r=["m1", "m2"], w=["s_re"])
                        p.tt(m1[:, 0:cn], w_im[:, 0:cn], Cv, ALU.mult, r=["w_im", tgc], w=["m1"], e=e1)
                        p.tt(m2[:, 0:cn], w_re[:, 0:cn], Sv, ALU.mult, r=["w_re", tgs], w=["m2"], e=e1)
                        p.tt(R(s_im[:, 0:cn]), m1[:, 0:cn], m2[:, 0:cn], ALU.add, r=["m1", "m2"], w=["s_im"])
                        p.mm(py[0:64, 0:cn], R(cc[d][t][:, 0, :]), R(s_re[:, 0:cn]), t == 0, False, r=[("cc", d, t), "s_re"], w=[pyk])
                        p.mm(py[0:64, 0:cn], R(cc[d][t][:, 1, :]), R(s_im[:, 0:cn]), False, t == 1, r=[("cc", d, t), "s_im"], w=[pyk])
                    p.copy(yo[d][:, c0:c0 + cn], py[0:64, 0:cn], r=[pyk], w=[("yo", d)], e="act")
            send_rows(yo[0], 0, ("yo", 0))
            send_rows(yo[1], 64, ("yo", 1))
        with p.scope():
          if "NA" not in _skip:
            qT = p.sb([64, TT]); kT = p.sb([64, TT]); vt = p.sb([64, 132, 64]); oT = p.sb([64, TT])
            bias = p.sb([128, 5, 576])
            S = p.sb([128, 832]); Pm = p.sb([128, 832]); PTs = p.sb([64, 13, 128]); dg = p.sb([128, 128]); st = p.sb([128, 4])
            psA = p.ps([128, 512]); psB = p.ps([128, 512]); psC = p.ps([128, 512])
            psT = [p.ps([128, 512]) for _ in range(4)]; pso = p.ps([128, 512])
            gather_seq(qT, 64, 1, "q", rr=True)
            gather_seq(kT, 64, 2, "k", rr=True)
            gather_seq(oT, 64, 3, "oT")
            for c in range(5):
                p.load(bias[:, c, :], nab_d[l, c], w=["bias"])
            for r0 in range(0, 132, 8):
                n8 = min(8, 132 - r0)
                bank = psT[(r0 // 8) % 4]
                bk = ("psT", (r0 // 8) % 4)
                for a in range(n8):
                    r_ = r0 + a
                    p.op("pe", lambda eng, o=bank[0:64, a * 64:(a + 1) * 64], i_=oT[:, 64 * r_:64 * r_ + 64]:
                         eng.transpose(o, i_, ident[0:64, 0:64]), r=["oT", "ident"], w=[bk])
                p.copy(R(vt[:, r0:r0 + n8, :]), bank[0:64, 0:n8 * 64].rearrange("p (a m) -> p a m", a=n8), r=[bk], w=["v"],
                       e="act" if (r0 // 8) % 2 == 0 else "dve")

            def block(qc0, lat, b):
                nk = 832 if lat else 256
                if lat:
                    kr0 = na_kr0(b)
                    kc0 = CTX + 64 * kr0
                    cls = na_cls(b)
                    p.mm(psA[:, 0:288], R(qT[:, qc0:qc0 + 128]), R(kT[:, kc0:kc0 + 288]), True, True, r=["q", "k"], w=["psA"])
                    p.mm(psB[:, 0:288], R(qT[:, qc0:qc0 + 128]), R(kT[:, kc0 + 288:kc0 + 576]), True, True, r=["q", "k"], w=["psB"])
                    p.mm(psC[:, 0:256], R(qT[:, qc0:qc0 + 128]), R(kT[:, 0:256]), True, True, r=["q", "k"], w=["psC"])
                    p.stt(S[:, 0:288], psA[:, 0:288], 0.125, bias[:, cls, 0:288], ALU.mult, ALU.add, r=["psA", "bias"], w=["S"])
                    p.stt(S[:, 288:576], psB[:, 0:288], 0.125, bias[:, cls, 288:576], ALU.mult, ALU.add, r=["psB", "bias"], w=["S"])
                    p.act(S[:, 576:832], psC[:, 0:256], AF.Copy, r=["psC"], w=["S"], scale=0.125)
                else:
                    p.mm(psC[:, 0:256], R(qT[:, qc0:qc0 + 128]), R(kT[:, 0:256]), True, True, r=["q", "k"], w=["psC"])
                    p.act(S[:, 0:256], psC[:, 0:256], AF.Copy, r=["psC"], w=["S"], scale=0.125)
                p.op("dve", lambda eng: eng.reduce_max(st[:, 0:1], S[:, 0:nk], AX.X), r=["S"], w=["st"])
                p.ts(st[:, 1:2], st[:, 0:1], -1.0, None, ALU.mult, None, r=["st"], w=["st"])
                p.act(R(Pm[:, 0:nk]), S[:, 0:nk], AF.Exp, r=["S", "st"], w=["P", "st2"], bias=st[:, 1:2], accum_out=st[:, 2:3])
                p.op("dve", lambda eng: eng.reciprocal(st[:, 3:4], st[:, 2:3]), r=["st2", "P"], w=["st3"])
                p.ts(R(dg[:]), ident[:], st[:, 3:4], None, ALU.mult, None, r=["ident", "st3"], w=["dg"])
                nt = nk // 64
                for kt in range(nt):
                    bank = psT[kt // 4]
                    p.mm(bank[0:64, (kt % 4) * 128:(kt % 4) * 128 + 128], R(Pm[:, kt * 64:(kt + 1) * 64]), R(dg[:]), True, True,
                         r=["P", "dg"], w=[("psT", kt // 4)])
                for bk_ in range((nt + 3) // 4):
                    n4 = min(4, nt - 4 * bk_)
                    p.copy(R(PTs[:, 4 * bk_:4 * bk_ + n4, :]), psT[bk_][0:64, 0:n4 * 128].rearrange("p (a m) -> p a m", a=n4),
                           r=[("psT", bk_)], w=["PTs"], e="act" if bk_ % 2 == 0 else "dve")
                for kt in range(nt):
                    if lat:
                        row = 4 + kr0 + kt if kt < 9 else kt - 9
                    else:
                        row = kt
                    p.mm(pso[0:64, 0:128], R(vt[:, row, :]), R(PTs[:, kt, :]), kt == 0, kt == nt - 1, r=["v", "PTs"], w=["pso"])
                p.copy(oT[:, qc0:qc0 + 128], pso[0:64, 0:128], r=["pso"], w=["oT"], e="act")

            for cb in range(2):
                block(128 * cb, False, cb)
            for b in range(64):
                block(CTX + 128 * b, True, b)
            send_rows(oT, 128, "oT")
        with p.scope():
            raw = p.sb([128, TT])
            cv_ = [p.sb([128, TT]) for _ in range(3)]
            cw = p.sb([128, 3, 4]); scal = p.sb([128, 4])
            tri = p.sb([128, 128]); nm = p.sb([128, 128]); ones = p.sb([128, 128]); zt = p.sb([128, 64])
            dtsel = p.sb([16, 2]); dtall = p.sb([128, NCH, 2])
            dt = p.sb([128, NCH]); dta = p.sb([128, NCH]); tA = p.sb([128, NCH]); tB = p.sb([128, NCH])
            nacs = p.sb([128, NCH]); wdec = p.sb([128, NCH]); dec = p.sb([128, NCH])
            yT = raw[0:64, :]
            xdt = p.sb([128, 64]); Bw = p.sb([128, 128]); dtab = p.sb([128, 128])
            E = p.sb([128, 128]); CE = p.sb([128, 128]); Rm = p.sb([128, 128]); LT = p.sb([128, 128]); MT = p.sb([128, 128])
            hT = p.sb([128, 64])
            ps_x = p.ps([128, 512]); ps_B = p.ps([128, 512]); ps_R = p.ps([128, 512]); ps_CB = p.ps([128, 512])
            ps_y = p.ps([128, 512]); ps_h = p.ps([128, 512]); ps_s = p.ps([128, 512])
            p.memset(ones[:], 1.0, w=["ones"]); p.memset(zt[:], 0.0, w=["zt"])
            p.load(cw[:], scw_d[l], w=["cw"])
            dt16 = raw[0:16, :]
            for r in range(8):
                p.load(dt16[:, CTX + 1024 * r:CTX + 1024 * r + 1024], gA_lat[r * SROWS + 3072:r * SROWS + 3088, :], r=["gA"], w=["raw"])
                p.load(dt16[:, 32 * r:32 * r + 32], gA_ctx[r * SROWS + 3072:r * SROWS + 3088, :], r=["gA"], w=["raw"])
            p.load(dtsel[:], dtsel_d, w=["dtsel"])
            for c in range(NCH):
                p.mm(ps_s[:, 2 * c:2 * c + 2], dt16[:, 128 * c:128 * c + 128], dtsel[:], True, True, r=["raw", "dtsel"], w=["ps_s"])
            p.copy(dtall[:], ps_s[:, 0:2 * NCH].rearrange("p (c d) -> p c d", d=2), r=["ps_s"], w=["dtall"])
            p.barrier()
            segs_ = [(0, CTX), (CTX, TT)]
            for ch in range(3):
                np_ = 64 if ch == 0 else 128
                gather_seq(raw, np_, 4 + ch, "raw")
                o = cv_[ch]
                for (a, b) in segs_:
                    p.ts(R(o[0:np_, a:b]), raw[0:np_, a:b], cw[0:np_, ch, 1:2], cw[0:np_, ch, 3:4], ALU.mult, ALU.add,
                         r=["raw", "cw"], w=[("cv", ch)])
                    p.stt(R(o[0:np_, a + 1:b]), raw[0:np_, a:b - 1], cw[0:np_, ch, 0:1], o[0:np_, a + 1:b], ALU.mult, ALU.add,
                          r=["raw", "cw", ("cv", ch)], w=[("cv", ch)])
                    p.stt(R(o[0:np_, a:b - 1]), raw[0:np_, a + 1:b], cw[0:np_, ch, 2:3], o[0:np_, a:b - 1], ALU.mult, ALU.add,
                          r=["raw", "cw", ("cv", ch)], w=[("cv", ch)])
                p.act(R(o[0:np_, :]), o[0:np_, :], AF.Silu, r=[("cv", ch)], w=[("cv", ch)])
            xs, Bm, Cm = cv_
            p.barrier()
            for d in range(2):
                rev = d == 1
                p.load(scal[:], ssc_d[l, d], w=["scal"])
                p.load(tri[:], tri_d[d], w=["tri"]); p.load(nm[:], nm_d[d], w=["nm"])
                p.ts(dt[:], dtall[:, :, d], scal[:, 0:1], None, ALU.add, None, r=["dtall", "scal"], w=["dt"])
                p.ts(tA[:], dt[:], 0.0, None, ALU.max, None, r=["dt"], w=["tA"])
                p.ts(tB[:], dt[:], 0.0, None, ALU.min, None, r=["dt"], w=["tB"])
                p.tt(tB[:], tB[:], tA[:], ALU.subtract, r=["tA", "tB"], w=["tB"])
                p.act(tB[:], tB[:], AF.Exp, r=["tB"], w=["tB"])
                p.act(tB[:], tB[:], AF.Ln, r=["tB"], w=["tB"], bias=1.0)
                p.tt(dt[:], tA[:], tB[:], ALU.add, r=["tA", "tB"], w=["dt"])
                p.act(scal[:, 3:4], scal[:, 1:2], AF.Exp, r=["scal"], w=["scal"])
                p.ts(dta[:], dt[:], scal[:, 3:4], -1.0, ALU.mult, ALU.mult, r=["dt", "scal"], w=["dta"])
                p.mm(ps_s[:, 0:NCH], tri[:], dta[:], True, True, r=["tri", "dta", "dt"], w=["ps_s"])
                p.ts(nacs[:], ps_s[:, 0:NCH], -1.0, None, ALU.mult, None, r=["ps_s"], w=["nacs"])
                p.mm(ps_s[:, 0:NCH], ones[:], dta[:], True, True, r=["ones", "dta", "nacs"], w=["ps_s"])
                p.tt(wdec[:], ps_s[:, 0:NCH], nacs[:], ALU.add, r=["ps_s", "nacs"], w=["wdec"])
                p.act(wdec[:], wdec[:], AF.Exp, r=["wdec"], w=["wdec"])
                p.act(dec[:], ps_s[:, 0:NCH], AF.Exp, r=["ps_s"], w=["dec"])
                p.copy(R(hT[:]), zt[:], r=["zt"], w=["hT"])
                corder = list(range(NCH)) if not rev else [1, 0] + list(range(NCH - 1, 1, -1))
                for c in corder:
                    cols = slice(128 * c, 128 * c + 128)
                    p.op("pe", lambda eng, i_=xs[0:64, cols]: eng.transpose(ps_x[:, 0:64], i_, ident[0:64, 0:64]),
                         r=[("cv", 0), "ident"], w=["ps_x"])
                    p.op("pe", lambda eng, i_=Bm[:, cols]: eng.transpose(ps_B[:, 0:128], i_, ident[:]),
                         r=[("cv", 1), "ident"], w=["ps_B"])
                    p.ts(R(xdt[:]), ps_x[:, 0:64], dt[:, c:c + 1], None, ALU.mult, None, r=["ps_x", "dt"], w=["xdt"])
                    p.ts(R(Bw[:]), ps_B[:, 0:128], wdec[:, c:c + 1], None, ALU.mult, None, r=["ps_B", "wdec"], w=["Bw"])
                    p.ts(dtab[:], ones[:], dta[:, c:c + 1], None, ALU.mult, None, r=["ones", "dta"], w=["dtab"])
                    p.mm(ps_R[:, 0:128], dtab[:], tri[:], True, True, r=["dtab", "tri"], w=["ps_R"])
                    p.act(E[:], ps_R[:, 0:128], AF.Exp, r=["ps_R"], w=["E"])
                    p.tt(R(CE[:]), Cm[:, cols], E[:], ALU.mult, r=[("cv", 2), "E"], w=["CE"])
                    p.tt(Rm[:], ps_R[:, 0:128], nm[:], ALU.add, r=["ps_R", "nm"], w=["Rm"])
                    p.act(LT[:], Rm[:], AF.Exp, r=["Rm", "nacs"], w=["LT"], bias=nacs[:, c:c + 1])
                    p.mm(ps_CB[:, 0:128], R(Bm[:, cols]), R(Cm[:, cols]), True, True, r=[("cv", 1), ("cv", 2)], w=["ps_CB"])
                    p.tt(R(MT[:]), ps_CB[:, 0:128], LT[:], ALU.mult, r=["ps_CB", "LT"], w=["MT"])
                    p.mm(ps_y[0:64, 0:128], R(xdt[:]), R(MT[:]), True, False, r=["MT", "xdt"], w=["ps_y"])
                    p.mm(ps_y[0:64, 0:128], R(hT[:]), R(CE[:]), False, True, r=["CE", "hT"], w=["ps_y"])
                    p.copy(yT[:, cols], ps_y[0:64, 0:128], r=["ps_y"], w=["yT"], e="act")
                    if d == 0:
                        p.stt(yT[:, cols], xs[0:64, cols], scal[0:64, 2:3], yT[:, cols], ALU.mult, ALU.add,
                              r=[("cv", 0), "scal", "yT"], w=["yT"])
                    p.mm(ps_h[:, 0:64], R(Bw[:]), R(xdt[:]), True, True, r=["Bw", "xdt"], w=["ps_h"])
                    p.ts(R(hT[:]), hT[:], dec[:, c:c + 1], None, ALU.mult, None, r=["hT", "dec"], w=["hT"])
                    p.tt(R(hT[:]), hT[:], ps_h[:, 0:64], ALU.add, r=["hT", "ps_h"], w=["hT"])
                send_rows(yT, 192 + 64 * d, "yT")
                p.barrier()
        with p.scope():
            BL = 512
            Q = p.sb([64, TT]); Kt = p.sb([64, TT]); LA = p.sb([64, TT]); Bt = p.sb([64, TT])
            vtm = p.sb([64, NCG, 64])
            vflat = vtm[:, :, :].rearrange("p c v -> p (c v)")
            gw = p.sb([16, 64]); gb = p.sb([64, 1]); ngb = p.sb([64, 1])
            rot = p.sb([64, 64]); um = p.sb([64, 64]); cmt = p.sb([64, 2112])
            obg = [p.sb([64, 512]) for _ in range(2)]
            csb = [p.sb([64, 2, BL]) for _ in range(2)]
            t1 = p.sb([64, BL]); t2 = p.sb([64, BL])
            ebl = p.sb([64, NCG])
            attT = p.sb([64, 64]); ktm = p.sb([64, 64]); S_ = p.sb([64, 64]); zt = p.sb([64, 64])
            qd = p.sb([64, 64]); kd = p.sb([64, 64])
            ps_l = p.ps([128, 512]); ps_r = p.ps([128, 512])
            ps_a = p.ps([128, 512]); ps_t = p.ps([128, 512]); ps_o = p.ps([128, 512]); ps_kv = p.ps([128, 512])
            nblk = (TT + BL - 1) // BL
            p.load(R(rot[:]), R(rot_d), w=["rot"]); p.memset(zt[:], 0.0, w=["zt"])
            p.load(cmt[:], gcm_d[:, 0:2112], w=["cmt"])
            gather_seq(Q, 64, 8, "Q", rr=True)
            gather_seq(Kt, 64, 9, "K", rr=True)
            for bi in range(nblk):
                c0 = bi * BL
                cn = min(BL, TT - c0)
                cb = csb[bi % 2]
                p.load(cb[:, 0, 0:cn], gcs_d[0][:, c0:c0 + cn], w=[("cs", bi % 2)])
                p.load(cb[:, 1, 0:cn], gcs_d[1][:, c0:c0 + cn], w=[("cs", bi % 2)])
                for X, xk in ((Q, "Q"), (Kt, "K")):
                    p.mm(ps_r[0:64, 0:cn], R(rot[:]), R(X[:, c0:c0 + cn]), True, True, r=["rot", xk], w=["ps_r"])
                    p.tt(t1[:, 0:cn], ps_r[0:64, 0:cn], cb[:, 1, 0:cn], ALU.mult, r=["ps_r", ("cs", bi % 2)], w=["t1"])
                    p.tt(t2[:, 0:cn], X[:, c0:c0 + cn], cb[:, 0, 0:cn], ALU.mult, r=[xk, ("cs", bi % 2)], w=["t2"])
                    p.tt(R(X[:, c0:c0 + cn]), t1[:, 0:cn], t2[:, 0:cn], ALU.add, r=["t1", "t2"], w=[xk])
            gather_seq(LA, 64, 10, "LA")
            for r0 in range(0, NCG, 8):
                n8 = min(8, NCG - r0)
                for a in range(n8):
                    r_ = r0 + a
                    p.op("pe", lambda eng, o=ps_t[0:64, a * 64:(a + 1) * 64], i_=LA[:, 64 * r_:64 * r_ + 64]:
                         eng.transpose(o, i_, ident[0:64, 0:64]), r=["LA", "ident"], w=["ps_t"])
                p.copy(R(vtm[:, r0:r0 + n8, :]), ps_t[0:64, 0:n8 * 64].rearrange("p (a m) -> p a m", a=n8), r=["ps_t"], w=["v"],
                       e="act" if (r0 // 8) % 2 == 0 else "dve")
            for d in range(2):
                rev = d == 1
                gather_seq(Bt, 16, 11 + d, "B")
                p.load(gw[:], ggw_d[l, d], w=["gw"]); p.load(gb[:], ggb_d[l, d], w=["gb"])
                p.load(um[:], um_d[d], w=["um"])
                p.ts(ngb[:], gb[:], -1.0, None, ALU.mult, None, r=["gb"], w=["ngb"])
                for bi in range(nblk):
                    c0 = bi * BL
                    cn = min(BL, TT - c0)
                    p.mm(ps_l[0:64, 0:cn], gw[:], Bt[0:16, c0:c0 + cn], True, True, r=["gw", "B"], w=["ps_l"])
                    p.act(t1[:, 0:cn], ps_l[0:64, 0:cn], AF.Exp, r=["ps_l", "ngb"], w=["t1"], scale=-1.0, bias=ngb[:, 0:1])
                    p.act(t1[:, 0:cn], t1[:, 0:cn], AF.Ln, r=["t1"], w=["t1"], bias=1.0)
                    p.ts(LA[:, c0:c0 + cn], t1[:, 0:cn], -1.0 / 16.0, None, ALU.mult, None, r=["t1"], w=["LA"])
                for h0 in range(0, TT, 2112):
                    p.op("dve", lambda eng, o=rv(Bt[:], h0, h0 + 2112, rev), a=rv(cmt[:], 0, 2112, rev),
                         b=rv(LA[:], h0, h0 + 2112, rev): eng.tensor_tensor_scan(o, a, b, 0.0, ALU.mult, ALU.add),
                         r=["cmt", "LA", "B"], w=["B"])
                p.act(LA[:], Bt[:], AF.Exp, r=["B"], w=["LA"])
                p.act(Bt[:], Bt[:], AF.Exp, r=["B"], w=["B"], scale=-1.0, bias=LNK)
                if not rev:
                    p.copy(ebl[:], LA[:, 63:TT:64], r=["LA"], w=["ebl"])
                else:
                    p.copy(ebl[:], LA[:, 0:TT:64], r=["LA"], w=["ebl"])
                p.copy(R(S_[:]), zt[:], r=["zt"], w=["S"])
                if not rev:
                    groups = [list(range(g, min(g + 8, NCG))) for g in range(0, NCG, 8)]
                else:
                    groups = [[3, 2, 1, 0]] + [list(range(g + 7, g - 1, -1)) for g in range(124, 3, -8)]
                for gi_, grp_ in enumerate(groups):
                  gmin = min(grp_)
                  ob = obg[gi_ % 2]
                  obk = ("obg", gi_ % 2)
                  for c in grp_:
                    cols = slice(64 * c, 64 * c + 64)
                    p.tt(R(qd[:]), Q[:, cols], LA[:, cols], ALU.mult, r=["Q", "LA"], w=["qd"])
                    p.tt(R(kd[:]), Kt[:, cols], Bt[:, cols], ALU.mult, r=["K", "B"], w=["kd"])
                    p.mm(ps_a[0:64, 0:64], R(kd[:]), R(qd[:]), True, True, r=["kd", "qd"], w=["ps_a"])
                    p.tt(R(attT[:]), ps_a[0:64, 0:64], um[:], ALU.mult, r=["ps_a", "um"], w=["attT"])
                    p.op("pe", lambda eng: eng.transpose(ps_t[0:64, 0:64], kd[:], ident[0:64, 0:64]),
                         r=["kd", "ident"], w=["ps_t"])
                    p.copy(R(ktm[:]), ps_t[0:64, 0:64], r=["ps_t"], w=["ktm"], e="act")
                    p.mm(ps_o[0:64, 0:64], R(vtm[:, c, :]), R(attT[:]), True, False, r=["attT", "v"], w=["ps_o"])
                    p.mm(ps_o[0:64, 0:64], R(S_[:]), R(qd[:]), False, True, r=["qd", "S"], w=["ps_o"])
                    p.copy(ob[:, (c - gmin) * 64:(c - gmin) * 64 + 64], ps_o[0:64, 0:64], r=["ps_o"], w=[obk], e="act")
                    p.mm(ps_kv[0:64, 0:64], R(ktm[:]), R(vtm[:, c, :]), True, True, r=["ktm", "v"], w=["ps_kv"])
                    p.ts(R(S_[:]), S_[:], ebl[:, c:c + 1], None, ALU.mult, None, r=["S", "ebl"], w=["S"])
                    p.stt(R(S_[:]), ps_kv[0:64, 0:64], ebl[:, c:c + 1], S_[:], ALU.mult, ALU.add, r=["ps_kv", "ebl", "S"], w=["S"])
                  send_rows(ob, 320 + 64 * d, obk, nat0=64 * gmin, nlen=64 * len(grp_))
        allgather(sB, gB, "sB", "gB")
        p.barrier()
        with p.scope():
            P_ = PW
            x = p.sb([128, KC, P_])
            hff = p.sb([128, 43, P_]); mi = hff
            mix = p.sb([128, KC, P_])
            wb = [p.sb([128, 8192]) for _ in range(2)]
            sc1 = None
            vec = p.sb([128, NV_C]); glu = p.sb([128, 4, 512]); onesN = p.sb([128, 128]); sq = p.sb([128, 512])
            mean = p.sb([128, P_]); rstd = p.sb([128, P_])
            t1 = p.sb([128, 4, P_]); t2 = p.sb([128, 4, P_]); gR = p.sb([128, 4, P_])
            ca = p.sb([128, P_]); cg = p.sb([128, P_]); zt = p.sb([128, 43, 1])
            ps_m = p.ps([128, 512]); ps_q = p.ps([128, 512]); psr = [p.ps([128, 512]) for _ in range(6)]
            pi = [0]

            def nps():
                pi[0] = (pi[0] + 1) % 6
                return psr[pi[0]], ("psr", pi[0])

            V_S5D, V_GLUB, V_SSDW, V_GLAW, V_LN, V_CW = 0, 4, 8, 12, 13, 77
            p.memset(onesN[:], 1.0, w=["ones"]); p.memset(zt[:], 0.0, w=["zt"])
            p.load(vec[:], vec_d[l], w=["vec"])
            p.load(R(glu[:]), R(glu_d[l].rearrange("(k p) m -> p k m", p=128)), w=["glu"])
            wi = [0]

            def nwb():
                wi[0] = (wi[0] + 1) % 2
                return wb[wi[0]], ("wb", wi[0])

            def stats():
                for k in range(KC):
                    p.act(sq[:, 0:P_], x[:, k, :], AF.Square, r=[("x", k)], w=["sq"])
                    p.mm(ps_m[:, 0:P_], onesN[:], x[:, k, :], k == 0, k == KC - 1, r=[("x", k), "ones"], w=["ps_m"])
                    p.mm(ps_q[:, 0:P_], onesN[:], sq[:, 0:P_], k == 0, k == KC - 1, r=["sq", "ones"], w=["ps_q"])
                p.ts(mean[:], ps_m[:, 0:P_], 1.0 / D, None, ALU.mult, None, r=["ps_m"], w=["mean"])
                p.tt(rstd[:], mean[:], mean[:], ALU.mult, r=["mean"], w=["rstd"])
                p.stt(rstd[:], ps_q[:, 0:P_], 1.0 / D, rstd[:], ALU.mult, ALU.subtract, r=["ps_q", "rstd"], w=["rstd"])
                p.ts(rstd[:], rstd[:], 1e-6, None, ALU.add, None, r=["rstd"], w=["rstd"])
                p.act(rstd[:], rstd[:], AF.Sqrt, r=["rstd"], w=["rstd"])
                p.op("dve", lambda eng: eng.reciprocal(rstd[:], rstd[:]), r=["rstd"], w=["rstd"])

            def ln_affine(gcol, bcol):
                stats()
                for k in range(KC):
                    p.tt(x[:, k, :], x[:, k, :], mean[:], ALU.subtract, r=[("x", k), "mean"], w=[("x", k)], e="pool")
                    p.tt(x[:, k, :], x[:, k, :], rstd[:], ALU.mult, r=[("x", k), "rstd"], w=[("x", k)])
                    p.act(x[:, k, :], x[:, k, :], AF.Identity, r=[("x", k), "vec"], w=[("x", k)],
                          scale=vec[:, gcol + k:gcol + k + 1], bias=vec[:, bcol + k:bcol + k + 1])

            for s in range(NPASS):
                w0 = PWIN[s]
                ncx = max(0, min(NCXF - w0, P_))

                def residual(ps, pk, m, v):
                    p.ts(x[:, m, :], x[:, m, :], ALPHA, None, ALU.mult, None, r=[("x", m)], w=[("x", m)], e="pool")
                    if ncx > 0:
                        p.stt(x[:, m, 0:ncx], ps[:, 0:ncx], MC(v, 1, m), x[:, m, 0:ncx], ALU.mult, ALU.add,
                              r=[("x", m), "modL", pk], w=[("x", m)])
                    p.stt(x[:, m, ncx:P_], ps[:, ncx:P_], MC(v, 0, m), x[:, m, ncx:P_], ALU.mult, ALU.add,
                          r=[("x", m), "modL", pk], w=[("x", m)])

                xv = xin.rearrange("(k p) n -> p k n", p=128)
                for k in range(KC):
                    p.load(x[:, k, :], xv[:, k, w0:w0 + P_], r=[f"xres{l % 2}"], w=[("x", k)])
                gch = {}
                order = [0, 1, None, 2, 3, 4, None, 5, 6, None]
                gi = 0
                for grp in range(10):
                    for kk in range(4):
                        k = grp * 4 + kk
                        if order[grp] is None:
                            lrow = {2: 0, 6: 512, 9: 1024}[grp] + 128 * kk
                            p.load(R(mi[:, k, :]), R(pxloc[lrow:lrow + 128, w0:w0 + P_]), r=["pxloc"], w=[("mi", grp), "hff"])
                        else:
                            col = (order[grp] * 4 + kk) * 4 + s
                            p.dma(lambda eng, o=R(mi[:, k, :]), col=col: eng.indirect_dma_start(
                                out=o, out_offset=None, in_=R(gB), in_offset=IndirectOffsetOnAxis(idxB[:, col:col + 1], 0)),
                                r=["gB", "idxB"], w=[("mi", grp), "hff"], q="pool")
                p.tt(t1[:], mi[:, 0:4, :], mi[:, 4:8, :], ALU.add, r=[("mi", 0), ("mi", 1)], w=["t1"])
                for k in range(4):
                    p.stt(t1[:, k, :], mi[:, 8 + k, :], vec[:, V_S5D + k:V_S5D + k + 1], t1[:, k, :], ALU.mult, ALU.add,
                          r=[("mi", 2), "vec", "t1"], w=["t1"])
                p.tt(t2[:], t1[:], t1[:], ALU.mult, r=["t1"], w=["t2"])
                p.ts(t2[:], t2[:], 0.044715, 1.0, ALU.mult, ALU.add, r=["t2"], w=["t2"])
                p.tt(t2[:], t2[:], t1[:], ALU.mult, r=["t1", "t2"], w=["t2"])
                p.act(t2[:], t2[:], AF.Sigmoid, r=["t2"], w=["t2"], scale=1.5957691216057308)
                p.tt(R(gR[:]), t1[:], t2[:], ALU.mult, r=["t1", "t2"], w=["gR"])
                for m in range(4):
                    ps, pk = nps()
                    for k in range(4):
                        p.mm(ps[:, 0:P_], R(glu[:, k, m * 128:(m + 1) * 128]), R(gR[:, k, :]), k == 0, k == 3,
                             r=["glu", "gR"], w=[pk])
                    p.act(t2[:, m, :], ps[:, 0:P_], AF.Sigmoid, r=[pk, "vec"], w=["t2"], bias=vec[:, V_GLUB + m:V_GLUB + m + 1])
                p.tt(R(mix[:, 0:4, :]), gR[:], t2[:], ALU.mult, r=["gR", "t2"], w=[("mix", 0)])
                p.copy(R(mix[:, 4:8, :]), mi[:, 12:16, :], r=[("mi", 3)], w=[("mix", 1)], e="pool")
                p.tt(t1[:], mi[:, 16:20, :], mi[:, 20:24, :], ALU.add, r=[("mi", 4), ("mi", 5)], w=["t1"])
                p.act(t2[:], mi[:, 24:28, :], AF.Silu, r=[("mi", 6)], w=["t2"])
                p.tt(t1[:], t1[:], t2[:], ALU.mult, r=["t1", "t2"], w=["t1"])
                p.tt(t2[:], t1[:], t1[:], ALU.mult, r=["t1"], w=["t2"])
                ps, pk = nps()
                for k in range(4):
                    p.mm(ps[:, 0:P_], onesN[:], t2[:, k, :], k == 0, k == 3, r=["ones", "t2"], w=[pk])
                p.ts(ca[:], ps[:, 0:P_], 1.0 / 512, 1e-6, ALU.mult, ALU.add, r=[pk], w=["ca"])
                p.act(ca[:], ca[:], AF.Sqrt, r=["ca"], w=["ca"])
                p.op("dve", lambda eng: eng.reciprocal(ca[:], ca[:]), r=["ca"], w=["ca"])
                for k in range(4):
                    p.stt(R(mix[:, 8 + k, :]), t1[:, k, :], vec[:, V_SSDW + k:V_SSDW + k + 1], ca[:], ALU.mult, ALU.mult,
                          r=["t1", "vec", "ca"], w=[("mix", 2)])
                p.tt(t1[:], mi[:, 28:32, :], mi[:, 32:36, :], ALU.add, r=[("mi", 7), ("mi", 8)], w=["t1"])
                p.tt(t2[:], t1[:], t1[:], ALU.mult, r=["t1"], w=["t2"])
                for k in range(4):
                    ps, pk = nps()
                    p.mm(ps[:, 0:P_], onesN[:], t2[:, k, :], True, True, r=["ones", "t2"], w=[pk])
                    p.ts(cg[:], ps[:, 0:P_], 1.0 / 128, 1e-6, ALU.mult, ALU.add, r=[pk], w=["cg"])
                    p.act(cg[:], cg[:], AF.Sqrt, r=["cg"], w=["cg"])
                    p.op("dve", lambda eng: eng.reciprocal(cg[:], cg[:]), r=["cg"], w=["cg"])
                    p.stt(t1[:, k, :], t1[:, k, :], vec[:, V_GLAW:V_GLAW + 1], cg[:], ALU.mult, ALU.mult,
                          r=["t1", "vec", "cg"], w=["t1"])
                p.act(t2[:], mi[:, 36:40, :], AF.Silu, r=[("mi", 9), "t2"], w=["t2"])
                p.tt(R(mix[:, 12:16, :]), t1[:], t2[:], ALU.mult, r=["t1", "t2"], w=[("mix", 3)])
                wov = wout_d[l].rearrange("(k p) m -> p k m", p=128)
                for g in range(4):
                    wbt, wk = nwb()
                    wview = wbt[:, :].rearrange("p (k m) -> p k m", k=KC)
                    p.load(R(wview), R(wov[:, :, g * 512:(g + 1) * 512]), w=[wk])
                    for mm_ in range(4):
                        m = g * 4 + mm_
                        ps, pk = nps()
                        for k in range(KC):
                            p.mm(ps[:, 0:P_], R(wview[:, k, mm_ * 128:(mm_ + 1) * 128]), R(mix[:, k, :]), k == 0, k == KC - 1,
                                 r=[wk, ("mix", k // 4)], w=[pk])
                        residual(ps, pk, m, 2)
                ln_affine(V_LN, V_LN + 16)
                if dbg and l == 0 and s == 0:
                    pass
                stats()
                for k in range(KC):
                    p.tt(R(mix[:, k, :]), x[:, k, :], mean[:], ALU.subtract, r=[("x", k), "mean"], w=[("mix", k // 4)], e="pool")
                    p.tt(R(mix[:, k, :]), mix[:, k, :], rstd[:], ALU.mult, r=["rstd", ("mix", k // 4)], w=[("mix", k // 4)])
                    if ncx > 0:
                        p.act(R(mix[:, k, 0:ncx]), mix[:, k, 0:ncx], AF.Identity, r=["modL", ("mix", k // 4)],
                              w=[("mix", k // 4)], scale=MC(4, 1, k), bias=MC(3, 1, k))
                    p.act(R(mix[:, k, ncx:P_]), mix[:, k, ncx:P_], AF.Identity, r=["modL", ("mix", k // 4)],
                          w=[("mix", k // 4)], scale=MC(4, 0, k), bias=MC(3, 0, k))
                edges = [(0, 0), (36, 4), (40, 8), (1068, 12)]
                for (gc, mcol) in edges:
                    a = gc - w0
                    if a < 0 or a + 4 > P_:
                        continue
                    for k in range(KC):
                        p.tt(R(mix[:, k, a:a + 4]), mix[:, k, a:a + 4], cmk[:, mcol:mcol + 4], ALU.mult,
                             r=["cmk", ("mix", k // 4)], w=[("mix", k // 4)])
                upv = up_d[l].rearrange("(k p) m -> p k m", p=128)
                p.copy(R(hff[:, :, 0:1]), zt[:], r=["zt"], w=["hff"] + [("mi", i) for i in range(10)])
                p.copy(R(hff[:, :, P_ - 1:P_]), zt[:], r=["zt"], w=["hff"])
                for j in range(43):
                    wbt, wk = nwb()
                    wview = wbt[:, 0:4096].rearrange("p (a k m) -> p a k m", a=2, k=KC)
                    p.load(R(wview[:, 0]), R(upv[:, :, j * 128:(j + 1) * 128]), w=[wk])
                    p.load(R(wview[:, 1]), R(upv[:, :, DFF + j * 128:DFF + (j + 1) * 128]), w=[wk])
                    outs = []
                    for a in range(2):
                        ps, pk = nps()
                        for k in range(KC):
                            p.mm(ps[:, 0:P_], R(wview[:, a, k, :]), R(mix[:, k, :]), k == 0, k == KC - 1,
                                 r=[wk, ("mix", k // 4)], w=[pk])
                        outs.append((ps, pk))
                    for a, (ps, pk) in enumerate(outs):
                        dst, dk = (ca, "ca") if a == 0 else (cg, "cg")
                        c = V_CW + (a * 43 + j) * 4
                        p.ts(dst[:, 1:P_ - 1], ps[:, 1:P_ - 1], vec[:, c + 1:c + 2], vec[:, c + 3:c + 4], ALU.mult, ALU.add,
                             r=[pk, "vec"], w=[dk])
                        p.stt(dst[:, 1:P_ - 1], ps[:, 0:P_ - 2], vec[:, c:c + 1], dst[:, 1:P_ - 1], ALU.mult, ALU.add,
                              r=[pk, "vec", dk], w=[dk])
                        p.stt(dst[:, 1:P_ - 1], ps[:, 2:P_], vec[:, c + 2:c + 3], dst[:, 1:P_ - 1], ALU.mult, ALU.add,
                              r=[pk, "vec", dk], w=[dk])
                    p.act(cg[:, 1:P_ - 1], cg[:, 1:P_ - 1], AF.Silu, r=["cg"], w=["cg"])
                    p.tt(R(hff[:, j, 1:P_ - 1]), ca[:, 1:P_ - 1], cg[:, 1:P_ - 1], ALU.mult, r=["ca", "cg"], w=["hff"], e="pool")
                dnv = dn_d[l].rearrange("(j p) m -> p j m", p=128)
                for m in range(KC):
                    wbt, wk = nwb()
                    wview = wbt[:, 0:43 * 128].rearrange("p (j m) -> p j m", j=43)
                    p.load(R(wview), R(dnv[:, :, m * 128:(m + 1) * 128]), w=[wk])
                    ps, pk = nps()
                    for j in range(43):
                        p.mm(ps[:, 0:P_], R(wview[:, j, :]), R(hff[:, j, :]), j == 0, j == 42, r=[wk, "hff"], w=[pk])
                    residual(ps, pk, m, 5)
                ln_affine(V_LN + 32, V_LN + 48)
                xo = xout.rearrange("(k p) n -> p k n", p=128)
                p.store(xo[:, :, w0 + 1:w0 + P_ - 1], x[:, :, 1:P_ - 1], r=[("x", k) for k in range(KC)], w=[f"xres{(l + 1) % 2}"])
        p.barrier()
    p.maxops = 1 << 60
    p.barrier()
    if dbg:
        p.dma(lambda eng: eng.dma_start(out=dbg_o["sB"], in_=sB), r=["sB"], w=["dbgB"])
    fin = xres[nlayers % 2]
    p.dma(lambda eng: eng.dma_start(out=out_d, in_=fin[:, 44:1068]), r=[f"xres{nlayers % 2}"], w=["out"])
    if dbg:
        p.dma(lambda eng: eng.dma_start(out=dbg_o["x1"], in_=fin), r=[f"xres{nlayers % 2}"], w=["dbgx"])
    p.finish()
    p.emit()
    return nc


def fused_inputs(P, nlayers=DEPTH):
    NL = nlayers
    xfull = np.concatenate([P["ctx"][0], P["x"][0]], 0)
    jj = np.arange(128)
    tri2 = np.stack([(jj[:, None] <= jj[None, :]), (jj[:, None] >= jj[None, :])]).astype(np.float32)
    negmask2 = np.where(tri2 > 0, 0.0, NEG).astype(np.float32)
    j6 = np.arange(64)
    umask2 = np.stack([(j6[:, None] <= j6[None, :]), (j6[:, None] >= j6[None, :])]).astype(np.float32)
    rot = np.zeros((64, 64), np.float32)
    for m in range(32):
        rot[m + 32, m] = -1.0
        rot[m, m + 32] = 1.0
    cmask = np.ones((64, TT), np.float32)
    cmask[:, ::64] = 0.0
    c2, s2 = rope_tables()
    gcs = np.ascontiguousarray(np.stack([c2.T, s2.T]))
    tau1 = np.tile(np.arange(1, 129, dtype=np.float32)[None, :], (128, 1))
    ident = np.eye(128, dtype=np.float32)
    cvh = np.stack([chunkcols(P["c"].reshape(-1)), chunkcols(P["c_ctx"].reshape(-1))], 2)
    vecs = np.zeros((DEPTH, 128, NV_C), np.float32)
    for l in range(DEPTH):
        vec = vecs[l]
        vec[:, 0:4] = chunkcols(P["s5_d"][l]); vec[:, 4:8] = chunkcols(P["s5_glu_b"][l])
        vec[:, 8:12] = chunkcols(P["ssd_norm_w"][l]); vec[:, 12:13] = chunkcols(P["gla_norm_w"][l])
        vec[:, 13:29] = chunkcols(P["ln_g"][l, 0]); vec[:, 29:45] = chunkcols(P["ln_b"][l, 0])
        vec[:, 45:61] = chunkcols(P["ln_g"][l, 1]); vec[:, 61:77] = chunkcols(P["ln_b"][l, 1])
        cw = P["ffn_conv_w"][l]
        cvv = np.stack([chunkcols(cw[0]), chunkcols(cw[1]), chunkcols(cw[2]), chunkcols(P["ffn_conv_b"][l])], 2)
        vec[:, 77:] = cvv.reshape(128, 86 * 4)
    shared = {"w_in": P["w_in"][:NL], "w_out": P["w_out"][:NL], "ffn_up": P["ffn_up"][:NL], "ffn_down": P["ffn_down"][:NL],
              "glu_w": P["s5_glu_w"][:NL], "vecs": vecs[:NL], "gla_cs": gcs, "tau1": tau1, "ident": ident, "tri2": tri2,
              "negmask2": negmask2, "rot": rot, "cmask": cmask, "umask2": umask2, "cv": cvh}
    in_maps = []
    pp = np.arange(128)
    for i in range(NCORES):
        g = i // 4
        hh, vh = i // 2, i % 2
        m = dict(shared)
        cr = np.arange(32 * i - 4, 32 * i + 36)
        lr = np.arange(1024 * i - 4, 1024 * i + 1028)
        idx = np.concatenate([np.where((cr >= 0) & (cr < CTX), cr, -1), np.where((lr >= 0) & (lr < SEQ), lr + CTX, -1)])
        m["x0T"] = _gather_T(xfull, idx)
        ex = (idx >= 0).astype(np.float32)
        flags = np.concatenate([ex[0:4], ex[36:40], ex[40:44], ex[1068:1072]])
        m["colmask"] = np.tile(flags[None, :], (128, 1)).astype(np.float32)
        bases = [64 * i, 512 + 64 * i, 1024 + 64 * i, 1536 + 64 * i, 2048 + 64 * i, 2560 + 128 * g, 2816 + 128 * g,
                 None, 3088 + 64 * hh, 3344 + 64 * hh, 3600 + 128 * hh + 64 * vh, 4112, 4128]
        idxA = np.zeros((128, NGT * 8), np.uint32)
        for t, b in enumerate(bases):
            for r in range(8):
                if t == 7:
                    col = np.zeros(128, np.int64)
                    col[0] = 3072 + i
                    col[1] = 3080 + i
                else:
                    col = b + pp
                idxA[:, t * 8 + r] = (r * SROWS + np.minimum(col, SROWS - 1)).astype(np.uint32)
        m["idxA"] = idxA
        idxB = np.zeros((128, 28 * 4), np.uint32)
        for gq in range(7):
            for kk in range(4):
                for s in range(4):
                    r = 2 * kk + pp // 64
                    rowid = 64 * gq + pp % 64
                    idxB[:, (gq * 4 + kk) * 4 + s] = (r * (BROWS * 32) + rowid * 32 + i * 4 + s).astype(np.uint32)
        m["idxB"] = idxB
        dtsel = np.zeros((16, 2), np.float32)
        dtsel[i, 0] = 1.0
        dtsel[8 + i, 1] = 1.0
        m["dtsel"] = dtsel
        l_, h_ = i // 2, i % 2
        m["wmod"] = np.ascontiguousarray(P["w_mod"][l_][:, h_ * 6144:(h_ + 1) * 6144])
        m["bmod"] = chunkcols(P["b_mod"][l_][h_ * 6144:(h_ + 1) * 6144])
        lanep = np.zeros((DEPTH, 2, 2, 128, 3), np.float32)
        bre = np.zeros((DEPTH, 2, 2, 128, 16), np.float32); bim = np.zeros((DEPTH, 2, 2, 128, 16), np.float32)
        cre = np.zeros((DEPTH, 2, 2, 128, 64), np.float32); cim = np.zeros((DEPTH, 2, 2, 128, 64), np.float32)
        nab = np.zeros((DEPTH, 5, 128, 576), np.float32)
        scw = np.zeros((DEPTH, 128, 3, 4), np.float32); ssc = np.zeros((DEPTH, 2, 128, 4), np.float32)
        ggw = np.zeros((DEPTH, 2, 16, 64), np.float32); ggb = np.zeros((DEPTH, 2, 64, 1), np.float32)
        colsets = [np.arange(64 * i, 64 * i + 64), 512 + np.arange(128 * g, 128 * g + 128), 768 + np.arange(128 * g, 128 * g + 128)]
        for l in range(DEPTH):
            for d in range(2):
                for t in range(2):
                    for h in range(2):
                        gl = 2 * t + h
                        gg_ = 4 * i + gl
                        rows = slice(64 * h, 64 * h + 64)
                        lanep[l, d, t, rows, 0] = P["s5_lam_re"][l, d, gg_]
                        lanep[l, d, t, rows, 1] = P["s5_lam_im"][l, d, gg_]
                        lanep[l, d, t, rows, 2] = P["s5_log_step"][l, d, gg_]
                        bre[l, d, t, rows] = P["s5_b_re"][l, d, gg_]; bim[l, d, t, rows] = P["s5_b_im"][l, d, gg_]
                        cre[l, d, t, rows, 16 * gl:16 * gl + 16] = P["s5_c_re"][l, d, gg_].T
                        cim[l, d, t, rows, 16 * gl:16 * gl + 16] = P["s5_c_im"][l, d, gg_].T
                ssc[l, d, :, 0] = P["ssd_dt_bias"][l, d, i]; ssc[l, d, :, 1] = P["ssd_a_log"][l, d, i]; ssc[l, d, :, 2] = P["ssd_d"][l, i]
                ggw[l, d] = P["gla_gate_w"][l, d][:, 64 * hh:64 * hh + 64]
                ggb[l, d, :, 0] = P["gla_gate_b"][l, d][64 * hh:64 * hh + 64]
            nab[l] = na_bias_tables(P["na_rpb"][l, i])
            for ch, cs_ in enumerate(colsets):
                scw[l, :len(cs_), ch, 0:3] = P["ssd_conv_w"][l][:, cs_].T
                scw[l, :len(cs_), ch, 3] = P["ssd_conv_b"][l][cs_]
        m.update({"lanep": lanep[:NL], "bre": bre[:NL], "bim": bim[:NL], "cre": cre[:NL], "cim": cim[:NL], "nabias": nab[:NL],
                  "ssd_cw": scw[:NL], "ssd_scal": ssc[:NL], "gla_gw": ggw[:NL], "gla_gb": ggb[:NL]})
        in_maps.append(m)
    return in_maps


def kernel_fused(nlayers=DEPTH, dbg=False, **inputs):
    P = {k: np.asarray(v, np.float32) for k, v in inputs.items()}
    nc = _prog(("F", nlayers, dbg), lambda: build_fused(nlayers, dbg))
    res = run_spmd(nc, fused_inputs(P, nlayers))
    out = np.concatenate([res[i]["outT"].T for i in range(NCORES)], 0)
    return np.ascontiguousarray(out[None]).astype(np.float32), res


def _pass_rows_f(j, s):
    cr = np.arange(32 * j - 4, 32 * j + 36)
    lr = np.arange(1024 * j - 4, 1024 * j + 1028)
    idx = np.concatenate([np.where((cr >= 0) & (cr < CTX), cr, -1), np.where((lr >= 0) & (lr < SEQ), lr + CTX, -1)])
    return idx[PWIN[s]:PWIN[s] + PW]
```

```python
import numpy as np
from contextlib import ExitStack
import concourse.bass as bass
import concourse.mybir as mybir
from concourse.bass_utils import run_bass_kernel_spmd

F32 = mybir.dt.float32
F32R = mybir.dt.float32r
BF16 = mybir.dt.bfloat16
AF = mybir.ActivationFunctionType
ALU = mybir.AluOpType
AX = mybir.AxisListType

NCORES = 8
D = 2048
KC = D // 128
DEPTH = 4
SEQ = 8192
CTX = 256
DPROJ = 5168
DFF = 5504


class Prog:
    ENG = ("pe", "act", "dve", "pool", "sp")

    def __init__(self, nc, ndma=24):
        self.nc = nc
        import os as _os
        self.maxops = int(_os.environ.get("PROG_MAXOPS", str(1 << 60)))
        nc.dge_precook = False
        self.stream = {e: [] for e in self.ENG}
        self.cnt = {e: 0 for e in self.ENG}
        self.clock = {e: {} for e in self.ENG}
        self.lastw = {}
        self.readers = {}
        self.ndma = ndma
        self.dma_uses = [0] * ndma
        self.dma_ev = [None] * ndma
        self.dma_rr = 0
        self.coll_extra = {}
        self.root = ExitStack()
        self.es = self.root
        self.nalloc = 0
        self.sems = None

    def sb(self, shape, dtype=F32, name=None):
        self.nalloc += 1
        name = name or f"sb{self.nalloc}"
        return self.es.enter_context(self.nc.sbuf_tensor(name, list(shape), dtype))

    def ps(self, shape, dtype=F32, name=None):
        self.nalloc += 1
        name = name or f"ps{self.nalloc}"
        return self.es.enter_context(self.nc.psum_tensor(name, list(shape), dtype))

    def _deps(self, e, reads, writes):
        evs = []
        for k in reads:
            ev = self.lastw.get(k)
            if ev is not None:
                evs.append(ev)
        for k in writes:
            ev = self.lastw.get(k)
            if ev is not None:
                evs.append(ev)
            evs.extend(self.readers.get(k, ()))
        clk = self.clock[e]
        waits = {}
        for (sk, v, snap) in evs:
            if e == "pe" and sk == "pe":
                continue
            if clk.get(sk, 0) >= v:
                continue
            waits[sk] = max(waits.get(sk, 0), v)
            for s2, v2 in snap.items():
                if clk.get(s2, 0) < v2:
                    clk[s2] = v2
            clk[sk] = v
        return list(waits.items())

    def _commit(self, ev, reads, writes):
        for k in writes:
            self.lastw[k] = ev
            self.readers[k] = []
        for k in reads:
            if k in writes:
                continue
            self.readers.setdefault(k, []).append(ev)

    def op(self, e, fn, r=(), w=()):
        self.nops = getattr(self, "nops", 0) + 1
        if self.nops > getattr(self, "maxops", 1 << 60):
            return
        waits = self._deps(e, r, w)
        self.cnt[e] += 1
        idx = self.cnt[e]
        ev = (e, idx, dict(self.clock[e]))
        self.stream[e].append((waits, fn, (e, 1)))
        self._commit(ev, r, w)

    def dma(self, fn, r=(), w=(), q="sp"):
        self.nops = getattr(self, "nops", 0) + 1
        if self.nops > getattr(self, "maxops", 1 << 60):
            return
        k = self.dma_rr
        self.dma_rr = (self.dma_rr + 1) % self.ndma
        waits = self._deps(q, r, w)
        prev = self.dma_ev[k]
        if prev is not None:
            sk, v, snap = prev
            if self.clock[q].get(sk, 0) < v:
                waits.append((sk, v))
                self.clock[q][sk] = v
        self.dma_uses[k] += 1
        sk = ("d", k)
        ev = (sk, 16 * self.dma_uses[k] + self.coll_extra.get(k, 0), dict(self.clock[q]))
        self.dma_ev[k] = ev
        self.stream[q].append((waits, fn, (sk, 16)))
        self._commit(ev, r, w)

    def coll(self, fn, r=(), w=()):
        self.nops = getattr(self, "nops", 0) + 1
        if self.nops > getattr(self, "maxops", 1 << 60):
            return
        k = self.dma_rr
        self.dma_rr = (self.dma_rr + 1) % self.ndma
        q = "pool"
        waits = self._deps(q, r, w)
        prev = self.dma_ev[k]
        if prev is not None:
            sk, v, snap = prev
            if self.clock[q].get(sk, 0) < v:
                waits.append((sk, v))
                self.clock[q][sk] = v
        sk = ("d", k)
        base = 16 * self.dma_uses[k] + self.coll_extra.get(k, 0)
        self.coll_extra[k] = self.coll_extra.get(k, 0) + 1
        ev = (sk, base + 1, dict(self.clock[q]))
        self.dma_ev[k] = ev
        self.stream[q].append((waits, fn, (sk, 1)))
        self._commit(ev, r, w)

    def finish(self):
        waits = []
        for ev in self.dma_ev:
            if ev is not None and self.clock["sp"].get(ev[0], 0) < ev[1]:
                waits.append((ev[0], ev[1]))
        self.stream["sp"].append((waits, None, None))

    def barrier(self):
        evs = [(e, self.cnt[e]) for e in self.ENG if self.cnt[e] > 0]
        evs += [(ev[0], ev[1]) for ev in self.dma_ev if ev is not None]
        for e in self.ENG:
            waits = []
            for sk, v in evs:
                if e == "pe" and sk == "pe":
                    continue
                if self.clock[e].get(sk, 0) < v:
                    waits.append((sk, v))
                    self.clock[e][sk] = v
            if waits:
                self.stream[e].append((waits, None, None))
        self.lastw.clear()
        self.readers.clear()

    def scope(self):
        prog = self

        class _Scope:
            def __enter__(self_):
                self_.old = prog.es
                prog.es = ExitStack()
                return prog

            def __exit__(self_, *a):
                if a[0] is None:
                    prog.barrier()
                    prog.flush()
                prog.es.close()
                prog.es = self_.old
                return False
        return _Scope()

    def emit(self):
        self.flush()
        self.root.close()

    def flush(self):
        nc = self.nc
        if self.sems is None:
            self.sems = {}
            for e in self.ENG:
                self.sems[e] = self.root.enter_context(nc.semaphore(f"s_{e}"))
            for k in range(self.ndma):
                self.sems[("d", k)] = self.root.enter_context(nc.semaphore(f"s_d{k}"))
        sems = self.sems
        streams = self.stream
        self.stream = {e: [] for e in self.ENG}

        def replay(e, eng):
            for waits, fn, inc in streams[e]:
                for sk, v in waits:
                    eng.wait_ge(sems[sk], v)
                if fn is None:
                    continue
                ins = fn(eng)
                ins.then_inc(sems[inc[0]], inc[1])

        with nc.Block() as block:
            @block.tensor
            def _(eng):
                replay("pe", eng)

            @block.scalar
            def _(eng):
                replay("act", eng)

            @block.vector
            def _(eng):
                replay("dve", eng)

            @block.gpsimd
            def _(eng):
                replay("pool", eng)

            @block.sync
            def _(eng):
                replay("sp", eng)

    def mm(self, out, lhsT, rhs, start, stop, r, w):
        self.op("pe", lambda eng: eng.matmul(out, lhsT, rhs, start=start, stop=stop), r, w)

    def act(self, out, in_, func, r, w, bias=None, scale=None, accum_out=None):
        kw = {}
        if bias is not None:
            kw["bias"] = bias
        if scale is not None:
            kw["scale"] = scale
        if accum_out is not None:
            kw["accum_out"] = accum_out
        self.op("act", lambda eng: eng.activation(out, in_, func, **kw), r, w)

    def tt(self, out, a, b, op, r, w, e="dve"):
        self.op(e, lambda eng: eng.tensor_tensor(out, a, b, op), r, w)

    def ts(self, out, a, s1, s2, op0, op1, r, w, e="dve"):
        if s2 is None:
            self.op(e, lambda eng: eng.tensor_scalar(out, a, s1, None, op0), r, w)
        else:
            self.op(e, lambda eng: eng.tensor_scalar(out, a, s1, s2, op0, op1), r, w)

    def stt(self, out, a, s, b, op0, op1, r, w, e="dve"):
        self.op(e, lambda eng: eng.scalar_tensor_tensor(out, a, s, b, op0, op1), r, w)

    def copy(self, out, in_, r, w, e="dve"):
        if e == "act":
            self.op(e, lambda eng: eng.copy(out, in_), r, w)
        else:
            self.op(e, lambda eng: eng.tensor_copy(out, in_), r, w)

    def memset(self, ap, val, w, e="dve"):
        self.op(e, lambda eng: eng.memset(ap, val), (), w)

    def load(self, out, in_, w, r=(), q="sp"):
        self.dma(lambda eng: eng.dma_start(out=out, in_=in_), r, w, q)

    def store(self, out, in_, r, w=(), q="sp"):
        self.dma(lambda eng: eng.dma_start(out=out, in_=in_), r, w, q)


def R(ap):
    return ap.bitcast(F32R)


def run_spmd(nc, in_maps):
    res = run_bass_kernel_spmd(nc, in_maps, core_ids=list(range(NCORES)))
    return res.results


def ln_stats(p, xT, ncols, tiles, onesN, sq, ps_m, ps_q, mean, rstd, tag):
    for (c0, cn) in tiles:
        for k in range(KC):
            p.act(sq[:, 0:cn], xT[:, k, c0:c0 + cn], AF.Square, r=[(tag, "x", k)], w=["sq"])
            p.mm(ps_m[:, 0:cn], onesN[:], xT[:, k, c0:c0 + cn], k == 0, k == KC - 1,
                 r=[(tag, "x", k), "ones"], w=["ps_m"])
            p.mm(ps_q[:, 0:cn], onesN[:], sq[:, 0:cn], k == 0, k == KC - 1,
                 r=["sq", "ones"], w=["ps_q"])
        p.copy(mean[:, c0:c0 + cn], ps_m[:, 0:cn], r=["ps_m"], w=[(tag, "mean")], e="act")
        p.tt(rstd[:, c0:c0 + cn], mean[:, c0:c0 + cn], mean[:, c0:c0 + cn], ALU.mult,
             r=[(tag, "mean")], w=[(tag, "rstd")])
        p.tt(rstd[:, c0:c0 + cn], ps_q[:, 0:cn], rstd[:, c0:c0 + cn], ALU.subtract,
             r=["ps_q", (tag, "rstd")], w=[(tag, "rstd")])
        p.ts(rstd[:, c0:c0 + cn], rstd[:, c0:c0 + cn], 1e-6, None, ALU.add, None,
             r=[(tag, "rstd")], w=[(tag, "rstd")])
        p.act(rstd[:, c0:c0 + cn], rstd[:, c0:c0 + cn], AF.Sqrt, r=[(tag, "rstd")], w=[(tag, "rstd")])
        p.op("dve", lambda eng, a=rstd[:, c0:c0 + cn]: eng.reciprocal(a, a),
             r=[(tag, "rstd")], w=[(tag, "rstd")])


def build_phase_a(NT, NCX):
    nc = bass.Bass("TRN2", target_bir_lowering=False)
    xT_d = nc.dram_tensor("xT", [D, NT], F32, kind="ExternalInput").ap()
    mod_d = nc.dram_tensor("modA", [128, 4 * KC], F32, kind="ExternalInput").ap()
    w_d = nc.dram_tensor("w_in", [D, DPROJ], F32, kind="ExternalInput").ap()
    out_d = nc.dram_tensor("pxT", [DPROJ, NT], F32, kind="ExternalOutput").ap()
    p = Prog(nc)
    xT = p.sb([128, KC, NT])
    hT = xT
    mod = p.sb([128, 4 * KC])
    onesN = p.sb([128, 128])
    sq = p.sb([128, 512])
    mean = p.sb([128, NT])
    rstd = p.sb([128, NT])
    GW = 512
    wbuf = [p.sb([128, KC, GW]) for _ in range(2)]
    obuf = [p.sb([128, NT]) for _ in range(2)]
    ps_m = p.ps([128, 512])
    ps_q = p.ps([128, 512])
    ps_o = [p.ps([128, 512]) for _ in range(4)]

    tiles = []
    c = 0
    while c < NT:
        cn = min(352, NT - c)
        tiles.append((c, cn))
        c += cn

    p.memset(onesN[:], 1.0 / D, w=["ones"])
    modr = p.sb([128, 4 * KC])
    p.load(modr[:], mod_d, w=["modr"])
    p.copy(mod[:], modr[:], r=["modr"], w=["mod"])
    p.ts(mod[:, 0:KC], modr[:, 0:KC], 1.0, None, ALU.add, None, r=["modr", "mod"], w=["mod"])
    p.ts(mod[:, 2 * KC:3 * KC], modr[:, 2 * KC:3 * KC], 1.0, None, ALU.add, None, r=["modr", "mod"], w=["mod"])
    xv = xT_d.rearrange("(k p) n -> p k n", p=128)
    for k in range(KC):
        p.load(R(xT[:, k, :]), R(xv[:, k, :]), w=[("A", "x", k)])
    ln_stats(p, xT, NT, tiles, onesN, sq, ps_m, ps_q, mean, rstd, "A")
    for k in range(KC):
        p.tt(R(hT[:, k, :]), xT[:, k, :], mean[:], ALU.subtract, r=[("A", "x", k), ("A", "mean")],
             w=[("h", k), ("A", "x", k)])
        p.tt(R(hT[:, k, :]), hT[:, k, :], rstd[:], ALU.mult, r=[("h", k), ("A", "rstd")], w=[("h", k)])
        if NCX > 0:
            p.act(R(hT[:, k, 0:NCX]), hT[:, k, 0:NCX], AF.Identity, r=[("h", k), "mod"], w=[("h", k)],
                  scale=mod[:, 2 * KC + k:2 * KC + k + 1], bias=mod[:, 3 * KC + k:3 * KC + k + 1])
        p.act(R(hT[:, k, NCX:NT]), hT[:, k, NCX:NT], AF.Identity, r=[("h", k), "mod"], w=[("h", k)],
              scale=mod[:, k:k + 1], bias=mod[:, KC + k:KC + k + 1])
    wv = w_d.rearrange("(k p) m -> p k m", p=128)
    ngrp = (DPROJ + GW - 1) // GW
    oi = 0
    for g in range(ngrp):
        g0 = g * GW
        gw = min(GW, DPROJ - g0)
        wb = wbuf[g % 2]
        for k in range(KC):
            p.load(R(wb[:, k, 0:gw]), R(wv[:, k, g0:g0 + gw]), w=[("w", g % 2, k)])
        for m0 in range(0, gw, 128):
            mw = min(128, gw - m0)
            ob = obuf[oi % 2]
            for ti, (c0, cn) in enumerate(tiles):
                ps = ps_o[ti % 4]
                for k in range(KC):
                    p.mm(ps[0:mw, 0:cn], R(wb[:, k, m0:m0 + mw]), R(hT[:, k, c0:c0 + cn]), k == 0, k == KC - 1,
                         r=[("w", g % 2, k), ("h", k)], w=[("pso", ti % 4)])
                p.copy(ob[0:mw, c0:c0 + cn], ps[0:mw, 0:cn], r=[("pso", ti % 4)], w=[("ob", oi % 2)],
                       e="act" if ti % 2 == 0 else "dve")
            p.store(out_d[g0 + m0:g0 + m0 + mw, :], ob[0:mw, :], r=[("ob", oi % 2)])
            oi += 1
    p.finish()
    p.emit()
    return nc


NPASS = 3
PC = 358
NCXC = 13
CT_N = 11
LT_N = 342
CT_OFF = (0, 11, 21)
LT_OFF = (0, 341, 682)
ALPHA = float((2 * DEPTH) ** 0.25)
NV_C = 4 + 4 + 4 + 1 + 64 + 86 * 4


def build_phase_c(inject=False):
    nc = bass.Bass("TRN2", target_bir_lowering=False)
    P_ = PC
    xT_d = nc.dram_tensor("xT", [NPASS, D, P_], F32, kind="ExternalInput").ap()
    mi_d = nc.dram_tensor("mixin", [NPASS, 40 * 128, P_], F32, kind="ExternalInput").ap()
    mod_d = nc.dram_tensor("modC", [128, 8 * KC], F32, kind="ExternalInput").ap()
    vec_d = nc.dram_tensor("vecs", [128, NV_C], F32, kind="ExternalInput").ap()
    hm_d = nc.dram_tensor("hmask", [NPASS, 128, 4], F32, kind="ExternalInput").ap()
    glu_d = nc.dram_tensor("glu_w", [512, 512], F32, kind="ExternalInput").ap()
    wo_d = nc.dram_tensor("w_out", [D, D], F32, kind="ExternalInput").ap()
    up_d = nc.dram_tensor("ffn_up", [D, 2 * DFF], F32, kind="ExternalInput").ap()
    dn_d = nc.dram_tensor("ffn_down", [DFF, D], F32, kind="ExternalInput").ap()
    out_d = nc.dram_tensor("x2T", [NPASS, D, P_], F32, kind="ExternalOutput").ap()
    dbg_d = nc.dram_tensor("mixT", [NPASS, D, P_], F32, kind="ExternalOutput").ap()
    p = Prog(nc)
    x = p.sb([128, KC, P_])
    hff = p.sb([128, 43, P_])
    mi = hff
    mix = p.sb([128, KC, P_])
    wb = [p.sb([128, 8192]) for _ in range(2)]
    mod = p.sb([128, 8 * KC])
    sc1 = p.sb([128, 2 * KC])
    vec = p.sb([128, NV_C])
    hm = p.sb([128, 4])
    glu = p.sb([128, 4, 512])
    onesN = p.sb([128, 128])
    sq = p.sb([128, 512])
    mean = p.sb([128, P_])
    rstd = p.sb([128, P_])
    t1 = p.sb([128, 4, P_])
    t2 = p.sb([128, 4, P_])
    gR = p.sb([128, 4, P_])
    ca = p.sb([128, P_])
    cg = p.sb([128, P_])
    ps_m = p.ps([128, 512])
    ps_q = p.ps([128, 512])
    psr = [p.ps([128, 512]) for _ in range(6)]
    pi = [0]

    def nps():
        pi[0] = (pi[0] + 1) % 6
        return psr[pi[0]], ("psr", pi[0])

    V_S5D, V_GLUB, V_SSDW, V_GLAW, V_LN, V_CW = 0, 4, 8, 12, 13, 77
    p.memset(onesN[:], 1.0, w=["ones"])
    zt = p.sb([128, 43, 1])
    p.memset(zt[:], 0.0, w=["zt"])
    p.load(mod[:], mod_d, w=["mod"])
    p.load(vec[:], vec_d, w=["vec"])
    p.load(R(glu[:]), R(glu_d.rearrange("(k p) m -> p k m", p=128)), w=["glu"])
    p.ts(sc1[:, 0:KC], mod[:, 3 * KC:4 * KC], 1.0, None, ALU.add, None, r=["mod"], w=["sc1"])
    p.ts(sc1[:, KC:2 * KC], mod[:, 5 * KC:6 * KC], 1.0, None, ALU.add, None, r=["mod"], w=["sc1"])
    tiles = [(0, P_)]
    wi = [0]

    def nwb():
        wi[0] = (wi[0] + 1) % 2
        return wb[wi[0]], ("wb", wi[0])

    def stats(tag):
        for k in range(KC):
            p.act(sq[:, 0:P_], x[:, k, :], AF.Square, r=[("x", k)], w=["sq"])
            p.mm(ps_m[:, 0:P_], onesN[:], x[:, k, :], k == 0, k == KC - 1, r=[("x", k), "ones"], w=["ps_m"])
            p.mm(ps_q[:, 0:P_], onesN[:], sq[:, 0:P_], k == 0, k == KC - 1, r=["sq", "ones"], w=["ps_q"])
        p.ts(mean[:], ps_m[:, 0:P_], 1.0 / D, None, ALU.mult, None, r=["ps_m"], w=["mean"])
        p.tt(rstd[:], mean[:], mean[:], ALU.mult, r=["mean"], w=["rstd"])
        p.stt(rstd[:], ps_q[:, 0:P_], 1.0 / D, rstd[:], ALU.mult, ALU.subtract, r=["ps_q", "rstd"], w=["rstd"])
        p.ts(rstd[:], rstd[:], 1e-6, None, ALU.add, None, r=["rstd"], w=["rstd"])
        p.act(rstd[:], rstd[:], AF.Sqrt, r=["rstd"], w=["rstd"])
        p.op("dve", lambda eng: eng.reciprocal(rstd[:], rstd[:]), r=["rstd"], w=["rstd"])

    def ln_affine(gcol, bcol):
        stats("x")
        for k in range(KC):
            p.tt(x[:, k, :], x[:, k, :], mean[:], ALU.subtract, r=[("x", k), "mean"], w=[("x", k)])
            p.tt(x[:, k, :], x[:, k, :], rstd[:], ALU.mult, r=[("x", k), "rstd"], w=[("x", k)])
            p.act(x[:, k, :], x[:, k, :], AF.Identity, r=[("x", k), "vec"], w=[("x", k)],
                  scale=vec[:, gcol + k:gcol + k + 1], bias=vec[:, bcol + k:bcol + k + 1])

    def residual(ps, m, gx, gc):
        p.ts(x[:, m, :], x[:, m, :], ALPHA, None, ALU.mult, None, r=[("x", m)], w=[("x", m)], e="pool")
        p.stt(x[:, m, 0:NCXC], ps[:, 0:NCXC], mod[:, gc + m:gc + m + 1], x[:, m, 0:NCXC], ALU.mult, ALU.add,
              r=[("x", m), "mod", pk], w=[("x", m)])
        p.stt(x[:, m, NCXC:P_], ps[:, NCXC:P_], mod[:, gx + m:gx + m + 1], x[:, m, NCXC:P_], ALU.mult, ALU.add,
              r=[("x", m), "mod", pk], w=[("x", m)])

    for s in range(NPASS):
        xv = xT_d[s].rearrange("(k p) n -> p k n", p=128)
        miv = mi_d[s].rearrange("(k p) n -> p k n", p=128)
        for k in range(KC):
            p.load(x[:, k, :], xv[:, k, :], w=[("x", k)])
        for k in range(40):
            p.load(R(mi[:, k, :]), R(miv[:, k, :]), w=[("mi", k // 4), "hff"])
        p.load(hm[:], hm_d[s], w=["hm"])
        if inject:
            for k in range(KC):
                p.copy(R(mix[:, k, :]), mi[:, k, :], r=[("mi", k // 4)], w=[("mix", k // 4)], e="pool")
        else:
            p.tt(t1[:], mi[:, 0:4, :], mi[:, 4:8, :], ALU.add, r=[("mi", 0), ("mi", 1)], w=["t1"])
            for k in range(4):
                p.stt(t1[:, k, :], mi[:, 8 + k, :], vec[:, V_S5D + k:V_S5D + k + 1], t1[:, k, :], ALU.mult, ALU.add,
                      r=[("mi", 2), "vec", "t1"], w=["t1"])
            p.tt(t2[:], t1[:], t1[:], ALU.mult, r=["t1"], w=["t2"])
            p.ts(t2[:], t2[:], 0.044715, 1.0, ALU.mult, ALU.add, r=["t2"], w=["t2"])
            p.tt(t2[:], t2[:], t1[:], ALU.mult, r=["t1", "t2"], w=["t2"])
            p.act(t2[:], t2[:], AF.Sigmoid, r=["t2"], w=["t2"], scale=1.5957691216057308)
            p.tt(R(gR[:]), t1[:], t2[:], ALU.mult, r=["t1", "t2"], w=["gR"])
            for m in range(4):
                ps, pk = nps()
                for k in range(4):
                    p.mm(ps[:, 0:P_], R(glu[:, k, m * 128:(m + 1) * 128]), R(gR[:, k, :]), k == 0, k == 3,
                         r=["glu", "gR"], w=[pk])
                p.act(t2[:, m, :], ps[:, 0:P_], AF.Sigmoid, r=[pk, "vec"], w=["t2"],
                      bias=vec[:, V_GLUB + m:V_GLUB + m + 1])
            p.tt(R(mix[:, 0:4, :]), gR[:], t2[:], ALU.mult, r=["gR", "t2"], w=[("mix", 0)])
            p.copy(R(mix[:, 4:8, :]), mi[:, 12:16, :], r=[("mi", 3)], w=[("mix", 1)], e="pool")
            p.tt(t1[:], mi[:, 16:20, :], mi[:, 20:24, :], ALU.add, r=[("mi", 4), ("mi", 5)], w=["t1"])
            p.act(t2[:], mi[:, 24:28, :], AF.Silu, r=[("mi", 6)], w=["t2"])
            p.tt(t1[:], t1[:], t2[:], ALU.mult, r=["t1", "t2"], w=["t1"])
            p.tt(t2[:], t1[:], t1[:], ALU.mult, r=["t1"], w=["t2"])
            ps, pk = nps()
            for k in range(4):
                p.mm(ps[:, 0:P_], onesN[:], t2[:, k, :], k == 0, k == 3, r=["ones", "t2"], w=[pk])
            p.ts(ca[:], ps[:, 0:P_], 1.0 / 512, 1e-6, ALU.mult, ALU.add, r=[pk], w=["ca"])
            p.act(ca[:], ca[:], AF.Sqrt, r=["ca"], w=["ca"])
            p.op("dve", lambda eng: eng.reciprocal(ca[:], ca[:]), r=["ca"], w=["ca"])
            for k in range(4):
                p.stt(R(mix[:, 8 + k, :]), t1[:, k, :], vec[:, V_SSDW + k:V_SSDW + k + 1], ca[:], ALU.mult, ALU.mult,
                      r=["t1", "vec", "ca"], w=[("mix", 2)])
            p.tt(t1[:], mi[:, 28:32, :], mi[:, 32:36, :], ALU.add, r=[("mi", 7), ("mi", 8)], w=["t1"])
            p.tt(t2[:], t1[:], t1[:], ALU.mult, r=["t1"], w=["t2"])
            for k in range(4):
                ps, pk = nps()
                p.mm(ps[:, 0:P_], onesN[:], t2[:, k, :], True, True, r=["ones", "t2"], w=[pk])
                p.ts(cg[:], ps[:, 0:P_], 1.0 / 128, 1e-6, ALU.mult, ALU.add, r=[pk], w=["cg"])
                p.act(cg[:], cg[:], AF.Sqrt, r=["cg"], w=["cg"])
                p.op("dve", lambda eng: eng.reciprocal(cg[:], cg[:]), r=["cg"], w=["cg"])
                p.stt(t1[:, k, :], t1[:, k, :], vec[:, V_GLAW:V_GLAW + 1], cg[:], ALU.mult, ALU.mult,
                      r=["t1", "vec", "cg"], w=["t1"])
            p.act(t2[:], mi[:, 36:40, :], AF.Silu, r=[("mi", 9), "t2"], w=["t2"])
            p.tt(R(mix[:, 12:16, :]), t1[:], t2[:], ALU.mult, r=["t1", "t2"], w=[("mix", 3)])
        p.store(dbg_d[s].rearrange("(k p) n -> p k n", p=128), mix[:], r=[("mix", i) for i in range(4)])
        wov = wo_d.rearrange("(k p) m -> p k m", p=128)
        for g in range(4):
            wbt, wk = nwb()
            wview = wbt[:, :].rearrange("p (k m) -> p k m", k=KC)
            p.load(R(wview), R(wov[:, :, g * 512:(g + 1) * 512]), w=[wk])
            for mm_ in range(4):
                m = g * 4 + mm_
                ps, pk = nps()
                for k in range(KC):
                    p.mm(ps[:, 0:P_], R(wview[:, k, mm_ * 128:(mm_ + 1) * 128]), R(mix[:, k, :]), k == 0, k == KC - 1,
                         r=[wk, ("mix", k // 4)], w=[pk])
                residual(ps, m, 0 * KC, 1 * KC)
        ln_affine(V_LN, V_LN + 16)
        stats("x")
        for k in range(KC):
            p.tt(R(mix[:, k, :]), x[:, k, :], mean[:], ALU.subtract, r=[("x", k), "mean"], w=[("mix", k // 4)])
            p.tt(R(mix[:, k, :]), mix[:, k, :], rstd[:], ALU.mult, r=["rstd", ("mix", k // 4)], w=[("mix", k // 4)])
            p.act(R(mix[:, k, 0:NCXC]), mix[:, k, 0:NCXC], AF.Identity, r=["sc1", "mod", ("mix", k // 4)],
                  w=[("mix", k // 4)], scale=sc1[:, KC + k:KC + k + 1], bias=mod[:, 4 * KC + k:4 * KC + k + 1])
            p.act(R(mix[:, k, NCXC:P_]), mix[:, k, NCXC:P_], AF.Identity, r=["sc1", "mod", ("mix", k // 4)],
                  w=[("mix", k // 4)], scale=sc1[:, k:k + 1], bias=mod[:, 2 * KC + k:2 * KC + k + 1])
        for hi, col in enumerate((0, NCXC - 1, NCXC, P_ - 2)):
            p.ts(R(mix[:, :, col:col + 1]), mix[:, :, col:col + 1], hm[:, hi:hi + 1], None, ALU.mult, None,
                 r=["hm"] + [("mix", i) for i in range(4)], w=[("mix", i) for i in range(4)])
        upv = up_d.rearrange("(k p) m -> p k m", p=128)
        p.copy(R(hff[:, :, 0:1]), zt[:], r=["zt"], w=["hff"] + [("mi", i) for i in range(10)])
        p.copy(R(hff[:, :, P_ - 1:P_]), zt[:], r=["zt"], w=["hff"])
        for j0 in range(0, 43, 2):
            nj = min(2, 43 - j0)
            wj = 128 * nj
            wbt, wk = nwb()
            wview = wbt[:, 0:2 * KC * wj].rearrange("p (a k m) -> p a k m", a=2, k=KC)
            p.load(R(wview[:, 0]), R(upv[:, :, j0 * 128:j0 * 128 + wj]), w=[wk])
            p.load(R(wview[:, 1]), R(upv[:, :, DFF + j0 * 128:DFF + j0 * 128 + wj]), w=[wk])
            for j in range(j0, j0 + nj):
                jo = (j - j0) * 128
                outs = []
                for a in range(2):
                    ps, pk = nps()
                    for k in range(KC):
                        p.mm(ps[:, 0:P_], R(wview[:, a, k, jo:jo + 128]), R(mix[:, k, :]), k == 0, k == KC - 1,
                             r=[wk, ("mix", k // 4)], w=[pk])
                    outs.append((ps, pk))
                for a, (ps, pk) in enumerate(outs):
                    dst, dk = (ca, "ca") if a == 0 else (cg, "cg")
                    c = V_CW + (a * 43 + j) * 4
                    p.ts(dst[:, 1:P_ - 1], ps[:, 1:P_ - 1], vec[:, c + 1:c + 2], vec[:, c + 3:c + 4], ALU.mult, ALU.add,
                         r=[pk, "vec"], w=[dk])
                    p.stt(dst[:, 1:P_ - 1], ps[:, 0:P_ - 2], vec[:, c:c + 1], dst[:, 1:P_ - 1], ALU.mult, ALU.add,
                          r=[pk, "vec", dk], w=[dk])
                    p.stt(dst[:, 1:P_ - 1], ps[:, 2:P_], vec[:, c + 2:c + 3], dst[:, 1:P_ - 1], ALU.mult, ALU.add,
                          r=[pk, "vec", dk], w=[dk])
                p.act(cg[:, 1:P_ - 1], cg[:, 1:P_ - 1], AF.Silu, r=["cg"], w=["cg"])
                p.tt(R(hff[:, j, 1:P_ - 1]), ca[:, 1:P_ - 1], cg[:, 1:P_ - 1], ALU.mult, r=["ca", "cg"], w=["hff"], e="pool")
        dnv = dn_d.rearrange("(j p) m -> p j m", p=128)
        for mg in range(4):
            accs = [nps() for _ in range(4)]
            for (ja, jb) in ((0, 11), (11, 22), (22, 33), (33, 43)):
                wbt, wk = nwb()
                wview = wbt[:, 0:(jb - ja) * 512].rearrange("p (j m) -> p j m", j=jb - ja)
                p.load(R(wview), R(dnv[:, ja:jb, mg * 512:(mg + 1) * 512]), w=[wk])
                for mm_ in range(4):
                    ps, pk = accs[mm_]
                    for j in range(ja, jb):
                        p.mm(ps[:, 0:P_], R(wview[:, j - ja, mm_ * 128:(mm_ + 1) * 128]), R(hff[:, j, :]), j == 0, j == 42,
                             r=[wk, "hff"], w=[pk])
            for mm_ in range(4):
                ps, pk = accs[mm_]
                residual(ps, mg * 4 + mm_, 6 * KC, 7 * KC)
        ln_affine(V_LN + 32, V_LN + 48)
        p.store(out_d[s].rearrange("(k p) n -> p k n", p=128), x[:], r=[("x", k) for k in range(KC)])
    p.finish()
    p.emit()
    return nc


def chunkcols(v):
    v = np.asarray(v, np.float32)
    return np.ascontiguousarray(v.reshape(-1, 128).T)


def _pass_rows(i, s):
    c0 = 32 * i + CT_OFF[s]
    l0 = 1024 * i + LT_OFF[s]
    idx = np.empty(PC, np.int64)
    cr = np.arange(c0 - 1, c0 + CT_N + 1)
    cr = np.where((cr >= 0) & (cr < CTX), cr, -1)
    lr = np.arange(l0 - 1, l0 + LT_N + 1)
    lr = np.where((lr >= 0) & (lr < SEQ), lr + CTX, -1)
    idx[:] = -1
    idx[:NCXC] = cr
    idx[NCXC:NCXC + LT_N + 2] = lr
    return idx


def _gather_T(full, idx):
    out = full[np.maximum(idx, 0)].T.copy()
    out[:, idx < 0] = 0
    return np.ascontiguousarray(out, np.float32)


def phase_c_inputs(l, xfull, mixfull, pxsel, mod_x, mod_c, P, inject=False):
    mx = [mod_x[j * D:(j + 1) * D] for j in range(6)]
    mc = [mod_c[j * D:(j + 1) * D] for j in range(6)]
    modC = np.concatenate([chunkcols(v) for v in (mx[2], mc[2], mx[3], mx[4], mc[3], mc[4], mx[5], mc[5])], 1)
    vec = np.zeros((128, NV_C), np.float32)
    vec[:, 0:4] = chunkcols(P["s5_d"][l])
    vec[:, 4:8] = chunkcols(P["s5_glu_b"][l])
    vec[:, 8:12] = chunkcols(P["ssd_norm_w"][l])
    vec[:, 12:13] = chunkcols(P["gla_norm_w"][l])
    vec[:, 13:29] = chunkcols(P["ln_g"][l, 0])
    vec[:, 29:45] = chunkcols(P["ln_b"][l, 0])
    vec[:, 45:61] = chunkcols(P["ln_g"][l, 1])
    vec[:, 61:77] = chunkcols(P["ln_b"][l, 1])
    cw = P["ffn_conv_w"][l]
    cb = P["ffn_conv_b"][l]
    cv = np.stack([chunkcols(cw[0]), chunkcols(cw[1]), chunkcols(cw[2]), chunkcols(cb)], 2)
    vec[:, 77:] = cv.reshape(128, 86 * 4)
    in_maps = []
    for i in range(NCORES):
        xs, ms, hs = [], [], []
        for s in range(NPASS):
            idx = _pass_rows(i, s)
            xs.append(_gather_T(xfull, idx))
            m = np.zeros((40 * 128, PC), np.float32)
            mt = _gather_T(mixfull, idx)
            m[:mt.shape[0]] = mt
            ms.append(m)
            flags = (idx[[0, NCXC - 1, NCXC, PC - 2]] >= 0).astype(np.float32)
            hs.append(np.tile(flags[None, :], (128, 1)))
        in_maps.append({
            "xT": np.stack(xs), "mixin": np.stack(ms), "modC": modC, "vecs": vec, "hmask": np.stack(hs),
            "glu_w": np.ascontiguousarray(P["s5_glu_w"][l]), "w_out": np.ascontiguousarray(P["w_out"][l]),
            "ffn_up": np.ascontiguousarray(P["ffn_up"][l]), "ffn_down": np.ascontiguousarray(P["ffn_down"][l]),
        })
    return in_maps


def phase_c_gather(res, name):
    out = np.zeros((CTX + SEQ, D), np.float32)
    for i in range(NCORES):
        o = res[i][name]
        for s in range(NPASS):
            c0 = 32 * i + CT_OFF[s]
            l0 = 1024 * i + LT_OFF[s]
            out[c0:c0 + CT_N] = o[s][:, 1:1 + CT_N].T
            out[CTX + l0:CTX + l0 + LT_N] = o[s][:, NCXC + 1:NCXC + 1 + LT_N].T
    return out


def build_mod():
    nc = bass.Bass("TRN2", target_bir_lowering=False)
    NCOL = 6144
    w_d = nc.dram_tensor("w", [D, NCOL], F32, kind="ExternalInput").ap()
    c_d = nc.dram_tensor("cv", [128, KC, 2], F32, kind="ExternalInput").ap()
    b_d = nc.dram_tensor("b", [128, 48], F32, kind="ExternalInput").ap()
    o_d = nc.dram_tensor("mod", [128, 48, 2], F32, kind="ExternalOutput").ap()
    p = Prog(nc)
    cv = p.sb([128, KC, 2])
    bb = p.sb([128, 48])
    ob = p.sb([128, 48, 2])
    wb = [p.sb([128, KC, 512]) for _ in range(2)]
    pss = [p.ps([128, 512]) for _ in range(2)]
    cv0 = p.sb([128, KC, 2])
    p.load(cv0[:], c_d, w=["cv0"])
    p.load(bb[:], b_d, w=["b"])
    p.act(R(cv[:]), cv0[:], AF.Silu, r=["cv0"], w=["cv"])
    wv = w_d.rearrange("(k p) m -> p k m", p=128)
    for g in range(12):
        wt = wb[g % 2]
        for k in range(KC):
            p.load(R(wt[:, k, :]), R(wv[:, k, g * 512:(g + 1) * 512]), w=[("w", g % 2, k)])
        for mm_ in range(4):
            j = g * 4 + mm_
            ps = pss[j % 2]
            for k in range(KC):
                p.mm(ps[:, 0:2], R(wt[:, k, mm_ * 128:(mm_ + 1) * 128]), R(cv[:, k, :]), k == 0, k == KC - 1,
                     r=[("w", g % 2, k), "cv"], w=[("ps", j % 2)])
            p.ts(ob[:, j, :], ps[:, 0:2], bb[:, j:j + 1], None, ALU.add, None, r=[("ps", j % 2), "b"], w=["ob"])
    p.store(o_d, ob[:], r=["ob"])
    p.finish()
    p.emit()
    return nc


def run_mod(c, c_ctx, w_mod, b_mod):
    nc = build_mod()
    cvh = np.stack([chunkcols(c.reshape(-1)), chunkcols(c_ctx.reshape(-1))], 2)
    in_maps = []
    for i in range(NCORES):
        l, h = i // 2, i % 2
        in_maps.append({"w": np.ascontiguousarray(w_mod[l][:, h * 6144:(h + 1) * 6144]), "cv": cvh,
                        "b": chunkcols(b_mod[l][h * 6144:(h + 1) * 6144])})
    res = run_spmd(nc, in_maps)
    mod_x = np.zeros((DEPTH, 6 * D), np.float32)
    mod_c = np.zeros((DEPTH, 6 * D), np.float32)
    for i in range(NCORES):
        l, h = i // 2, i % 2
        o = res[i]["mod"]
        mod_x[l, h * 6144:(h + 1) * 6144] = o[:, :, 0].T.reshape(-1)
        mod_c[l, h * 6144:(h + 1) * 6144] = o[:, :, 1].T.reshape(-1)
    return mod_x, mod_c


TT = CTX + SEQ
I32 = mybir.dt.int32
TWO_PI_HI = 6.28125
TWO_PI_LO = 2.0 * np.pi - 6.28125


def trig(p, x, n, tmp, ki, cos_o, sin_o, tag):
    a, b, c = tmp
    kx = [tag + "a", tag + "b", tag + "c", tag + "k"]
    p.ts(a, x, 1.0 / (2.0 * np.pi), None, ALU.mult, None, r=[tag + "x"], w=[kx[0]])
    p.copy(ki, a, r=[kx[0]], w=[kx[3]])
    p.copy(a, ki, r=[kx[3]], w=[kx[0]])
    p.stt(b, a, -TWO_PI_HI, x, ALU.mult, ALU.add, r=[kx[0], tag + "x"], w=[kx[1]])
    p.stt(b, a, -TWO_PI_LO, b, ALU.mult, ALU.add, r=[kx[0], kx[1]], w=[kx[1]])
    p.act(a, b, AF.Sin, r=[kx[1]], w=[kx[0]], scale=0.25)
    p.ts(b, b, 0.25, float(np.pi / 2), ALU.mult, ALU.add, r=[kx[1]], w=[kx[1]])
    p.act(b, b, AF.Sin, r=[kx[1]], w=[kx[1]])
    for it in range(2):
        p.tt(c, a, b, ALU.mult, r=[kx[0], kx[1]], w=[kx[2]])
        p.tt(b, b, b, ALU.mult, r=[kx[1]], w=[kx[1]])
        p.tt(a, a, a, ALU.mult, r=[kx[0]], w=[kx[0]])
        p.tt(b, b, a, ALU.subtract, r=[kx[0], kx[1]], w=[kx[1]])
        p.ts(a, c, 2.0, None, ALU.mult, None, r=[kx[2]], w=[kx[0]])
    p.copy(cos_o, b, r=[kx[1]], w=[tag + "cos"])
    p.copy(sin_o, a, r=[kx[0]], w=[tag + "sin"])


def build_s5():
    nc = bass.Bass("TRN2", target_bir_lowering=False)
    u_d = nc.dram_tensor("u", [2, 64, TT], F32, kind="ExternalInput").ap()
    lp_d = nc.dram_tensor("lanep", [2, 2, 128, 3], F32, kind="ExternalInput").ap()
    bre_d = nc.dram_tensor("bre", [2, 2, 128, 16], F32, kind="ExternalInput").ap()
    bim_d = nc.dram_tensor("bim", [2, 2, 128, 16], F32, kind="ExternalInput").ap()
    cre_d = nc.dram_tensor("cre", [2, 2, 128, 64], F32, kind="ExternalInput").ap()
    cim_d = nc.dram_tensor("cim", [2, 2, 128, 64], F32, kind="ExternalInput").ap()
    tau_d = nc.dram_tensor("tau1", [128, 128], F32, kind="ExternalInput").ap()
    id_d = nc.dram_tensor("ident", [128, 128], F32, kind="ExternalInput").ap()
    y_d = nc.dram_tensor("y", [2, 64, TT], F32, kind="ExternalOutput").ap()
    p = Prog(nc)
    BL = 512
    u = p.sb([64, TT])
    yb = [p.sb([64, BL]) for _ in range(2)]
    tau = p.sb([128, 128])
    ident = p.sb([128, 128])
    lp = p.sb([128, 3])
    sc = p.sb([128, 16])
    braw = p.sb([128, 2, 16])
    bbf = p.sb([128, 2, 64])
    bbT = [[p.sb([64, 2, 128]) for _ in range(2)] for _ in range(2)]
    cc = [[p.sb([128, 2, 64]) for _ in range(2)] for _ in range(2)]
    cosT = [[p.sb([128, BL]) for _ in range(2)] for _ in range(2)]
    sinT = [[p.sb([128, BL]) for _ in range(2)] for _ in range(2)]
    rhoT = [[p.sb([128, 128]) for _ in range(2)] for _ in range(2)]
    cq = [[p.sb([128, 2]) for _ in range(2)] for _ in range(2)]
    tmp = [p.sb([128, 128]) for _ in range(3)]
    tki = p.sb([128, 128], I32)
    ang = p.sb([128, 128])
    b_re = p.sb([128, BL]); b_im = p.sb([128, BL])
    v_re = p.sb([128, BL]); v_im = p.sb([128, BL])
    m1 = p.sb([128, BL]); m2 = p.sb([128, BL])
    w_re = p.sb([128, BL]); w_im = p.sb([128, BL])
    s_re = p.sb([128, BL]); s_im = p.sb([128, BL])
    car = [p.sb([128, 2]) for _ in range(2)]
    ct = p.sb([128, 2])
    psb = [p.ps([128, 512]) for _ in range(4)]
    psy = [p.ps([128, 512]) for _ in range(2)]
    pst = p.ps([128, 512])

    p.load(tau[:], tau_d, w=["tau"])
    p.load(ident[:], id_d, w=["ident"])
    for d in range(2):
        for t in range(2):
            tg = f"s{d}{t}"
            p.load(lp[:], lp_d[d, t], w=["lp"])
            p.load(braw[:, 0, :], bre_d[d, t], w=["braw"])
            p.load(braw[:, 1, :], bim_d[d, t], w=["braw"])
            p.load(R(cc[d][t][:, 0, :]), R(cre_d[d, t]), w=[("cc", d, t)])
            p.load(R(cc[d][t][:, 1, :]), R(cim_d[d, t]), w=[("cc", d, t)])
            p.act(sc[:, 0:1], lp[:, 2:3], AF.Exp, r=["lp"], w=["sc"])
            p.tt(sc[:, 1:2], lp[:, 0:1], sc[:, 0:1], ALU.mult, r=["lp", "sc"], w=["sc"])
            p.act(sc[:, 2:3], sc[:, 1:2], AF.Exp, r=["sc"], w=["sc"])
            p.tt(sc[:, 3:4], lp[:, 1:2], sc[:, 0:1], ALU.mult, r=["lp", "sc"], w=["sc"])
            p.ts(ang[:], tau[:], sc[:, 3:4], None, ALU.mult, None, r=["tau", "sc"], w=[tg + "x"])
            trig(p, ang[:], 128, [tmp[0][:], tmp[1][:], tmp[2][:]], tki[:], cosT[d][t][:, 0:128], sinT[d][t][:, 0:128], tg)
            for rep in range(1, 4):
                p.copy(cosT[d][t][:, rep * 128:(rep + 1) * 128], cosT[d][t][:, 0:128], r=[tg + "cos"], w=[tg + "cos"], e="pool")
                p.copy(sinT[d][t][:, rep * 128:(rep + 1) * 128], sinT[d][t][:, 0:128], r=[tg + "sin"], w=[tg + "sin"], e="pool")
            p.copy(cq[d][t][:, 0:1], cosT[d][t][:, 127:128], r=[tg + "cos"], w=[("cq", d, t)])
            p.copy(cq[d][t][:, 1:2], sinT[d][t][:, 127:128], r=[tg + "sin"], w=[("cq", d, t)])
            p.memset(rhoT[d][t][:], 1.0, w=[("rho", d, t)])
            p.ts(rhoT[d][t][:], rhoT[d][t][:], sc[:, 2:3], None, ALU.mult, None, r=["sc", ("rho", d, t)], w=[("rho", d, t)])
            p.tt(sc[:, 4:5], sc[:, 2:3], cosT[d][t][:, 0:1], ALU.mult, r=["sc", tg + "cos"], w=["sc"])
            p.ts(sc[:, 4:5], sc[:, 4:5], -1.0, None, ALU.add, None, r=["sc"], w=["sc"])
            p.tt(sc[:, 5:6], sc[:, 2:3], sinT[d][t][:, 0:1], ALU.mult, r=["sc", tg + "sin"], w=["sc"])
            p.tt(sc[:, 6:7], lp[:, 0:1], lp[:, 0:1], ALU.mult, r=["lp"], w=["sc"])
            p.tt(sc[:, 9:10], lp[:, 1:2], lp[:, 1:2], ALU.mult, r=["lp"], w=["sc"])
            p.tt(sc[:, 6:7], sc[:, 6:7], sc[:, 9:10], ALU.add, r=["sc"], w=["sc"])
            p.op("dve", lambda eng: eng.reciprocal(sc[:, 6:7], sc[:, 6:7]), r=["sc"], w=["sc"])
            p.tt(sc[:, 7:8], sc[:, 4:5], lp[:, 0:1], ALU.mult, r=["sc", "lp"], w=["sc"])
            p.tt(sc[:, 9:10], sc[:, 5:6], lp[:, 1:2], ALU.mult, r=["sc", "lp"], w=["sc"])
            p.tt(sc[:, 7:8], sc[:, 7:8], sc[:, 9:10], ALU.add, r=["sc"], w=["sc"])
            p.tt(sc[:, 7:8], sc[:, 7:8], sc[:, 6:7], ALU.mult, r=["sc"], w=["sc"])
            p.tt(sc[:, 8:9], sc[:, 5:6], lp[:, 0:1], ALU.mult, r=["sc", "lp"], w=["sc"])
            p.tt(sc[:, 9:10], sc[:, 4:5], lp[:, 1:2], ALU.mult, r=["sc", "lp"], w=["sc"])
            p.tt(sc[:, 8:9], sc[:, 8:9], sc[:, 9:10], ALU.subtract, r=["sc"], w=["sc"])
            p.tt(sc[:, 8:9], sc[:, 8:9], sc[:, 6:7], ALU.mult, r=["sc"], w=["sc"])
            p.memset(bbf[:], 0.0, w=["bbf"])
            for half in range(2):
                rows = slice(64 * half, 64 * half + 64)
                co = 16 * (2 * t + half)
                p.ts(bbf[rows, 0, co:co + 16], braw[rows, 1, :], sc[rows, 8:9], -1.0, ALU.mult, ALU.mult,
                     r=["braw", "sc"], w=["bbf"])
                p.stt(bbf[rows, 0, co:co + 16], braw[rows, 0, :], sc[rows, 7:8], bbf[rows, 0, co:co + 16], ALU.mult, ALU.add,
                      r=["braw", "sc", "bbf"], w=["bbf"])
                p.ts(bbf[rows, 1, co:co + 16], braw[rows, 0, :], sc[rows, 8:9], None, ALU.mult, None,
                     r=["braw", "sc"], w=["bbf"])
                p.stt(bbf[rows, 1, co:co + 16], braw[rows, 1, :], sc[rows, 7:8], bbf[rows, 1, co:co + 16], ALU.mult, ALU.add,
                      r=["braw", "sc", "bbf"], w=["bbf"])
            for c2 in range(2):
                p.op("pe", lambda eng, o=pst[0:64, c2 * 128:(c2 + 1) * 128], i_=bbf[:, c2, :]: eng.transpose(o, i_, ident[:]),
                     r=["bbf", "ident"], w=["pst"])
            p.copy(R(bbT[d][t][:, :, :]), pst[0:64, 0:256].rearrange("p (a m) -> p a m", a=2), r=["pst"], w=[("bbT", d, t)], e="act")
            p.ts(R(cc[d][t][:, 1, :]), cc[d][t][:, 1, :], -1.0, None, ALU.mult, None, r=[("cc", d, t)], w=[("cc", d, t)])
            p.copy(R(cc[d][t][:, 0, :]), cc[d][t][:, 0, :], r=[("cc", d, t)], w=[("cc", d, t)])
    nblk = (TT + BL - 1) // BL
    for d in range(2):
        for k0 in range(0, TT, 2112):
            p.load(R(u[:, k0:k0 + 2112]), R(u_d[d][:, k0:k0 + 2112]), w=[("u", k0)])
        for t in range(2):
            p.memset(car[t][:], 0.0, w=[("car", t)])
        for bi in range(nblk):
            c0 = bi * BL
            cn = min(BL, TT - c0)
            uk = ("u", (c0 // 2112) * 2112)
            py = psy[bi % 2]
            pyk = ("psy", bi % 2)
            for t in range(2):
                pr, pim = psb[2 * t], psb[2 * t + 1]
                p.mm(pr[:, 0:cn], R(bbT[d][t][:, 0, :]), R(u[:, c0:c0 + cn]), True, True, r=[("bbT", d, t), uk], w=[("psb", 2 * t)])
                p.mm(pim[:, 0:cn], R(bbT[d][t][:, 1, :]), R(u[:, c0:c0 + cn]), True, True, r=[("bbT", d, t), uk], w=[("psb", 2 * t + 1)])
                p.copy(b_re[:, 0:cn], pr[:, 0:cn], r=[("psb", 2 * t)], w=["b_re"], e="act")
                p.copy(b_im[:, 0:cn], pim[:, 0:cn], r=[("psb", 2 * t + 1)], w=["b_im"], e="act")
                C_, S_ = cosT[d][t], sinT[d][t]
                tgc, tgs = f"s{d}{t}cos", f"s{d}{t}sin"
                p.tt(m1[:, 0:cn], b_re[:, 0:cn], C_[:, 0:cn], ALU.mult, r=["b_re", tgc], w=["m1"])
                p.tt(m2[:, 0:cn], b_im[:, 0:cn], S_[:, 0:cn], ALU.mult, r=["b_im", tgs], w=["m2"])
                p.tt(v_re[:, 0:cn], m1[:, 0:cn], m2[:, 0:cn], ALU.add, r=["m1", "m2"], w=["v_re"])
                p.tt(m1[:, 0:cn], b_im[:, 0:cn], C_[:, 0:cn], ALU.mult, r=["b_im", tgc], w=["m1"])
                p.tt(m2[:, 0:cn], b_re[:, 0:cn], S_[:, 0:cn], ALU.mult, r=["b_re", tgs], w=["m2"])
                p.tt(v_im[:, 0:cn], m1[:, 0:cn], m2[:, 0:cn], ALU.subtract, r=["m1", "m2"], w=["v_im"])
                for q0 in range(0, cn, 128):
                    sl = slice(q0, q0 + 128)
                    p.op("dve", lambda eng, o=w_re[:, sl], a=rhoT[d][t][:], b=v_re[:, sl], i_=car[t][:, 0:1]:
                         eng.tensor_tensor_scan(o, a, b, i_, ALU.mult, ALU.add),
                         r=[("rho", d, t), "v_re", ("car", t)], w=["w_re"])
                    p.op("dve", lambda eng, o=w_im[:, sl], a=rhoT[d][t][:], b=v_im[:, sl], i_=car[t][:, 1:2]:
                         eng.tensor_tensor_scan(o, a, b, i_, ALU.mult, ALU.add),
                         r=[("rho", d, t), "v_im", ("car", t)], w=["w_im"])
                    last = q0 + 127
                    cqt = cq[d][t]
                    p.ts(ct[:, 0:1], w_im[:, last:last + 1], cqt[:, 1:2], None, ALU.mult, None, r=["w_im", ("cq", d, t)], w=["ct"])
                    p.ts(ct[:, 1:2], w_re[:, last:last + 1], cqt[:, 1:2], None, ALU.mult, None, r=["w_re", ("cq", d, t)], w=["ct"])
                    p.stt(car[t][:, 0:1], w_re[:, last:last + 1], cqt[:, 0:1], ct[:, 0:1], ALU.mult, ALU.subtract,
                          r=["w_re", ("cq", d, t), "ct"], w=[("car", t)])
                    p.stt(car[t][:, 1:2], w_im[:, last:last + 1], cqt[:, 0:1], ct[:, 1:2], ALU.mult, ALU.add,
                          r=["w_im", ("cq", d, t), "ct"], w=[("car", t)])
                p.tt(m1[:, 0:cn], w_re[:, 0:cn], C_[:, 0:cn], ALU.mult, r=["w_re", tgc], w=["m1"])
                p.tt(m2[:, 0:cn], w_im[:, 0:cn], S_[:, 0:cn], ALU.mult, r=["w_im", tgs], w=["m2"])
                p.tt(R(s_re[:, 0:cn]), m1[:, 0:cn], m2[:, 0:cn], ALU.subtract, r=["m1", "m2"], w=["s_re"])
                p.tt(m1[:, 0:cn], w_im[:, 0:cn], C_[:, 0:cn], ALU.mult, r=["w_im", tgc], w=["m1"])
                p.tt(m2[:, 0:cn], w_re[:, 0:cn], S_[:, 0:cn], ALU.mult, r=["w_re", tgs], w=["m2"])
                p.tt(R(s_im[:, 0:cn]), m1[:, 0:cn], m2[:, 0:cn], ALU.add, r=["m1", "m2"], w=["s_im"])
                p.mm(py[0:64, 0:cn], R(cc[d][t][:, 0, :]), R(s_re[:, 0:cn]), t == 0, False, r=[("cc", d, t), "s_re"], w=[pyk])
                p.mm(py[0:64, 0:cn], R(cc[d][t][:, 1, :]), R(s_im[:, 0:cn]), False, t == 1, r=[("cc", d, t), "s_im"], w=[pyk])
            ybt = yb[bi % 2]
            p.copy(ybt[:, 0:cn], py[0:64, 0:cn], r=[pyk], w=[("yb", bi % 2)], e="act")
            p.store(y_d[d][:, c0:c0 + cn], ybt[:, 0:cn], r=[("yb", bi % 2)])
    p.finish()
    p.emit()
    return nc


def s5_inputs(l, u_full, P):
    tau1 = np.tile(np.arange(1, 129, dtype=np.float32)[None, :], (128, 1))
    ident = np.eye(128, dtype=np.float32)
    in_maps = []
    for i in range(NCORES):
        uc = u_full[:, 64 * i:64 * i + 64]
        uf = uc.T
        ub = np.concatenate([uc[:CTX][::-1], uc[CTX:][::-1]], 0).T
        lanep = np.zeros((2, 2, 128, 3), np.float32)
        bre = np.zeros((2, 2, 128, 16), np.float32)
        bim = np.zeros((2, 2, 128, 16), np.float32)
        cre = np.zeros((2, 2, 128, 64), np.float32)
        cim = np.zeros((2, 2, 128, 64), np.float32)
        for d in range(2):
            for t in range(2):
                for h in range(2):
                    gl = 2 * t + h
                    g = 4 * i + gl
                    rows = slice(64 * h, 64 * h + 64)
                    lanep[d, t, rows, 0] = P["s5_lam_re"][l, d, g]
                    lanep[d, t, rows, 1] = P["s5_lam_im"][l, d, g]
                    lanep[d, t, rows, 2] = P["s5_log_step"][l, d, g]
                    bre[d, t, rows] = P["s5_b_re"][l, d, g]
                    bim[d, t, rows] = P["s5_b_im"][l, d, g]
                    cre[d, t, rows, 16 * gl:16 * gl + 16] = P["s5_c_re"][l, d, g].T
                    cim[d, t, rows, 16 * gl:16 * gl + 16] = P["s5_c_im"][l, d, g].T
        in_maps.append({"u": np.ascontiguousarray(np.stack([uf, ub])), "lanep": lanep, "bre": bre, "bim": bim,
                        "cre": cre, "cim": cim, "tau1": tau1, "ident": ident})
    return in_maps


def unflip(y):
    yt = y.T
    return np.concatenate([yt[:CTX][::-1], yt[CTX:][::-1]], 0)


def s5_gather(res):
    yf = np.concatenate([res[i]["y"][0].T for i in range(NCORES)], 1)
    ybk = np.concatenate([unflip(res[i]["y"][1]) for i in range(NCORES)], 1)
    return yf, ybk


NEG = -30000.0


def na_cls(b):
    return 0 if b == 0 else 1 if b == 1 else 3 if b == 62 else 4 if b == 63 else 2


def na_kr0(b):
    return min(max(2 * b - 4, 0), 119)


def build_na():
    nc = bass.Bass("TRN2", target_bir_lowering=False)
    q_d = nc.dram_tensor("qT", [64, TT], F32, kind="ExternalInput").ap()
    k_d = nc.dram_tensor("kT", [64, TT], F32, kind="ExternalInput").ap()
    v_d = nc.dram_tensor("vt", [64, 132, 64], F32, kind="ExternalInput").ap()
    b_d = nc.dram_tensor("bias", [5, 128, 576], F32, kind="ExternalInput").ap()
    id_d = nc.dram_tensor("ident", [128, 128], F32, kind="ExternalInput").ap()
    o_d = nc.dram_tensor("oT", [64, TT], F32, kind="ExternalOutput").ap()
    p = Prog(nc)
    qT = p.sb([64, TT]); kT = p.sb([64, TT]); vt = p.sb([64, 132, 64]); oT = p.sb([64, TT])
    bias = p.sb([128, 5, 576])
    ident = p.sb([128, 128])
    S = p.sb([128, 832]); Pm = p.sb([128, 832])
    PTs = p.sb([64, 13, 128])
    dg = p.sb([128, 128])
    st = p.sb([128, 4])
    psA = p.ps([128, 512]); psB = p.ps([128, 512]); psC = p.ps([128, 512])
    psT = [p.ps([128, 512]) for _ in range(4)]
    pso = p.ps([128, 512])
    for k0 in range(0, TT, 2112):
        p.load(R(qT[:, k0:k0 + 2112]), R(q_d[:, k0:k0 + 2112]), w=["q"])
        p.load(R(kT[:, k0:k0 + 2112]), R(k_d[:, k0:k0 + 2112]), w=["k"])
    for r0 in range(0, 132, 33):
        p.load(R(vt[:, r0:r0 + 33, :]), R(v_d[:, r0:r0 + 33, :]), w=["v"])
    for c in range(5):
        p.load(bias[:, c, :], b_d[c], w=["bias"])
    p.load(ident[:], id_d, w=["ident"])

    def block(qc0, lat, b):
        nk = 832 if lat else 256
        if lat:
            kr0 = na_kr0(b)
            kc0 = CTX + 64 * kr0
            cls = na_cls(b)
            p.mm(psA[:, 0:288], R(qT[:, qc0:qc0 + 128]), R(kT[:, kc0:kc0 + 288]), True, True, r=["q", "k"], w=["psA"])
            p.mm(psB[:, 0:288], R(qT[:, qc0:qc0 + 128]), R(kT[:, kc0 + 288:kc0 + 576]), True, True, r=["q", "k"], w=["psB"])
            p.mm(psC[:, 0:256], R(qT[:, qc0:qc0 + 128]), R(kT[:, 0:256]), True, True, r=["q", "k"], w=["psC"])
            p.stt(S[:, 0:288], psA[:, 0:288], 0.125, bias[:, cls, 0:288], ALU.mult, ALU.add, r=["psA", "bias"], w=["S"])
            p.stt(S[:, 288:576], psB[:, 0:288], 0.125, bias[:, cls, 288:576], ALU.mult, ALU.add, r=["psB", "bias"], w=["S"])
            p.act(S[:, 576:832], psC[:, 0:256], AF.Copy, r=["psC"], w=["S"], scale=0.125)
        else:
            p.mm(psC[:, 0:256], R(qT[:, qc0:qc0 + 128]), R(kT[:, 0:256]), True, True, r=["q", "k"], w=["psC"])
            p.act(S[:, 0:256], psC[:, 0:256], AF.Copy, r=["psC"], w=["S"], scale=0.125)
        p.op("dve", lambda eng: eng.reduce_max(st[:, 0:1], S[:, 0:nk], AX.X), r=["S"], w=["st"])
        p.ts(st[:, 1:2], st[:, 0:1], -1.0, None, ALU.mult, None, r=["st"], w=["st"])
        p.act(R(Pm[:, 0:nk]), S[:, 0:nk], AF.Exp, r=["S", "st"], w=["P", "st2"], bias=st[:, 1:2], accum_out=st[:, 2:3])
        p.op("dve", lambda eng: eng.reciprocal(st[:, 3:4], st[:, 2:3]), r=["st2", "P"], w=["st3"])
        p.ts(R(dg[:]), ident[:], st[:, 3:4], None, ALU.mult, None, r=["ident", "st3"], w=["dg"])
        nt = nk // 64
        for kt in range(nt):
            bank = psT[kt // 4]
            p.mm(bank[0:64, (kt % 4) * 128:(kt % 4) * 128 + 128], R(Pm[:, kt * 64:(kt + 1) * 64]), R(dg[:]), True, True,
                 r=["P", "dg"], w=[("psT", kt // 4)])
        for bk in range((nt + 3) // 4):
            n4 = min(4, nt - 4 * bk)
            p.copy(R(PTs[:, 4 * bk:4 * bk + n4, :]), psT[bk][0:64, 0:n4 * 128].rearrange("p (a m) -> p a m", a=n4),
                   r=[("psT", bk)], w=["PTs"], e="act" if bk % 2 == 0 else "dve")
        for kt in range(nt):
            if lat:
                row = 4 + kr0 + kt if kt < 9 else kt - 9
            else:
                row = kt
            p.mm(pso[0:64, 0:128], R(vt[:, row, :]), R(PTs[:, kt, :]), kt == 0, kt == nt - 1, r=["v", "PTs"], w=["pso"])
        p.copy(oT[:, qc0:qc0 + 128], pso[0:64, 0:128], r=["pso"], w=["oT"], e="pool" if False else "act")

    for cb in range(2):
        block(128 * cb, False, cb)
    for b in range(64):
        block(CTX + 128 * b, True, b)
    for k0 in range(0, TT, 2112):
        p.store(o_d[:, k0:k0 + 2112], oT[:, k0:k0 + 2112], r=["oT"])
    p.finish()
    p.emit()
    return nc


def na_bias_tables(rpb_h):
    out = np.full((5, 128, 576), NEG, np.float32)
    for ci, b in enumerate((0, 1, 2, 62, 63)):
        kr0 = na_kr0(b)
        for q in range(128):
            r = 2 * b + q // 64
            c = q % 64
            rs = min(max(r - 4, 0), 120)
            cs = min(max(c - 8, 0), 48)
            for kr in range(rs, rs + 8):
                sl = (kr - kr0) * 64
                out[ci, q, sl + cs:sl + cs + 16] = rpb_h[kr - r + 7, cs - c + 15:cs - c + 31]
    return out


def na_inputs(l, q_full, k_full, v_full, P):
    ident = np.eye(128, dtype=np.float32)
    in_maps = []
    for i in range(NCORES):
        sl = slice(64 * i, 64 * i + 64)
        vt = v_full[:, sl].reshape(132, 64, 64).transpose(1, 0, 2)
        in_maps.append({"qT": np.ascontiguousarray(q_full[:, sl].T), "kT": np.ascontiguousarray(k_full[:, sl].T),
                        "vt": np.ascontiguousarray(vt), "bias": na_bias_tables(P["na_rpb"][l, i]), "ident": ident})
    return in_maps


def na_gather(res):
    return np.concatenate([res[i]["oT"].T for i in range(NCORES)], 1)


NCH = TT // 128


def build_ssd(nch=NCH, do_conv=True, do_setup=True):
    nc = bass.Bass("TRN2", target_bir_lowering=False)
    x_d = nc.dram_tensor("xbc", [2, 3, 128, TT], F32, kind="ExternalInput").ap()
    cw_d = nc.dram_tensor("cw", [2, 128, 3, 4], F32, kind="ExternalInput").ap()
    dt_d = nc.dram_tensor("dtm", [2, 128, NCH], F32, kind="ExternalInput").ap()
    sc_d = nc.dram_tensor("scal", [2, 128, 4], F32, kind="ExternalInput").ap()
    tri_d = nc.dram_tensor("tri", [128, 128], F32, kind="ExternalInput").ap()
    nm_d = nc.dram_tensor("negmask", [128, 128], F32, kind="ExternalInput").ap()
    id_d = nc.dram_tensor("ident", [128, 128], F32, kind="ExternalInput").ap()
    y_d = nc.dram_tensor("y", [2, 128, NCH, 64], F32, kind="ExternalOutput").ap()
    p = Prog(nc)
    raw = p.sb([128, TT])
    cv = [p.sb([128, TT]) for _ in range(3)]
    cw = p.sb([128, 3, 4]); scal = p.sb([128, 4])
    tri = p.sb([128, 128]); nm = p.sb([128, 128]); ident = p.sb([128, 128]); ones = p.sb([128, 128])
    zt = p.sb([128, 64])
    dt = p.sb([128, NCH]); dta = p.sb([128, NCH]); tA = p.sb([128, NCH]); tB = p.sb([128, NCH])
    nacs = p.sb([128, NCH]); wdec = p.sb([128, NCH]); dec = p.sb([128, NCH])
    ybuf = p.sb([128, NCH, 64])
    xdt = p.sb([128, 64]); Bw = p.sb([128, 128]); dtab = p.sb([128, 128])
    E = p.sb([128, 128]); CE = p.sb([128, 128]); Rm = p.sb([128, 128]); LT = p.sb([128, 128]); MT = p.sb([128, 128])
    hT = p.sb([128, 64])
    ps_x = p.ps([128, 512]); ps_B = p.ps([128, 512]); ps_R = p.ps([128, 512]); ps_CB = p.ps([128, 512])
    ps_y = p.ps([128, 512]); ps_h = p.ps([128, 512]); ps_s = p.ps([128, 512])
    p.load(tri[:], tri_d, w=["tri"]); p.load(nm[:], nm_d, w=["nm"]); p.load(ident[:], id_d, w=["ident"])
    p.memset(ones[:], 1.0, w=["ones"]); p.memset(zt[:], 0.0, w=["zt"])
    segs = [(0, CTX), (CTX, TT)]
    for d in range(2):
        p.load(cw[:], cw_d[d], w=["cw"]); p.load(scal[:], sc_d[d], w=["scal"]); p.load(dt[:], dt_d[d], w=["dt"])
        p.ts(dt[:], dt[:], scal[:, 0:1], None, ALU.add, None, r=["dt", "scal"], w=["dt"])
        p.ts(tA[:], dt[:], 0.0, None, ALU.max, None, r=["dt"], w=["tA"])
        p.ts(tB[:], dt[:], 0.0, None, ALU.min, None, r=["dt"], w=["tB"])
        p.tt(tB[:], tB[:], tA[:], ALU.subtract, r=["tA", "tB"], w=["tB"])
        p.act(tB[:], tB[:], AF.Exp, r=["tB"], w=["tB"])
        p.act(tB[:], tB[:], AF.Ln, r=["tB"], w=["tB"], bias=1.0)
        p.tt(dt[:], tA[:], tB[:], ALU.add, r=["tA", "tB"], w=["dt"])
        p.act(scal[:, 3:4], scal[:, 1:2], AF.Exp, r=["scal"], w=["scal"])
        p.ts(dta[:], dt[:], scal[:, 3:4], -1.0, ALU.mult, ALU.mult, r=["dt", "scal"], w=["dta"])
        p.mm(ps_s[:, 0:NCH], tri[:], dta[:], True, True, r=["tri", "dta"], w=["ps_s"])
        p.ts(nacs[:], ps_s[:, 0:NCH], -1.0, None, ALU.mult, None, r=["ps_s"], w=["nacs"])
        p.mm(ps_s[:, 0:NCH], ones[:], dta[:], True, True, r=["ones", "dta", "nacs"], w=["ps_s"])
        p.tt(wdec[:], ps_s[:, 0:NCH], nacs[:], ALU.add, r=["ps_s", "nacs"], w=["wdec"])
        p.act(wdec[:], wdec[:], AF.Exp, r=["wdec"], w=["wdec"])
        p.act(dec[:], ps_s[:, 0:NCH], AF.Exp, r=["ps_s"], w=["dec"])
        for ch in range(3 if do_conv else 0):
            np_ = 64 if ch == 0 else 128
            for k0 in range(0, TT, 2112):
                p.load(raw[0:np_, k0:k0 + 2112], x_d[d, ch, 0:np_, k0:k0 + 2112], w=["raw"])
            o = cv[ch]
            for (a, b) in segs:
                p.ts(R(o[0:np_, a:b]), raw[0:np_, a:b], cw[0:np_, ch, 1:2], cw[0:np_, ch, 3:4], ALU.mult, ALU.add,
                     r=["raw", "cw"], w=[("cv", ch)])
                p.stt(R(o[0:np_, a + 1:b]), raw[0:np_, a:b - 1], cw[0:np_, ch, 0:1], o[0:np_, a + 1:b], ALU.mult, ALU.add,
                      r=["raw", "cw", ("cv", ch)], w=[("cv", ch)])
                p.stt(R(o[0:np_, a:b - 1]), raw[0:np_, a + 1:b], cw[0:np_, ch, 2:3], o[0:np_, a:b - 1], ALU.mult, ALU.add,
                      r=["raw", "cw", ("cv", ch)], w=[("cv", ch)])
            p.act(R(o[0:np_, :]), o[0:np_, :], AF.Silu, r=[("cv", ch)], w=[("cv", ch)])
        xs, Bm, Cm = cv
        p.copy(R(hT[:]), zt[:], r=["zt"], w=["hT"])
        for c in range(nch):
            cols = slice(128 * c, 128 * c + 128)
            p.op("pe", lambda eng, i_=xs[0:64, cols]: eng.transpose(ps_x[:, 0:64], i_, ident[0:64, 0:64]),
                 r=[("cv", 0), "ident"], w=["ps_x"])
            p.op("pe", lambda eng, i_=Bm[:, cols]: eng.transpose(ps_B[:, 0:128], i_, ident[:]),
                 r=[("cv", 1), "ident"], w=["ps_B"])
            p.ts(R(xdt[:]), ps_x[:, 0:64], dt[:, c:c + 1], None, ALU.mult, None, r=["ps_x", "dt"], w=["xdt"])
            p.ts(R(Bw[:]), ps_B[:, 0:128], wdec[:, c:c + 1], None, ALU.mult, None, r=["ps_B", "wdec"], w=["Bw"])
            p.ts(dtab[:], ones[:], dta[:, c:c + 1], None, ALU.mult, None, r=["ones", "dta"], w=["dtab"])
            p.mm(ps_R[:, 0:128], dtab[:], tri[:], True, True, r=["dtab", "tri"], w=["ps_R"])
            p.act(E[:], ps_R[:, 0:128], AF.Exp, r=["ps_R"], w=["E"])
            p.tt(R(CE[:]), Cm[:, cols], E[:], ALU.mult, r=[("cv", 2), "E"], w=["CE"])
            p.tt(Rm[:], ps_R[:, 0:128], nm[:], ALU.add, r=["ps_R", "nm"], w=["Rm"])
            p.act(LT[:], Rm[:], AF.Exp, r=["Rm", "nacs"], w=["LT"], bias=nacs[:, c:c + 1])
            p.mm(ps_CB[:, 0:128], R(Bm[:, cols]), R(Cm[:, cols]), True, True, r=[("cv", 1), ("cv", 2)], w=["ps_CB"])
            p.tt(R(MT[:]), ps_CB[:, 0:128], LT[:], ALU.mult, r=["ps_CB", "LT"], w=["MT"])
            p.mm(ps_y[:, 0:64], R(MT[:]), R(xdt[:]), True, False, r=["MT", "xdt"], w=["ps_y"])
            p.mm(ps_y[:, 0:64], R(CE[:]), R(hT[:]), False, True, r=["CE", "hT"], w=["ps_y"])
            p.copy(ybuf[:, c, :], ps_y[:, 0:64], r=["ps_y"], w=[("yb", c)], e="act")
            if d == 0:
                p.stt(ybuf[:, c, :], ps_x[:, 0:64], scal[:, 2:3], ybuf[:, c, :], ALU.mult, ALU.add,
                      r=["ps_x", "scal", ("yb", c)], w=[("yb", c)])
            p.mm(ps_h[:, 0:64], R(Bw[:]), R(xdt[:]), True, True, r=["Bw", "xdt"], w=["ps_h"])
            p.ts(R(hT[:]), hT[:], dec[:, c:c + 1], None, ALU.mult, None, r=["hT", "dec"], w=["hT"])
            p.tt(R(hT[:]), hT[:], ps_h[:, 0:64], ALU.add, r=["hT", "ps_h"], w=["hT"])
        p.store(y_d[d], ybuf[:], r=[("yb", c) for c in range(NCH)])
    p.finish()
    p.emit()
    return nc


def flipseq(a):
    return np.concatenate([a[:CTX][::-1], a[CTX:][::-1]], 0)


def ssd_inputs(l, xbc_full, dt_full, P):
    jj = np.arange(128)
    tri = (jj[:, None] <= jj[None, :]).astype(np.float32)
    negmask = np.where(jj[:, None] <= jj[None, :], 0.0, NEG).astype(np.float32)
    ident = np.eye(128, dtype=np.float32)
    cwl, cbl = P["ssd_conv_w"][l], P["ssd_conv_b"][l]
    in_maps = []
    for i in range(NCORES):
        g = i // 4
        colsets = [np.arange(64 * i, 64 * i + 64), 512 + np.arange(128 * g, 128 * g + 128),
                   768 + np.arange(128 * g, 128 * g + 128)]
        xbc = np.zeros((2, 3, 128, TT), np.float32)
        cw = np.zeros((2, 128, 3, 4), np.float32)
        dtm = np.zeros((2, 128, NCH), np.float32)
        scal = np.zeros((2, 128, 4), np.float32)
        for d in range(2):
            for ch, cs in enumerate(colsets):
                a = xbc_full[:, cs]
                if d == 1:
                    a = flipseq(a)
                xbc[d, ch, :len(cs)] = a.T
                taps = cwl[:, cs] if d == 0 else cwl[::-1][:, cs]
                cw[d, :len(cs), ch, 0:3] = taps.T
                cw[d, :len(cs), ch, 3] = cbl[cs]
            dcol = dt_full[:, d * 8 + i]
            if d == 1:
                dcol = flipseq(dcol[:, None])[:, 0]
            dtm[d] = dcol.reshape(NCH, 128).T
            scal[d, :, 0] = P["ssd_dt_bias"][l, d, i]
            scal[d, :, 1] = P["ssd_a_log"][l, d, i]
            scal[d, :, 2] = P["ssd_d"][l, i]
        in_maps.append({"xbc": xbc, "cw": cw, "dtm": dtm, "scal": scal, "tri": tri, "negmask": negmask, "ident": ident})
    return in_maps


def tm_to_nat(y, rev):
    a = y.transpose(1, 0, 2).reshape(TT, -1)
    if rev:
        a = np.concatenate([a[:CTX][::-1], a[CTX:][::-1]], 0)
    return a


def ssd_gather(res):
    yf = np.concatenate([tm_to_nat(res[i]["y"][0], False) for i in range(NCORES)], 1)
    yb = np.concatenate([tm_to_nat(res[i]["y"][1], True) for i in range(NCORES)], 1)
    return yf, yb


NCG = TT // 64
LNK = float(np.log(64.0 ** -0.5))


def build_gla():
    nc = bass.Bass("TRN2", target_bir_lowering=False)
    qk_d = nc.dram_tensor("qk", [2, 2, 64, TT], F32, kind="ExternalInput").ap()
    v_d = nc.dram_tensor("vtm", [2, 64, NCG, 64], F32, kind="ExternalInput").ap()
    g_d = nc.dram_tensor("gT", [2, 16, TT], F32, kind="ExternalInput").ap()
    gw_d = nc.dram_tensor("gw", [2, 16, 64], F32, kind="ExternalInput").ap()
    gb_d = nc.dram_tensor("gb", [2, 64, 1], F32, kind="ExternalInput").ap()
    cs_d = nc.dram_tensor("cs", [2, 2, 64, TT], F32, kind="ExternalInput").ap()
    rot_d = nc.dram_tensor("rot", [64, 64], F32, kind="ExternalInput").ap()
    cm_d = nc.dram_tensor("cmask", [64, TT], F32, kind="ExternalInput").ap()
    um_d = nc.dram_tensor("umask", [64, 64], F32, kind="ExternalInput").ap()
    id_d = nc.dram_tensor("ident", [128, 128], F32, kind="ExternalInput").ap()
    o_d = nc.dram_tensor("o", [2, 64, NCG, 64], F32, kind="ExternalOutput").ap()
    p = Prog(nc)
    BL = 512
    Q = p.sb([64, TT]); Kt = p.sb([64, TT]); LA = p.sb([64, TT]); Bt = p.sb([64, TT])
    vtm = p.sb([64, NCG, 64])
    gT = vtm[:, :, :].rearrange("p c v -> p (c v)")[0:16, :]
    gw = p.sb([16, 64]); gb = p.sb([64, 1]); ngb = p.sb([64, 1])
    rot = p.sb([64, 64]); um = p.sb([64, 64]); ident = p.sb([128, 128])
    csb = [p.sb([64, 2, BL]) for _ in range(2)]
    t1 = p.sb([64, BL]); t2 = p.sb([64, BL])
    ebl = p.sb([64, NCG])
    attT = p.sb([64, 64]); ktm = p.sb([64, 64]); S = p.sb([64, 64]); zt = p.sb([64, 64])
    obuf = [p.sb([64, 8, 64]) for _ in range(2)]
    ps_l = p.ps([128, 512]); ps_r = p.ps([128, 512])
    ps_a = p.ps([128, 512]); ps_t = p.ps([128, 512]); ps_o = p.ps([128, 512]); ps_kv = p.ps([128, 512])
    p.load(R(rot[:]), R(rot_d), w=["rot"]); p.load(um[:], um_d, w=["um"]); p.load(ident[:], id_d, w=["ident"])
    p.memset(zt[:], 0.0, w=["zt"])
    nblk = (TT + BL - 1) // BL
    for d in range(2):
        for k0 in range(0, TT, 2112):
            p.load(R(Q[:, k0:k0 + 2112]), R(qk_d[d, 0][:, k0:k0 + 2112]), w=["Q"])
            p.load(R(Kt[:, k0:k0 + 2112]), R(qk_d[d, 1][:, k0:k0 + 2112]), w=["K"])
            p.load(R(gT[:, k0:k0 + 2112]), R(g_d[d][:, k0:k0 + 2112]), w=["v"])
            p.load(Bt[:, k0:k0 + 2112], cm_d[:, k0:k0 + 2112], w=["B"])
        p.load(R(gw[:]), R(gw_d[d]), w=["gw"]); p.load(gb[:], gb_d[d], w=["gb"])
        p.ts(ngb[:], gb[:], -1.0, None, ALU.mult, None, r=["gb"], w=["ngb"])
        for bi in range(nblk):
            c0 = bi * BL
            cn = min(BL, TT - c0)
            p.mm(ps_l[0:64, 0:cn], R(gw[:]), R(gT[:, c0:c0 + cn]), True, True, r=["gw", "v"], w=["ps_l"])
            p.act(t1[:, 0:cn], ps_l[0:64, 0:cn], AF.Exp, r=["ps_l", "ngb"], w=["t1"], scale=-1.0, bias=ngb[:, 0:1])
            p.act(t1[:, 0:cn], t1[:, 0:cn], AF.Ln, r=["t1"], w=["t1"], bias=1.0)
            p.ts(LA[:, c0:c0 + cn], t1[:, 0:cn], -1.0 / 16.0, None, ALU.mult, None, r=["t1"], w=["LA"])
        for c0 in range(0, NCG, 33):
            p.load(R(vtm[:, c0:c0 + 33, :]), R(v_d[d][:, c0:c0 + 33, :]), w=["v"])
        for h0 in range(0, TT, 2112):
            p.op("dve", lambda eng, o=Bt[:, h0:h0 + 2112], a=Bt[:, h0:h0 + 2112], b=LA[:, h0:h0 + 2112]:
                 eng.tensor_tensor_scan(o, a, b, 0.0, ALU.mult, ALU.add), r=["B", "LA"], w=["B"])
        p.act(LA[:], Bt[:], AF.Exp, r=["B"], w=["LA"])
        p.act(Bt[:], Bt[:], AF.Exp, r=["B"], w=["B"], scale=-1.0, bias=LNK)
        p.copy(ebl[:], LA[:, 63:TT:64], r=["LA"], w=["ebl"])
        for bi in range(nblk):
            c0 = bi * BL
            cn = min(BL, TT - c0)
            cb = csb[bi % 2]
            p.load(cb[:, 0, 0:cn], cs_d[d, 0][:, c0:c0 + cn], w=[("cs", bi % 2)])
            p.load(cb[:, 1, 0:cn], cs_d[d, 1][:, c0:c0 + cn], w=[("cs", bi % 2)])
            for X, xk, E, ek in ((Q, "Q", LA, "LA"), (Kt, "K", Bt, "B")):
                p.mm(ps_r[0:64, 0:cn], R(rot[:]), R(X[:, c0:c0 + cn]), True, True, r=["rot", xk], w=["ps_r"])
                p.tt(t1[:, 0:cn], ps_r[0:64, 0:cn], cb[:, 1, 0:cn], ALU.mult, r=["ps_r", ("cs", bi % 2)], w=["t1"])
                p.tt(t2[:, 0:cn], X[:, c0:c0 + cn], cb[:, 0, 0:cn], ALU.mult, r=[xk, ("cs", bi % 2)], w=["t2"], e="pool")
                p.tt(t1[:, 0:cn], t1[:, 0:cn], t2[:, 0:cn], ALU.add, r=["t1", "t2"], w=["t1"])
                p.tt(R(X[:, c0:c0 + cn]), t1[:, 0:cn], E[:, c0:c0 + cn], ALU.mult, r=["t1", ek], w=[xk])
        p.copy(R(S[:]), zt[:], r=["zt"], w=["S"])
        for c in range(NCG):
            cols = slice(64 * c, 64 * c + 64)
            ob = obuf[(c // 8) % 2]
            obk = ("ob", (c // 8) % 2)
            p.mm(ps_a[0:64, 0:64], R(Kt[:, cols]), R(Q[:, cols]), True, True, r=["K", "Q"], w=["ps_a"])
            p.tt(R(attT[:]), ps_a[0:64, 0:64], um[:], ALU.mult, r=["ps_a", "um"], w=["attT"])
            p.op("pe", lambda eng, i_=Kt[:, cols]: eng.transpose(ps_t[0:64, 0:64], i_, ident[0:64, 0:64]),
                 r=["K", "ident"], w=["ps_t"])
            p.copy(R(ktm[:]), ps_t[0:64, 0:64], r=["ps_t"], w=["ktm"], e="act")
            p.mm(ps_o[0:64, 0:64], R(attT[:]), R(vtm[:, c, :]), True, False, r=["attT", "v"], w=["ps_o"])
            p.mm(ps_o[0:64, 0:64], R(Q[:, cols]), R(S[:]), False, True, r=["Q", "S"], w=["ps_o"])
            p.copy(ob[:, c % 8, :], ps_o[0:64, 0:64], r=["ps_o"], w=[obk], e="act")
            p.mm(ps_kv[0:64, 0:64], R(ktm[:]), R(vtm[:, c, :]), True, True, r=["ktm", "v"], w=["ps_kv"])
            p.ts(R(S[:]), S[:], ebl[:, c:c + 1], None, ALU.mult, None, r=["S", "ebl"], w=["S"])
            p.stt(R(S[:]), ps_kv[0:64, 0:64], ebl[:, c:c + 1], S[:], ALU.mult, ALU.add, r=["ps_kv", "ebl", "S"], w=["S"])
            if c % 8 == 7 or c == NCG - 1:
                g0 = (c // 8) * 8
                p.store(o_d[d][:, g0:c + 1, :], ob[:, 0:c + 1 - g0, :], r=[obk])
    p.finish()
    p.emit()
    return nc


def rope_tables():
    pos = np.arange(SEQ)
    rows = (pos // 64).astype(np.float32)
    cols = (pos % 64).astype(np.float32)
    inv = (np.float32(10000.0) ** (-np.arange(16, dtype=np.float32) / np.float32(16))).astype(np.float32)
    ang = np.concatenate([rows[:, None] * inv, cols[:, None] * inv], -1).astype(np.float32)
    cos = np.cos(ang).astype(np.float32)
    sin = np.sin(ang).astype(np.float32)
    c2 = np.concatenate([np.ones((CTX, 64), np.float32), np.concatenate([cos, cos], 1)], 0)
    s2 = np.concatenate([np.zeros((CTX, 64), np.float32), np.concatenate([sin, sin], 1)], 0)
    return c2, s2


def gla_inputs(l, q_full, k_full, v_full, g_full, P):
    c2, s2 = rope_tables()
    rot = np.zeros((64, 64), np.float32)
    for m in range(32):
        rot[m + 32, m] = -1.0
        rot[m, m + 32] = 1.0
    cmask = np.ones((64, TT), np.float32)
    cmask[:, ::64] = 0.0
    jj = np.arange(64)
    umask = (jj[:, None] <= jj[None, :]).astype(np.float32)
    ident = np.eye(128, dtype=np.float32)
    tabs = []
    for d in range(2):
        a, b = (c2, s2) if d == 0 else (flipseq(c2), flipseq(s2))
        tabs.append(np.stack([a.T, b.T]))
    cs = np.ascontiguousarray(np.stack(tabs))
    in_maps = []
    for i in range(NCORES):
        hh, vh = i // 2, i % 2
        qk = np.zeros((2, 2, 64, TT), np.float32)
        vtm = np.zeros((2, 64, NCG, 64), np.float32)
        gT = np.zeros((2, 16, TT), np.float32)
        gw = np.zeros((2, 16, 64), np.float32)
        gb = np.zeros((2, 64, 1), np.float32)
        for d in range(2):
            f = (lambda a: a) if d == 0 else flipseq
            qk[d, 0] = f(q_full[:, 64 * hh:64 * hh + 64]).T
            qk[d, 1] = f(k_full[:, 64 * hh:64 * hh + 64]).T
            vv = f(v_full[:, 128 * hh + 64 * vh:128 * hh + 64 * vh + 64])
            vtm[d] = vv.reshape(NCG, 64, 64).transpose(1, 0, 2)
            gT[d] = f(g_full[:, 16 * d:16 * d + 16]).T
            gw[d] = P["gla_gate_w"][l, d][:, 64 * hh:64 * hh + 64]
            gb[d, :, 0] = P["gla_gate_b"][l, d][64 * hh:64 * hh + 64]
        in_maps.append({"qk": qk, "vtm": vtm, "gT": gT, "gw": gw, "gb": gb, "cs": cs, "rot": rot,
                        "cmask": cmask, "umask": umask, "ident": ident})
    return in_maps


def gla_tm_to_nat(o, rev):
    a = o.transpose(1, 0, 2).reshape(TT, -1)
    if rev:
        a = np.concatenate([a[:CTX][::-1], a[CTX:][::-1]], 0)
    return a


def gla_gather(res):
    of = np.concatenate([gla_tm_to_nat(res[i]["o"][0], False) for i in range(NCORES)], 1)
    ob = np.concatenate([gla_tm_to_nat(res[i]["o"][1], True) for i in range(NCORES)], 1)
    return of, ob


_PROGS = {}


def _prog(name, fn):
    if name not in _PROGS:
        _PROGS[name] = fn()
    return _PROGS[name]


def phase_a_run(l, xfull, mod_x, mod_c, w_in):
    mx = [mod_x[j * D:(j + 1) * D] for j in range(6)]
    mc = [mod_c[j * D:(j + 1) * D] for j in range(6)]
    modA = np.concatenate([chunkcols(v) for v in (mx[1], mx[0], mc[1], mc[0])], 1)
    nc = _prog("A", lambda: build_phase_a(1056, 32))
    in_maps = []
    for i in range(NCORES):
        xt = np.concatenate([xfull[32 * i:32 * i + 32], xfull[CTX + 1024 * i:CTX + 1024 * i + 1024]], 0).T
        in_maps.append({"xT": np.ascontiguousarray(xt), "modA": modA, "w_in": np.ascontiguousarray(w_in)})
    res = run_spmd(nc, in_maps)
    pf = np.zeros((TT, DPROJ), np.float32)
    for i in range(NCORES):
        o = res[i]["pxT"].T
        pf[32 * i:32 * i + 32] = o[:32]
        pf[CTX + 1024 * i:CTX + 1024 * i + 1024] = o[32:]
    return pf


def layer_forward(l, xfull, mod_x, mod_c, P):
    pf = phase_a_run(l, xfull, mod_x, mod_c, P["w_in"][l])
    u = pf[:, 0:512]
    naq, nak, nav = pf[:, 512:1024], pf[:, 1024:1536], pf[:, 1536:2048]
    z = pf[:, 2048:2560]
    xbc = pf[:, 2560:3584]
    dtc = pf[:, 3584:3600]
    gq, gk, gv, gr, gg = pf[:, 3600:3856], pf[:, 3856:4112], pf[:, 4112:4624], pf[:, 4624:5136], pf[:, 5136:5168]
    s5f, s5b = s5_gather(run_spmd(_prog("S5", build_s5), s5_inputs(l, u, P)))
    nao = na_gather(run_spmd(_prog("NA", build_na), na_inputs(l, naq, nak, nav, P)))
    ssf, ssb = ssd_gather(run_spmd(_prog("SSD", build_ssd), ssd_inputs(l, xbc, dtc, P)))
    glf, glb = gla_gather(run_spmd(_prog("GLA", build_gla), gla_inputs(l, gq, gk, gv, gg, P)))
    mixfull = np.concatenate([s5f, s5b, u, nao, ssf, ssb, z, glf, glb, gr], 1)
    res = run_spmd(_prog("C", build_phase_c), phase_c_inputs(l, xfull, mixfull, None, mod_x, mod_c, P))
    return phase_c_gather(res, "x2T"), res


def kernel(**inputs):
    P = {k: np.asarray(v, np.float32) for k, v in inputs.items()}
    mod_x, mod_c = run_mod(P["c"], P["c_ctx"], P["w_mod"], P["b_mod"])
    xfull = np.concatenate([P["ctx"][0], P["x"][0]], 0)
    for l in range(DEPTH):
        xfull, _ = layer_forward(l, xfull, mod_x[l], mod_c[l], P)
    return np.ascontiguousarray(xfull[CTX:][None]).astype(np.float32)


from concourse.bass import IndirectOffsetOnAxis
U32 = mybir.dt.uint32
NTC = 1072
NCXF = 40
SROWS = 4144
BROWS = 448
PW = 270
PWIN = (0, 268, 536, 802)
NGT = 13


def rv(tile_ap, a, b, rev):
    if not rev:
        return tile_ap[:, a:b]
    if a == 0:
        return tile_ap[:, b - 1::-1]
    return tile_ap[:, b - 1:a - 1:-1]


def build_fused(nlayers=DEPTH, dbg=False):
    nc = bass.Bass("TRN2", target_bir_lowering=False)
    L = nlayers
    EI = lambda name, shape, dt=F32: nc.dram_tensor(name, list(shape), dt, kind="ExternalInput").ap()
    x0_d = EI("x0T", [D, NTC])
    cmk_d = EI("colmask", [128, 16])
    idxA_d = EI("idxA", [128, NGT * 8], U32)
    idxB_d = EI("idxB", [128, 28 * 4], U32)
    wmod_d = EI("wmod", [D, 6144]); bmod_d = EI("bmod", [128, 48]); cv_d = EI("cv", [128, KC, 2])
    win_d = EI("w_in", [L, D, DPROJ]); wout_d = EI("w_out", [L, D, D])
    up_d = EI("ffn_up", [L, D, 2 * DFF]); dn_d = EI("ffn_down", [L, DFF, D]); glu_d = EI("glu_w", [L, 512, 512])
    vec_d = EI("vecs", [L, 128, NV_C])
    lp_d = EI("lanep", [L, 2, 2, 128, 3]); bre_d = EI("bre", [L, 2, 2, 128, 16]); bim_d = EI("bim", [L, 2, 2, 128, 16])
    cre_d = EI("cre", [L, 2, 2, 128, 64]); cim_d = EI("cim", [L, 2, 2, 128, 64])
    nab_d = EI("nabias", [L, 5, 128, 576])
    scw_d = EI("ssd_cw", [L, 128, 3, 4]); ssc_d = EI("ssd_scal", [L, 2, 128, 4])
    ggw_d = EI("gla_gw", [L, 2, 16, 64]); ggb_d = EI("gla_gb", [L, 2, 64, 1]); gcs_d = EI("gla_cs", [2, 64, TT])
    tau_d = EI("tau1", [128, 128]); id_d = EI("ident", [128, 128]); dtsel_d = EI("dtsel", [16, 2])
    tri_d = EI("tri2", [2, 128, 128]); nm_d = EI("negmask2", [2, 128, 128])
    rot_d = EI("rot", [64, 64]); gcm_d = EI("cmask", [64, TT]); um_d = EI("umask2", [2, 64, 64])
    out_d = nc.dram_tensor("outT", [D, 1024], F32, kind="ExternalOutput").ap()
    IT = lambda name, shape: nc.dram_tensor(name, list(shape), F32).ap()
    xres = [IT("xres0", [D, NTC]), IT("xres1", [D, NTC])]
    pxloc = IT("pxloc", [1536, NTC])
    sA_lat = IT("sA_lat", [SROWS, 1024]); sA_ctx = IT("sA_ctx", [SROWS, 32])
    gA_lat = IT("gA_lat", [8 * SROWS, 1024]); gA_ctx = IT("gA_ctx", [8 * SROWS, 32])
    sB = IT("sB", [BROWS * 32, PW]); gB = IT("gB", [8 * BROWS * 32, PW])
    msend = IT("msend", [128, 96]); mg = IT("mg", [8 * 128, 96])
    dtscr = IT("dtscr", [2, TT])
    dbg_o = {}
    if dbg:
        dbg_o["px_send"] = nc.dram_tensor("dbg_sA", [SROWS, 1024], F32, kind="ExternalOutput").ap()
        dbg_o["sB"] = nc.dram_tensor("dbg_sB", [BROWS * 32, PW], F32, kind="ExternalOutput").ap()
        dbg_o["x1"] = nc.dram_tensor("dbg_x", [D, NTC], F32, kind="ExternalOutput").ap()
    rg = [list(range(NCORES))]
    p = Prog(nc)
    modsb = p.sb([128, 8, 96])
    idxA = p.sb([128, NGT * 8], U32); idxB = p.sb([128, 28 * 4], U32)
    cmk = p.sb([128, 16]); ident = p.sb([128, 128])
    modL = p.sb([128, 12, KC])
    p.load(idxA[:], idxA_d, w=["idxA"]); p.load(idxB[:], idxB_d, w=["idxB"])
    p.load(cmk[:], cmk_d, w=["cmk"]); p.load(ident[:], id_d, w=["ident"])

    def allgather(src, dst, rk, wk):
        p.coll(lambda eng: eng.collective_compute("AllGather", ALU.bypass, replica_groups=rg, ins=[src.opt()], outs=[dst.opt()]),
               r=[rk], w=[wk])

    def gatherA(out_lat, out_ctx, npart, t, r, wkeys):
        col = t * 8 + r
        p.dma(lambda eng: eng.indirect_dma_start(out=out_lat, out_offset=None, in_=gA_lat,
                                                 in_offset=IndirectOffsetOnAxis(idxA[0:npart, col:col + 1], 0)),
              r=["gA", "idxA"], w=wkeys, q="pool")
        p.dma(lambda eng: eng.indirect_dma_start(out=out_ctx, out_offset=None, in_=gA_ctx,
                                                 in_offset=IndirectOffsetOnAxis(idxA[0:npart, col:col + 1], 0)),
              r=["gA", "idxA"], w=wkeys, q="pool")

    def gather_seq(tile, npart, t, key, rr=False):
        for r in range(8):
            ol = tile[0:npart, CTX + 1024 * r:CTX + 1024 * r + 1024]
            oc = tile[0:npart, 32 * r:32 * r + 32]
            gatherA(R(ol) if rr else ol, R(oc) if rr else oc, npart, t, r, [key])

    sBv = sB.rearrange("(q b) c -> q b c", b=32)

    def send_rows(tile, rowbase, key, nat0=0, nlen=TT):
        for j in range(8):
            for s in range(4):
                w0 = PWIN[s]
                pieces = []
                ca, cb = max(w0, 0), min(w0 + PW, NCXF)
                if ca < cb:
                    t0 = 32 * j - 4 + ca
                    t1 = 32 * j - 4 + cb
                    lo, hi = max(t0, 0), min(t1, CTX)
                    if lo < hi:
                        pieces.append((ca - w0 + (lo - t0), lo, hi - lo))
                la, lb = max(w0, NCXF), min(w0 + PW, NTC)
                if la < lb:
                    t0 = 1024 * j - 4 + (la - NCXF)
                    t1 = 1024 * j - 4 + (lb - NCXF)
                    lo, hi = max(t0, 0), min(t1, SEQ)
                    if lo < hi:
                        pieces.append((la - w0 + (lo - t0), CTX + lo, hi - lo))
                for (off, nat, ln) in pieces:
                    lo, hi = max(nat, nat0), min(nat + ln, nat0 + nlen)
                    if lo >= hi:
                        continue
                    p.store(sBv[rowbase:rowbase + 64, j * 4 + s, off + (lo - nat):off + (hi - nat)],
                            tile[0:64, lo - nat0:hi - nat0], r=[key], w=["sB"])

    with p.scope():
        z = p.sb([128, 14 * PW])
        p.memset(z[:], 0.0, w=["z"])
        sBz = sB.rearrange("(a p b) c -> a p (b c)", p=128, b=14)
        for a in range(8):
            p.store(sBz[a], z[:], r=["z"], w=["sB"])
    with p.scope():
        cv0 = p.sb([128, KC, 2]); cv = p.sb([128, KC, 2]); bb = p.sb([128, 48]); ob = p.sb([128, 48, 2])
        wb = [p.sb([128, KC, 512]) for _ in range(2)]
        pss = [p.ps([128, 512]) for _ in range(2)]
        p.load(cv0[:], cv_d, w=["cv0"]); p.load(bb[:], bmod_d, w=["b"])
        p.act(R(cv[:]), cv0[:], AF.Silu, r=["cv0"], w=["cv"])
        wv = wmod_d.rearrange("(k p) m -> p k m", p=128)
        for g in range(12):
            wt = wb[g % 2]
            for k in range(KC):
                p.load(R(wt[:, k, :]), R(wv[:, k, g * 512:(g + 1) * 512]), w=[("w", g % 2, k)])
            for mm_ in range(4):
                j = g * 4 + mm_
                ps = pss[j % 2]
                for k in range(KC):
                    p.mm(ps[:, 0:2], R(wt[:, k, mm_ * 128:(mm_ + 1) * 128]), R(cv[:, k, :]), k == 0, k == KC - 1,
                         r=[("w", g % 2, k), "cv"], w=[("ps", j % 2)])
                p.ts(ob[:, j, :], ps[:, 0:2], bb[:, j:j + 1], None, ALU.add, None, r=[("ps", j % 2), "b"], w=["ob"])
        p.store(msend, ob[:].rearrange("p j t -> p (j t)"), r=["ob"], w=["msend"])
        allgather(msend, mg, "msend", "mg")
        p.load(modsb[:], mg.rearrange("(r p) n -> p r n", p=128), r=["mg"], w=["modsb"])
    p.dma(lambda eng: eng.dma_start(out=xres[0], in_=x0_d), r=[], w=["xres0"])
    p.barrier()

    for l in range(nlayers):
        xin, xout = xres[l % 2], xres[(l + 1) % 2]
        for v in range(6):
            for wh in range(2):
                src = modsb[:, 2 * l + v // 3, (16 * (v % 3)) * 2 + wh:(16 * (v % 3) + 16) * 2:2]
                if v in (1, 4):
                    p.ts(modL[:, 2 * v + wh, :], src, 1.0, None, ALU.add, None, r=["modsb"], w=["modL"])
                else:
                    p.copy(modL[:, 2 * v + wh, :], src, r=["modsb"], w=["modL"])
        MC = lambda v, wh, k: modL[:, 2 * v + wh, k:k + 1]
        with p.scope():
            NT = NTC
            xT = p.sb([128, KC, NT]); hT = xT
            onesN = p.sb([128, 128]); sq = p.sb([128, 512]); mean = p.sb([128, NT]); rstd = p.sb([128, NT])
            GW = 512
            wbuf = [p.sb([128, KC, GW]) for _ in range(2)]
            obuf = [p.sb([128, NT]) for _ in range(2)]
            ps_m = p.ps([128, 512]); ps_q = p.ps([128, 512]); ps_o = [p.ps([128, 512]) for _ in range(4)]
            tiles = [(0, 268), (268, 268), (536, 268), (804, 268)]
            p.memset(onesN[:], 1.0 / D, w=["ones"])
            xv = xin.rearrange("(k p) n -> p k n", p=128)
            for k in range(KC):
                p.load(R(xT[:, k, :]), R(xv[:, k, :]), r=[f"xres{l % 2}"], w=[("A", "x", k)])
            ln_stats(p, xT, NT, tiles, onesN, sq, ps_m, ps_q, mean, rstd, "A")
            for k in range(KC):
                p.tt(R(hT[:, k, :]), xT[:, k, :], mean[:], ALU.subtract, r=[("A", "x", k), ("A", "mean")],
                     w=[("h", k), ("A", "x", k)], e="pool")
                p.tt(R(hT[:, k, :]), hT[:, k, :], rstd[:], ALU.mult, r=[("h", k), ("A", "rstd")], w=[("h", k)])
                p.act(R(hT[:, k, 0:NCXF]), hT[:, k, 0:NCXF], AF.Identity, r=[("h", k), "modL"], w=[("h", k)],
                      scale=MC(1, 1, k), bias=MC(0, 1, k))
                p.act(R(hT[:, k, NCXF:NT]), hT[:, k, NCXF:NT], AF.Identity, r=[("h", k), "modL"], w=[("h", k)],
                      scale=MC(1, 0, k), bias=MC(0, 0, k))
            wv = win_d[l].rearrange("(k p) m -> p k m", p=128)
            ngrp = (DPROJ + GW - 1) // GW
            oi = 0
            segs = [(0, 512, "su", 0), (512, 2048, "s", 0), (2048, 2560, "l", 512 - 2048), (2560, 4624, "s", -512),
                    (4624, 5136, "l", 1024 - 4624), (5136, 5168, "s", -1024)]
            for g in range(ngrp):
                g0 = g * GW
                gw_ = min(GW, DPROJ - g0)
                wbt = wbuf[g % 2]
                for k in range(KC):
                    p.load(R(wbt[:, k, 0:gw_]), R(wv[:, k, g0:g0 + gw_]), w=[("w", g % 2, k)])
                for m0 in range(0, gw_, 128):
                    mw = min(128, gw_ - m0)
                    obt = obuf[oi % 2]
                    for ti, (c0, cn) in enumerate(tiles):
                        ps = ps_o[ti % 4]
                        for k in range(KC):
                            p.mm(ps[0:mw, 0:cn], R(wbt[:, k, m0:m0 + mw]), R(hT[:, k, c0:c0 + cn]), k == 0, k == KC - 1,
                                 r=[("w", g % 2, k), ("h", k)], w=[("pso", ti % 4)])
                        p.copy(obt[0:mw, c0:c0 + cn], ps[0:mw, 0:cn], r=[("pso", ti % 4)], w=[("ob", oi % 2)],
                               e="act" if ti % 2 == 0 else "dve")
                    ca0 = g0 + m0
                    for (a, b, kind, sh) in segs:
                        lo, hi = max(a, ca0), min(b, ca0 + mw)
                        if lo >= hi:
                            continue
                        pr = slice(lo - ca0, hi - ca0)
                        if kind in ("s", "su"):
                            ro = lo + (sh if kind == "s" else 0)
                            p.store(sA_lat[ro:ro + hi - lo, :], obt[pr, 44:1068], r=[("ob", oi % 2)], w=["sA"])
                            p.store(sA_ctx[ro:ro + hi - lo, :], obt[pr, 4:36], r=[("ob", oi % 2)], w=["sA"])
                        if kind in ("l", "su"):
                            ro = lo + (sh if kind == "l" else 0)
                            p.store(pxloc[ro:ro + hi - lo, :], obt[pr, :], r=[("ob", oi % 2)], w=["pxloc"])
                    oi += 1
        allgather(sA_lat, gA_lat, "sA", "gA")
        allgather(sA_ctx, gA_ctx, "sA", "gA")
        if dbg and l == 0:
            p.dma(lambda eng: eng.dma_start(out=dbg_o["px_send"], in_=sA_lat), r=["sA"], w=["dbgA"])
        p.barrier()
        import os as _os
        _skip = _os.environ.get("FUSED_SKIP", "").split(",")
        with p.scope():
          if "S5" not in _skip:
            BL = 512
            u = p.sb([64, TT]); yo = [p.sb([64, TT]) for _ in range(2)]
            tau = p.sb([128, 128]); lp = p.sb([128, 3]); sc = p.sb([128, 16])
            braw = p.sb([128, 2, 16]); bbf = p.sb([128, 2, 64])
            bbT = [[p.sb([64, 2, 128]) for _ in range(2)] for _ in range(2)]
            cc = [[p.sb([128, 2, 64]) for _ in range(2)] for _ in range(2)]
            cosT = [[p.sb([128, BL]) for _ in range(2)] for _ in range(2)]
            sinT = [[p.sb([128, BL]) for _ in range(2)] for _ in range(2)]
            rhoT = [[p.sb([128, 128]) for _ in range(2)] for _ in range(2)]
            cq = [[p.sb([128, 2]) for _ in range(2)] for _ in range(2)]
            tmp = [p.sb([128, 128]) for _ in range(3)]
            tki = p.sb([128, 128], I32); ang = p.sb([128, 128])
            b_re = p.sb([128, BL]); b_im = p.sb([128, BL]); v_re = p.sb([128, BL]); v_im = p.sb([128, BL])
            m1 = p.sb([128, BL]); m2 = p.sb([128, BL]); w_re = p.sb([128, BL]); w_im = p.sb([128, BL])
            s_re = p.sb([128, BL]); s_im = p.sb([128, BL])
            car = [p.sb([128, 2]) for _ in range(2)]; ct = p.sb([128, 2])
            psb = [p.ps([128, 512]) for _ in range(4)]; psy = [p.ps([128, 512]) for _ in range(2)]; pst = p.ps([128, 512])
            p.load(tau[:], tau_d, w=["tau"])
            gather_seq(u, 64, 0, "u", rr=True)
            for d in range(2):
                for t in range(2):
                    tg = f"s{d}{t}"
                    p.load(lp[:], lp_d[l, d, t], w=["lp"])
                    p.load(braw[:, 0, :], bre_d[l, d, t], w=["braw"]); p.load(braw[:, 1, :], bim_d[l, d, t], w=["braw"])
                    p.load(R(cc[d][t][:, 0, :]), R(cre_d[l, d, t]), w=[("cc", d, t)])
                    p.load(R(cc[d][t][:, 1, :]), R(cim_d[l, d, t]), w=[("cc", d, t)])
                    p.act(sc[:, 0:1], lp[:, 2:3], AF.Exp, r=["lp"], w=["sc"])
                    p.tt(sc[:, 1:2], lp[:, 0:1], sc[:, 0:1], ALU.mult, r=["lp", "sc"], w=["sc"])
                    p.act(sc[:, 2:3], sc[:, 1:2], AF.Exp, r=["sc"], w=["sc"])
                    p.tt(sc[:, 3:4], lp[:, 1:2], sc[:, 0:1], ALU.mult, r=["lp", "sc"], w=["sc"])
                    p.ts(ang[:], tau[:], sc[:, 3:4], None, ALU.mult, None, r=["tau", "sc"], w=[tg + "x"])
                    trig(p, ang[:], 128, [tmp[0][:], tmp[1][:], tmp[2][:]], tki[:], cosT[d][t][:, 0:128], sinT[d][t][:, 0:128], tg)
                    for rep in range(1, 4):
                        p.copy(cosT[d][t][:, rep * 128:(rep + 1) * 128], cosT[d][t][:, 0:128], r=[tg + "cos"], w=[tg + "cos"])
                        p.copy(sinT[d][t][:, rep * 128:(rep + 1) * 128], sinT[d][t][:, 0:128], r=[tg + "sin"], w=[tg + "sin"])
                    p.copy(cq[d][t][:, 0:1], cosT[d][t][:, 127:128], r=[tg + "cos"], w=[("cq", d, t)])
                    p.copy(cq[d][t][:, 1:2], sinT[d][t][:, 127:128], r=[tg + "sin"], w=[("cq", d, t)])
                    p.memset(rhoT[d][t][:], 1.0, w=[("rho", d, t)])
                    p.ts(rhoT[d][t][:], rhoT[d][t][:], sc[:, 2:3], None, ALU.mult, None, r=["sc", ("rho", d, t)], w=[("rho", d, t)])
                    p.tt(sc[:, 4:5], sc[:, 2:3], cosT[d][t][:, 0:1], ALU.mult, r=["sc", tg + "cos"], w=["sc"])
                    p.ts(sc[:, 4:5], sc[:, 4:5], -1.0, None, ALU.add, None, r=["sc"], w=["sc"])
                    p.tt(sc[:, 5:6], sc[:, 2:3], sinT[d][t][:, 0:1], ALU.mult, r=["sc", tg + "sin"], w=["sc"])
                    p.tt(sc[:, 6:7], lp[:, 0:1], lp[:, 0:1], ALU.mult, r=["lp"], w=["sc"])
                    p.tt(sc[:, 9:10], lp[:, 1:2], lp[:, 1:2], ALU.mult, r=["lp"], w=["sc"])
                    p.tt(sc[:, 6:7], sc[:, 6:7], sc[:, 9:10], ALU.add, r=["sc"], w=["sc"])
                    p.op("dve", lambda eng: eng.reciprocal(sc[:, 6:7], sc[:, 6:7]), r=["sc"], w=["sc"])
                    p.tt(sc[:, 7:8], sc[:, 4:5], lp[:, 0:1], ALU.mult, r=["sc", "lp"], w=["sc"])
                    p.tt(sc[:, 9:10], sc[:, 5:6], lp[:, 1:2], ALU.mult, r=["sc", "lp"], w=["sc"])
                    p.tt(sc[:, 7:8], sc[:, 7:8], sc[:, 9:10], ALU.add, r=["sc"], w=["sc"])
                    p.tt(sc[:, 7:8], sc[:, 7:8], sc[:, 6:7], ALU.mult, r=["sc"], w=["sc"])
                    p.tt(sc[:, 8:9], sc[:, 5:6], lp[:, 0:1], ALU.mult, r=["sc", "lp"], w=["sc"])
                    p.tt(sc[:, 9:10], sc[:, 4:5], lp[:, 1:2], ALU.mult, r=["sc", "lp"], w=["sc"])
                    p.tt(sc[:, 8:9], sc[:, 8:9], sc[:, 9:10], ALU.subtract, r=["sc"], w=["sc"])
                    p.tt(sc[:, 8:9], sc[:, 8:9], sc[:, 6:7], ALU.mult, r=["sc"], w=["sc"])
                    p.memset(bbf[:], 0.0, w=["bbf"])
                    for half in range(2):
                        rows = slice(64 * half, 64 * half + 64)
                        co = 16 * (2 * t + half)
                        p.ts(bbf[rows, 0, co:co + 16], braw[rows, 1, :], sc[rows, 8:9], -1.0, ALU.mult, ALU.mult,
                             r=["braw", "sc"], w=["bbf"])
                        p.stt(bbf[rows, 0, co:co + 16], braw[rows, 0, :], sc[rows, 7:8], bbf[rows, 0, co:co + 16], ALU.mult, ALU.add,
                              r=["braw", "sc", "bbf"], w=["bbf"])
                        p.ts(bbf[rows, 1, co:co + 16], braw[rows, 0, :], sc[rows, 8:9], None, ALU.mult, None,
                             r=["braw", "sc"], w=["bbf"])
                        p.stt(bbf[rows, 1, co:co + 16], braw[rows, 1, :], sc[rows, 7:8], bbf[rows, 1, co:co + 16], ALU.mult, ALU.add,
                              r=["braw", "sc", "bbf"], w=["bbf"])
                    for c2 in range(2):
                        p.op("pe", lambda eng, o=pst[0:64, c2 * 128:(c2 + 1) * 128], i_=bbf[:, c2, :]: eng.transpose(o, i_, ident[:]),
                             r=["bbf", "ident"], w=["pst"])
                    p.copy(R(bbT[d][t][:, :, :]), pst[0:64, 0:256].rearrange("p (a m) -> p a m", a=2), r=["pst"], w=[("bbT", d, t)], e="act")
                    p.ts(R(cc[d][t][:, 1, :]), cc[d][t][:, 1, :], -1.0, None, ALU.mult, None, r=[("cc", d, t)], w=[("cc", d, t)])
            blocks = [(0, 256)] + [(CTX + BL * k, BL) for k in range(16)]
            for d in range(2):
                rev = d == 1
                order = blocks if not rev else [blocks[0]] + blocks[:0:-1]
                for t in range(2):
                    p.memset(car[t][:], 0.0, w=[("car", t)])
                for bi, (c0, cn) in enumerate(order):
                    py = psy[bi % 2]
                    pyk = ("psy", bi % 2)
                    for t in range(2):
                        pr, pim = psb[2 * t], psb[2 * t + 1]
                        p.mm(pr[:, 0:cn], R(bbT[d][t][:, 0, :]), R(u[:, c0:c0 + cn]), True, True, r=[("bbT", d, t), "u"], w=[("psb", 2 * t)])
                        p.mm(pim[:, 0:cn], R(bbT[d][t][:, 1, :]), R(u[:, c0:c0 + cn]), True, True, r=[("bbT", d, t), "u"], w=[("psb", 2 * t + 1)])
                        p.copy(b_re[:, 0:cn], pr[:, 0:cn], r=[("psb", 2 * t)], w=["b_re"], e="act")
                        p.copy(b_im[:, 0:cn], pim[:, 0:cn], r=[("psb", 2 * t + 1)], w=["b_im"], e="act")
                        Cv = rv(cosT[d][t][:], 0, cn, rev); Sv = rv(sinT[d][t][:], 0, cn, rev)
                        tgc, tgs = f"s{d}{t}cos", f"s{d}{t}sin"
                        e1 = "dve" if rev else "pool"
                        p.tt(m1[:, 0:cn], b_re[:, 0:cn], Cv, ALU.mult, r=["b_re", tgc], w=["m1"], e=e1)
                        p.tt(m2[:, 0:cn], b_im[:, 0:cn], Sv, ALU.mult, r=["b_im", tgs], w=["m2"], e=e1)
                        p.tt(v_re[:, 0:cn], m1[:, 0:cn], m2[:, 0:cn], ALU.add, r=["m1", "m2"], w=["v_re"], e=e1)
                        p.tt(m1[:, 0:cn], b_im[:, 0:cn], Cv, ALU.mult, r=["b_im", tgc], w=["m1"], e=e1)
                        p.tt(m2[:, 0:cn], b_re[:, 0:cn], Sv, ALU.mult, r=["b_re", tgs], w=["m2"], e=e1)
                        p.tt(v_im[:, 0:cn], m1[:, 0:cn], m2[:, 0:cn], ALU.subtract, r=["m1", "m2"], w=["v_im"], e=e1)
                        qs = list(range(0, cn, 128))
                        if rev:
                            qs = qs[::-1]
                        for q0 in qs:
                            p.op("dve", lambda eng, o=rv(w_re[:], q0, q0 + 128, rev), a=rhoT[d][t][:], b=rv(v_re[:], q0, q0 + 128, rev),
                                 i_=car[t][:, 0:1]: eng.tensor_tensor_scan(o, a, b, i_, ALU.mult, ALU.add),
                                 r=[("rho", d, t), "v_re", ("car", t)], w=["w_re"])
                            p.op("dve", lambda eng, o=rv(w_im[:], q0, q0 + 128, rev), a=rhoT[d][t][:], b=rv(v_im[:], q0, q0 + 128, rev),
                                 i_=car[t][:, 1:2]: eng.tensor_tensor_scan(o, a, b, i_, ALU.mult, ALU.add),
                                 r=[("rho", d, t), "v_im", ("car", t)], w=["w_im"])
                            last = q0 if rev else q0 + 127
                            cqt = cq[d][t]
                            p.ts(ct[:, 0:1], w_im[:, last:last + 1], cqt[:, 1:2], None, ALU.mult, None, r=["w_im", ("cq", d, t)], w=["ct"])
                            p.ts(ct[:, 1:2], w_re[:, last:last + 1], cqt[:, 1:2], None, ALU.mult, None, r=["w_re", ("cq", d, t)], w=["ct"])
                            p.stt(car[t][:, 0:1], w_re[:, last:last + 1], cqt[:, 0:1], ct[:, 0:1], ALU.mult, ALU.subtract,
                                  r=["w_re", ("cq", d, t), "ct"], w=[("car", t)])
                            p.stt(car[t][:, 1:2], w_im[:, last:last + 1], cqt[:, 0:1], ct[:, 1:2], ALU.mult, ALU.add,
                                  r=["w_im", ("cq", d, t), "ct"], w=[("car", t)])
                        p.tt(m1[:, 0:cn], w_re[:, 0:cn], Cv, ALU.mult, r=["w_re", tgc], w=["m1"], e=e1)
                        p.tt(m2[:, 0:cn], w_im[:, 0:cn], Sv, ALU.mult, r=["w_im", tgs], w=["m2"], e=e1)
                        p.tt(R(s_re[:, 0:cn]), m1[:, 0:cn], m2[:, 0:cn], ALU.subtract, r=["m1", "m2"], w=["s_re"])
                        p.tt(m1[:, 0:cn], w_im[:, 0:cn], Cv, ALU.mult, r=["w_im", tgc], w=["m1"], e=e1)
                        p.tt(m2[:, 0:cn], w_re[:, 0:cn], Sv, ALU.mult, r=["w_re", tgs], w=["m2"], e=e1)
                        p.tt(R(s_im[:, 0:cn]), m1[:, 0:cn], m2[:, 0:cn], ALU.add, r=["m1", "m2"], w=["s_im"])
                        p.mm(py[0:64, 0:cn], R(cc[d][t][:, 0, :]), R(s_re[:, 0:cn]), t == 0, False, r=[("cc", d, t), "s_re"], w=[pyk])
                        p.mm(py[0:64, 0:cn], R(cc[d][t][:, 1, :]), R(s_im[:, 0:cn]), False, t == 1, r=[("cc", d, t), "s_im"], w=[pyk])
                    p.copy(yo[d][:, c0:c0 + cn], py[0:64, 0:cn], r=[pyk], w=[("yo", d)], e="act")
            send_rows(yo[0], 0, ("yo", 0))
            send_rows(yo[1], 64, ("yo", 1))
        with p.scope():
          if "NA" not in _skip:
            qT = p.sb([64, TT]); kT = p.sb([64, TT]); vt = p.sb([64, 132, 64]); oT = p.sb([64, TT])
            bias = p.sb([128, 5, 576])
            S = p.sb([128, 832]); Pm = p.sb([128, 832]); PTs = p.sb([64, 13, 128]); dg = p.sb([128, 128]); st = p.sb([128, 4])
            psA = p.ps([128, 512]); psB = p.ps([128, 512]); psC = p.ps([128, 512])
            psT = [p.ps([128, 512]) for _ in range(4)]; pso = p.ps([128, 512])
            gather_seq(qT, 64, 1, "q", rr=True)
            gather_seq(kT, 64, 2, "k", rr=True)
            gather_seq(oT, 64, 3, "oT")
            for c in range(5):
                p.load(bias[:, c, :], nab_d[l, c], w=["bias"])
            for r0 in range(0, 132, 8):
                n8 = min(8, 132 - r0)
                bank = psT[(r0 // 8) % 4]
                bk = ("psT", (r0 // 8) % 4)
                for a in range(n8):
                    r_ = r0 + a
                    p.op("pe", lambda eng, o=bank[0:64, a * 64:(a + 1) * 64], i_=oT[:, 64 * r_:64 * r_ + 64]:
                         eng.transpose(o, i_, ident[0:64, 0:64]), r=["oT", "ident"], w=[bk])
                p.copy(R(vt[:, r0:r0 + n8, :]), bank[0:64, 0:n8 * 64].rearrange("p (a m) -> p a m", a=n8), r=[bk], w=["v"],
                       e="act" if (r0 // 8) % 2 == 0 else "dve")

            def block(qc0, lat, b):
                nk = 832 if lat else 256
                if lat:
                    kr0 = na_kr0(b)
                    kc0 = CTX + 64 * kr0
                    cls = na_cls(b)
                    p.mm(psA[:, 0:288], R(qT[:, qc0:qc0 + 128]), R(kT[:, kc0:kc0 + 288]), True, True, r=["q", "k"], w=["psA"])
                    p.mm(psB[:, 0:288], R(qT[:, qc0:qc0 + 128]), R(kT[:, kc0 + 288:kc0 + 576]), True, True, r=["q", "k"], w=["psB"])
                    p.mm(psC[:, 0:256], R(qT[:, qc0:qc0 + 128]), R(kT[:, 0:256]), True, True, r=["q", "k"], w=["psC"])
                    p.stt(S[:, 0:288], psA[:, 0:288], 0.125, bias[:, cls, 0:288], ALU.mult, ALU.add, r=["psA", "bias"], w=["S"])
                    p.stt(S[:, 288:576], psB[:, 0:288], 0.125, bias[:, cls, 288:576], ALU.mult, ALU.add, r=["psB", "bias"], w=["S"])
                    p.act(S[:, 576:832], psC[:, 0:256], AF.Copy, r=["psC"], w=["S"], scale=0.125)
                else:
                    p.mm(psC[:, 0:256], R(qT[:, qc0:qc0 + 128]), R(kT[:, 0:256]), True, True, r=["q", "k"], w=["psC"])
                    p.act(S[:, 0:256], psC[:, 0:256], AF.Copy, r=["psC"], w=["S"], scale=0.125)
                p.op("dve", lambda eng: eng.reduce_max(st[:, 0:1], S[:, 0:nk], AX.X), r=["S"], w=["st"])
                p.ts(st[:, 1:2], st[:, 0:1], -1.0, None, ALU.mult, None, r=["st"], w=["st"])
                p.act(R(Pm[:, 0:nk]), S[:, 0:nk], AF.Exp, r=["S", "st"], w=["P", "st2"], bias=st[:, 1:2], accum_out=st[:, 2:3])
                p.op("dve", lambda eng: eng.reciprocal(st[:, 3:4], st[:, 2:3]), r=["st2", "P"], w=["st3"])
                p.ts(R(dg[:]), ident[:], st[:, 3:4], None, ALU.mult, None, r=["ident", "st3"], w=["dg"])
                nt = nk // 64
                for kt in range(nt):
                    bank = psT[kt // 4]
                    p.mm(bank[0:64, (kt % 4) * 128:(kt % 4) * 128 + 128], R(Pm[:, kt * 64:(kt + 1) * 64]), R(dg[:]), True, True,
                         r=["P", "dg"], w=[("psT", kt // 4)])
                for bk_ in range((nt + 3) // 4):
                    n4 = min(4, nt - 4 * bk_)
                    p.copy(R(PTs[:, 4 * bk_:4 * bk_ + n4, :]), psT[bk_][0:64, 0:n4 * 128].rearrange("p (a m) -> p a m", a=n4),
                           r=[("psT", bk_)], w=["PTs"], e="act" if bk_ % 2 == 0 else "dve")
                for kt in range(nt):
                    if lat:
                        row = 4 + kr0 + kt if kt < 9 else kt - 9
                    else:
                        row = kt
                    p.mm(pso[0:64, 0:128], R(vt[:, row, :]), R(PTs[:, kt, :]), kt == 0, kt == nt - 1, r=["v", "PTs"], w=["pso"])
                p.copy(oT[:, qc0:qc0 + 128], pso[0:64, 0:128], r=["pso"], w=["oT"], e="act")

            for cb in range(2):
                block(128 * cb, False, cb)
            for b in range(64):
                block(CTX + 128 * b, True, b)
            send_rows(oT, 128, "oT")
        with p.scope():
            raw = p.sb([128, TT])
            cv_ = [p.sb([128, TT]) for _ in range(3)]
            cw = p.sb([128, 3, 4]); scal = p.sb([128, 4])
            tri = p.sb([128, 128]); nm = p.sb([128, 128]); ones = p.sb([128, 128]); zt = p.sb([128, 64])
            dtsel = p.sb([16, 2]); dtall = p.sb([128, NCH, 2])
            dt = p.sb([128, NCH]); dta = p.sb([128, NCH]); tA = p.sb([128, NCH]); tB = p.sb([128, NCH])
            nacs = p.sb([128, NCH]); wdec = p.sb([128, NCH]); dec = p.sb([128, NCH])
            yT = raw[0:64, :]
            xdt = p.sb([128, 64]); Bw = p.sb([128, 128]); dtab = p.sb([128, 128])
            E = p.sb([128, 128]); CE = p.sb([128, 128]); Rm = p.sb([128, 128]); LT = p.sb([128, 128]); MT = p.sb([128, 128])
            hT = p.sb([128, 64])
            ps_x = p.ps([128, 512]); ps_B = p.ps([128, 512]); ps_R = p.ps([128, 512]); ps_CB = p.ps([128, 512])
            ps_y = p.ps([128, 512]); ps_h = p.ps([128, 512]); ps_s = p.ps([128, 512])
            p.memset(ones[:], 1.0, w=["ones"]); p.memset(zt[:], 0.0, w=["zt"])
            p.load(cw[:], scw_d[l], w=["cw"])
            dt16 = raw[0:16, :]
            for r in range(8):
                p.load(dt16[:, CTX + 1024 * r:CTX + 1024 * r + 1024], gA_lat[r * SROWS + 3072:r * SROWS + 3088, :], r=["gA"], w=["raw"])
                p.load(dt16[:, 32 * r:32 * r + 32], gA_ctx[r * SROWS + 3072:r * SROWS + 3088, :], r=["gA"], w=["raw"])
            p.load(dtsel[:], dtsel_d, w=["dtsel"])
            for c in range(NCH):
                p.mm(ps_s[:, 2 * c:2 * c + 2], dt16[:, 128 * c:128 * c + 128], dtsel[:], True, True, r=["raw", "dtsel"], w=["ps_s"])
            p.copy(dtall[:], ps_s[:, 0:2 * NCH].rearrange("p (c d) -> p c d", d=2), r=["ps_s"], w=["dtall"])
            p.barrier()
            segs_ = [(0, CTX), (CTX, TT)]
            for ch in range(3):
                np_ = 64 if ch == 0 else 128
                gather_seq(raw, np_, 4 + ch, "raw")
                o = cv_[ch]
                for (a, b) in segs_:
                    p.ts(R(o[0:np_, a:b]), raw[0:np_, a:b], cw[0:np_, ch, 1:2], cw[0:np_, ch, 3:4], ALU.mult, ALU.add,
                         r=["raw", "cw"], w=[("cv", ch)])
                    p.stt(R(o[0:np_, a + 1:b]), raw[0:np_, a:b - 1], cw[0:np_, ch, 0:1], o[0:np_, a + 1:b], ALU.mult, ALU.add,
                          r=["raw", "cw", ("cv", ch)], w=[("cv", ch)])
                    p.stt(R(o[0:np_, a:b - 1]), raw[0:np_, a + 1:b], cw[0:np_, ch, 2:3], o[0:np_, a:b - 1], ALU.mult, ALU.add,
                          r=["raw", "cw", ("cv", ch)], w=[("cv", ch)])
                p.act(R(o[0:np_, :]), o[0:np_, :], AF.Silu, r=[("cv", ch)], w=[("cv", ch)])
            xs, Bm, Cm = cv_
            p.barrier()
            for d in range(2):
                rev = d == 1
                p.load(scal[:], ssc_d[l, d], w=["scal"])
                p.load(tri[:], tri_d[d], w=["tri"]); p.load(nm[:], nm_d[d], w=["nm"])
                p.ts(dt[:], dtall[:, :, d], scal[:, 0:1], None, ALU.add, None, r=["dtall", "scal"], w=["dt"])
                p.ts(tA[:], dt[:], 0.0, None, ALU.max, None, r=["dt"], w=["tA"])
                p.ts(tB[:], dt[:], 0.0, None, ALU.min, None, r=["dt"], w=["tB"])
                p.tt(tB[:], tB[:], tA[:], ALU.subtract, r=["tA", "tB"], w=["tB"])
                p.act(tB[:], tB[:], AF.Exp, r=["tB"], w=["tB"])
                p.act(tB[:], tB[:], AF.Ln, r=["tB"], w=["tB"], bias=1.0)
                p.tt(dt[:], tA[:], tB[:], ALU.add, r=["tA", "tB"], w=["dt"])
                p.act(scal[:, 3:4], scal[:, 1:2], AF.Exp, r=["scal"], w=["scal"])
                p.ts(dta[:], dt[:], scal[:, 3:4], -1.0, ALU.mult, ALU.mult, r=["dt", "scal"], w=["dta"])
                p.mm(ps_s[:, 0:NCH], tri[:], dta[:], True, True, r=["tri", "dta", "dt"], w=["ps_s"])
                p.ts(nacs[:], ps_s[:, 0:NCH], -1.0, None, ALU.mult, None, r=["ps_s"], w=["nacs"])
                p.mm(ps_s[:, 0:NCH], ones[:], dta[:], True, True, r=["ones", "dta", "nacs"], w=["ps_s"])
                p.tt(wdec[:], ps_s[:, 0:NCH], nacs[:], ALU.add, r=["ps_s", "nacs"], w=["wdec"])
                p.act(wdec[:], wdec[:], AF.Exp, r=["wdec"], w=["wdec"])
                p.act(dec[:], ps_s[:, 0:NCH], AF.Exp, r=["ps_s"], w=["dec"])
                p.copy(R(hT[:]), zt[:], r=["zt"], w=["hT"])
                corder = list(range(NCH)) if not rev else [1, 0] + list(range(NCH - 1, 1, -1))
                for c in corder:
                    cols = slice(128 * c, 128 * c + 128)
                    p.op("pe", lambda eng, i_=xs[0:64, cols]: eng.transpose(ps_x[:, 0:64], i_, ident[0:64, 0:64]),
                         r=[("cv", 0), "ident"], w=["ps_x"])
                    p.op("pe", lambda eng, i_=Bm[:, cols]: eng.transpose(ps_B[:, 0:128], i_, ident[:]),
                         r=[("cv", 1), "ident"], w=["ps_B"])
                    p.ts(R(xdt[:]), ps_x[:, 0:64], dt[:, c:c + 1], None, ALU.mult, None, r=["ps_x", "dt"], w=["xdt"])
                    p.ts(R(Bw[:]), ps_B[:, 0:128], wdec[:, c:c + 1], None, ALU.mult, None, r=["ps_B", "wdec"], w=["Bw"])
                    p.ts(dtab[:], ones[:], dta[:, c:c + 1], None, ALU.mult, None, r=["ones", "dta"], w=["dtab"])
                    p.mm(ps_R[:, 0:128], dtab[:], tri[:], True, True, r=["dtab", "tri"], w=["ps_R"])
                    p.act(E[:], ps_R[:, 0:128], AF.Exp, r=["ps_R"], w=["E"])
                    p.tt(R(CE[:]), Cm[:, cols], E[:], ALU.mult, r=[("cv", 2), "E"], w=["CE"])
                    p.tt(Rm[:], ps_R[:, 0:128], nm[:], ALU.add, r=["ps_R", "nm"], w=["Rm"])
                    p.act(LT[:], Rm[:], AF.Exp, r=["Rm", "nacs"], w=["LT"], bias=nacs[:, c:c + 1])
                    p.mm(ps_CB[:, 0:128], R(Bm[:, cols]), R(Cm[:, cols]), True, True, r=[("cv", 1), ("cv", 2)], w=["ps_CB"])
                    p.tt(R(MT[:]), ps_CB[:, 0:128], LT[:], ALU.mult, r=["ps_CB", "LT"], w=["MT"])
                    p.mm(ps_y[0:64, 0:128], R(xdt[:]), R(MT[:]), True, False, r=["MT", "xdt"], w=["ps_y"])
                    p.mm(ps_y[0:64, 0:128], R(hT[:]), R(CE[:]), False, True, r=["CE", "hT"], w=["ps_y"])
                    p.copy(yT[:, cols], ps_y[0:64, 0:128], r=["ps_y"], w=["yT"], e="act")
                    if d == 0:
                        p.stt(yT[:, cols], xs[0:64, cols], scal[0:64, 2:3], yT[:, cols], ALU.mult, ALU.add,
                              r=[("cv", 0), "scal", "yT"], w=["yT"])
                    p.mm(ps_h[:, 0:64], R(Bw[:]), R(xdt[:]), True, True, r=["Bw", "xdt"], w=["ps_h"])
                    p.ts(R(hT[:]), hT[:], dec[:, c:c + 1], None, ALU.mult, None, r=["hT", "dec"], w=["hT"])
                    p.tt(R(hT[:]), hT[:], ps_h[:, 0:64], ALU.add, r=["hT", "ps_h"], w=["hT"])
                send_rows(yT, 192 + 64 * d, "yT")
                p.barrier()
        with p.scope():
            BL = 512
            Q = p.sb([64, TT]); Kt = p.sb([64, TT]); LA = p.sb([64, TT]); Bt = p.sb([64, TT])
            vtm = p.sb([64, NCG, 64])
            vflat = vtm[:, :, :].rearrange("p c v -> p (c v)")
            gw = p.sb([16, 64]); gb = p.sb([64, 1]); ngb = p.sb([64, 1])
            rot = p.sb([64, 64]); um = p.sb([64, 64]); cmt = p.sb([64, 2112])
            obg = [p.sb([64, 512]) for _ in range(2)]
            csb = [p.sb([64, 2, BL]) for _ in range(2)]
            t1 = p.sb([64, BL]); t2 = p.sb([64, BL])
            ebl = p.sb([64, NCG])
            attT = p.sb([64, 64]); ktm = p.sb([64, 64]); S_ = p.sb([64, 64]); zt = p.sb([64, 64])
            qd = p.sb([64, 64]); kd = p.sb([64, 64])
            ps_l = p.ps([128, 512]); ps_r = p.ps([128, 512])
            ps_a = p.ps([128, 512]); ps_t = p.ps([128, 512]); ps_o = p.ps([128, 512]); ps_kv = p.ps([128, 512])
            nblk = (TT + BL - 1) // BL
            p.load(R(rot[:]), R(rot_d), w=["rot"]); p.memset(zt[:], 0.0, w=["zt"])
            p.load(cmt[:], gcm_d[:, 0:2112], w=["cmt"])
            gather_seq(Q, 64, 8, "Q", rr=True)
            gather_seq(Kt, 64, 9, "K", rr=True)
            for bi in range(nblk):
                c0 = bi * BL
                cn = min(BL, TT - c0)
                cb = csb[bi % 2]
                p.load(cb[:, 0, 0:cn], gcs_d[0][:, c0:c0 + cn], w=[("cs", bi % 2)])
                p.load(cb[:, 1, 0:cn], gcs_d[1][:, c0:c0 + cn], w=[("cs", bi % 2)])
                for X, xk in ((Q, "Q"), (Kt, "K")):
                    p.mm(ps_r[0:64, 0:cn], R(rot[:]), R(X[:, c0:c0 + cn]), True, True, r=["rot", xk], w=["ps_r"])
                    p.tt(t1[:, 0:cn], ps_r[0:64, 0:cn], cb[:, 1, 0:cn], ALU.mult, r=["ps_r", ("cs", bi % 2)], w=["t1"])
                    p.tt(t2[:, 0:cn], X[:, c0:c0 + cn], cb[:, 0, 0:cn], ALU.mult, r=[xk, ("cs", bi % 2)], w=["t2"])
                    p.tt(R(X[:, c0:c0 + cn]), t1[:, 0:cn], t2[:, 0:cn], ALU.add, r=["t1", "t2"], w=[xk])
            gather_seq(LA, 64, 10, "LA")
            for r0 in range(0, NCG, 8):
                n8 = min(8, NCG - r0)
                for a in range(n8):
                    r_ = r0 + a
                    p.op("pe", lambda eng, o=ps_t[0:64, a * 64:(a + 1) * 64], i_=LA[:, 64 * r_:64 * r_ + 64]:
                         eng.transpose(o, i_, ident[0:64, 0:64]), r=["LA", "ident"], w=["ps_t"])
                p.copy(R(vtm[:, r0:r0 + n8, :]), ps_t[0:64, 0:n8 * 64].rearrange("p (a m) -> p a m", a=n8), r=["ps_t"], w=["v"],
                       e="act" if (r0 // 8) % 2 == 0 else "dve")
            for d in range(2):
                rev = d == 1
                gather_seq(Bt, 16, 11 + d, "B")
                p.load(gw[:], ggw_d[l, d], w=["gw"]); p.load(gb[:], ggb_d[l, d], w=["gb"])
                p.load(um[:], um_d[d], w=["um"])
                p.ts(ngb[:], gb[:], -1.0, None, ALU.mult, None, r=["gb"], w=["ngb"])
                for bi in range(nblk):
                    c0 = bi * BL
                    cn = min(BL, TT - c0)
                    p.mm(ps_l[0:64, 0:cn], gw[:], Bt[0:16, c0:c0 + cn], True, True, r=["gw", "B"], w=["ps_l"])
                    p.act(t1[:, 0:cn], ps_l[0:64, 0:cn], AF.Exp, r=["ps_l", "ngb"], w=["t1"], scale=-1.0, bias=ngb[:, 0:1])
                    p.act(t1[:, 0:cn], t1[:, 0:cn], AF.Ln, r=["t1"], w=["t1"], bias=1.0)
                    p.ts(LA[:, c0:c0 + cn], t1[:, 0:cn], -1.0 / 16.0, None, ALU.mult, None, r=["t1"], w=["LA"])
                for h0 in range(0, TT, 2112):
                    p.op("dve", lambda eng, o=rv(Bt[:], h0, h0 + 2112, rev), a=rv(cmt[:], 0, 2112, rev),
                         b=rv(LA[:], h0, h0 + 2112, rev): eng.tensor_tensor_scan(o, a, b, 0.0, ALU.mult, ALU.add),
                         r=["cmt", "LA", "B"], w=["B"])
                p.act(LA[:], Bt[:], AF.Exp, r=["B"], w=["LA"])
                p.act(Bt[:], Bt[:], AF.Exp, r=["B"], w=["B"], scale=-1.0, bias=LNK)
                if not rev:
                    p.copy(ebl[:], LA[:, 63:TT:64], r=["LA"], w=["ebl"])
                else:
                    p.copy(ebl[:], LA[:, 0:TT:64], r=["LA"], w=["ebl"])
                p.copy(R(S_[:]), zt[:], r=["zt"], w=["S"])
                if not rev:
                    groups = [list(range(g, min(g + 8, NCG))) for g in range(0, NCG, 8)]
                else:
                    groups = [[3, 2, 1, 0]] + [list(range(g + 7, g - 1, -1)) for g in range(124, 3, -8)]
                for gi_, grp_ in enumerate(groups):
                  gmin = min(grp_)
                  ob = obg[gi_ % 2]
                  obk = ("obg", gi_ % 2)
                  for c in grp_:
                    cols = slice(64 * c, 64 * c + 64)
                    p.tt(R(qd[:]), Q[:, cols], LA[:, cols], ALU.mult, r=["Q", "LA"], w=["qd"])
                    p.tt(R(kd[:]), Kt[:, cols], Bt[:, cols], ALU.mult, r=["K", "B"], w=["kd"])
                    p.mm(ps_a[0:64, 0:64], R(kd[:]), R(qd[:]), True, True, r=["kd", "qd"], w=["ps_a"])
                    p.tt(R(attT[:]), ps_a[0:64, 0:64], um[:], ALU.mult, r=["ps_a", "um"], w=["attT"])
                    p.op("pe", lambda eng: eng.transpose(ps_t[0:64, 0:64], kd[:], ident[0:64, 0:64]),
                         r=["kd", "ident"], w=["ps_t"])
                    p.copy(R(ktm[:]), ps_t[0:64, 0:64], r=["ps_t"], w=["ktm"], e="act")
                    p.mm(ps_o[0:64, 0:64], R(vtm[:, c, :]), R(attT[:]), True, False, r=["attT", "v"], w=["ps_o"])
                    p.mm(ps_o[0:64, 0:64], R(S_[:]), R(qd[:]), False, True, r=["qd", "S"], w=["ps_o"])
                    p.copy(ob[:, (c - gmin) * 64:(c - gmin) * 64 + 64], ps_o[0:64, 0:64], r=["ps_o"], w=[obk], e="act")
                    p.mm(ps_kv[0:64, 0:64], R(ktm[:]), R(vtm[:, c, :]), True, True, r=["ktm", "v"], w=["ps_kv"])
                    p.ts(R(S_[:]), S_[:], ebl[:, c:c + 1], None, ALU.mult, None, r=["S", "ebl"], w=["S"])
                    p.stt(R(S_[:]), ps_kv[0:64, 0:64], ebl[:, c:c + 1], S_[:], ALU.mult, ALU.add, r=["ps_kv", "ebl", "S"], w=["S"])
                  send_rows(ob, 320 + 64 * d, obk, nat0=64 * gmin, nlen=64 * len(grp_))
        allgather(sB, gB, "sB", "gB")
        p.barrier()
        with p.scope():
            P_ = PW
            x = p.sb([128, KC, P_])
            hff = p.sb([128, 43, P_]); mi = hff
            mix = p.sb([128, KC, P_])
            wb = [p.sb([128, 8192]) for _ in range(2)]
            sc1 = None
            vec = p.sb([128, NV_C]); glu = p.sb([128, 4, 512]); onesN = p.sb([128, 128]); sq = p.sb([128, 512])
            mean = p.sb([128, P_]); rstd = p.sb([128, P_])
            t1 = p.sb([128, 4, P_]); t2 = p.sb([128, 4, P_]); gR = p.sb([128, 4, P_])
            ca = p.sb([128, P_]); cg = p.sb([128, P_]); zt = p.sb([128, 43, 1])
            ps_m = p.ps([128, 512]); ps_q = p.ps([128, 512]); psr = [p.ps([128, 512]) for _ in range(6)]
            pi = [0]

            def nps():
                pi[0] = (pi[0] + 1) % 6
                return psr[pi[0]], ("psr", pi[0])

            V_S5D, V_GLUB, V_SSDW, V_GLAW, V_LN, V_CW = 0, 4, 8, 12, 13, 77
            p.memset(onesN[:], 1.0, w=["ones"]); p.memset(zt[:], 0.0, w=["zt"])
            p.load(vec[:], vec_d[l], w=["vec"])
            p.load(R(glu[:]), R(glu_d[l].rearrange("(k p) m -> p k m", p=128)), w=["glu"])
            wi = [0]

            def nwb():
                wi[0] = (wi[0] + 1) % 2
                return wb[wi[0]], ("wb", wi[0])

            def stats():
                for k in range(KC):
                    p.act(sq[:, 0:P_], x[:, k, :], AF.Square, r=[("x", k)], w=["sq"])
                    p.mm(ps_m[:, 0:P_], onesN[:], x[:, k, :], k == 0, k == KC - 1, r=[("x", k), "ones"], w=["ps_m"])
                    p.mm(ps_q[:, 0:P_], onesN[:], sq[:, 0:P_], k == 0, k == KC - 1, r=["sq", "ones"], w=["ps_q"])
                p.ts(mean[:], ps_m[:, 0:P_], 1.0 / D, None, ALU.mult, None, r=["ps_m"], w=["mean"])
                p.tt(rstd[:], mean[:], mean[:], ALU.mult, r=["mean"], w=["rstd"])
                p.stt(rstd[:], ps_q[:, 0:P_], 1.0 / D, rstd[:], ALU.mult, ALU.subtract, r=["ps_q", "rstd"], w=["rstd"])
                p.ts(rstd[:], rstd[:], 1e-6, None, ALU.add, None, r=["rstd"], w=["rstd"])
                p.act(rstd[:], rstd[:], AF.Sqrt, r=["rstd"], w=["rstd"])
                p.op("dve", lambda eng: eng.reciprocal(rstd[:], rstd[:]), r=["rstd"], w=["rstd"])

            def ln_affine(gcol, bcol):
                stats()
                for k in range(KC):
                    p.tt(x[:, k, :], x[:, k, :], mean[:], ALU.subtract, r=[("x", k), "mean"], w=[("x", k)], e="pool")
                    p.tt(x[:, k, :], x[:, k, :], rstd[:], ALU.mult, r=[("x", k), "rstd"], w=[("x", k)])
                    p.act(x[:, k, :], x[:, k, :], AF.Identity, r=[("x", k), "vec"], w=[("x", k)],
                          scale=vec[:, gcol + k:gcol + k + 1], bias=vec[:, bcol + k:bcol + k + 1])

            for s in range(NPASS):
                w0 = PWIN[s]
                ncx = max(0, min(NCXF - w0, P_))

                def residual(ps, pk, m, v):
                    p.ts(x[:, m, :], x[:, m, :], ALPHA, None, ALU.mult, None, r=[("x", m)], w=[("x", m)], e="pool")
                    if ncx > 0:
                        p.stt(x[:, m, 0:ncx], ps[:, 0:ncx], MC(v, 1, m), x[:, m, 0:ncx], ALU.mult, ALU.add,
                              r=[("x", m), "modL", pk], w=[("x", m)])
                    p.stt(x[:, m, ncx:P_], ps[:, ncx:P_], MC(v, 0, m), x[:, m, ncx:P_], ALU.mult, ALU.add,
                          r=[("x", m), "modL", pk], w=[("x", m)])

                xv = xin.rearrange("(k p) n -> p k n", p=128)
                for k in range(KC):
                    p.load(x[:, k, :], xv[:, k, w0:w0 + P_], r=[f"xres{l % 2}"], w=[("x", k)])
                gch = {}
                order = [0, 1, None, 2, 3, 4, None, 5, 6, None]
                gi = 0
                for grp in range(10):
                    for kk in range(4):
                        k = grp * 4 + kk
                        if order[grp] is None:
                            lrow = {2: 0, 6: 512, 9: 1024}[grp] + 128 * kk
                            p.load(R(mi[:, k, :]), R(pxloc[lrow:lrow + 128, w0:w0 + P_]), r=["pxloc"], w=[("mi", grp), "hff"])
                        else:
                            col = (order[grp] * 4 + kk) * 4 + s
                            p.dma(lambda eng, o=R(mi[:, k, :]), col=col: eng.indirect_dma_start(
                                out=o, out_offset=None, in_=R(gB), in_offset=IndirectOffsetOnAxis(idxB[:, col:col + 1], 0)),
                                r=["gB", "idxB"], w=[("mi", grp), "hff"], q="pool")
                p.tt(t1[:], mi[:, 0:4, :], mi[:, 4:8, :], ALU.add, r=[("mi", 0), ("mi", 1)], w=["t1"])
                for k in range(4):
                    p.stt(t1[:, k, :], mi[:, 8 + k, :], vec[:, V_S5D + k:V_S5D + k + 1], t1[:, k, :], ALU.mult, ALU.add,
                          r=[("mi", 2), "vec", "t1"], w=["t1"])
                p.tt(t2[:], t1[:], t1[:], ALU.mult, r=["t1"], w=["t2"])
                p.ts(t2[:], t2[:], 0.044715, 1.0, ALU.mult, ALU.add, r=["t2"], w=["t2"])
                p.tt(t2[:], t2[:], t1[:], ALU.mult, r=["t1", "t2"], w=["t2"])
                p.act(t2[:], t2[:], AF.Sigmoid, r=["t2"], w=["t2"], scale=1.5957691216057308)
                p.tt(R(gR[:]), t1[:], t2[:], ALU.mult, r=["t1", "t2"], w=["gR"])
                for m in range(4):
                    ps, pk = nps()
                    for k in range(4):
                        p.mm(ps[:, 0:P_], R(glu[:, k, m * 128:(m + 1) * 128]), R(gR[:, k, :]), k == 0, k == 3,
                             r=["glu", "gR"], w=[pk])
                    p.act(t2[:, m, :], ps[:, 0:P_], AF.Sigmoid, r=[pk, "vec"], w=["t2"], bias=vec[:, V_GLUB + m:V_GLUB + m + 1])
                p.tt(R(mix[:, 0:4, :]), gR[:], t2[:], ALU.mult, r=["gR", "t2"], w=[("mix", 0)])
                p.copy(R(mix[:, 4:8, :]), mi[:, 12:16, :], r=[("mi", 3)], w=[("mix", 1)], e="pool")
                p.tt(t1[:], mi[:, 16:20, :], mi[:, 20:24, :], ALU.add, r=[("mi", 4), ("mi", 5)], w=["t1"])
                p.act(t2[:], mi[:, 24:28, :], AF.Silu, r=[("mi", 6)], w=["t2"])
                p.tt(t1[:], t1[:], t2[:], ALU.mult, r=["t1", "t2"], w=["t1"])
                p.tt(t2[:], t1[:], t1[:], ALU.mult, r=["t1"], w=["t2"])
                ps, pk = nps()
                for k in range(4):
                    p.mm(ps[:, 0:P_], onesN[:], t2[:, k, :], k == 0, k == 3, r=["ones", "t2"], w=[pk])
                p.ts(ca[:], ps[:, 0:P_], 1.0 / 512, 1e-6, ALU.mult, ALU.add, r=[pk], w=["ca"])
                p.act(ca[:], ca[:], AF.Sqrt, r=["ca"], w=["ca"])
                p.op("dve", lambda eng: eng.reciprocal(ca[:], ca[:]), r=["ca"], w=["ca"])
                for k in range(4):
                    p.stt(R(mix[:, 8 + k, :]), t1[:, k, :], vec[:, V_SSDW + k:V_SSDW + k + 1], ca[:], ALU.mult, ALU.mult,
                          r=["t1", "vec", "ca"], w=[("mix", 2)])
                p.tt(t1[:], mi[:, 28:32, :], mi[:, 32:36, :], ALU.add, r=[("mi", 7), ("mi", 8)], w=["t1"])
                p.tt(t2[:], t1[:], t1[:], ALU.mult, r=["t1"], w=["t2"])
                for k in range(4):
                    ps, pk = nps()
                    p.mm(ps[:, 0:P_], onesN[:], t2[:, k, :], True, True, r=["ones", "t2"], w=[pk])
                    p.ts(cg[:], ps[:, 0:P_], 1.0 / 128, 1e-6, ALU.mult, ALU.add, r=[pk], w=["cg"])
                    p.act(cg[:], cg[:], AF.Sqrt, r=["cg"], w=["cg"])
                    p.op("dve", lambda eng: eng.reciprocal(cg[:], cg[:]), r=["cg"], w=["cg"])
                    p.stt(t1[:, k, :], t1[:, k, :], vec[:, V_GLAW:V_GLAW + 1], cg[:], ALU.mult, ALU.mult,
                          r=["t1", "vec", "cg"], w=["t1"])
                p.act(t2[:], mi[:, 36:40, :], AF.Silu, r=[("mi", 9), "t2"], w=["t2"])
                p.tt(R(mix[:, 12:16, :]), t1[:], t2[:], ALU.mult, r=["t1", "t2"], w=[("mix", 3)])
                wov = wout_d[l].rearrange("(k p) m -> p k m", p=128)
                for g in range(4):
                    wbt, wk = nwb()
                    wview = wbt[:, :].rearrange("p (k m) -> p k m", k=KC)
                    p.load(R(wview), R(wov[:, :, g * 512:(g + 1) * 512]), w=[wk])
                    for mm_ in range(4):
                        m = g * 4 + mm_
                        ps, pk = nps()
                        for k in range(KC):
                            p.mm(ps[:, 0:P_], R(wview[:, k, mm_ * 128:(mm_ + 1) * 128]), R(mix[:, k, :]), k == 0, k == KC - 1,
                                 r=[wk, ("mix", k // 4)], w=[pk])
                        residual(ps, pk, m, 2)
                ln_affine(V_LN, V_LN + 16)
                if dbg and l == 0 and s == 0:
                    pass
                stats()
                for k in range(KC):
                    p.tt(R(mix[:, k, :]), x[:, k, :], mean[:], ALU.subtract, r=[("x", k), "mean"], w=[("mix", k // 4)], e="pool")
                    p.tt(R(mix[:, k, :]), mix[:, k, :], rstd[:], ALU.mult, r=["rstd", ("mix", k // 4)], w=[("mix", k // 4)])
                    if ncx > 0:
                        p.act(R(mix[:, k, 0:ncx]), mix[:, k, 0:ncx], AF.Identity, r=["modL", ("mix", k // 4)],
                              w=[("mix", k // 4)], scale=MC(4, 1, k), bias=MC(3, 1, k))
                    p.act(R(mix[:, k, ncx:P_]), mix[:, k, ncx:P_], AF.Identity, r=["modL", ("mix", k // 4)],
                          w=[("mix", k // 4)], scale=MC(4, 0, k), bias=MC(3, 0, k))
                edges = [(0, 0), (36, 4), (40, 8), (1068, 12)]
                for (gc, mcol) in edges:
                    a = gc - w0
                    if a < 0 or a + 4 > P_:
                        continue
                    for k in range(KC):
                        p.tt(R(mix[:, k, a:a + 4]), mix[:, k, a:a + 4], cmk[:, mcol:mcol + 4], ALU.mult,
                             r=["cmk", ("mix", k // 4)], w=[("mix", k // 4)])
                upv = up_d[l].rearrange("(k p) m -> p k m", p=128)
                p.copy(R(hff[:, :, 0:1]), zt[:], r=["zt"], w=["hff"] + [("mi", i) for i in range(10)])
                p.copy(R(hff[:, :, P_ - 1:P_]), zt[:], r=["zt"], w=["hff"])
                for j in range(43):
                    wbt, wk = nwb()
                    wview = wbt[:, 0:4096].rearrange("p (a k m) -> p a k m", a=2, k=KC)
                    p.load(R(wview[:, 0]), R(upv[:, :, j * 128:(j + 1) * 128]), w=[wk])
                    p.load(R(wview[:, 1]), R(upv[:, :, DFF + j * 128:DFF + (j + 1) * 128]), w=[wk])
                    outs = []
                    for a in range(2):
                        ps, pk = nps()
                        for k in range(KC):
                            p.mm(ps[:, 0:P_], R(wview[:, a, k, :]), R(mix[:, k, :]), k == 0, k == KC - 1,
                                 r=[wk, ("mix", k // 4)], w=[pk])
                        outs.append((ps, pk))
                    for a, (ps, pk) in enumerate(outs):
                        dst, dk = (ca, "ca") if a == 0 else (cg, "cg")
                        c = V_CW + (a * 43 + j) * 4
                        p.ts(dst[:, 1:P_ - 1], ps[:, 1:P_ - 1], vec[:, c + 1:c + 2], vec[:, c + 3:c + 4], ALU.mult, ALU.add,
                             r=[pk, "vec"], w=[dk])
                        p.stt(dst[:, 1:P_ - 1], ps[:, 0:P_ - 2], vec[:, c:c + 1], dst[:, 1:P_ - 1], ALU.mult, ALU.add,
                              r=[pk, "vec", dk], w=[dk])
                        p.stt(dst[:, 1:P_ - 1], ps[:, 2:P_], vec[:, c + 2:c + 3], dst[:, 1:P_ - 1], ALU.mult, ALU.add,
                              r=[pk, "vec", dk], w=[dk])
                    p.act(cg[:, 1:P_ - 1], cg[:, 1:P_ - 1], AF.Silu, r=["cg"], w=["cg"])
                    p.tt(R(hff[:, j, 1:P_ - 1]), ca[:, 1:P_ - 1], cg[:, 1:P_ - 1], ALU.mult, r=["ca", "cg"], w=["hff"], e="pool")
                dnv = dn_d[l].rearrange("(j p) m -> p j m", p=128)
                for m in range(KC):
                    wbt, wk = nwb()
                    wview = wbt[:, 0:43 * 128].rearrange("p (j m) -> p j m", j=43)
                    p.load(R(wview), R(dnv[:, :, m * 128:(m + 1) * 128]), w=[wk])
                    ps, pk = nps()
                    for j in range(43):
                        p.mm(ps[:, 0:P_], R(wview[:, j, :]), R(hff[:, j, :]), j == 0, j == 42, r=[wk, "hff"], w=[pk])
                    residual(ps, pk, m, 5)
                ln_affine(V_LN + 32, V_LN + 48)
                xo = xout.rearrange("(k p) n -> p k n", p=128)
                p.store(xo[:, :, w0 + 1:w0 + P_ - 1], x[:, :, 1:P_ - 1], r=[("x", k) for k in range(KC)], w=[f"xres{(l + 1) % 2}"])
        p.barrier()
    p.maxops = 1 << 60
    p.barrier()
    if dbg:
        p.dma(lambda eng: eng.dma_start(out=dbg_o["sB"], in_=sB), r=["sB"], w=["dbgB"])
    fin = xres[nlayers % 2]
    p.dma(lambda eng: eng.dma_start(out=out_d, in_=fin[:, 44:1068]), r=[f"xres{nlayers % 2}"], w=["out"])
    if dbg:
        p.dma(lambda eng: eng.dma_start(out=dbg_o["x1"], in_=fin), r=[f"xres{nlayers % 2}"], w=["dbgx"])
    p.finish()
    p.emit()
    return nc


def fused_inputs(P, nlayers=DEPTH):
    NL = nlayers
    xfull = np.concatenate([P["ctx"][0], P["x"][0]], 0)
    jj = np.arange(128)
    tri2 = np.stack([(jj[:, None] <= jj[None, :]), (jj[:, None] >= jj[None, :])]).astype(np.float32)
    negmask2 = np.where(tri2 > 0, 0.0, NEG).astype(np.float32)
    j6 = np.arange(64)
    umask2 = np.stack([(j6[:, None] <= j6[None, :]), (j6[:, None] >= j6[None, :])]).astype(np.float32)
    rot = np.zeros((64, 64), np.float32)
    for m in range(32):
        rot[m + 32, m] = -1.0
        rot[m, m + 32] = 1.0
    cmask = np.ones((64, TT), np.float32)
    cmask[:, ::64] = 0.0
    c2, s2 = rope_tables()
    gcs = np.ascontiguousarray(np.stack([c2.T, s2.T]))
    tau1 = np.tile(np.arange(1, 129, dtype=np.float32)[None, :], (128, 1))
    ident = np.eye(128, dtype=np.float32)
    cvh = np.stack([chunkcols(P["c"].reshape(-1)), chunkcols(P["c_ctx"].reshape(-1))], 2)
    vecs = np.zeros((DEPTH, 128, NV_C), np.float32)
    for l in range(DEPTH):
        vec = vecs[l]
        vec[:, 0:4] = chunkcols(P["s5_d"][l]); vec[:, 4:8] = chunkcols(P["s5_glu_b"][l])
        vec[:, 8:12] = chunkcols(P["ssd_norm_w"][l]); vec[:, 12:13] = chunkcols(P["gla_norm_w"][l])
        vec[:, 13:29] = chunkcols(P["ln_g"][l, 0]); vec[:, 29:45] = chunkcols(P["ln_b"][l, 0])
        vec[:, 45:61] = chunkcols(P["ln_g"][l, 1]); vec[:, 61:77] = chunkcols(P["ln_b"][l, 1])
        cw = P["ffn_conv_w"][l]
        cvv = np.stack([chunkcols(cw[0]), chunkcols(cw[1]), chunkcols(cw[2]), chunkcols(P["ffn_conv_b"][l])], 2)
        vec[:, 77:] = cvv.reshape(128, 86 * 4)
    shared = {"w_in": P["w_in"][:NL], "w_out": P["w_out"][:NL], "ffn_up": P["ffn_up"][:NL], "ffn_down": P["ffn_down"][:NL],
              "glu_w": P["s5_glu_w"][:NL], "vecs": vecs[:NL], "gla_cs": gcs, "tau1": tau1, "ident": ident, "tri2": tri2,
              "negmask2": negmask2, "rot": rot, "cmask": cmask, "umask2": umask2, "cv": cvh}
    in_maps = []
    pp = np.arange(128)
    for i in range(NCORES):
        g = i // 4
        hh, vh = i // 2, i % 2
        m = dict(shared)
        cr = np.arange(32 * i - 4, 32 * i + 36)
        lr = np.arange(1024 * i - 4, 1024 * i + 1028)
        idx = np.concatenate([np.where((cr >= 0) & (cr < CTX), cr, -1), np.where((lr >= 0) & (lr < SEQ), lr + CTX, -1)])
        m["x0T"] = _gather_T(xfull, idx)
        ex = (idx >= 0).astype(np.float32)
        flags = np.concatenate([ex[0:4], ex[36:40], ex[40:44], ex[1068:1072]])
        m["colmask"] = np.tile(flags[None, :], (128, 1)).astype(np.float32)
        bases = [64 * i, 512 + 64 * i, 1024 + 64 * i, 1536 + 64 * i, 2048 + 64 * i, 2560 + 128 * g, 2816 + 128 * g,
                 None, 3088 + 64 * hh, 3344 + 64 * hh, 3600 + 128 * hh + 64 * vh, 4112, 4128]
        idxA = np.zeros((128, NGT * 8), np.uint32)
        for t, b in enumerate(bases):
            for r in range(8):
                if t == 7:
                    col = np.zeros(128, np.int64)
                    col[0] = 3072 + i
                    col[1] = 3080 + i
                else:
                    col = b + pp
                idxA[:, t * 8 + r] = (r * SROWS + np.minimum(col, SROWS - 1)).astype(np.uint32)
        m["idxA"] = idxA
        idxB = np.zeros((128, 28 * 4), np.uint32)
        for gq in range(7):
            for kk in range(4):
                for s in range(4):
                    r = 2 * kk + pp // 64
                    rowid = 64 * gq + pp % 64
                    idxB[:, (gq * 4 + kk) * 4 + s] = (r * (BROWS * 32) + rowid * 32 + i * 4 + s).astype(np.uint32)
        m["idxB"] = idxB
        dtsel = np.zeros((16, 2), np.float32)
        dtsel[i, 0] = 1.0
        dtsel[8 + i, 1] = 1.0
        m["dtsel"] = dtsel
        l_, h_ = i // 2, i % 2
        m["wmod"] = np.ascontiguousarray(P["w_mod"][l_][:, h_ * 6144:(h_ + 1) * 6144])
        m["bmod"] = chunkcols(P["b_mod"][l_][h_ * 6144:(h_ + 1) * 6144])
        lanep = np.zeros((DEPTH, 2, 2, 128, 3), np.float32)
        bre = np.zeros((DEPTH, 2, 2, 128, 16), np.float32); bim = np.zeros((DEPTH, 2, 2, 128, 16), np.float32)
        cre = np.zeros((DEPTH, 2, 2, 128, 64), np.float32); cim = np.zeros((DEPTH, 2, 2, 128, 64), np.float32)
        nab = np.zeros((DEPTH, 5, 128, 576), np.float32)
        scw = np.zeros((DEPTH, 128, 3, 4), np.float32); ssc = np.zeros((DEPTH, 2, 128, 4), np.float32)
        ggw = np.zeros((DEPTH, 2, 16, 64), np.float32); ggb = np.zeros((DEPTH, 2, 64, 1), np.float32)
        colsets = [np.arange(64 * i, 64 * i + 64), 512 + np.arange(128 * g, 128 * g + 128), 768 + np.arange(128 * g, 128 * g + 128)]
        for l in range(DEPTH):
            for d in range(2):
                for t in range(2):
                    for h in range(2):
                        gl = 2 * t + h
                        gg_ = 4 * i + gl
                        rows = slice(64 * h, 64 * h + 64)
                        lanep[l, d, t, rows, 0] = P["s5_lam_re"][l, d, gg_]
                        lanep[l, d, t, rows, 1] = P["s5_lam_im"][l, d, gg_]
                        lanep[l, d, t, rows, 2] = P["s5_log_step"][l, d, gg_]
                        bre[l, d, t, rows] = P["s5_b_re"][l, d, gg_]; bim[l, d, t, rows] = P["s5_b_im"][l, d, gg_]
                        cre[l, d, t, rows, 16 * gl:16 * gl + 16] = P["s5_c_re"][l, d, gg_].T
                        cim[l, d, t, rows, 16 * gl:16 * gl + 16] = P["s5_c_im"][l, d, gg_].T
                ssc[l, d, :, 0] = P["ssd_dt_bias"][l, d, i]; ssc[l, d, :, 1] = P["ssd_a_log"][l, d, i]; ssc[l, d, :, 2] = P["ssd_d"][l, i]
                ggw[l, d] = P["gla_gate_w"][l, d][:, 64 * hh:64 * hh + 64]
                ggb[l, d, :, 0] = P["gla_gate_b"][l, d][64 * hh:64 * hh + 64]
            nab[l] = na_bias_tables(P["na_rpb"][l, i])
            for ch, cs_ in enumerate(colsets):
                scw[l, :len(cs_), ch, 0:3] = P["ssd_conv_w"][l][:, cs_].T
                scw[l, :len(cs_), ch, 3] = P["ssd_conv_b"][l][cs_]
        m.update({"lanep": lanep[:NL], "bre": bre[:NL], "bim": bim[:NL], "cre": cre[:NL], "cim": cim[:NL], "nabias": nab[:NL],
                  "ssd_cw": scw[:NL], "ssd_scal": ssc[:NL], "gla_gw": ggw[:NL], "gla_gb": ggb[:NL]})
        in_maps.append(m)
    return in_maps


def kernel_fused(nlayers=DEPTH, dbg=False, **inputs):
    P = {k: np.asarray(v, np.float32) for k, v in inputs.items()}
    nc = _prog(("F", nlayers, dbg), lambda: build_fused(nlayers, dbg))
    res = run_spmd(nc, fused_inputs(P, nlayers))
    out = np.concatenate([res[i]["outT"].T for i in range(NCORES)], 0)
    return np.ascontiguousarray(out[None]).astype(np.float32), res


def _pass_rows_f(j, s):
    cr = np.arange(32 * j - 4, 32 * j + 36)
    lr = np.arange(1024 * j - 4, 1024 * j + 1028)
    idx = np.concatenate([np.where((cr >= 0) & (cr < CTX), cr, -1), np.where((lr >= 0) & (lr < SEQ), lr + CTX, -1)])
    return idx[PWIN[s]:PWIN[s] + PW]
```
